# Optimizing a Trainium2 kernel written in Bass

```python
import math
import jax, jax.numpy as jnp
from jax import lax
import numpy as np

D_MODEL = 1024
BATCH = 2
SEQ = 8192
DEPTH = 2

POOL_GROUPS = 4
POOL_GROUP_DIM = 128
POOL_WINDOWS = (2, 4, 8, 16)
POOL_DIM = POOL_GROUPS * POOL_GROUP_DIM
MLA_HEADS = 8
MLA_NOPE = 64
MLA_ROPE = 32
MLA_V = 64
MLA_Q_RANK = 384
MLA_KV_RANK = 256
ROPE_THETA = 10000.0
Q_BLOCK = 128
POS_OFFSET_MAX = 4096
GLA_HEADS = 4
GLA_DK = 64
GLA_DV = 128
GLA_GATE_RANK = 16
GLA_GATE_NORM = 16.0
GLA_CHUNK = 64
N_BRANCHES = 3
N_EXPERTS = 16
CAPACITY_FACTOR = 2
D_EXPERT = 2048
DN_ALPHA = (2 * DEPTH) ** 0.25
DN_BETA = (8 * DEPTH) ** (-0.25)
LN_EPS = 1e-5
RMS_EPS = 1e-6

IN_SPLITS = (
    POOL_DIM,
    MLA_Q_RANK,
    MLA_KV_RANK,
    MLA_ROPE,
    GLA_HEADS * GLA_DK,
    GLA_HEADS * GLA_DK,
    GLA_HEADS * GLA_DV,
    GLA_HEADS * GLA_DV,
    2 * GLA_GATE_RANK,
    N_BRANCHES * D_MODEL,
)
D_IN = sum(IN_SPLITS)

kernel_name = "hybrid_pool_mla_gla_ecmoe_deepnorm"


def layer_norm(x, g, b):
    xf = x.astype(jnp.float32)
    mu = jnp.mean(xf, axis=-1, keepdims=True)
    var = jnp.mean(jnp.square(xf - mu), axis=-1, keepdims=True)
    return ((xf - mu) * lax.rsqrt(var + LN_EPS) * g + b).astype(x.dtype)


def rms_norm(x, g):
    xf = x.astype(jnp.float32)
    ms = jnp.mean(jnp.square(xf), axis=-1, keepdims=True)
    return (xf * lax.rsqrt(ms + RMS_EPS) * g).astype(x.dtype)


def split_columns(proj):
    idx, acc = [], 0
    for s in IN_SPLITS[:-1]:
        acc += s
        idx.append(acc)
    return jnp.split(proj, idx, axis=-1)


def pool_mixer(u, w_group, scale):
    B, S, _ = u.shape
    uf = u.astype(jnp.float32).reshape(B, S, POOL_GROUPS, POOL_GROUP_DIM)
    csum = jnp.concatenate(
        [jnp.zeros((B, 1, POOL_GROUPS, POOL_GROUP_DIM), jnp.float32), jnp.cumsum(uf, axis=1)], axis=1)
    t = jnp.arange(S)
    outs = []
    for g, w in enumerate(POOL_WINDOWS):
        lo = jnp.clip(t - w // 2, 0, S)
        hi = jnp.clip(t + w // 2, 0, S)
        cg = csum[:, :, g]
        cnt = (hi - lo).astype(jnp.float32)[None, :, None]
        outs.append((cg[:, hi] - cg[:, lo]) / cnt - uf[:, :, g])
    pooled = jnp.stack(outs, axis=2)
    mixed = jnp.einsum('bsgc,gcd->bsgd', pooled, w_group.astype(jnp.float32)).reshape(B, S, POOL_DIM)
    return (mixed * scale).astype(u.dtype)


def rope_angles(positions, dim):
    half = dim // 2
    freqs = ROPE_THETA ** (-jnp.arange(half, dtype=jnp.float32) / half)
    ang = positions.astype(jnp.float32)[..., None] * freqs
    return jnp.cos(ang), jnp.sin(ang)


def apply_rope(x, cos, sin):
    half = x.shape[-1] // 2
    xf = x.astype(jnp.float32)
    x1, x2 = xf[..., :half], xf[..., half:]
    return jnp.concatenate([x1 * cos - x2 * sin, x1 * sin + x2 * cos], axis=-1).astype(x.dtype)


def mla(c_q, c_kv, k_r, positions, q_norm, w_uq, kv_norm, w_ukv):
    B, S, _ = c_q.shape
    q = jnp.einsum('bsr,rn->bsn', rms_norm(c_q, q_norm), w_uq).reshape(B, S, MLA_HEADS, MLA_NOPE + MLA_ROPE)
    kv = jnp.einsum('bsr,rn->bsn', rms_norm(c_kv, kv_norm), w_ukv).reshape(B, S, MLA_HEADS, MLA_NOPE + MLA_V)
    q_nope, q_rope = q[..., :MLA_NOPE], q[..., MLA_NOPE:]
    k_nope, v = kv[..., :MLA_NOPE], kv[..., MLA_NOPE:]
    cos, sin = rope_angles(positions, MLA_ROPE)
    q_rope = apply_rope(q_rope, cos[:, :, None, :], sin[:, :, None, :])
    k_rope = apply_rope(k_r, cos, sin)
    scale = (MLA_NOPE + MLA_ROPE) ** -0.5
    nb = S // Q_BLOCK
    qn_b = (q_nope * scale).reshape(B, nb, Q_BLOCK, MLA_HEADS, MLA_NOPE).transpose(1, 0, 2, 3, 4)
    qr_b = (q_rope * scale).reshape(B, nb, Q_BLOCK, MLA_HEADS, MLA_ROPE).transpose(1, 0, 2, 3, 4)

    def attend(blk):
        qn, qr = blk
        s = (jnp.einsum('bqhd,bkhd->bhqk', qn, k_nope).astype(jnp.float32)
             + jnp.einsum('bqhr,bkr->bhqk', qr, k_rope).astype(jnp.float32))
        p = jax.nn.softmax(s, axis=-1).astype(v.dtype)
        return jnp.einsum('bhqk,bkhd->bqhd', p, v)

    o = lax.map(attend, (qn_b, qr_b))
    return o.transpose(1, 0, 2, 3, 4).reshape(B, S, MLA_HEADS * MLA_V)


def gla_direction(q, k, v, g):
    B, S, H, DK = q.shape
    L = GLA_CHUNK
    N = S // L

    def resh(t):
        return t.reshape(B, N, L, H, t.shape[-1]).transpose(0, 3, 1, 2, 4)

    q, k, v, g = resh(q), resh(k), resh(v), resh(g)
    b = jnp.cumsum(g, axis=3)
    b_last = b[:, :, :, -1:, :]
    q_in = q * jnp.exp(b)
    k_in = k * jnp.exp(-b)
    k_st = k * jnp.exp(b_last - b)
    mask = jnp.tril(jnp.ones((L, L), dtype=bool))
    a = jnp.where(mask, jnp.einsum('bhnid,bhnjd->bhnij', q_in, k_in), 0.0)
    o_intra = jnp.einsum('bhnij,bhnjv->bhniv', a, v)
    dec = jnp.exp(b_last[:, :, :, 0, :])
    kv_chunk = jnp.einsum('bhnjd,bhnjv->bhndv', k_st, v)

    def step(state, inp):
        dec_n, kv_n = inp
        return dec_n[..., None] * state + kv_n, state

    init = jnp.zeros((B, H, DK, v.shape[-1]), jnp.float32)
    _, s_prev = lax.scan(step, init, (jnp.moveaxis(dec, 2, 0), jnp.moveaxis(kv_chunk, 2, 0)))
    s_prev = jnp.moveaxis(s_prev, 0, 2)
    o_inter = jnp.einsum('bhnid,bhndv->bhniv', q_in, s_prev)
    return (o_intra + o_inter).transpose(0, 2, 3, 1, 4).reshape(B, S, H, v.shape[-1])


def gla(q, k, v, r, dlow, w_dec2, b_dec, norm_g):
    B, S, _ = q.shape
    qf = q.astype(jnp.float32).reshape(B, S, GLA_HEADS, GLA_DK) * (GLA_DK ** -0.5)
    kf = k.astype(jnp.float32).reshape(B, S, GLA_HEADS, GLA_DK)
    vf = v.astype(jnp.float32).reshape(B, S, GLA_HEADS, GLA_DV)
    logits = jnp.einsum('bsir,irk->bsik', dlow.reshape(B, S, 2, GLA_GATE_RANK), w_dec2) + b_dec
    g = (jax.nn.log_sigmoid(logits.astype(jnp.float32)) / GLA_GATE_NORM).reshape(B, S, 2, GLA_HEADS, GLA_DK)
    o_f = gla_direction(qf, kf, vf, g[:, :, 0])
    flip = lambda t: jnp.flip(t, axis=1)
    o_b = flip(gla_direction(flip(qf), flip(kf), flip(vf), flip(g[:, :, 1])))
    o = rms_norm(o_f + o_b, norm_g).reshape(B, S, GLA_HEADS * GLA_DV)
    return (o * jax.nn.silu(r.astype(jnp.float32))).astype(q.dtype)


def hybrid_mixer(h, positions, w_in, b_gate, pool_w, pool_scale, w_up_a,
                 mla_q_norm, mla_w_uq, mla_kv_norm, mla_w_ukv, w_up_b,
                 gla_w_dec, gla_b_dec, gla_norm, w_up_c, w_out):
    B, S, D = h.shape
    proj = jnp.einsum('bsd,dn->bsn', h, w_in)
    u_pool, c_q, c_kv, k_r, g_q, g_k, g_v, g_r, g_dec, gate_logits = split_columns(proj)
    y_a = jnp.einsum('bsc,cd->bsd', pool_mixer(u_pool, pool_w, pool_scale), w_up_a)
    y_b = jnp.einsum('bsc,cd->bsd', mla(c_q, c_kv, k_r, positions, mla_q_norm, mla_w_uq, mla_kv_norm, mla_w_ukv), w_up_b)
    y_c = jnp.einsum('bsc,cd->bsd', gla(g_q, g_k, g_v, g_r, g_dec, gla_w_dec, gla_b_dec, gla_norm), w_up_c)
    gates = jax.nn.sigmoid((gate_logits + b_gate).astype(jnp.float32)).reshape(B, S, N_BRANCHES, D).astype(h.dtype)
    merged = gates[:, :, 0] * y_a + gates[:, :, 1] * y_b + gates[:, :, 2] * y_c
    return jnp.einsum('bsd,de->bse', merged, w_out)


def expert_choice_ffn(h, router_w, w_gate, w_up, w_down):
    B, S, D = h.shape
    cap = CAPACITY_FACTOR * S // N_EXPERTS
    affinity = jax.nn.softmax(jnp.einsum('bsd,de->bse', h, router_w).astype(jnp.float32), axis=-1)
    gate, idx = lax.top_k(jnp.transpose(affinity, (0, 2, 1)), cap)
    xe = jax.vmap(lambda hb, ib: hb[ib])(h, idx)
    hid = jax.nn.silu(jnp.einsum('becd,edf->becf', xe, w_gate)) * jnp.einsum('becd,edf->becf', xe, w_up)
    ye = jnp.einsum('becf,efd->becd', hid, w_down) * gate[..., None].astype(h.dtype)
    return jax.vmap(lambda ib, yb: jnp.zeros((S, D), yb.dtype).at[ib.reshape(-1)].add(yb.reshape(-1, D)))(idx, ye)


def setup_inputs(seed: int = 0) -> dict:
    key = jax.random.key(seed)
    ks = jax.random.split(key, 32)
    L = DEPTH
    f32 = jnp.float32

    def w(k, shape, fan_in, gain=1.0):
        return jax.random.normal(k, shape, f32) * (gain * fan_in ** -0.5)

    def gain(k, shape):
        return 1.0 + 0.02 * jax.random.normal(k, shape, f32)

    def bias(k, shape):
        return 0.02 * jax.random.normal(k, shape, f32)

    x = jax.random.normal(ks[0], (BATCH, SEQ, D_MODEL), f32)
    positions = (jnp.arange(SEQ, dtype=jnp.int32)[None, :]
                 + jax.random.randint(ks[1], (BATCH, 1), 0, POS_OFFSET_MAX, dtype=jnp.int32))
    return {
        "x": x,
        "positions": positions,
        "ln0_g": gain(ks[2], (D_MODEL,)),
        "ln0_b": bias(ks[3], (D_MODEL,)),
        "w_in": w(ks[4], (L, D_MODEL, D_IN), D_MODEL),
        "b_gate": bias(ks[5], (L, N_BRANCHES * D_MODEL)),
        "pool_w": w(ks[6], (L, POOL_GROUPS, POOL_GROUP_DIM, POOL_GROUP_DIM), POOL_GROUP_DIM),
        "pool_scale": gain(ks[7], (L, POOL_DIM)),
        "w_up_a": w(ks[8], (L, POOL_DIM, D_MODEL), POOL_DIM),
        "mla_q_norm": gain(ks[9], (L, MLA_Q_RANK)),
        "mla_w_uq": w(ks[10], (L, MLA_Q_RANK, MLA_HEADS * (MLA_NOPE + MLA_ROPE)), MLA_Q_RANK),
        "mla_kv_norm": gain(ks[11], (L, MLA_KV_RANK)),
        "mla_w_ukv": w(ks[12], (L, MLA_KV_RANK, MLA_HEADS * (MLA_NOPE + MLA_V)), MLA_KV_RANK),
        "w_up_b": w(ks[13], (L, MLA_HEADS * MLA_V, D_MODEL), MLA_HEADS * MLA_V),
        "gla_w_dec": w(ks[14], (L, 2, GLA_GATE_RANK, GLA_HEADS * GLA_DK), GLA_GATE_RANK),
        "gla_b_dec": bias(ks[15], (L, 2, GLA_HEADS * GLA_DK)),
        "gla_norm": gain(ks[16], (L, GLA_DV)),
        "w_up_c": w(ks[17], (L, GLA_HEADS * GLA_DV, D_MODEL), GLA_HEADS * GLA_DV),
        "w_out": w(ks[18], (L, D_MODEL, D_MODEL), D_MODEL, DN_BETA),
        "ln1_g": gain(ks[19], (L, D_MODEL)),
        "ln1_b": bias(ks[20], (L, D_MODEL)),
        "router_w": w(ks[21], (L, D_MODEL, N_EXPERTS), D_MODEL),
        "exp_w_gate": w(ks[22], (L, N_EXPERTS, D_MODEL, D_EXPERT), D_MODEL),
        "exp_w_up": w(ks[23], (L, N_EXPERTS, D_MODEL, D_EXPERT), D_MODEL),
        "exp_w_down": w(ks[24], (L, N_EXPERTS, D_EXPERT, D_MODEL), D_EXPERT, DN_BETA),
        "ln2_g": gain(ks[25], (L, D_MODEL)),
        "ln2_b": bias(ks[26], (L, D_MODEL)),
    }


def reference(x, positions, ln0_g, ln0_b, w_in, b_gate, pool_w, pool_scale, w_up_a,
              mla_q_norm, mla_w_uq, mla_kv_norm, mla_w_ukv, w_up_b,
              gla_w_dec, gla_b_dec, gla_norm, w_up_c, w_out, ln1_g, ln1_b,
              router_w, exp_w_gate, exp_w_up, exp_w_down, ln2_g, ln2_b):
    h = layer_norm(x, ln0_g, ln0_b)
    for l in range(DEPTH):
        mix = hybrid_mixer(h, positions, w_in[l], b_gate[l], pool_w[l], pool_scale[l], w_up_a[l],
                           mla_q_norm[l], mla_w_uq[l], mla_kv_norm[l], mla_w_ukv[l], w_up_b[l],
                           gla_w_dec[l], gla_b_dec[l], gla_norm[l], w_up_c[l], w_out[l])
        h = layer_norm(DN_ALPHA * h + mix, ln1_g[l], ln1_b[l])
        ffn = expert_choice_ffn(h, router_w[l], exp_w_gate[l], exp_w_up[l], exp_w_down[l])
        h = layer_norm(DN_ALPHA * h + ffn, ln2_g[l], ln2_b[l])
    return h
```

```python
from contextlib import ExitStack
import math
import numpy as np
import concourse.bass as bass
import concourse.mybir as mybir
from concourse.bass_utils import run_bass_kernel_spmd

F32 = mybir.dt.float32
BF16 = mybir.dt.bfloat16
I32 = mybir.dt.int32
U32 = mybir.dt.uint32
AF = mybir.ActivationFunctionType
ALU = mybir.AluOpType
AX = mybir.AxisListType

COMPUTE = ("pe", "act", "dve", "pool")
DMAQ_K = 8


def dsize(dt):
    return {F32: 4, BF16: 2, I32: 4, U32: 4}[dt]


class Op:
    __slots__ = ("eng", "fn", "deps", "is_dma", "idx", "marked", "done", "q")

    def __init__(self, eng, fn, is_dma=False, q=None):
        self.eng = eng
        self.fn = fn
        self.deps = []
        self.is_dma = is_dma
        self.idx = None
        self.marked = False
        self.done = None
        self.q = q


class Prog:
    def __init__(self):
        self.nc = bass.Bass("TRN2", target_bir_lowering=False)
        self.stack = ExitStack()
        self.ops = {e: [] for e in ("pe", "act", "dve", "pool", "sp")}
        self.reg = {}
        self.dma_count = {"sp": 0, "pool": 0}
        self.dma_ops = {"sp": [], "pool": []}
        self.out_dmas = []
        self.same_engine_sync = True
        self.debug_out = set()

    def sb(self, name, shape, dt):
        return self.stack.enter_context(self.nc.sbuf_tensor(name, list(shape), dt))

    def ps(self, name, shape, dt):
        return self.stack.enter_context(self.nc.psum_tensor(name, list(shape), dt))

    def dram(self, name, shape, dt, kind="Internal"):
        if kind == "Internal" and name in self.debug_out:
            kind = "ExternalOutput"
        return self.nc.dram_tensor(name, list(shape), dt, kind=kind).ap()

    def _deps(self, op, reads, writes):
        for k in reads:
            st = self.reg.get(k)
            if st is None:
                st = self.reg[k] = [None, []]
            if st[0] is not None:
                op.deps.append(st[0])
            st[1].append(op)
        for k in writes:
            st = self.reg.get(k)
            if st is None:
                st = self.reg[k] = [None, []]
            if st[0] is not None:
                op.deps.append(st[0])
            for r in st[1]:
                if r is not op:
                    op.deps.append(r)
            st[0] = op
            st[1] = []

    def op(self, eng, fn, reads=(), writes=()):
        o = Op(eng, fn)
        self._deps(o, reads, writes)
        self.ops[eng].append(o)
        return o

    def dma(self, fn, reads=(), writes=(), q="sp", is_out=False):
        o = Op(q, fn, is_dma=True, q=q)
        o.idx = self.dma_count[q]
        self.dma_count[q] += 1
        self.dma_ops[q].append(o)
        self._deps(o, reads, writes)
        self.ops[q].append(o)
        if is_out:
            self.out_dmas.append(o)
        return o

    def barrier(self):
        lasts = []
        for e in COMPUTE:
            for o in reversed(self.ops[e]):
                if not o.is_dma and o.fn is not None:
                    lasts.append(o)
                    break
        for q in ("sp", "pool"):
            lasts.extend(self.dma_ops[q][-DMAQ_K:])
        for e in ("pe", "act", "dve", "pool", "sp"):
            o = Op(e, None)
            o.deps = list(lasts)
            self.ops[e].append(o)
        self.reg = {}

    def finalize(self):
        nc = self.nc
        sync_same = self.same_engine_sync

        def skip(d, o):
            return (not d.is_dma) and (not o.is_dma) and d.eng == o.eng and o.fn is not None and (d.eng == "pe" or not sync_same)

        for e, lst in self.ops.items():
            for o in lst:
                for d in o.deps:
                    if d.is_dma or skip(d, o):
                        continue
                    d.marked = True
        sems = {}
        for e in COMPUTE:
            sems[e] = self.stack.enter_context(nc.semaphore("s_" + e))
            n = 0
            for o in self.ops[e]:
                if o.is_dma or o.fn is None:
                    continue
                if o.marked:
                    n += 1
                    o.done = (sems[e], n)
        dsems = {}
        for q in ("sp", "pool"):
            if self.dma_count[q] == 0:
                continue
            dsems[q] = [self.stack.enter_context(nc.semaphore("d_%s%d" % (q, i))) for i in range(DMAQ_K)]
            for o in self.dma_ops[q]:
                o.done = (dsems[q][o.idx % DMAQ_K], 16 * (o.idx // DMAQ_K + 1))
        block = self.stack.enter_context(nc.Block())

        def emit(ename, e):
            waited = {}
            for o in self.ops[ename]:
                need = {}
                for d in o.deps:
                    if d.done is None or skip(d, o):
                        continue
                    s, v = d.done
                    if need.get(s.num, (None, 0))[1] < v:
                        need[s.num] = (s, v)
                if o.is_dma and o.idx >= DMAQ_K:
                    s = dsems[o.q][o.idx % DMAQ_K]
                    v = 16 * (o.idx // DMAQ_K)
                    if need.get(s.num, (None, 0))[1] < v:
                        need[s.num] = (s, v)
                for sn, (s, v) in need.items():
                    if waited.get(sn, 0) < v:
                        e.wait_ge(s, v)
                        waited[sn] = v
                if o.fn is None:
                    continue
                ins = o.fn(e)
                if o.is_dma:
                    ins.then_inc(o.done[0], 16)
                elif o.marked:
                    ins.then_inc(o.done[0], 1)
            if ename == "sp":
                for o in self.out_dmas:
                    s, v = o.done
                    if waited.get(s.num, 0) < v:
                        e.wait_ge(s, v)
                        waited[s.num] = v

        @block.tensor
        def _(e):
            emit("pe", e)

        @block.scalar
        def _(e):
            emit("act", e)

        @block.vector
        def _(e):
            emit("dve", e)

        @block.gpsimd
        def _(e):
            emit("pool", e)

        @block.sync
        def _(e):
            emit("sp", e)

        self.stack.close()
        return nc


class Arena:
    def __init__(self, P, kbytes):
        self.words = kbytes * 256
        self.t = P.sb("arena", [128, self.words], F32)
        self.off = 0

    def reset(self, to=0):
        self.off = to

    def alloc(self, free_shape, dt, parts=128, p0=0):
        n = 1
        for s in free_shape:
            n *= s
        nb = n * dsize(dt)
        w = (nb + 31) // 32 * 8
        assert self.off + w <= self.words, ("arena overflow", self.off, w, self.words)
        v = self.t[p0:p0 + parts, self.off:self.off + w]
        self.off += w
        if dt != F32:
            v = v.bitcast(dt)
        v = v[:, 0:n]
        if len(free_shape) == 2:
            v = v.rearrange("p (a b) -> p a b", a=free_shape[0])
        elif len(free_shape) == 3:
            v = v.rearrange("p (a b c) -> p a b c", a=free_shape[0], b=free_shape[1])
        return v


S = 8192
D = 1024
NT = S // 128
NS = S // 512
POOL_WINDOWS = (2, 4, 8, 16)
LN_EPS = 1e-5
RMS_EPS = 1e-6
DN_ALPHA = 4 ** 0.25
QSCALE = 96 ** -0.5
TWO_PI = 2.0 * math.pi
CW1 = 6.28125
CW2 = TWO_PI - CW1

C_UP, C_CQ, C_CKV, C_KR, C_GQ, C_GK, C_GV, C_GR, C_GD, C_GATE = 0, 512, 896, 1152, 1184, 1440, 1696, 2208, 2720, 2752


class K:
    def __init__(self, debug_out=(), layers=(0, 1), phases=None, ns_limit=None, skip=(), no_big=False):
        self.ns_limit = ns_limit
        self.skip = set(skip)
        self.no_big = no_big
        self.P = P = Prog()
        P.debug_out = set(debug_out)
        self.layers = layers
        self.phases = phases
        self.nc = P.nc
        self.A = Arena(P, 204)
        self.banks = [P.ps("bank%d" % i, [128, 512], F32) for i in range(8)]
        self.inp = {}
        self.scr = {}

    def din(self, name, shape, dt=F32):
        self.inp[name] = self.P.dram(name, shape, dt, kind="ExternalInput")
        return self.inp[name]

    def scratch(self, name, shape, dt):
        self.scr[name] = self.P.dram(name, shape, dt)
        return self.scr[name]

    def mm(self, out, lhsT, rhs, start, stop, reads, writes):
        self.P.op("pe", lambda e: e.matmul(out, lhsT=lhsT, rhs=rhs, start=start, stop=stop), reads, writes)

    def tr(self, out, in_, ident, reads, writes):
        self.P.op("pe", lambda e: e.transpose(out=out, in_=in_, identity=ident), reads, writes)

    def act(self, out, in_, func, reads, writes, bias=None, scale=None, accum_out=None):
        kw = {}
        if bias is not None:
            kw["bias"] = bias
        if scale is not None:
            kw["scale"] = scale
        if accum_out is not None:
            kw["accum_out"] = accum_out
        self.P.op("act", lambda e: e.activation(out=out, in_=in_, func=func, **kw), reads, writes)

    def ts(self, eng, out, in0, s1, s2, op0, op1, reads, writes):
        if op1 is None:
            self.P.op(eng, lambda e: e.tensor_scalar(out=out, in0=in0, scalar1=s1, scalar2=None, op0=op0), reads, writes)
        else:
            self.P.op(eng, lambda e: e.tensor_scalar(out=out, in0=in0, scalar1=s1, scalar2=s2, op0=op0, op1=op1), reads, writes)

    def tt(self, eng, out, in0, in1, op, reads, writes):
        self.P.op(eng, lambda e: e.tensor_tensor(out=out, in0=in0, in1=in1, op=op), reads, writes)

    def stt(self, eng, out, in0, scalar, in1, op0, op1, reads, writes):
        self.P.op(eng, lambda e: e.scalar_tensor_tensor(out=out, in0=in0, scalar=scalar, in1=in1, op0=op0, op1=op1), reads, writes)

    def cp(self, eng, out, in_, reads, writes):
        if eng == "act":
            self.P.op("act", lambda e: e.copy(out=out, in_=in_), reads, writes)
        else:
            self.P.op(eng, lambda e: e.tensor_copy(out=out, in_=in_), reads, writes)

    def memset(self, eng, ap, val, writes):
        self.P.op(eng, lambda e: e.memset(ap, val), (), writes)

    def ld(self, out, in_, reads, writes, q="sp", slow=False):
        if slow:
            self.P.dma(lambda e: e.dma_start(out=out, in_=in_, allow_slow_non_contiguous=True), reads, writes, q=q)
        else:
            self.P.dma(lambda e: e.dma_start(out=out, in_=in_), reads, writes, q=q)

    def st(self, out, in_, reads, writes, q="pool", is_out=False):
        self.P.dma(lambda e: e.dma_start(out=out, in_=in_), reads, writes, q=q, is_out=is_out)

    def declare(self):
        din = self.din
        din("x", [S, D])
        din("positions", [S], I32)
        din("ln0_g", [D]); din("ln0_b", [D])
        din("w_in", [2, D, 5824]); din("b_gate", [2, 3072])
        din("pool_w", [2, 4, 128, 128]); din("pool_scale", [2, 512]); din("w_up_a", [2, 512, D])
        din("mla_q_norm", [2, 384]); din("mla_w_uq", [2, 384, 768]); din("mla_kv_norm", [2, 256])
        din("mla_w_ukv", [2, 256, 1024]); din("w_up_b", [2, 512, D])
        din("gla_w_dec", [2, 2, 16, 256]); din("gla_b_dec", [2, 2, 256]); din("gla_norm", [2, 128])
        din("w_up_c", [2, 512, D]); din("w_out", [2, D, D])
        din("ln1_g", [2, D]); din("ln1_b", [2, D]); din("router_w", [2, D, 16])
        if not self.no_big:
            din("exp_w_gate", [2, 16, D, 2048]); din("exp_w_up", [2, 16, D, 2048]); din("exp_w_down", [2, 16, 2048, D])
        din("ln2_g", [2, D]); din("ln2_b", [2, D])
        din("c_ident", [128, 128])
        din("c_freq", [16])
        din("c_tri", [4, 64, 64])
        self.out = self.P.dram("out", [S, D], F32, kind="ExternalOutput")
        sc = self.scratch
        sc("hbuf", [S, D], F32)
        sc("cosF", [32, S], F32); sc("sinF", [32, S], F32)
        sc("uT", [512, S], F32)
        sc("gqT", [256, S], F32); sc("gkT", [256, S], F32); sc("gdT", [32, S], F32)
        sc("gk_tok", [S, 256], F32); sc("gv", [S, 512], BF16); sc("gr", [S, 512], BF16)
        sc("QT", [8, 96, S], BF16); sc("KT", [8, 64, S], BF16); sc("KRT", [32, S], BF16); sc("V", [S, 512], BF16)
        sc("pmT", [512, S], BF16); sc("mlT", [512, S], BF16); sc("glT", [512, S], BF16)
        sc("h1", [S, D], F32); sc("h1b", [S, D], BF16); sc("acc", [S, D], F32)

    def consts(self):
        A = self.A
        self.ident_f = A.alloc([128], F32)
        self.ident_b = A.alloc([128], BF16)
        self.eps_ln = A.alloc([1], F32)
        self.eps_rms = A.alloc([1], F32)
        self.cosT = A.alloc([NT, 16], F32)
        self.sinT = A.alloc([NT, 16], F32)
        self.ld(self.ident_f, self.inp["c_ident"][:, :], (), ["ident_f"])
        self.cp("dve", self.ident_b, self.ident_f, ["ident_f"], ["ident_b"])
        self.memset("pool", self.eps_ln, LN_EPS, ["eps_ln"])
        self.memset("pool", self.eps_rms, RMS_EPS, ["eps_rms"])
        self.IDX = A.alloc([S // 1024, 16], I32)
        self.GATE = A.alloc([S // 1024, 16], F32)
        self.const_top = A.off

    def range_reduce_sin(self, ang, tmp, tmpi, res, shift, k_ang, k_tmp, k_res):
        self.ts("dve", tmp, ang, 1.0 / TWO_PI, None, ALU.mult, None, [k_ang], [k_tmp])
        self.cp("dve", tmpi, tmp, [k_tmp], [k_tmp + "i"])
        self.cp("dve", tmp, tmpi, [k_tmp + "i"], [k_tmp])
        self.stt("dve", res, tmp, -CW1, ang, ALU.mult, ALU.add, [k_tmp, k_ang], [k_res])
        self.stt("dve", res, tmp, -CW2, res, ALU.mult, ALU.add, [k_tmp, k_res], [k_res])
        if shift != 0.0:
            self.ts("dve", res, res, shift, None, ALU.add, None, [k_res], [k_res])
        self.ts("dve", tmp, res, math.pi, None, ALU.is_gt, None, [k_res], [k_tmp])
        self.stt("dve", res, tmp, -TWO_PI, res, ALU.mult, ALU.add, [k_tmp, k_res], [k_res])
        self.ts("dve", res, res, math.pi, -math.pi, ALU.min, ALU.max, [k_res], [k_res])
        self.act(res, res, AF.Sin, [k_res], [k_res])

    def phase0(self):
        A, P = self.A, self.P
        A.reset(self.const_top)
        posi = A.alloc([NT], I32)
        posf = A.alloc([NT], F32)
        freq = A.alloc([16], F32)
        ang = A.alloc([NT, 16], F32)
        tmp = A.alloc([NT, 16], F32)
        tmpi = A.alloc([NT, 16], I32)
        pos = self.inp["positions"]
        pv = pos.rearrange("(j p) -> p j", p=128)
        for a in range(0, NT, 8):
            self.ld(posi[:, a:a + 8], pv[:, a:a + 8], (), ["posi"], slow=True)
        self.ld(freq, self.inp["c_freq"][None, :].to_broadcast([128, 16]), (), ["freq"])
        self.cp("dve", posf, posi, ["posi"], ["posf"])
        for j in range(NT):
            self.ts("dve", ang[:, j, :], freq, posf[:, j:j + 1], None, ALU.mult, None, ["posf", "freq"], ["angT"])
        self.range_reduce_sin(ang, tmp, tmpi, self.sinT, 0.0, "angT", "tmpT", "sinT")
        self.range_reduce_sin(ang, tmp, tmpi, self.cosT, math.pi / 2, "angT", "tmpT", "cosT")
        CH = min(2048, S)
        prow_i = A.alloc([CH], I32)
        prow = A.alloc([CH], F32)
        fcol = A.alloc([1], F32)
        scol_c = A.alloc([1], F32)
        scol_s = A.alloc([1], F32)
        angF = A.alloc([CH], F32)
        tmpF = A.alloc([CH], F32)
        tmpFi = A.alloc([CH], I32)
        resF = A.alloc([CH], F32)
        fr = self.inp["c_freq"]
        self.ld(fcol[64:80, :], fr.rearrange("(a b) -> a b", b=1), (), ["fcol"], slow=True)
        self.ld(fcol[80:96, :], fr.rearrange("(a b) -> a b", b=1), ["fcol"], ["fcol"], slow=True)
        self.memset("pool", scol_c[64:96, :], QSCALE, ["scol_c"])
        self.memset("pool", scol_s[64:96, :], QSCALE, ["scol_s"])
        self.memset("pool", scol_s[64:80, :], -QSCALE, ["scol_s"])
        sl = slice(64, 96)
        for c in range(S // CH):
            self.ld(prow_i[sl, :], pos[None, c * CH:(c + 1) * CH].to_broadcast([32, CH]), (), ["prow_i"])
            self.cp("dve", prow[sl, :], prow_i[sl, :], ["prow_i"], ["prow"])
            self.ts("dve", angF[sl, :], prow[sl, :], fcol[sl, 0:1], None, ALU.mult, None, ["prow", "fcol"], ["angF"])
            for (shift, scol, dst) in ((0.0, scol_s, "sinF"), (math.pi / 2, scol_c, "cosF")):
                self.range_reduce_sin(angF[sl, :], tmpF[sl, :], tmpFi[sl, :], resF[sl, :], shift, "angF", "tmpF", "resF")
                self.ts("dve", resF[sl, :], resF[sl, :], scol[sl, 0:1], None, ALU.mult, None, ["resF", "scol_s", "scol_c"], ["resF"])
                self.st(self.scr[dst][:, c * CH:(c + 1) * CH], resF[sl, :], ["resF"], [dst])
        P.barrier()

    def layernorm_tile(self, x, out_f, out_b, gam, bet, st6, mv, rstd, nb, key, reads, writes_f, writes_b):
        for c in range(2):
            self.P.op("dve", lambda e, c=c: e.bn_stats(out=st6[:, c, :], in_=x[:, c * 512:(c + 1) * 512]), reads, [(key, "st", c)])
        self.P.op("dve", lambda e: e.bn_aggr(out=mv, in_=st6.rearrange("p a b -> p (a b)")), [(key, "st", 0), (key, "st", 1)], [(key, "mv")])
        self.act(rstd, mv[:, 1:2], AF.Ln, [(key, "mv"), "eps_ln"], [(key, "rstd")], bias=self.eps_ln[:, 0:1], scale=1.0)
        self.act(rstd, rstd, AF.Exp, [(key, "rstd")], [(key, "rstd")], scale=-0.5)
        self.stt("dve", nb, mv[:, 0:1], -1.0, rstd, ALU.mult, ALU.mult, [(key, "mv"), (key, "rstd")], [(key, "nb")])
        tmpk = (key, "xn")
        self.act(self.ln_tmp, x, AF.Identity, list(reads) + [(key, "rstd"), (key, "nb")], [tmpk], bias=nb[:, 0:1], scale=rstd[:, 0:1])
        self.tt("dve", self.ln_tmp, self.ln_tmp, gam, ALU.mult, [tmpk, "lnpar"], [tmpk])
        if out_f is not None:
            self.tt("dve", out_f, self.ln_tmp, bet, ALU.add, [tmpk, "lnpar"], writes_f)
            if out_b is not None:
                self.cp("pool", out_b, out_f, writes_f, writes_b)
        else:
            self.tt("dve", out_b, self.ln_tmp, bet, ALU.add, [tmpk, "lnpar"], writes_b)


    def load_cast(self, dst, src, n_inner, key, reads=()):
        shp = list(dst.shape)
        if len(shp) == 3:
            C = shp[1]
            cstep = max(1, 1024 // shp[0])
            nsp = (n_inner + 2047) // 2048
            step = (n_inner + nsp - 1) // nsp
            for c0 in range(0, C, cstep):
                c1 = min(C, c0 + cstep)
                for a in range(0, n_inner, step):
                    b = min(n_inner, a + step)
                    self.P.dma(lambda e, a=a, b=b, c0=c0, c1=c1: e.dma_start(out=dst[:, c0:c1, a:b], in_=src[:, c0:c1, a:b]), reads, [key], q="pool")
        else:
            self.P.dma(lambda e: e.dma_start(out=dst, in_=src), reads, [key], q="pool")

    def phase1(self, l):
        A, P, I, SC = self.A, self.P, self.inp, self.scr
        A.reset(self.const_top)
        bk = self.banks
        win = A.alloc([8, 2752], BF16)
        wuq = A.alloc([3, 768], BF16)
        wuqr = A.alloc([3, 768], BF16)
        wukv = A.alloc([2, 1024], BF16)
        qng = A.alloc([3], F32)
        kvng = A.alloc([2], F32)
        gam = A.alloc([D], F32)
        bet = A.alloc([D], F32)
        self.ln_tmp = A.alloc([D], F32)
        xt = [A.alloc([4, D], F32) for _ in range(2)]
        hb = A.alloc([4, D], BF16)
        hT = [A.alloc([8, 512], BF16) for _ in range(2)]
        st6 = A.alloc([2, 6], F32); mv = A.alloc([2], F32); rstd = A.alloc([1], F32); nb = A.alloc([1], F32)
        ss = A.alloc([4], F32)
        cqn = A.alloc([384], BF16)
        ckvn = A.alloc([256], BF16)
        cqnT = A.alloc([3, 512], BF16)
        ckvnT = A.alloc([2, 512], BF16)
        junk = A.alloc([384], F32)
        junk2 = A.alloc([256], F32)
        st_u = A.alloc([4, 512], F32)
        st_q = A.alloc([2, 512], F32)
        st_k = A.alloc([2, 512], F32)
        st_d = A.alloc([512], F32)
        st_kt = A.alloc([4, 256], F32)
        st_v = A.alloc([4, 512], BF16)
        st_r = A.alloc([4, 512], BF16)
        st_V = A.alloc([4, 512], BF16)
        st_Q = A.alloc([8, 512], BF16)
        st_K = A.alloc([8, 512], BF16)
        st_kr = A.alloc([512], BF16)
        krt = A.alloc([32], F32)
        krb = A.alloc([128], BF16)
        rt1 = A.alloc([16], F32); rt2 = A.alloc([16], F32)
        cosF = A.alloc([512], F32); sinF = A.alloc([512], F32)
        qt1 = A.alloc([512], F32); qt2 = A.alloc([512], F32)

        self.memset("pool", krb, 0.0, ["krb"])
        w_in = I["w_in"][l]
        self.load_cast(win, w_in[:, 0:2752].rearrange("(c p) n -> p c n", p=128), 2752, "win")
        self.load_cast(wuq, I["mla_w_uq"][l].rearrange("(c p) n -> p c n", p=128), 768, "wuq")
        wq4 = I["mla_w_uq"][l].rearrange("(c p) (h x) -> p c h x", p=128, x=96)
        wuqr4 = wuqr.rearrange("p c (h x) -> p c h x", x=96)
        for c in range(3):
            self.load_cast(wuqr4[:, c, :, 64:80], wq4[:, c, :, 80:96], 16, "wuqr")
            self.load_cast(wuqr4[:, c, :, 80:96], wq4[:, c, :, 64:80], 16, "wuqr")
        self.memset("pool", wuqr4[:, :, :, 0:64], 0.0, ["wuqr"])
        self.load_cast(wukv, I["mla_w_ukv"][l].rearrange("(c p) n -> p c n", p=128), 1024, "wukv")
        self.ld(qng, I["mla_q_norm"][l].rearrange("(c p) -> p c", p=128), (), ["qng"], slow=True)
        self.ld(kvng, I["mla_kv_norm"][l].rearrange("(c p) -> p c", p=128), (), ["kvng"], slow=True)
        if l == 0:
            self.ld(gam, I["ln0_g"][None, :].to_broadcast([128, D]), (), ["lnpar"])
            self.ld(bet, I["ln0_b"][None, :].to_broadcast([128, D]), (), ["lnpar"])
        src = I["x"] if l == 0 else SC["hbuf"]
        pT = bk[0][:, 0:512].bitcast(BF16).rearrange("p (a b) -> p a b", a=8)
        fm_blocks = [("u", g, C_UP + g * 128, 128) for g in range(4)] + [("q", m, C_GQ + m * 128, 128) for m in range(2)] \
            + [("k", m, C_GK + m * 128, 128) for m in range(2)] + [("d", 0, C_GD, 32)]
        nfm = 0
        ntm = 0
        for s in range(self.ns_limit or NS):
            sl = s % 2
            t0 = s * 512
            xk = ("xt", sl)
            self.ld(xt[sl], src[t0:t0 + 512, :].rearrange("(j p) d -> p j d", p=128), (), [xk])
            self.ld(cosF[64:96, :], SC["cosF"][:, t0:t0 + 512], (), ["cosFs"])
            self.ld(sinF[64:96, :], SC["sinF"][:, t0:t0 + 512], (), ["sinFs"])
            hTk = ("hT", sl)
            for j in range(4):
                hbk = ("hb", j)
                if l == 0:
                    self.layernorm_tile(xt[sl][:, j, :], xt[sl][:, j, :], hb[:, j, :], gam, bet, st6, mv, rstd, nb, "ln", [xk], [xk], [hbk])
                else:
                    self.cp("act", hb[:, j, :], xt[sl][:, j, :], [xk], [hbk])
                for c in range(8):
                    self.tr(pT[:, c, :], hb[:, j, c * 128:(c + 1) * 128], self.ident_b, [hbk, "ident_b"], ["pT"])
                self.cp("dve", hT[sl][:, :, j * 128:(j + 1) * 128], pT, [], [hTk, "pT"])
            if l == 0:
                self.st(SC["hbuf"][t0:t0 + 512, :].rearrange("(j p) d -> p j d", p=128), xt[sl], [xk], ["hbuf"])
            for (kind, idx, c0, w) in (fm_blocks if 'fm' not in self.skip else []):
                pb = bk[1 + nfm % 2]
                pk = ("fm", nfm % 2)
                nfm += 1
                for c in range(8):
                    self.mm(pb[0:w, :], win[:, c, c0:c0 + w], hT[sl][:, c, :], c == 0, c == 7, ["win", hTk], [pk])
                if kind == "u":
                    self.cp("act", st_u[:, idx, :], pb[:, :], [], ["st_u", pk])
                elif kind == "q":
                    self.act(st_q[:, idx, :], pb[:, :], AF.Identity, [], ["st_q", pk], scale=0.125)
                elif kind == "k":
                    self.cp("dve", st_k[:, idx, :], pb[:, :], [], ["st_k", pk])
                else:
                    self.cp("dve", st_d[0:32, :], pb[0:32, :], [], ["st_d", pk])
            if 'fm' not in self.skip:
                self.st(SC["uT"][:, t0:t0 + 512].rearrange("(g p) t -> p g t", p=128), st_u, ["st_u"], ["uT"])
                self.st(SC["gqT"][:, t0:t0 + 512].rearrange("(g p) t -> p g t", p=128), st_q, ["st_q"], ["gqT"])
                self.st(SC["gkT"][:, t0:t0 + 512].rearrange("(g p) t -> p g t", p=128), st_k, ["st_k"], ["gkT"])
                self.st(SC["gdT"][:, t0:t0 + 512], st_d[0:32, :], ["st_d"], ["gdT"])
            for j in range(4):
                tt = s * 4 + j
                lh = lambda c: hT[sl][:, c, j * 128:(j + 1) * 128]

                def tmb(pb, pk, c0, w):
                    for c in range(8):
                        self.mm(pb[:, 0:w], lh(c), win[:, c, c0:c0 + w], c == 0, c == 7, ["win", hTk], [pk])

                def tm(c0, w):
                    nonlocal ntm
                    pb = bk[3 + ntm % 2]
                    pk = ("tm", ntm % 2)
                    ntm += 1
                    tmb(pb, pk, c0, w)
                    return pb, pk
                pbq, pkq = bk[6], "pqa"
                pbk, pkk_ = bk[7], "pqb"
                tmb(pbq, pkq, C_CQ, 384)
                tmb(pbk, pkk_, C_CKV, 288)
                self.act(junk[:, 0:384], pbq[:, 0:384], AF.Square, [], ["junk", "ss0", pkq], accum_out=ss[:, 0:1])
                self.act(ss[:, 1:2], ss[:, 0:1], AF.Ln, ["ss0", "eps_rms"], ["ss1"], bias=self.eps_rms[:, 0:1], scale=1.0 / 384)
                self.act(ss[:, 1:2], ss[:, 1:2], AF.Exp, ["ss1"], ["ss1"], scale=-0.5)
                self.ts("dve", cqn, pbq[:, 0:384], ss[:, 1:2], None, ALU.mult, None, ["ss1"], ["cqn", pkq])
                self.act(junk2[:, 0:256], pbk[:, 0:256], AF.Square, [], ["junk2", "ss2", pkk_], accum_out=ss[:, 2:3])
                self.act(ss[:, 3:4], ss[:, 2:3], AF.Ln, ["ss2", "eps_rms"], ["ss3"], bias=self.eps_rms[:, 0:1], scale=1.0 / 256)
                self.act(ss[:, 3:4], ss[:, 3:4], AF.Exp, ["ss3"], ["ss3"], scale=-0.5)
                self.ts("dve", ckvn, pbk[:, 0:256], ss[:, 3:4], None, ALU.mult, None, ["ss3"], ["ckvn", pkk_])
                cs = self.cosT[:, tt, :]
                sn = self.sinT[:, tt, :]
                self.cp("dve", krt, pbk[:, 256:288], [], ["krt", pkk_])
                self.tt("dve", rt1, krt[:, 0:16], cs, ALU.mult, ["krt", "cosT"], ["rt1"])
                self.tt("dve", rt2, krt[:, 16:32], sn, ALU.mult, ["krt", "sinT"], ["rt2"])
                self.tt("dve", krb[:, 0:16], rt1, rt2, ALU.subtract, ["rt1", "rt2"], ["krb"])
                self.tt("dve", rt1, krt[:, 0:16], sn, ALU.mult, ["krt", "sinT"], ["rt1"])
                self.tt("dve", rt2, krt[:, 16:32], cs, ALU.mult, ["krt", "cosT"], ["rt2"])
                self.tt("dve", krb[:, 16:32], rt1, rt2, ALU.add, ["rt1", "rt2"], ["krb"])
                pb, pk = tm(C_GK, 256)
                self.cp("act", st_kt[:, j, :], pb[:, 0:256], [], ["st_kt", pk])
                pb, pk = tm(C_GV, 512)
                self.cp("dve", st_v[:, j, :], pb[:, :], [], ["st_v", pk])
                pb, pk = tm(C_GR, 512)
                self.act(st_r[:, j, :], pb[:, :], AF.Silu, [], ["st_r", pk])
                pq = bk[5][:, 0:192].bitcast(BF16).rearrange("p (a b) -> p a b", a=3)
                for c in range(3):
                    self.tr(pq[:, c, :], cqn[:, c * 128:(c + 1) * 128], self.ident_b, ["cqn", "ident_b"], ["b5"])
                self.tt("dve", cqnT[:, :, j * 128:(j + 1) * 128], pq, qng[:, :, None].to_broadcast([128, 3, 128]), ALU.mult, ["qng"], ["cqnT", "b5"])
                pkv = bk[5][:, 256:448].bitcast(BF16).rearrange("p (a b) -> p a b", a=3)
                for c in range(2):
                    self.tr(pkv[:, c, :], ckvn[:, c * 128:(c + 1) * 128], self.ident_b, ["ckvn", "ident_b"], ["b5"])
                self.tr(pkv[:, 2, :], krb[:, :], self.ident_b, ["krb", "ident_b"], ["b5"])
                self.tt("dve", ckvnT[:, :, j * 128:(j + 1) * 128], pkv[:, 0:2, :], kvng[:, :, None].to_broadcast([128, 2, 128]), ALU.mult, ["kvng"], ["ckvnT", "b5"])
                self.cp("act", st_kr[0:32, j * 128:(j + 1) * 128], pkv[0:32, 2, :], [], ["st_kr", "b5"])
            rows = lambda name: SC[name][t0:t0 + 512, :].rearrange("(j p) d -> p j d", p=128)
            self.st(rows("gk_tok"), st_kt, ["st_kt"], ["gk_tok"])
            self.st(rows("gv"), st_v, ["st_v"], ["gv"])
            self.st(rows("gr"), st_r, ["st_r"], ["gr"])
            self.st(SC["KRT"][:, t0:t0 + 512], st_kr[0:32, :], ["st_kr"], ["KRT"])
            for h in (range(8) if 'mlaup' not in self.skip else []):
                pqa = bk[6]
                pqb = bk[7]
                for c in range(3):
                    self.mm(pqa[0:96, :], wuq[:, c, h * 96:(h + 1) * 96], cqnT[:, c, :], c == 0, c == 2, ["wuq", "cqnT"], ["pqa"])
                for c in range(3):
                    self.mm(pqb[0:96, :], wuqr[:, c, h * 96:(h + 1) * 96], cqnT[:, c, :], c == 0, c == 2, ["wuqr", "cqnT"], ["pqb"])
                self.act(st_Q[0:64, h, :], pqa[0:64, :], AF.Identity, [], ["st_Q", "pqa"], scale=QSCALE)
                self.tt("dve", qt1[64:96, :], pqa[64:96, :], cosF[64:96, :], ALU.mult, ["cosFs"], ["qt1", "pqa"])
                self.tt("dve", qt2[64:96, :], pqb[64:96, :], sinF[64:96, :], ALU.mult, ["sinFs"], ["qt2", "pqb"])
                self.tt("pool", st_Q[64:96, h, :], qt1[64:96, :], qt2[64:96, :], ALU.add, ["qt1", "qt2"], ["st_Q"])
                pka = bk[1 + h % 2]
                pkk = ("fm", h % 2)
                for c in range(2):
                    self.mm(pka[0:64, :], wukv[:, c, h * 128:h * 128 + 64], ckvnT[:, c, :], c == 0, c == 1, ["wukv", "ckvnT"], [pkk])
                self.cp("act", st_K[0:64, h, :], pka[0:64, :], [], ["st_K", pkk])
            self.st(SC["QT"][:, :, t0:t0 + 512].rearrange("h r t -> r h t"), st_Q[0:96, :, :], ["st_Q"], ["QT"])
            self.st(SC["KT"][:, :, t0:t0 + 512].rearrange("h r t -> r h t"), st_K[0:64, :, :], ["st_K"], ["KT"])
            wv = wukv.rearrange("p c (h x) -> p c h x", x=128)
            for j in (range(4) if 'vproj' not in self.skip else []):
                pb = bk[3 + ntm % 2]
                pk = ("tm", ntm % 2)
                ntm += 1
                for c in range(2):
                    self.mm(pb[:, :].rearrange("p (h x) -> p h x", x=64), ckvnT[:, c, j * 128:(j + 1) * 128], wv[:, c, :, 64:128], c == 0, c == 1, ["wukv", "ckvnT"], [pk])
                self.cp("dve", st_V[:, j, :], pb[:, :], [], ["st_V", pk])
            self.st(rows("V"), st_V, ["st_V"], ["V"])
        P.barrier()

    def phase2(self, l):
        A, P, SC = self.A, self.P, self.scr
        A.reset(self.const_top)
        bk = self.banks
        KTs = [A.alloc([S], BF16) for _ in range(2)]
        Vh = [A.alloc([NT, 65], BF16) for _ in range(2)]
        Qs = [A.alloc([512], BF16) for _ in range(2)]
        pts = [A.alloc([512], BF16) for _ in range(4)]
        ones = A.alloc([64], F32)
        rcp = A.alloc([512], F32)
        bc = A.alloc([512], F32)
        ost = [A.alloc([512], BF16) for _ in range(2)]
        self.memset("pool", ones, 1.0, ["ones"])
        for i in range(2):
            self.memset("pool", Vh[i][:, :, 64:65], 1.0, [("Vh", i)])
        nheads = self.ns_limit or 8
        nqg = self.ns_limit or NS
        LOOK = 2
        items = [(h, qg) for h in range(nheads) for qg in range(nqg)]
        loaded_heads = set()

        def load_head(h):
            if h in loaded_heads or h >= nheads:
                return
            loaded_heads.add(h)
            hs = h % 2
            self.ld(KTs[hs][0:64, :], SC["KT"][h], ["KT"], [("KTs", hs)])
            self.ld(KTs[hs][64:96, :], SC["KRT"][:, :], ["KRT"], [("KTs", hs)])
            vsrc = SC["V"][:, h * 64:(h + 1) * 64].rearrange("(kt p) v -> p kt v", p=128)
            for k0 in range(0, NT, 8):
                self.ld(Vh[hs][:, k0:k0 + 8, 0:64], vsrc[:, k0:k0 + 8, :], ["V"], [("Vh", hs)])

        def load_q(ii):
            if ii >= len(items):
                return
            h, qg = items[ii]
            qs = ii % 2
            self.ld(Qs[qs][0:96, :], SC["QT"][h, :, qg * 512:qg * 512 + 512], ["QT"], [("Qs", qs)])

        def qk(ii, kt):
            h, qg = items[ii]
            hs, qs = h % 2, ii % 2
            g = (ii * NT + kt) % 4
            self.mm(bk[g][:, :], KTs[hs][0:96, kt * 128:(kt + 1) * 128], Qs[qs][0:96, :], True, True, [("KTs", hs), ("Qs", qs)], [("B", g)])

        def epilogue(ii):
            h, qg = items[ii]
            qs = ii % 2
            q0 = qg * 512
            ob = bk[4 + qs]
            ok = ("B", 4 + qs)
            self.P.op("dve", lambda e, ob=ob: e.reciprocal(out=rcp[64:65, :], in_=ob[64:65, :]), [], ["rcp", ok])
            self.mm(bk[6][0:64, :], ones[64:65, :], rcp[64:65, :], True, True, ["ones", "rcp"], [("B", 6)])
            self.cp("act", bc[0:64, :], bk[6][0:64, :], [], ["bc", ("B", 6)])
            self.tt("dve", ost[qs][0:64, :], ob[0:64, :], bc[0:64, :], ALU.mult, ["bc"], [("ost", qs), ok])
            self.st(SC["mlT"][h * 64:(h + 1) * 64, q0:q0 + 512], ost[qs][0:64, :], [("ost", qs)], ["mlT"])

        load_head(0)
        load_q(0)
        for kt in range(LOOK):
            qk(0, kt)
        for ii, (h, qg) in enumerate(items):
            hs, qs = h % 2, ii % 2
            if qg == 0:
                load_head(h + 1)
            load_q(ii + 1)
            ob = bk[4 + qs]
            ok = ("B", 4 + qs)
            for kt in range(NT):
                nk = kt + LOOK
                if nk < NT:
                    qk(ii, nk)
                elif ii + 1 < len(items):
                    qk(ii + 1, nk - NT)
                g = (ii * NT + kt) % 4
                pt = pts[g]
                pk = ("pt", g)
                self.act(pt, bk[g][:, :], AF.Exp, [], [pk, ("B", g)])
                self.mm(ob[0:65, :], Vh[hs][:, kt, :], pt, kt == 0, kt == NT - 1, [("Vh", hs), pk], [ok])
                if kt == 3 and ii > 0:
                    epilogue(ii - 1)
            if ii == len(items) - 1:
                epilogue(ii)
        P.barrier()

    def phase4(self, l):
        A, P, I, SC = self.A, self.P, self.inp, self.scr
        A.reset(self.const_top)
        bk = self.banks
        wp = A.alloc([4, 128], BF16)
        psc = A.alloc([4], F32)
        inv_first = A.alloc([4, 512], F32)
        inv_last = A.alloc([4, 512], F32)
        U = [A.alloc([528], F32) for _ in range(2)]
        sA = A.alloc([528], F32)
        sB = A.alloc([528], F32)
        pl = A.alloc([512], BF16)
        pst = [A.alloc([512], BF16) for _ in range(2)]
        self.load_cast(wp, I["pool_w"][l].rearrange("g c d -> c g d"), 128, "wp")
        self.ld(psc, I["pool_scale"][l].rearrange("(g p) -> p g", p=128), (), ["psc"], slow=True)
        for g, w in enumerate(POOL_WINDOWS):
            self.memset("pool", inv_first[:, g, :], 1.0 / w, ["inv_first"])
            self.memset("pool", inv_last[:, g, :], 1.0 / w, ["inv_last"])
            for t in range(w // 2):
                self.memset("pool", inv_first[:, g, t:t + 1], 1.0 / (t + w // 2), ["inv_first"])
            for j in range(1, w // 2):
                self.memset("pool", inv_last[:, g, 512 - j:512 - j + 1], 1.0 / (j + w // 2), ["inv_last"])
        nch = self.ns_limit or NS
        it = 0
        for g, w in enumerate(POOL_WINDOWS):
            for ch in range(nch):
                us = it % 2
                it += 1
                t0 = ch * 512
                uk = ("U", us)
                lo = max(t0 - 8, 0)
                hi = min(t0 + 520, S)
                if ch == 0:
                    self.memset("pool", U[us][:, 0:8], 0.0, [uk])
                if ch == NS - 1:
                    self.memset("pool", U[us][:, 520:528], 0.0, [uk])
                self.ld(U[us][:, lo - (t0 - 8):hi - (t0 - 8)], SC["uT"][g * 128:(g + 1) * 128, lo:hi], ["uT"], [uk])
                cur = U[us]
                curk = uk
                k = 1
                dst = [sA, sB]
                di = 0
                while k < w:
                    d_ = dst[di]
                    dk = ("s", di)
                    self.tt("dve", d_[:, k:528], cur[:, k:528], cur[:, 0:528 - k], ALU.add, [curk], [dk])
                    cur, curk = d_, dk
                    di ^= 1
                    k *= 2
                off = 8 + w // 2 - 1
                sw = cur[:, off:off + 512]
                if ch == 0 or ch == NS - 1:
                    inv = inv_first if ch == 0 else inv_last
                    other, okey = (sB, ("s", 1)) if cur is sA else (sA, ("s", 0))
                    self.tt("dve", other[:, 0:512], sw, inv[:, g, :], ALU.mult, [curk, "inv_first", "inv_last"], [okey])
                    self.tt("dve", pl, other[:, 0:512], U[us][:, 8:520], ALU.subtract, [okey, uk], ["pl"])
                else:
                    self.stt("dve", pl, sw, 1.0 / w, U[us][:, 8:520], ALU.mult, ALU.subtract, [curk, uk], ["pl"])
                pb = bk[it % 2]
                pk = ("B", it % 2)
                self.mm(pb[:, :], wp[:, g, :], pl, True, True, ["wp", "pl"], [pk])
                self.ts("dve", pst[us], pb[:, :], psc[:, g:g + 1], None, ALU.mult, None, ["psc"], [("pst", us), pk])
                self.st(SC["pmT"][g * 128:(g + 1) * 128, t0:t0 + 512], pst[us], [("pst", us)], ["pmT"])
        P.barrier()

    def phase3(self, l):
        A, P, I, SC = self.A, self.P, self.inp, self.scr
        A.reset(self.const_top)
        bk = self.banks
        if "of" not in SC:
            self.scratch("of", [S, 512], F32)
        tri = A.alloc([4, 64], F32)
        one_c = A.alloc([1], F32)
        wdec = A.alloc([2, 256], F32)
        gn = A.alloc([128], F32)
        dl = [A.alloc([128], F32) for _ in range(2)]
        gq = [A.alloc([4, 128], F32) for _ in range(2)]
        gk = [A.alloc([4, 128], F32) for _ in range(2)]
        gkt = [A.alloc([2, 256], F32) for _ in range(2)]
        vv = [A.alloc([2, 512], BF16) for _ in range(2)]
        of = [A.alloc([2, 512], F32) for _ in range(2)]
        grt = [A.alloc([2, 512], BF16) for _ in range(2)]
        esb = A.alloc([512], F32)
        gp = A.alloc([2, 256], F32)
        E1 = A.alloc([4, 2, 64], F32)
        E2 = A.alloc([4, 2, 64], F32)
        E3 = A.alloc([2, 256], F32)
        qin = A.alloc([4, 128], BF16)
        kin = A.alloc([4, 128], BF16)
        kst = A.alloc([2, 256], BF16)
        ATm = A.alloc([4, 64], BF16)
        St = A.alloc([4, 128], F32)
        Sb = A.alloc([4, 128], BF16)
        ost = [A.alloc([2, 512], F32) for _ in range(2)]
        sq = A.alloc([1024], F32)
        ssq = A.alloc([8], F32)
        rs8 = A.alloc([8], F32)
        ob16 = A.alloc([2, 512], BF16)
        gst = [A.alloc([4, 128], BF16) for _ in range(2)]
        self.ld(tri[0:64, :, :], I["c_tri"].rearrange("k a b -> a k b"), (), ["tri"])
        self.memset("pool", one_c, 1.0, ["one_c"])
        for d_ in range(2):
            self.ld(wdec[0:16, d_, :], I["gla_w_dec"][l, d_], (), ["wdec"])
            self.ld(wdec[16:17, d_, :], I["gla_b_dec"][l, d_:d_ + 1, :], (), ["wdec"])
            self.memset("pool", dl[d_][0:17, :], 1.0, [("dl", d_)])
        self.ld(gn[0:64, :], I["gla_norm"][l][None, :].to_broadcast([64, 128]), (), ["gn"])
        LG, BT, CC, AT, OO = bk[0], bk[1], bk[2], bk[3], bk[4]
        BTv = BT[0:64, :].rearrange("p (h n i) -> p h n i", h=4, n=2)
        NG = S // 128
        ng = self.ns_limit or NG
        for dirn in range(2):
            self.memset("pool", St[0:64], 0.0, ["St"])
            self.memset("pool", Sb[0:64], 0.0, ["Sb"])
            inc_i, rest_i = (0, 1) if dirn == 0 else (2, 3)
            groups = list(range(ng)) if dirn == 0 else list(range(NG - 1, NG - 1 - ng, -1))
            for gi, g in enumerate(groups):
                bs = gi % 2
                t0 = g * 128
                self.ld(dl[bs][0:16, :], SC["gdT"][dirn * 16:(dirn + 1) * 16, t0:t0 + 128], ["gdT"], [("dl", bs)])
                self.ld(gq[bs][0:64], SC["gqT"][:, t0:t0 + 128].rearrange("(h d) t -> d h t", d=64), ["gqT"], [("gq", bs)])
                self.ld(gk[bs][0:64], SC["gkT"][:, t0:t0 + 128].rearrange("(h d) t -> d h t", d=64), ["gkT"], [("gk", bs)])
                self.ld(gkt[bs][0:64], SC["gk_tok"][t0:t0 + 128, :].rearrange("(n p) c -> p n c", p=64), ["gk_tok"], [("gkt", bs)])
                self.ld(vv[bs][0:64], SC["gv"][t0:t0 + 128, :].rearrange("(n p) c -> p n c", p=64), ["gv"], [("vv", bs)])
                if dirn == 1:
                    self.ld(of[bs][0:64], SC["of"][t0:t0 + 128, :].rearrange("(n p) c -> p n c", p=64), ["of"], [("of", bs)])
                    self.ld(grt[bs][0:64], SC["gr"][t0:t0 + 128, :].rearrange("(n p) c -> p n c", p=64), ["gr"], [("grt", bs)])
                for n in range(2):
                    self.mm(LG[0:64, n * 256:(n + 1) * 256], dl[bs][0:17, n * 64:(n + 1) * 64], wdec[0:17, dirn, :], True, True, [("dl", bs), "wdec"], [("B", 0)])
                self.act(esb[0:64, :], LG[0:64, :], AF.Exp, [], ["esb", ("B", 0)], scale=-1.0)
                self.act(gp[0:64].rearrange("p n c -> p (n c)"), esb[0:64, :], AF.Ln, ["esb", "one_c"], ["gp"], bias=one_c[0:64, 0:1], scale=1.0)
                for h in range(4):
                    for n in range(2):
                        self.mm(BTv[:, h, n, :], gp[0:64, n, h * 64:(h + 1) * 64], tri[0:64, inc_i, :], True, True, ["gp", "tri"], [("B", 1)])
                for n in range(2):
                    self.mm(CC[0:64, n * 256:(n + 1) * 256], tri[0:64, rest_i, :], gp[0:64, n, :], True, True, ["gp", "tri"], [("B", 2)])
                f = lambda t: t[0:64].rearrange("p a b c -> p (a b c)")
                self.act(f(E1), BT[0:64, :], AF.Exp, [], ["E1", ("B", 1)], scale=-1.0 / 16)
                self.act(f(E2), BT[0:64, :], AF.Exp, [], ["E2", ("B", 1)], scale=1.0 / 16)
                self.act(E3[0:64].rearrange("p n c -> p (n c)"), CC[0:64, :], AF.Exp, [], ["E3", ("B", 2)], scale=-1.0 / 16)
                self.tt("dve", qin[0:64], gq[bs][0:64], E1[0:64].rearrange("p h n i -> p h (n i)"), ALU.mult, [("gq", bs), "E1"], ["qin"])
                self.tt("dve", kin[0:64], gk[bs][0:64], E2[0:64].rearrange("p h n i -> p h (n i)"), ALU.mult, [("gk", bs), "E2"], ["kin"])
                self.tt("dve", kst[0:64], gkt[bs][0:64], E3[0:64], ALU.mult, [("gkt", bs), "E3"], ["kst"])
                order = (0, 1) if dirn == 0 else (1, 0)
                for ci, n in enumerate(order):
                    cs = slice(n * 64, (n + 1) * 64)
                    for h in range(4):
                        self.mm(AT[0:64, h * 64:(h + 1) * 64], kin[0:64, h, cs], qin[0:64, h, cs], True, True, ["kin", "qin"], [("B", 3)])
                    self.tt("dve", ATm[0:64], AT[0:64, 0:256].rearrange("p (h i) -> p h i", h=4), tri[0:64, inc_i:inc_i + 1, :].to_broadcast([64, 4, 64]), ALU.mult, ["tri"], ["ATm", ("B", 3)])
                    for h in range(4):
                        hv = slice(h * 128, (h + 1) * 128)
                        self.mm(OO[0:64, hv], ATm[0:64, h, :], vv[bs][0:64, n, hv], True, False, ["ATm", ("vv", bs)], [("B", 4)])
                        self.mm(OO[0:64, hv], qin[0:64, h, cs], Sb[0:64, h, :], False, True, ["qin", "Sb"], [("B", 4)])
                    kvb = bk[5 + (gi * 2 + ci) % 2]
                    kvk = ("B", 5 + (gi * 2 + ci) % 2)
                    for h in range(4):
                        hv = slice(h * 128, (h + 1) * 128)
                        self.mm(kvb[0:64, hv], kst[0:64, n, h * 64:(h + 1) * 64], vv[bs][0:64, n, hv], True, True, ["kst", ("vv", bs)], [kvk])
                    dcol = 63 if dirn == 0 else 0
                    for h in range(4):
                        hv = slice(h * 128, (h + 1) * 128)
                        self.stt("dve", St[0:64, h, :], St[0:64, h, :], E1[0:64, h, n, dcol:dcol + 1], kvb[0:64, hv], ALU.mult, ALU.add, ["E1"], ["St", kvk])
                    self.cp("act", Sb[0:64].rearrange("p h v -> p (h v)"), St[0:64].rearrange("p h v -> p (h v)"), ["St"], ["Sb"])
                    if dirn == 0:
                        self.cp("act", ost[bs][0:64, n, :], OO[0:64, :], [], [("ost", bs), ("B", 4)])
                    else:
                        self.tt("dve", ost[bs][0:64, n, :], OO[0:64, :], of[bs][0:64, n, :], ALU.add, [("of", bs)], [("ost", bs), ("B", 4)])
                if dirn == 0:
                    self.st(SC["of"][t0:t0 + 128, :].rearrange("(n p) c -> p n c", p=64), ost[bs][0:64], [("ost", bs)], ["of"])
                else:
                    o2 = ost[bs][0:64].rearrange("p n c -> p (n c)")
                    self.tt("dve", sq[0:64, :], o2, o2, ALU.mult, [("ost", bs)], ["sq"])
                    self.P.op("dve", lambda e: e.reduce_sum(out=ssq[0:64, :], in_=sq[0:64, :].rearrange("p (a b) -> p a b", b=128), axis=AX.X), ["sq"], ["ssq"])
                    self.act(rs8[0:64, :], ssq[0:64, :], AF.Ln, ["ssq", "eps_rms"], ["rs8"], bias=self.eps_rms[0:64, 0:1], scale=1.0 / 128)
                    self.act(rs8[0:64, :], rs8[0:64, :], AF.Exp, ["rs8"], ["rs8"], scale=-0.5)
                    o3 = ost[bs][0:64].rearrange("p n (h v) -> p (n h) v", v=128)
                    self.tt("dve", o3, o3, rs8[0:64, :, None].to_broadcast([64, 8, 128]), ALU.mult, ["rs8"], [("ost", bs)])
                    self.tt("dve", o3, o3, gn[0:64, None, :].to_broadcast([64, 8, 128]), ALU.mult, ["gn"], [("ost", bs)])
                    self.tt("dve", ob16[0:64], ost[bs][0:64], grt[bs][0:64], ALU.mult, [("ost", bs), ("grt", bs)], ["ob16"])
                    TP = bk[7][:, 0:256].bitcast(BF16).rearrange("p (b t) -> p b t", b=4)
                    for n in range(2):
                        for b_ in range(4):
                            self.tr(TP[:, b_, n * 64:(n + 1) * 64], ob16[0:64, n, b_ * 128:(b_ + 1) * 128], self.ident_b[0:64, 0:64], ["ob16", "ident_b"], [("B", 7)])
                    self.cp("act", gst[bs], TP, [], [("gst", bs), ("B", 7)])
                    self.st(SC["glT"][:, t0:t0 + 128].rearrange("(b p) t -> p b t", p=128), gst[bs], [("gst", bs)], ["glT"])
        P.barrier()

    def phase5(self, l):
        A, P, I, SC = self.A, self.P, self.inp, self.scr
        A.reset(self.const_top)
        bk = self.banks
        self.affT = A.alloc([S], F32)
        self.p6_base = A.off
        wg = A.alloc([8, 3072], BF16)
        wup = [A.alloc([4, D], BF16) for _ in range(3)]
        wout = A.alloc([8, D], BF16)
        rw = A.alloc([8, 16], F32)
        bgr = A.alloc([3072], BF16)
        ones1 = A.alloc([128], BF16)
        gam = A.alloc([D], F32)
        bet = A.alloc([D], F32)
        self.ln_tmp = A.alloc([D], F32)
        ht = [A.alloc([D], F32) for _ in range(2)]
        hb = A.alloc([D], BF16)
        hT = A.alloc([8, 128], BF16)
        gates = A.alloc([3072], F32)
        xT = [[A.alloc([4, 128], BF16)] * 2 for _ in range(3)]
        mrg = A.alloc([D], F32)
        tmpm = A.alloc([D], F32)
        mb = A.alloc([D], BF16)
        mT = A.alloc([8, 128], BF16)
        h1b = A.alloc([D], BF16)
        h1a = A.alloc([D], F32)
        h1T = A.alloc([8, 128], F32)
        st6 = A.alloc([2, 6], F32); mv = A.alloc([2], F32); rstd = A.alloc([1], F32); nb = A.alloc([1], F32)
        rmax = A.alloc([1], F32); rsum = A.alloc([1], F32); aff = A.alloc([16], F32)
        self.load_cast(wg, I["w_in"][l][:, C_GATE:C_GATE + 3072].rearrange("(c p) n -> p c n", p=128), 3072, "wg")
        for i, nm in enumerate(("w_up_a", "w_up_b", "w_up_c")):
            self.load_cast(wup[i], I[nm][l].rearrange("(c p) n -> p c n", p=128), D, ("wup", i))
        self.load_cast(wout, I["w_out"][l].rearrange("(c p) n -> p c n", p=128), D, "wout")
        self.ld(rw, I["router_w"][l].rearrange("(c p) e -> p c e", p=128), (), ["rw"])
        for a_ in range(0, 3072, 1536):
            self.P.dma(lambda e, a_=a_: e.dma_start(out=bgr[0:1, a_:a_ + 1536], in_=I["b_gate"][l:l + 1, a_:a_ + 1536]), (), ["bgr"], q="pool")
        self.memset("pool", ones1, 1.0, ["ones1"])
        self.ld(gam, I["ln1_g"][l][None, :].to_broadcast([128, D]), (), ["lnpar"])
        self.ld(bet, I["ln1_b"][l][None, :].to_broadcast([128, D]), (), ["lnpar"])
        srcs = ("pmT", "mlT", "glT")
        pT = bk[0][:, :].bitcast(BF16).rearrange("p (a b) -> p a b", a=8)
        nt = (self.ns_limit or NS) * 4

        def X(t):
            bs = t % 2
            r0 = t * 128
            self.ld(ht[bs], SC["hbuf"][r0:r0 + 128, :], ["hbuf"], [("ht", bs)])
            for i in range(3):
                self.ld(xT[i][bs], SC[srcs[i]][:, r0:r0 + 128].rearrange("(c p) t -> p c t", p=128), [srcs[i]], [("xT", i)])
            self.cp("act", hb, ht[bs], [("ht", bs)], ["hb"])
            for c in range(8):
                self.tr(pT[:, c, :], hb[:, c * 128:(c + 1) * 128], self.ident_b, ["hb", "ident_b"], [("B", 0)])
            self.cp("dve", hT, pT, [], ["hT", ("B", 0)])
            for gI in range(6):
                pb = bk[1 + gI % 2]
                pk = ("B", 1 + gI % 2)
                cs = slice(gI * 512, (gI + 1) * 512)
                for c in range(8):
                    self.mm(pb[:, :], hT[:, c, :], wg[:, c, cs], c == 0, False, ["hT", "wg"], [pk])
                self.mm(pb[:, :], ones1[0:1, :], bgr[0:1, cs], False, True, ["ones1", "bgr"], [pk])
                self.act(gates[:, cs], pb[:, :], AF.Sigmoid, [], [("gates", gI), pk])
            for i in range(3):
                for g2 in range(2):
                    pb = bk[3 + (i * 2 + g2) % 2]
                    pk = ("B", 3 + (i * 2 + g2) % 2)
                    cs = slice(g2 * 512, (g2 + 1) * 512)
                    for c in range(4):
                        self.mm(pb[:, :], xT[i][bs][:, c, :], wup[i][:, c, cs], c == 0, c == 3, [("xT", i), ("wup", i)], [pk])
                    gsl = gates[:, i * D + g2 * 512:i * D + (g2 + 1) * 512]
                    gk_ = ("gates", i * 2 + g2)
                    if i == 0:
                        self.tt("dve", mrg[:, cs], pb[:, :], gsl, ALU.mult, [gk_], [("mrg", g2), pk])
                    else:
                        self.tt("dve", tmpm[:, cs], pb[:, :], gsl, ALU.mult, [gk_], [("tmpm", g2), pk])
                        if i == 1:
                            self.tt("pool", mrg[:, cs], mrg[:, cs], tmpm[:, cs], ALU.add, [("tmpm", g2)], [("mrg", g2)])
                        else:
                            self.tt("pool", mb[:, cs], mrg[:, cs], tmpm[:, cs], ALU.add, [("tmpm", g2), ("mrg", g2)], [("mb", g2)])
            for c in range(8):
                self.tr(pT[:, c, :], mb[:, c * 128:(c + 1) * 128], self.ident_b, [("mb", c // 4), "ident_b"], [("B", 0)])
            self.cp("dve", mT, pT, [], ["mT", ("B", 0)])
            for g2 in range(2):
                pb = bk[5 + g2]
                pk = ("B", 5 + g2)
                cs = slice(g2 * 512, (g2 + 1) * 512)
                for c in range(8):
                    self.mm(pb[:, :], mT[:, c, :], wout[:, c, cs], c == 0, c == 7, ["mT", "wout"], [pk])
                self.stt("dve", ht[bs][:, cs], ht[bs][:, cs], DN_ALPHA, pb[:, :], ALU.mult, ALU.add, [], [("ht", bs), pk])

        def Y(t):
            bs = t % 2
            r0 = t * 128
            h1 = ht[bs]
            self.layernorm_tile(h1, h1, h1b, gam, bet, st6, mv, rstd, nb, "ln", [("ht", bs)], [("ht", bs)], ["h1b"])
            self.st(SC["h1"][r0:r0 + 128, :], h1, [("ht", bs)], ["h1d"])
            self.st(SC["h1b"][r0:r0 + 128, :], h1b, ["h1b"], ["h1bd"])
            self.act(h1a, h1, AF.Identity, [("ht", bs)], ["h1a"], scale=DN_ALPHA)
            self.st(SC["acc"][r0:r0 + 128, :], h1a, ["h1a"], ["acc"])
            pTf = [bk[7][:, :].rearrange("p (a b) -> p a b", a=4), bk[0][:, :].rearrange("p (a b) -> p a b", a=4)]
            for c in range(8):
                self.tr(pTf[c // 4][:, c % 4, :], h1[:, c * 128:(c + 1) * 128], self.ident_f, [("ht", bs), "ident_f"], [("B", 7 if c < 4 else 0)])
            self.cp("dve", h1T[:, 0:4, :], pTf[0], [], [("h1T", 0), ("B", 7)])
            self.cp("dve", h1T[:, 4:8, :], pTf[1], [], [("h1T", 1), ("B", 0)])
            lg = bk[7]
            for c in range(8):
                self.mm(lg[:, 0:16], h1T[:, c, :], rw[:, c, :], c == 0, c == 7, [("h1T", c // 4), "rw"], [("B", 7)])
            self.P.op("dve", lambda e: e.reduce_max(out=rmax, in_=lg[:, 0:16], axis=AX.X), [], ["rmax", ("B", 7)])
            self.ts("dve", rmax, rmax, -1.0, None, ALU.mult, None, ["rmax"], ["rmax"])
            self.act(aff, lg[:, 0:16], AF.Exp, ["rmax"], ["aff", "rsum", ("B", 7)], bias=rmax[:, 0:1], scale=1.0, accum_out=rsum[:, 0:1])
            self.P.op("dve", lambda e: e.reciprocal(out=rsum, in_=rsum), ["rsum"], ["rsum"])
            self.ts("dve", aff, aff, rsum[:, 0:1], None, ALU.mult, None, ["rsum"], ["aff"])
            self.tr(bk[7][0:16, 128:256], aff, self.ident_f, ["aff", "ident_f"], [("B", 7)])
            self.cp("dve", self.affT[0:16, r0:r0 + 128], bk[7][0:16, 128:256], [], ["affT", ("B", 7)])

        X(0)
        for t in range(nt):
            if t + 1 < nt:
                X(t + 1)
            Y(t)
        P.barrier()

    def phase6(self, l):
        A, P = self.A, self.P
        A.reset(self.p6_base)
        bk = self.banks
        CAP = S // 8
        NJ = CAP // 128
        work = A.alloc([S], F32)
        vals = A.alloc([CAP], F32)
        idxu = A.alloc([CAP], U32)
        idxf = A.alloc([CAP], F32)
        af = self.affT[0:16, :]
        cur = af
        curk = "affT"
        for it in range(CAP // 8):
            v8 = vals[0:16, it * 8:(it + 1) * 8]
            self.P.op("dve", lambda e, v8=v8, cur=cur: e.max(out=v8, in_=cur), [curk], ["vals"])
            i8 = idxu[0:16, it * 8:(it + 1) * 8]
            self.P.op("dve", lambda e, v8=v8, cur=cur, i8=i8: e.max_index(out=i8, in_max=v8, in_values=cur), [curk, "vals"], ["idxu"])
            self.P.op("dve", lambda e, v8=v8, cur=cur: e.match_replace(out=work[0:16, :], in_to_replace=v8, in_values=cur, imm_value=-1.0), [curk, "vals"], ["work"])
            cur = work[0:16, :]
            curk = "work"
        self.cp("dve", idxf[0:16, :], idxu[0:16, :], ["idxu"], ["idxf"])
        for j in range(NJ):
            pj = bk[j % 2]
            pk = ("B", j % 2)
            self.tr(pj[:, 0:16], idxf[0:16, j * 128:(j + 1) * 128], self.ident_f[0:16, 0:16], ["idxf", "ident_f"], [pk])
            self.tr(pj[:, 16:32], vals[0:16, j * 128:(j + 1) * 128], self.ident_f[0:16, 0:16], ["vals", "ident_f"], [pk])
            self.cp("dve", self.IDX[:, j, :], pj[:, 0:16], [], ["IDX", pk])
            self.cp("dve", self.GATE[:, j, :], pj[:, 16:32], [], ["GATE", pk])
        P.barrier()

    def phase7(self, l):
        A, P, I, SC = self.A, self.P, self.inp, self.scr
        A.reset(self.const_top)
        bk = self.banks
        CAP = S // 8
        NJ = CAP // 128
        wgh = [A.alloc([8, 1024], BF16) for _ in range(2)]
        wuh = [A.alloc([8, 1024], BF16) for _ in range(2)]
        wdt = A.alloc([16, D], BF16)
        xg = A.alloc([NJ, D], BF16)
        xeT = A.alloc([8, CAP], BF16)
        hidT = A.alloc([16, CAP], BF16)
        sil = [A.alloc([512], F32) for _ in range(2)]
        ysb = [A.alloc([D], F32) for _ in range(2)]
        pT = bk[0][:, :].bitcast(BF16).rearrange("p (a b) -> p a b", a=8)
        nexp = self.ns_limit and min(16, self.ns_limit * 2) or 16
        NH = (CAP + 511) // 512
        HW_ = CAP // NH

        def ld_gu(ex, hf):
            fs = slice(hf * 1024, (hf + 1) * 1024)
            self.load_cast(wgh[hf], I["exp_w_gate"][l, ex][:, fs].rearrange("(c p) n -> p c n", p=128), 1024, ("wg", hf))
            self.load_cast(wuh[hf], I["exp_w_up"][l, ex][:, fs].rearrange("(c p) n -> p c n", p=128), 1024, ("wu", hf))

        def ld_d(ex):
            self.load_cast(wdt, I["exp_w_down"][l, ex].rearrange("(c p) n -> p c n", p=128), D, "wdt")

        def gathers(ex):
            for j in range(NJ):
                self.P.dma(lambda e, j=j, ex=ex: e.indirect_dma_start(
                    out=xg[:, j, :], out_offset=None, in_=SC["h1b"][:, :],
                    in_offset=bass.IndirectOffsetOnAxis(ap=self.IDX[:, j, ex:ex + 1], axis=0)),
                    ["IDX", "h1bd"], [("xg", j)], q="pool")

        ld_gu(0, 0)
        ld_gu(0, 1)
        gathers(0)
        ld_d(0)
        cnt = 0
        for ex in range(nexp):
            for j in range(NJ):
                for c in range(8):
                    self.tr(pT[:, c, :], xg[:, j, c * 128:(c + 1) * 128], self.ident_b, [("xg", j), "ident_b"], [("B", 0)])
                self.cp("dve", xeT[:, :, j * 128:(j + 1) * 128], pT, [], ["xeT", ("B", 0)])
            for fb in range(16):
                fh = fb // 8
                fs = slice((fb % 8) * 128, (fb % 8 + 1) * 128)
                for hf in range(NH):
                    ss_ = slice(hf * HW_, (hf + 1) * HW_)
                    gb, ub = bk[1 + cnt % 2], bk[3 + cnt % 2]
                    gk_, uk_ = ("B", 1 + cnt % 2), ("B", 3 + cnt % 2)
                    sl_ = sil[cnt % 2]
                    sk_ = ("sil", cnt % 2)
                    cnt += 1
                    for c in range(8):
                        self.mm(gb[:, 0:HW_], wgh[fh][:, c, fs], xeT[:, c, ss_], c == 0, c == 7, [("wg", fh), "xeT"], [gk_])
                    for c in range(8):
                        self.mm(ub[:, 0:HW_], wuh[fh][:, c, fs], xeT[:, c, ss_], c == 0, c == 7, [("wu", fh), "xeT"], [uk_])
                    self.act(sl_[:, 0:HW_], gb[:, 0:HW_], AF.Silu, [], [sk_, gk_])
                    self.tt("dve", hidT[:, fb, ss_], ub[:, 0:HW_], sl_[:, 0:HW_], ALU.mult, [sk_], ["hidT", uk_])
                if fb % 8 == 7 and ex + 1 < nexp:
                    ld_gu(ex + 1, fh)
            if ex + 1 < nexp:
                gathers(ex + 1)
            for j in range(NJ):
                yb = ysb[j % 2]
                yk = ("ysb", j % 2)
                for g2 in range(2):
                    pb = bk[5 + g2]
                    pk = ("B", 5 + g2)
                    cs = slice(g2 * 512, (g2 + 1) * 512)
                    for fb in range(16):
                        self.mm(pb[:, :], hidT[:, fb, j * 128:(j + 1) * 128], wdt[:, fb, cs], fb == 0, fb == 15, ["hidT", "wdt"], [pk])
                    self.act(yb[:, cs], pb[:, :], AF.Identity, ["GATE"], [yk, pk], scale=self.GATE[:, j, ex:ex + 1])
                self.P.dma(lambda e, j=j, ex=ex, yb=yb: e.indirect_dma_start(
                    out=SC["acc"][:, :], out_offset=bass.IndirectOffsetOnAxis(ap=self.IDX[:, j, ex:ex + 1], axis=0),
                    in_=yb, in_offset=None, compute_op=ALU.add),
                    ["IDX", yk], ["acc"], q="pool")
            if ex + 1 < nexp:
                ld_d(ex + 1)
        P.barrier()

    def phase8(self, l, final):
        A, P, I, SC = self.A, self.P, self.inp, self.scr
        A.reset(self.const_top)
        gam = A.alloc([D], F32)
        bet = A.alloc([D], F32)
        self.ln_tmp = A.alloc([D], F32)
        at = [A.alloc([4, D], F32) for _ in range(2)]
        st6 = A.alloc([2, 6], F32); mv = A.alloc([2], F32); rstd = A.alloc([1], F32); nb = A.alloc([1], F32)
        self.ld(gam, I["ln2_g"][l][None, :].to_broadcast([128, D]), (), ["lnpar"])
        self.ld(bet, I["ln2_b"][l][None, :].to_broadcast([128, D]), (), ["lnpar"])
        dst = self.out if final else SC["hbuf"]
        for s_ in range(self.ns_limit or NS):
            bs = s_ % 2
            t0 = s_ * 512
            ak = ("at", bs)
            self.ld(at[bs], SC["acc"][t0:t0 + 512, :].rearrange("(j p) d -> p j d", p=128), ["acc"], [ak])
            for j in range(4):
                self.layernorm_tile(at[bs][:, j, :], at[bs][:, j, :], None, gam, bet, st6, mv, rstd, nb, "ln", [ak], [ak], None)
            self.st(dst[t0:t0 + 512, :].rearrange("(j p) d -> p j d", p=128), at[bs], [ak], ["dst"], is_out=final)
        P.barrier()

    def build(self):
        self.declare()
        self.consts()
        self.phase0()
        for l in self.layers:
            self.phase1(l)
            self.phase2(l)
            self.phase3(l)
            self.phase4(l)
            self.phase5(l)
            self.phase6(l)
            self.phase7(l)
            self.phase8(l, final=(l == self.layers[-1]))
        return self.P.finalize()

def _consts():
    half = 16
    freq = (10000.0 ** (-np.arange(half, dtype=np.float32) / half)).astype(np.float32)
    tp = np.arange(64)[:, None]
    ii = np.arange(64)[None, :]
    tri = np.stack([tp <= ii, tp > ii, tp >= ii, tp < ii]).astype(np.float32)
    return {"c_ident": np.eye(128, dtype=np.float32), "c_freq": freq, "c_tri": tri}


_CACHE = {}


def kernel(**inputs):
    if "nc" not in _CACHE:
        k = K()
        _CACHE["nc"] = k.build()
        _CACHE["names"] = list(k.inp.keys())
    nc = _CACHE["nc"]
    cst = _consts()
    maps = []
    for b in range(2):
        m = {}
        for n in _CACHE["names"]:
            if n == "x":
                m[n] = np.ascontiguousarray(inputs["x"][b], dtype=np.float32)
            elif n == "positions":
                m[n] = np.ascontiguousarray(inputs["positions"][b], dtype=np.int32)
            elif n in cst:
                m[n] = cst[n]
            else:
                m[n] = np.ascontiguousarray(inputs[n], dtype=np.float32)
        maps.append(m)
    res = run_bass_kernel_spmd(nc, maps, core_ids=[0, 1])
    return np.stack([np.asarray(res.results[b]["out"], dtype=np.float32) for b in range(2)], axis=0)
```

```python
from contextlib import ExitStack
import math
import numpy as np
import concourse.bass as bass
import concourse.mybir as mybir
from concourse.bass_utils import run_bass_kernel_spmd

F32 = mybir.dt.float32
BF16 = mybir.dt.bfloat16
I32 = mybir.dt.int32
U32 = mybir.dt.uint32
AF = mybir.ActivationFunctionType
ALU = mybir.AluOpType
AX = mybir.AxisListType

COMPUTE = ("pe", "act", "dve", "pool")
DMAQ_K = 8


def dsize(dt):
    return {F32: 4, BF16: 2, I32: 4, U32: 4}[dt]


class Op:
    __slots__ = ("eng", "fn", "deps", "is_dma", "idx", "marked", "done", "q")

    def __init__(self, eng, fn, is_dma=False, q=None):
        self.eng = eng
        self.fn = fn
        self.deps = []
        self.is_dma = is_dma
        self.idx = None
        self.marked = False
        self.done = None
        self.q = q


class Prog:
    def __init__(self):
        self.nc = bass.Bass("TRN2", target_bir_lowering=False)
        self.stack = ExitStack()
        self.ops = {e: [] for e in ("pe", "act", "dve", "pool", "sp")}
        self.reg = {}
        self.dma_count = {"sp": 0, "pool": 0}
        self.dma_ops = {"sp": [], "pool": []}
        self.out_dmas = []
        self.same_engine_sync = True
        self.debug_out = set()

    def sb(self, name, shape, dt):
        return self.stack.enter_context(self.nc.sbuf_tensor(name, list(shape), dt))

    def ps(self, name, shape, dt):
        return self.stack.enter_context(self.nc.psum_tensor(name, list(shape), dt))

    def dram(self, name, shape, dt, kind="Internal"):
        if kind == "Internal" and name in self.debug_out:
            kind = "ExternalOutput"
        return self.nc.dram_tensor(name, list(shape), dt, kind=kind).ap()

    def _deps(self, op, reads, writes):
        for k in reads:
            st = self.reg.get(k)
            if st is None:
                st = self.reg[k] = [None, []]
            if st[0] is not None:
                op.deps.append(st[0])
            st[1].append(op)
        for k in writes:
            st = self.reg.get(k)
            if st is None:
                st = self.reg[k] = [None, []]
            if st[0] is not None:
                op.deps.append(st[0])
            for r in st[1]:
                if r is not op:
                    op.deps.append(r)
            st[0] = op
            st[1] = []

    def op(self, eng, fn, reads=(), writes=()):
        o = Op(eng, fn)
        self._deps(o, reads, writes)
        self.ops[eng].append(o)
        return o

    def dma(self, fn, reads=(), writes=(), q="sp", is_out=False):
        o = Op(q, fn, is_dma=True, q=q)
        o.idx = self.dma_count[q]
        self.dma_count[q] += 1
        self.dma_ops[q].append(o)
        self._deps(o, reads, writes)
        self.ops[q].append(o)
        if is_out:
            self.out_dmas.append(o)
        return o

    def barrier(self):
        lasts = []
        for e in COMPUTE:
            for o in reversed(self.ops[e]):
                if not o.is_dma and o.fn is not None:
                    lasts.append(o)
                    break
        for q in ("sp", "pool"):
            lasts.extend(self.dma_ops[q][-DMAQ_K:])
        for e in ("pe", "act", "dve", "pool", "sp"):
            o = Op(e, None)
            o.deps = list(lasts)
            self.ops[e].append(o)
        self.reg = {}

    def finalize(self):
        nc = self.nc
        sync_same = self.same_engine_sync

        def skip(d, o):
            return (not d.is_dma) and (not o.is_dma) and d.eng == o.eng and o.fn is not None and (d.eng == "pe" or not sync_same)

        for e, lst in self.ops.items():
            for o in lst:
                for d in o.deps:
                    if d.is_dma or skip(d, o):
                        continue
                    d.marked = True
        sems = {}
        for e in COMPUTE:
            sems[e] = self.stack.enter_context(nc.semaphore("s_" + e))
            n = 0
            for o in self.ops[e]:
                if o.is_dma or o.fn is None:
                    continue
                if o.marked:
                    n += 1
                    o.done = (sems[e], n)
        dsems = {}
        for q in ("sp", "pool"):
            if self.dma_count[q] == 0:
                continue
            dsems[q] = [self.stack.enter_context(nc.semaphore("d_%s%d" % (q, i))) for i in range(DMAQ_K)]
            for o in self.dma_ops[q]:
                o.done = (dsems[q][o.idx % DMAQ_K], 16 * (o.idx // DMAQ_K + 1))
        block = self.stack.enter_context(nc.Block())

        def emit(ename, e):
            waited = {}
            for o in self.ops[ename]:
                need = {}
                for d in o.deps:
                    if d.done is None or skip(d, o):
                        continue
                    s, v = d.done
                    if need.get(s.num, (None, 0))[1] < v:
                        need[s.num] = (s, v)
                if o.is_dma and o.idx >= DMAQ_K:
                    s = dsems[o.q][o.idx % DMAQ_K]
                    v = 16 * (o.idx // DMAQ_K)
                    if need.get(s.num, (None, 0))[1] < v:
                        need[s.num] = (s, v)
                for sn, (s, v) in need.items():
                    if waited.get(sn, 0) < v:
                        e.wait_ge(s, v)
                        waited[sn] = v
                if o.fn is None:
                    continue
                ins = o.fn(e)
                if o.is_dma:
                    ins.then_inc(o.done[0], 16)
                elif o.marked:
                    ins.then_inc(o.done[0], 1)
            if ename == "sp":
                for o in self.out_dmas:
                    s, v = o.done
                    if waited.get(s.num, 0) < v:
                        e.wait_ge(s, v)
                        waited[s.num] = v

        @block.tensor
        def _(e):
            emit("pe", e)

        @block.scalar
        def _(e):
            emit("act", e)

        @block.vector
        def _(e):
            emit("dve", e)

        @block.gpsimd
        def _(e):
            emit("pool", e)

        @block.sync
        def _(e):
            emit("sp", e)

        self.stack.close()
        return nc


class Arena:
    def __init__(self, P, kbytes):
        self.words = kbytes * 256
        self.t = P.sb("arena", [128, self.words], F32)
        self.off = 0

    def reset(self, to=0):
        self.off = to

    def alloc(self, free_shape, dt, parts=128, p0=0):
        n = 1
        for s in free_shape:
            n *= s
        nb = n * dsize(dt)
        w = (nb + 31) // 32 * 8
        assert self.off + w <= self.words, ("arena overflow", self.off, w, self.words)
        v = self.t[p0:p0 + parts, self.off:self.off + w]
        self.off += w
        if dt != F32:
            v = v.bitcast(dt)
        v = v[:, 0:n]
        if len(free_shape) == 2:
            v = v.rearrange("p (a b) -> p a b", a=free_shape[0])
        elif len(free_shape) == 3:
            v = v.rearrange("p (a b c) -> p a b c", a=free_shape[0], b=free_shape[1])
        return v


S = 8192
D = 1024
NT = S // 128
NS = S // 512
POOL_WINDOWS = (2, 4, 8, 16)
LN_EPS = 1e-5
RMS_EPS = 1e-6
DN_ALPHA = 4 ** 0.25
QSCALE = 96 ** -0.5
TWO_PI = 2.0 * math.pi
CW1 = 6.28125
CW2 = TWO_PI - CW1

C_UP, C_CQ, C_CKV, C_KR, C_GQ, C_GK, C_GV, C_GR, C_GD, C_GATE = 0, 512, 896, 1152, 1184, 1440, 1696, 2208, 2720, 2752


class K:
    def __init__(self, debug_out=(), layers=(0, 1), phases=None, ns_limit=None, skip=(), no_big=False):
        self.ns_limit = ns_limit
        self.skip = set(skip)
        self.no_big = no_big
        self.P = P = Prog()
        P.debug_out = set(debug_out)
        self.layers = layers
        self.phases = phases
        self.nc = P.nc
        self.A = Arena(P, 204)
        self.banks = [P.ps("bank%d" % i, [128, 512], F32) for i in range(8)]
        self.inp = {}
        self.scr = {}

    def din(self, name, shape, dt=F32):
        self.inp[name] = self.P.dram(name, shape, dt, kind="ExternalInput")
        return self.inp[name]

    def scratch(self, name, shape, dt):
        self.scr[name] = self.P.dram(name, shape, dt)
        return self.scr[name]

    def mm(self, out, lhsT, rhs, start, stop, reads, writes):
        self.P.op("pe", lambda e: e.matmul(out, lhsT=lhsT, rhs=rhs, start=start, stop=stop), reads, writes)

    def tr(self, out, in_, ident, reads, writes):
        self.P.op("pe", lambda e: e.transpose(out=out, in_=in_, identity=ident), reads, writes)

    def act(self, out, in_, func, reads, writes, bias=None, scale=None, accum_out=None):
        kw = {}
        if bias is not None:
            kw["bias"] = bias
        if scale is not None:
            kw["scale"] = scale
        if accum_out is not None:
            kw["accum_out"] = accum_out
        self.P.op("act", lambda e: e.activation(out=out, in_=in_, func=func, **kw), reads, writes)

    def ts(self, eng, out, in0, s1, s2, op0, op1, reads, writes):
        if op1 is None:
            self.P.op(eng, lambda e: e.tensor_scalar(out=out, in0=in0, scalar1=s1, scalar2=None, op0=op0), reads, writes)
        else:
            self.P.op(eng, lambda e: e.tensor_scalar(out=out, in0=in0, scalar1=s1, scalar2=s2, op0=op0, op1=op1), reads, writes)

    def tt(self, eng, out, in0, in1, op, reads, writes):
        self.P.op(eng, lambda e: e.tensor_tensor(out=out, in0=in0, in1=in1, op=op), reads, writes)

    def stt(self, eng, out, in0, scalar, in1, op0, op1, reads, writes):
        self.P.op(eng, lambda e: e.scalar_tensor_tensor(out=out, in0=in0, scalar=scalar, in1=in1, op0=op0, op1=op1), reads, writes)

    def cp(self, eng, out, in_, reads, writes):
        if eng == "act":
            self.P.op("act", lambda e: e.copy(out=out, in_=in_), reads, writes)
        else:
            self.P.op(eng, lambda e: e.tensor_copy(out=out, in_=in_), reads, writes)

    def memset(self, eng, ap, val, writes):
        self.P.op(eng, lambda e: e.memset(ap, val), (), writes)

    def ld(self, out, in_, reads, writes, q="sp", slow=False):
        if slow:
            self.P.dma(lambda e: e.dma_start(out=out, in_=in_, allow_slow_non_contiguous=True), reads, writes, q=q)
        else:
            self.P.dma(lambda e: e.dma_start(out=out, in_=in_), reads, writes, q=q)

    def st(self, out, in_, reads, writes, q="pool", is_out=False):
        self.P.dma(lambda e: e.dma_start(out=out, in_=in_), reads, writes, q=q, is_out=is_out)

    def declare(self):
        din = self.din
        din("x", [S, D])
        din("positions", [S], I32)
        din("ln0_g", [D]); din("ln0_b", [D])
        din("w_in", [2, D, 5824]); din("b_gate", [2, 3072])
        din("pool_w", [2, 4, 128, 128]); din("pool_scale", [2, 512]); din("w_up_a", [2, 512, D])
        din("mla_q_norm", [2, 384]); din("mla_w_uq", [2, 384, 768]); din("mla_kv_norm", [2, 256])
        din("mla_w_ukv", [2, 256, 1024]); din("w_up_b", [2, 512, D])
        din("gla_w_dec", [2, 2, 16, 256]); din("gla_b_dec", [2, 2, 256]); din("gla_norm", [2, 128])
        din("w_up_c", [2, 512, D]); din("w_out", [2, D, D])
        din("ln1_g", [2, D]); din("ln1_b", [2, D]); din("router_w", [2, D, 16])
        if not self.no_big:
            din("exp_w_gate", [2, 16, D, 2048]); din("exp_w_up", [2, 16, D, 2048]); din("exp_w_down", [2, 16, 2048, D])
        din("ln2_g", [2, D]); din("ln2_b", [2, D])
        din("c_ident", [128, 128])
        din("c_freq", [16])
        din("c_tri", [4, 64, 64])
        self.out = self.P.dram("out", [S, D], F32, kind="ExternalOutput")
        sc = self.scratch
        sc("hbuf", [S, D], F32)
        sc("cosF", [32, S], F32); sc("sinF", [32, S], F32)
        sc("uT", [512, S], F32)
        sc("gqT", [256, S], F32); sc("gkT", [256, S], F32); sc("gdT", [32, S], F32)
        sc("gk_tok", [S, 256], F32); sc("gv", [S, 512], BF16); sc("gr", [S, 512], BF16)
        sc("QT", [8, 96, S], BF16); sc("KT", [8, 64, S], BF16); sc("KRT", [32, S], BF16); sc("V", [S, 512], BF16)
        sc("pmT", [512, S], BF16); sc("mlT", [512, S], BF16); sc("glT", [512, S], BF16)
        sc("h1", [S, D], F32); sc("h1b", [S, D], BF16); sc("acc", [S, D], F32)

    def consts(self):
        A = self.A
        self.ident_f = A.alloc([128], F32)
        self.ident_b = A.alloc([128], BF16)
        self.eps_ln = A.alloc([1], F32)
        self.eps_rms = A.alloc([1], F32)
        self.cosT = A.alloc([NT, 16], F32)
        self.sinT = A.alloc([NT, 16], F32)
        self.ld(self.ident_f, self.inp["c_ident"][:, :], (), ["ident_f"])
        self.cp("dve", self.ident_b, self.ident_f, ["ident_f"], ["ident_b"])
        self.memset("pool", self.eps_ln, LN_EPS, ["eps_ln"])
        self.memset("pool", self.eps_rms, RMS_EPS, ["eps_rms"])
        self.IDX = A.alloc([S // 1024, 16], I32)
        self.GATE = A.alloc([S // 1024, 16], F32)
        self.const_top = A.off

    def range_reduce_sin(self, ang, tmp, tmpi, res, shift, k_ang, k_tmp, k_res):
        self.ts("dve", tmp, ang, 1.0 / TWO_PI, None, ALU.mult, None, [k_ang], [k_tmp])
        self.cp("dve", tmpi, tmp, [k_tmp], [k_tmp + "i"])
        self.cp("dve", tmp, tmpi, [k_tmp + "i"], [k_tmp])
        self.stt("dve", res, tmp, -CW1, ang, ALU.mult, ALU.add, [k_tmp, k_ang], [k_res])
        self.stt("dve", res, tmp, -CW2, res, ALU.mult, ALU.add, [k_tmp, k_res], [k_res])
        if shift != 0.0:
            self.ts("dve", res, res, shift, None, ALU.add, None, [k_res], [k_res])
        self.ts("dve", tmp, res, math.pi, None, ALU.is_gt, None, [k_res], [k_tmp])
        self.stt("dve", res, tmp, -TWO_PI, res, ALU.mult, ALU.add, [k_tmp, k_res], [k_res])
        self.ts("dve", res, res, math.pi, -math.pi, ALU.min, ALU.max, [k_res], [k_res])
        self.act(res, res, AF.Sin, [k_res], [k_res])

    def phase0(self):
        A, P = self.A, self.P
        A.reset(self.const_top)
        posi = A.alloc([NT], I32)
        posf = A.alloc([NT], F32)
        freq = A.alloc([16], F32)
        ang = A.alloc([NT, 16], F32)
        tmp = A.alloc([NT, 16], F32)
        tmpi = A.alloc([NT, 16], I32)
        pos = self.inp["positions"]
        pv = pos.rearrange("(j p) -> p j", p=128)
        for a in range(0, NT, 8):
            self.ld(posi[:, a:a + 8], pv[:, a:a + 8], (), ["posi"], slow=True)
        self.ld(freq, self.inp["c_freq"][None, :].to_broadcast([128, 16]), (), ["freq"])
        self.cp("dve", posf, posi, ["posi"], ["posf"])
        for j in range(NT):
            self.ts("dve", ang[:, j, :], freq, posf[:, j:j + 1], None, ALU.mult, None, ["posf", "freq"], ["angT"])
        self.range_reduce_sin(ang, tmp, tmpi, self.sinT, 0.0, "angT", "tmpT", "sinT")
        self.range_reduce_sin(ang, tmp, tmpi, self.cosT, math.pi / 2, "angT", "tmpT", "cosT")
        CH = min(2048, S)
        prow_i = A.alloc([CH], I32)
        prow = A.alloc([CH], F32)
        fcol = A.alloc([1], F32)
        scol_c = A.alloc([1], F32)
        scol_s = A.alloc([1], F32)
        angF = A.alloc([CH], F32)
        tmpF = A.alloc([CH], F32)
        tmpFi = A.alloc([CH], I32)
        resF = A.alloc([CH], F32)
        fr = self.inp["c_freq"]
        self.ld(fcol[64:80, :], fr.rearrange("(a b) -> a b", b=1), (), ["fcol"], slow=True)
        self.ld(fcol[80:96, :], fr.rearrange("(a b) -> a b", b=1), ["fcol"], ["fcol"], slow=True)
        self.memset("pool", scol_c[64:96, :], QSCALE, ["scol_c"])
        self.memset("pool", scol_s[64:96, :], QSCALE, ["scol_s"])
        self.memset("pool", scol_s[64:80, :], -QSCALE, ["scol_s"])
        sl = slice(64, 96)
        for c in range(S // CH):
            self.ld(prow_i[sl, :], pos[None, c * CH:(c + 1) * CH].to_broadcast([32, CH]), (), ["prow_i"])
            self.cp("dve", prow[sl, :], prow_i[sl, :], ["prow_i"], ["prow"])
            self.ts("dve", angF[sl, :], prow[sl, :], fcol[sl, 0:1], None, ALU.mult, None, ["prow", "fcol"], ["angF"])
            for (shift, scol, dst) in ((0.0, scol_s, "sinF"), (math.pi / 2, scol_c, "cosF")):
                self.range_reduce_sin(angF[sl, :], tmpF[sl, :], tmpFi[sl, :], resF[sl, :], shift, "angF", "tmpF", "resF")
                self.ts("dve", resF[sl, :], resF[sl, :], scol[sl, 0:1], None, ALU.mult, None, ["resF", "scol_s", "scol_c"], ["resF"])
                self.st(self.scr[dst][:, c * CH:(c + 1) * CH], resF[sl, :], ["resF"], [dst])
        P.barrier()

    def layernorm_tile(self, x, out_f, out_b, gam, bet, st6, mv, rstd, nb, key, reads, writes_f, writes_b, cast_eng="pool"):
        for c in range(2):
            self.P.op("dve", lambda e, c=c: e.bn_stats(out=st6[:, c, :], in_=x[:, c * 512:(c + 1) * 512]), reads, [(key, "st", c)])
        self.P.op("dve", lambda e: e.bn_aggr(out=mv, in_=st6.rearrange("p a b -> p (a b)")), [(key, "st", 0), (key, "st", 1)], [(key, "mv")])
        self.act(rstd, mv[:, 1:2], AF.Ln, [(key, "mv"), "eps_ln"], [(key, "rstd")], bias=self.eps_ln[:, 0:1], scale=1.0)
        self.act(rstd, rstd, AF.Exp, [(key, "rstd")], [(key, "rstd")], scale=-0.5)
        self.stt("dve", nb, mv[:, 0:1], -1.0, rstd, ALU.mult, ALU.mult, [(key, "mv"), (key, "rstd")], [(key, "nb")])
        tmpk = (key, "xn")
        self.act(self.ln_tmp, x, AF.Identity, list(reads) + [(key, "rstd"), (key, "nb")], [tmpk], bias=nb[:, 0:1], scale=rstd[:, 0:1])
        self.tt("dve", self.ln_tmp, self.ln_tmp, gam, ALU.mult, [tmpk, "lnpar"], [tmpk])
        if out_f is not None:
            self.tt("dve", out_f, self.ln_tmp, bet, ALU.add, [tmpk, "lnpar"], writes_f)
            if out_b is not None:
                self.cp(cast_eng, out_b, out_f, writes_f, writes_b)
        else:
            self.tt("dve", out_b, self.ln_tmp, bet, ALU.add, [tmpk, "lnpar"], writes_b)


    def load_cast(self, dst, src, n_inner, key, reads=()):
        shp = list(dst.shape)
        if len(shp) == 3:
            C = shp[1]
            cstep = max(1, 1024 // shp[0])
            nsp = (n_inner + 2047) // 2048
            step = (n_inner + nsp - 1) // nsp
            for c0 in range(0, C, cstep):
                c1 = min(C, c0 + cstep)
                for a in range(0, n_inner, step):
                    b = min(n_inner, a + step)
                    self.P.dma(lambda e, a=a, b=b, c0=c0, c1=c1: e.dma_start(out=dst[:, c0:c1, a:b], in_=src[:, c0:c1, a:b]), reads, [key], q="pool")
        else:
            self.P.dma(lambda e: e.dma_start(out=dst, in_=src), reads, [key], q="pool")

    def phase1(self, l):
        A, P, I, SC = self.A, self.P, self.inp, self.scr
        A.reset(self.const_top)
        bk = self.banks
        win = A.alloc([8, 2752], BF16)
        wuq = A.alloc([3, 768], BF16)
        wuqr = A.alloc([3, 768], BF16)
        wukv = A.alloc([2, 1024], BF16)
        qng = A.alloc([3], F32)
        kvng = A.alloc([2], F32)
        gam = A.alloc([D], F32)
        bet = A.alloc([D], F32)
        self.ln_tmp = A.alloc([D], F32)
        xt = [A.alloc([4, D], F32) for _ in range(2)]
        hb = A.alloc([4, D], BF16)
        hT = [A.alloc([8, 512], BF16) for _ in range(2)]
        st6 = A.alloc([2, 6], F32); mv = A.alloc([2], F32); rstd = A.alloc([1], F32); nb = A.alloc([1], F32)
        ss = A.alloc([4], F32)
        cqn = A.alloc([384], BF16)
        ckvn = A.alloc([256], BF16)
        cqnT = A.alloc([3, 512], BF16)
        ckvnT = A.alloc([2, 512], BF16)
        junk = A.alloc([384], F32)
        junk2 = A.alloc([256], F32)
        st_u = A.alloc([4, 512], F32)
        st_q = A.alloc([2, 512], F32)
        st_k = A.alloc([2, 512], F32)
        st_d = A.alloc([512], F32)
        st_kt = A.alloc([4, 256], F32)
        st_v = A.alloc([4, 512], BF16)
        st_r = A.alloc([4, 512], BF16)
        st_V = A.alloc([4, 512], BF16)
        st_Q = A.alloc([8, 512], BF16)
        st_K = A.alloc([8, 512], BF16)
        st_kr = A.alloc([512], BF16)
        krt = A.alloc([32], F32)
        krb = A.alloc([128], BF16)
        rt1 = A.alloc([16], F32); rt2 = A.alloc([16], F32)
        cosF = A.alloc([512], F32); sinF = A.alloc([512], F32)
        qt1 = A.alloc([512], F32); qt2 = A.alloc([512], F32)

        self.memset("pool", krb, 0.0, ["krb"])
        w_in = I["w_in"][l]
        self.load_cast(win, w_in[:, 0:2752].rearrange("(c p) n -> p c n", p=128), 2752, "win")
        self.load_cast(wuq, I["mla_w_uq"][l].rearrange("(c p) n -> p c n", p=128), 768, "wuq")
        wq4 = I["mla_w_uq"][l].rearrange("(c p) (h x) -> p c h x", p=128, x=96)
        wuqr4 = wuqr.rearrange("p c (h x) -> p c h x", x=96)
        for c in range(3):
            self.load_cast(wuqr4[:, c, :, 64:80], wq4[:, c, :, 80:96], 16, "wuqr")
            self.load_cast(wuqr4[:, c, :, 80:96], wq4[:, c, :, 64:80], 16, "wuqr")
        self.memset("pool", wuqr4[:, :, :, 0:64], 0.0, ["wuqr"])
        self.load_cast(wukv, I["mla_w_ukv"][l].rearrange("(c p) n -> p c n", p=128), 1024, "wukv")
        self.ld(qng, I["mla_q_norm"][l].rearrange("(c p) -> p c", p=128), (), ["qng"], slow=True)
        self.ld(kvng, I["mla_kv_norm"][l].rearrange("(c p) -> p c", p=128), (), ["kvng"], slow=True)
        if l == 0:
            self.ld(gam, I["ln0_g"][None, :].to_broadcast([128, D]), (), ["lnpar"])
            self.ld(bet, I["ln0_b"][None, :].to_broadcast([128, D]), (), ["lnpar"])
        src = I["x"] if l == 0 else SC["hbuf"]
        pT = bk[0][:, 0:512].bitcast(BF16).rearrange("p (a b) -> p a b", a=8)
        fm_blocks = [("u", g, C_UP + g * 128, 128) for g in range(4)] + [("q", m, C_GQ + m * 128, 128) for m in range(2)] \
            + [("k", m, C_GK + m * 128, 128) for m in range(2)] + [("d", 0, C_GD, 32)]
        nfm = 0
        ntm = 0
        for s in range(self.ns_limit or NS):
            sl = s % 2
            t0 = s * 512
            xk = ("xt", sl)
            self.ld(xt[sl], src[t0:t0 + 512, :].rearrange("(j p) d -> p j d", p=128), (), [xk])
            self.ld(cosF[64:96, :], SC["cosF"][:, t0:t0 + 512], (), ["cosFs"])
            self.ld(sinF[64:96, :], SC["sinF"][:, t0:t0 + 512], (), ["sinFs"])
            hTk = ("hT", sl)
            for j in range(4):
                hbk = ("hb", j)
                if l == 0:
                    self.layernorm_tile(xt[sl][:, j, :], xt[sl][:, j, :], hb[:, j, :], gam, bet, st6, mv, rstd, nb, "ln", [xk], [xk], [hbk], cast_eng="act")
                else:
                    self.cp("act", hb[:, j, :], xt[sl][:, j, :], [xk], [hbk])
                for c in range(8):
                    self.tr(pT[:, c, :], hb[:, j, c * 128:(c + 1) * 128], self.ident_b, [hbk, "ident_b"], ["pT"])
                self.cp("dve", hT[sl][:, :, j * 128:(j + 1) * 128], pT, [], [hTk, "pT"])
            if l == 0:
                self.st(SC["hbuf"][t0:t0 + 512, :].rearrange("(j p) d -> p j d", p=128), xt[sl], [xk], ["hbuf"])
            for (kind, idx, c0, w) in (fm_blocks if 'fm' not in self.skip else []):
                pb = bk[1 + nfm % 2]
                pk = ("fm", nfm % 2)
                nfm += 1
                for c in range(8):
                    self.mm(pb[0:w, :], win[:, c, c0:c0 + w], hT[sl][:, c, :], c == 0, c == 7, ["win", hTk], [pk])
                if kind == "u":
                    self.cp("act", st_u[:, idx, :], pb[:, :], [], ["st_u", pk])
                elif kind == "q":
                    self.act(st_q[:, idx, :], pb[:, :], AF.Identity, [], ["st_q", pk], scale=0.125)
                elif kind == "k":
                    self.cp("dve", st_k[:, idx, :], pb[:, :], [], ["st_k", pk])
                else:
                    self.cp("dve", st_d[0:32, :], pb[0:32, :], [], ["st_d", pk])
            if 'fm' not in self.skip:
                self.st(SC["uT"][:, t0:t0 + 512].rearrange("(g p) t -> p g t", p=128), st_u, ["st_u"], ["uT"])
                self.st(SC["gqT"][:, t0:t0 + 512].rearrange("(g p) t -> p g t", p=128), st_q, ["st_q"], ["gqT"])
                self.st(SC["gkT"][:, t0:t0 + 512].rearrange("(g p) t -> p g t", p=128), st_k, ["st_k"], ["gkT"])
                self.st(SC["gdT"][:, t0:t0 + 512], st_d[0:32, :], ["st_d"], ["gdT"])
            for j in range(4):
                tt = s * 4 + j
                lh = lambda c: hT[sl][:, c, j * 128:(j + 1) * 128]

                def tmb(pb, pk, c0, w):
                    for c in range(8):
                        self.mm(pb[:, 0:w], lh(c), win[:, c, c0:c0 + w], c == 0, c == 7, ["win", hTk], [pk])

                def tm(c0, w):
                    nonlocal ntm
                    pb = bk[3 + ntm % 2]
                    pk = ("tm", ntm % 2)
                    ntm += 1
                    tmb(pb, pk, c0, w)
                    return pb, pk
                pbq, pkq = bk[6], "pqa"
                pbk, pkk_ = bk[7], "pqb"
                tmb(pbq, pkq, C_CQ, 384)
                tmb(pbk, pkk_, C_CKV, 288)
                self.act(junk[:, 0:384], pbq[:, 0:384], AF.Square, [], ["junk", "ss0", pkq], accum_out=ss[:, 0:1])
                self.act(ss[:, 1:2], ss[:, 0:1], AF.Ln, ["ss0", "eps_rms"], ["ss1"], bias=self.eps_rms[:, 0:1], scale=1.0 / 384)
                self.act(ss[:, 1:2], ss[:, 1:2], AF.Exp, ["ss1"], ["ss1"], scale=-0.5)
                self.ts("dve", cqn, pbq[:, 0:384], ss[:, 1:2], None, ALU.mult, None, ["ss1"], ["cqn", pkq])
                self.act(junk2[:, 0:256], pbk[:, 0:256], AF.Square, [], ["junk2", "ss2", pkk_], accum_out=ss[:, 2:3])
                self.act(ss[:, 3:4], ss[:, 2:3], AF.Ln, ["ss2", "eps_rms"], ["ss3"], bias=self.eps_rms[:, 0:1], scale=1.0 / 256)
                self.act(ss[:, 3:4], ss[:, 3:4], AF.Exp, ["ss3"], ["ss3"], scale=-0.5)
                self.ts("dve", ckvn, pbk[:, 0:256], ss[:, 3:4], None, ALU.mult, None, ["ss3"], ["ckvn", pkk_])
                cs = self.cosT[:, tt, :]
                sn = self.sinT[:, tt, :]
                self.cp("dve", krt, pbk[:, 256:288], [], ["krt", pkk_])
                self.tt("dve", rt1, krt[:, 0:16], cs, ALU.mult, ["krt", "cosT"], ["rt1"])
                self.tt("dve", rt2, krt[:, 16:32], sn, ALU.mult, ["krt", "sinT"], ["rt2"])
                self.tt("dve", krb[:, 0:16], rt1, rt2, ALU.subtract, ["rt1", "rt2"], ["krb"])
                self.tt("dve", rt1, krt[:, 0:16], sn, ALU.mult, ["krt", "sinT"], ["rt1"])
                self.tt("dve", rt2, krt[:, 16:32], cs, ALU.mult, ["krt", "cosT"], ["rt2"])
                self.tt("dve", krb[:, 16:32], rt1, rt2, ALU.add, ["rt1", "rt2"], ["krb"])
                pb, pk = tm(C_GK, 256)
                self.cp("act", st_kt[:, j, :], pb[:, 0:256], [], ["st_kt", pk])
                pb, pk = tm(C_GV, 512)
                self.cp("dve", st_v[:, j, :], pb[:, :], [], ["st_v", pk])
                pb, pk = tm(C_GR, 512)
                self.act(st_r[:, j, :], pb[:, :], AF.Silu, [], ["st_r", pk])
                pq = bk[5][:, 0:192].bitcast(BF16).rearrange("p (a b) -> p a b", a=3)
                for c in range(3):
                    self.tr(pq[:, c, :], cqn[:, c * 128:(c + 1) * 128], self.ident_b, ["cqn", "ident_b"], ["b5"])
                self.tt("dve", cqnT[:, :, j * 128:(j + 1) * 128], pq, qng[:, :, None].to_broadcast([128, 3, 128]), ALU.mult, ["qng"], ["cqnT", "b5"])
                pkv = bk[5][:, 256:448].bitcast(BF16).rearrange("p (a b) -> p a b", a=3)
                for c in range(2):
                    self.tr(pkv[:, c, :], ckvn[:, c * 128:(c + 1) * 128], self.ident_b, ["ckvn", "ident_b"], ["b5"])
                self.tr(pkv[:, 2, :], krb[:, :], self.ident_b, ["krb", "ident_b"], ["b5"])
                self.tt("dve", ckvnT[:, :, j * 128:(j + 1) * 128], pkv[:, 0:2, :], kvng[:, :, None].to_broadcast([128, 2, 128]), ALU.mult, ["kvng"], ["ckvnT", "b5"])
                self.cp("act", st_kr[0:32, j * 128:(j + 1) * 128], pkv[0:32, 2, :], [], ["st_kr", "b5"])
            rows = lambda name: SC[name][t0:t0 + 512, :].rearrange("(j p) d -> p j d", p=128)
            self.st(rows("gk_tok"), st_kt, ["st_kt"], ["gk_tok"])
            self.st(rows("gv"), st_v, ["st_v"], ["gv"])
            self.st(rows("gr"), st_r, ["st_r"], ["gr"])
            self.st(SC["KRT"][:, t0:t0 + 512], st_kr[0:32, :], ["st_kr"], ["KRT"])
            for h in (range(8) if 'mlaup' not in self.skip else []):
                pqa = bk[6]
                pqb = bk[7]
                for c in range(3):
                    self.mm(pqa[0:96, :], wuq[:, c, h * 96:(h + 1) * 96], cqnT[:, c, :], c == 0, c == 2, ["wuq", "cqnT"], ["pqa"])
                for c in range(3):
                    self.mm(pqb[0:96, :], wuqr[:, c, h * 96:(h + 1) * 96], cqnT[:, c, :], c == 0, c == 2, ["wuqr", "cqnT"], ["pqb"])
                self.act(st_Q[0:64, h, :], pqa[0:64, :], AF.Identity, [], ["st_Q", "pqa"], scale=QSCALE)
                self.tt("dve", qt1[64:96, :], pqa[64:96, :], cosF[64:96, :], ALU.mult, ["cosFs"], ["qt1", "pqa"])
                self.tt("dve", qt2[64:96, :], pqb[64:96, :], sinF[64:96, :], ALU.mult, ["sinFs"], ["qt2", "pqb"])
                self.tt("pool", st_Q[64:96, h, :], qt1[64:96, :], qt2[64:96, :], ALU.add, ["qt1", "qt2"], ["st_Q"])
                pka = bk[1 + h % 2]
                pkk = ("fm", h % 2)
                for c in range(2):
                    self.mm(pka[0:64, :], wukv[:, c, h * 128:h * 128 + 64], ckvnT[:, c, :], c == 0, c == 1, ["wukv", "ckvnT"], [pkk])
                self.cp("act", st_K[0:64, h, :], pka[0:64, :], [], ["st_K", pkk])
            self.st(SC["QT"][:, :, t0:t0 + 512].rearrange("h r t -> r h t"), st_Q[0:96, :, :], ["st_Q"], ["QT"])
            self.st(SC["KT"][:, :, t0:t0 + 512].rearrange("h r t -> r h t"), st_K[0:64, :, :], ["st_K"], ["KT"])
            wv = wukv.rearrange("p c (h x) -> p c h x", x=128)
            for j in (range(4) if 'vproj' not in self.skip else []):
                pb = bk[3 + ntm % 2]
                pk = ("tm", ntm % 2)
                ntm += 1
                for c in range(2):
                    self.mm(pb[:, :].rearrange("p (h x) -> p h x", x=64), ckvnT[:, c, j * 128:(j + 1) * 128], wv[:, c, :, 64:128], c == 0, c == 1, ["wukv", "ckvnT"], [pk])
                self.cp("dve", st_V[:, j, :], pb[:, :], [], ["st_V", pk])
            self.st(rows("V"), st_V, ["st_V"], ["V"])
        P.barrier()

    def phase2(self, l):
        A, P, SC = self.A, self.P, self.scr
        A.reset(self.const_top)
        bk = self.banks
        KTs = [A.alloc([S], BF16) for _ in range(2)]
        Vh = [A.alloc([NT, 65], BF16) for _ in range(2)]
        Qs = [A.alloc([512], BF16) for _ in range(2)]
        pts = [A.alloc([512], BF16) for _ in range(4)]
        ones = A.alloc([64], F32)
        rcp = A.alloc([512], F32)
        bc = A.alloc([512], F32)
        ost = [A.alloc([512], BF16) for _ in range(2)]
        self.memset("pool", ones, 1.0, ["ones"])
        for i in range(2):
            self.memset("pool", Vh[i][:, :, 64:65], 1.0, [("Vh", i)])
        nheads = self.ns_limit or 8
        nqg = self.ns_limit or NS
        LOOK = 2
        items = [(h, qg) for h in range(nheads) for qg in range(nqg)]
        loaded_heads = set()

        def load_head(h):
            if h in loaded_heads or h >= nheads:
                return
            loaded_heads.add(h)
            hs = h % 2
            self.ld(KTs[hs][0:64, :], SC["KT"][h], ["KT"], [("KTs", hs)])
            self.ld(KTs[hs][64:96, :], SC["KRT"][:, :], ["KRT"], [("KTs", hs)])
            vsrc = SC["V"][:, h * 64:(h + 1) * 64].rearrange("(kt p) v -> p kt v", p=128)
            for k0 in range(0, NT, 8):
                self.ld(Vh[hs][:, k0:k0 + 8, 0:64], vsrc[:, k0:k0 + 8, :], ["V"], [("Vh", hs)])

        def load_q(ii):
            if ii >= len(items):
                return
            h, qg = items[ii]
            qs = ii % 2
            self.ld(Qs[qs][0:96, :], SC["QT"][h, :, qg * 512:qg * 512 + 512], ["QT"], [("Qs", qs)])

        def qk(ii, kt):
            h, qg = items[ii]
            hs, qs = h % 2, ii % 2
            g = (ii * NT + kt) % 4
            self.mm(bk[g][:, :], KTs[hs][0:96, kt * 128:(kt + 1) * 128], Qs[qs][0:96, :], True, True, [("KTs", hs), ("Qs", qs)], [("B", g)])

        def epilogue(ii):
            h, qg = items[ii]
            qs = ii % 2
            q0 = qg * 512
            ob = bk[4 + qs]
            ok = ("B", 4 + qs)
            self.P.op("dve", lambda e, ob=ob: e.reciprocal(out=rcp[64:65, :], in_=ob[64:65, :]), [], ["rcp", ok])
            self.mm(bk[6][0:64, :], ones[64:65, :], rcp[64:65, :], True, True, ["ones", "rcp"], [("B", 6)])
            self.cp("act", bc[0:64, :], bk[6][0:64, :], [], ["bc", ("B", 6)])
            self.tt("dve", ost[qs][0:64, :], ob[0:64, :], bc[0:64, :], ALU.mult, ["bc"], [("ost", qs), ok])
            self.st(SC["mlT"][h * 64:(h + 1) * 64, q0:q0 + 512], ost[qs][0:64, :], [("ost", qs)], ["mlT"])

        load_head(0)
        load_q(0)
        for kt in range(LOOK):
            qk(0, kt)
        for ii, (h, qg) in enumerate(items):
            hs, qs = h % 2, ii % 2
            if qg == 0:
                load_head(h + 1)
            load_q(ii + 1)
            ob = bk[4 + qs]
            ok = ("B", 4 + qs)
            for kt in range(NT):
                nk = kt + LOOK
                if nk < NT:
                    qk(ii, nk)
                elif ii + 1 < len(items):
                    qk(ii + 1, nk - NT)
                g = (ii * NT + kt) % 4
                pt = pts[g]
                pk = ("pt", g)
                self.act(pt, bk[g][:, :], AF.Exp, [], [pk, ("B", g)])
                self.mm(ob[0:65, :], Vh[hs][:, kt, :], pt, kt == 0, kt == NT - 1, [("Vh", hs), pk], [ok])
                if kt == 3 and ii > 0:
                    epilogue(ii - 1)
            if ii == len(items) - 1:
                epilogue(ii)
        P.barrier()

    def phase4(self, l):
        A, P, I, SC = self.A, self.P, self.inp, self.scr
        A.reset(self.const_top)
        bk = self.banks
        wp = A.alloc([4, 128], BF16)
        psc = A.alloc([4], F32)
        inv_first = A.alloc([4, 512], F32)
        inv_last = A.alloc([4, 512], F32)
        U = [A.alloc([528], F32) for _ in range(2)]
        sA = A.alloc([528], F32)
        sB = A.alloc([528], F32)
        pl = A.alloc([512], BF16)
        pst = [A.alloc([512], BF16) for _ in range(2)]
        self.load_cast(wp, I["pool_w"][l].rearrange("g c d -> c g d"), 128, "wp")
        self.ld(psc, I["pool_scale"][l].rearrange("(g p) -> p g", p=128), (), ["psc"], slow=True)
        for g, w in enumerate(POOL_WINDOWS):
            self.memset("pool", inv_first[:, g, :], 1.0 / w, ["inv_first"])
            self.memset("pool", inv_last[:, g, :], 1.0 / w, ["inv_last"])
            for t in range(w // 2):
                self.memset("pool", inv_first[:, g, t:t + 1], 1.0 / (t + w // 2), ["inv_first"])
            for j in range(1, w // 2):
                self.memset("pool", inv_last[:, g, 512 - j:512 - j + 1], 1.0 / (j + w // 2), ["inv_last"])
        nch = self.ns_limit or NS
        it = 0
        for g, w in enumerate(POOL_WINDOWS):
            for ch in range(nch):
                us = it % 2
                it += 1
                t0 = ch * 512
                uk = ("U", us)
                lo = max(t0 - 8, 0)
                hi = min(t0 + 520, S)
                if ch == 0:
                    self.memset("pool", U[us][:, 0:8], 0.0, [uk])
                if ch == NS - 1:
                    self.memset("pool", U[us][:, 520:528], 0.0, [uk])
                self.ld(U[us][:, lo - (t0 - 8):hi - (t0 - 8)], SC["uT"][g * 128:(g + 1) * 128, lo:hi], ["uT"], [uk])
                cur = U[us]
                curk = uk
                k = 1
                dst = [sA, sB]
                di = 0
                while k < w:
                    d_ = dst[di]
                    dk = ("s", di)
                    self.tt("dve", d_[:, k:528], cur[:, k:528], cur[:, 0:528 - k], ALU.add, [curk], [dk])
                    cur, curk = d_, dk
                    di ^= 1
                    k *= 2
                off = 8 + w // 2 - 1
                sw = cur[:, off:off + 512]
                if ch == 0 or ch == NS - 1:
                    inv = inv_first if ch == 0 else inv_last
                    other, okey = (sB, ("s", 1)) if cur is sA else (sA, ("s", 0))
                    self.tt("dve", other[:, 0:512], sw, inv[:, g, :], ALU.mult, [curk, "inv_first", "inv_last"], [okey])
                    self.tt("dve", pl, other[:, 0:512], U[us][:, 8:520], ALU.subtract, [okey, uk], ["pl"])
                else:
                    self.stt("dve", pl, sw, 1.0 / w, U[us][:, 8:520], ALU.mult, ALU.subtract, [curk, uk], ["pl"])
                pb = bk[it % 2]
                pk = ("B", it % 2)
                self.mm(pb[:, :], wp[:, g, :], pl, True, True, ["wp", "pl"], [pk])
                self.ts("dve", pst[us], pb[:, :], psc[:, g:g + 1], None, ALU.mult, None, ["psc"], [("pst", us), pk])
                self.st(SC["pmT"][g * 128:(g + 1) * 128, t0:t0 + 512], pst[us], [("pst", us)], ["pmT"])
        P.barrier()

    def phase3(self, l):
        A, P, I, SC = self.A, self.P, self.inp, self.scr
        A.reset(self.const_top)
        bk = self.banks
        if "of" not in SC:
            self.scratch("of", [S, 512], F32)
        tri = A.alloc([4, 64], F32)
        one_c = A.alloc([1], F32)
        wdec = A.alloc([2, 256], F32)
        gn = A.alloc([128], F32)
        dl = [A.alloc([128], F32) for _ in range(2)]
        gq = [A.alloc([4, 128], F32) for _ in range(2)]
        gk = [A.alloc([4, 128], F32) for _ in range(2)]
        gkt = [A.alloc([2, 256], F32) for _ in range(2)]
        vv = [A.alloc([2, 512], BF16) for _ in range(2)]
        of = [A.alloc([2, 512], F32) for _ in range(2)]
        grt = [A.alloc([2, 512], BF16) for _ in range(2)]
        esb = A.alloc([512], F32)
        gp = A.alloc([2, 256], F32)
        E1s = [A.alloc([4, 2, 64], F32) for _ in range(2)]
        E2 = A.alloc([4, 2, 64], F32)
        E3 = A.alloc([2, 256], F32)
        qins = [A.alloc([4, 128], BF16) for _ in range(2)]
        kins = [A.alloc([4, 128], BF16) for _ in range(2)]
        ksts = [A.alloc([2, 256], BF16) for _ in range(2)]
        ATm = A.alloc([4, 64], BF16)
        St = A.alloc([4, 128], F32)
        Sb = A.alloc([4, 128], BF16)
        ost = [A.alloc([2, 512], F32) for _ in range(2)]
        sq = A.alloc([1024], F32)
        ssq = A.alloc([8], F32)
        rs8 = A.alloc([8], F32)
        ob16 = A.alloc([2, 512], BF16)
        gst = [A.alloc([4, 128], BF16) for _ in range(2)]
        self.ld(tri[0:64, :, :], I["c_tri"].rearrange("k a b -> a k b"), (), ["tri"])
        self.memset("pool", one_c, 1.0, ["one_c"])
        for d_ in range(2):
            self.ld(wdec[0:16, d_, :], I["gla_w_dec"][l, d_], (), ["wdec"])
            self.ld(wdec[16:17, d_, :], I["gla_b_dec"][l, d_:d_ + 1, :], (), ["wdec"])
            self.memset("pool", dl[d_][0:17, :], 1.0, [("dl", d_)])
        self.ld(gn[0:64, :], I["gla_norm"][l][None, :].to_broadcast([64, 128]), (), ["gn"])
        LG, BT, CC, AT, OO = bk[0], bk[1], bk[2], bk[3], bk[4]
        BTv = BT[0:64, :].rearrange("p (h n i) -> p h n i", h=4, n=2)
        NG = S // 128
        ng = self.ns_limit or NG
        for dirn in range(2):
            self.memset("pool", St[0:64], 0.0, ["St"])
            self.memset("pool", Sb[0:64], 0.0, ["Sb"])
            inc_i, rest_i = (0, 1) if dirn == 0 else (2, 3)
            groups = list(range(ng)) if dirn == 0 else list(range(NG - 1, NG - 1 - ng, -1))
            def prep(gi, g):
                bs = gi % 2
                t0 = g * 128
                E1, qin, kin, kst = E1s[bs], qins[bs], kins[bs], ksts[bs]
                kE1, kq, kk, kks = ("E1", bs), ("qin", bs), ("kin", bs), ("kst", bs)
                self.ld(dl[bs][0:16, :], SC["gdT"][dirn * 16:(dirn + 1) * 16, t0:t0 + 128], ["gdT"], [("dl", bs)])
                self.ld(gq[bs][0:64], SC["gqT"][:, t0:t0 + 128].rearrange("(h d) t -> d h t", d=64), ["gqT"], [("gq", bs)])
                self.ld(gk[bs][0:64], SC["gkT"][:, t0:t0 + 128].rearrange("(h d) t -> d h t", d=64), ["gkT"], [("gk", bs)])
                self.ld(gkt[bs][0:64], SC["gk_tok"][t0:t0 + 128, :].rearrange("(n p) c -> p n c", p=64), ["gk_tok"], [("gkt", bs)])
                self.ld(vv[bs][0:64], SC["gv"][t0:t0 + 128, :].rearrange("(n p) c -> p n c", p=64), ["gv"], [("vv", bs)])
                if dirn == 1:
                    self.ld(of[bs][0:64], SC["of"][t0:t0 + 128, :].rearrange("(n p) c -> p n c", p=64), ["of"], [("of", bs)])
                    self.ld(grt[bs][0:64], SC["gr"][t0:t0 + 128, :].rearrange("(n p) c -> p n c", p=64), ["gr"], [("grt", bs)])
                for n in range(2):
                    self.mm(LG[0:64, n * 256:(n + 1) * 256], dl[bs][0:17, n * 64:(n + 1) * 64], wdec[0:17, dirn, :], True, True, [("dl", bs), "wdec"], [("B", 0)])
                self.act(esb[0:64, :], LG[0:64, :], AF.Exp, [], ["esb", ("B", 0)], scale=-1.0)
                self.act(gp[0:64].rearrange("p n c -> p (n c)"), esb[0:64, :], AF.Ln, ["esb", "one_c"], ["gp"], bias=one_c[0:64, 0:1], scale=1.0)
                for h in range(4):
                    for n in range(2):
                        self.mm(BTv[:, h, n, :], gp[0:64, n, h * 64:(h + 1) * 64], tri[0:64, inc_i, :], True, True, ["gp", "tri"], [("B", 1)])
                for n in range(2):
                    self.mm(CC[0:64, n * 256:(n + 1) * 256], tri[0:64, rest_i, :], gp[0:64, n, :], True, True, ["gp", "tri"], [("B", 2)])
                f = lambda t: t[0:64].rearrange("p a b c -> p (a b c)")
                self.act(f(E1), BT[0:64, :], AF.Exp, [], [kE1, ("B", 1)], scale=-1.0 / 16)
                self.act(f(E2), BT[0:64, :], AF.Exp, [], ["E2", ("B", 1)], scale=1.0 / 16)
                self.act(E3[0:64].rearrange("p n c -> p (n c)"), CC[0:64, :], AF.Exp, [], ["E3", ("B", 2)], scale=-1.0 / 16)
                self.tt("dve", qin[0:64], gq[bs][0:64], E1[0:64].rearrange("p h n i -> p h (n i)"), ALU.mult, [("gq", bs), kE1], [kq])
                self.tt("dve", kin[0:64], gk[bs][0:64], E2[0:64].rearrange("p h n i -> p h (n i)"), ALU.mult, [("gk", bs), "E2"], [kk])
                self.tt("dve", kst[0:64], gkt[bs][0:64], E3[0:64], ALU.mult, [("gkt", bs), "E3"], [kks])

            def chunks(gi, g):
                bs = gi % 2
                t0 = g * 128
                E1, qin, kin, kst = E1s[bs], qins[bs], kins[bs], ksts[bs]
                kE1, kq, kk, kks = ("E1", bs), ("qin", bs), ("kin", bs), ("kst", bs)
                order = (0, 1) if dirn == 0 else (1, 0)
                for ci, n in enumerate(order):
                    cs = slice(n * 64, (n + 1) * 64)
                    for h in range(4):
                        self.mm(AT[0:64, h * 64:(h + 1) * 64], kin[0:64, h, cs], qin[0:64, h, cs], True, True, [kk, kq], [("B", 3)])
                    self.tt("dve", ATm[0:64], AT[0:64, 0:256].rearrange("p (h i) -> p h i", h=4), tri[0:64, inc_i:inc_i + 1, :].to_broadcast([64, 4, 64]), ALU.mult, ["tri"], ["ATm", ("B", 3)])
                    for h in range(4):
                        hv = slice(h * 128, (h + 1) * 128)
                        self.mm(OO[0:64, hv], ATm[0:64, h, :], vv[bs][0:64, n, hv], True, False, ["ATm", ("vv", bs)], [("B", 4)])
                        self.mm(OO[0:64, hv], qin[0:64, h, cs], Sb[0:64, h, :], False, True, [kq, "Sb"], [("B", 4)])
                    kvb = bk[5 + (gi * 2 + ci) % 2]
                    kvk = ("B", 5 + (gi * 2 + ci) % 2)
                    for h in range(4):
                        hv = slice(h * 128, (h + 1) * 128)
                        self.mm(kvb[0:64, hv], kst[0:64, n, h * 64:(h + 1) * 64], vv[bs][0:64, n, hv], True, True, [kks, ("vv", bs)], [kvk])
                    dcol = 63 if dirn == 0 else 0
                    for h in range(4):
                        hv = slice(h * 128, (h + 1) * 128)
                        self.stt("dve", St[0:64, h, :], St[0:64, h, :], E1[0:64, h, n, dcol:dcol + 1], kvb[0:64, hv], ALU.mult, ALU.add, [kE1], ["St", kvk])
                    self.cp("act", Sb[0:64].rearrange("p h v -> p (h v)"), St[0:64].rearrange("p h v -> p (h v)"), ["St"], ["Sb"])
                    if dirn == 0:
                        self.cp("act", ost[bs][0:64, n, :], OO[0:64, :], [], [("ost", bs), ("B", 4)])
                    else:
                        self.tt("dve", ost[bs][0:64, n, :], OO[0:64, :], of[bs][0:64, n, :], ALU.add, [("of", bs)], [("ost", bs), ("B", 4)])
                if dirn == 0:
                    self.st(SC["of"][t0:t0 + 128, :].rearrange("(n p) c -> p n c", p=64), ost[bs][0:64], [("ost", bs)], ["of"])
                else:
                    o2 = ost[bs][0:64].rearrange("p n c -> p (n c)")
                    self.tt("dve", sq[0:64, :], o2, o2, ALU.mult, [("ost", bs)], ["sq"])
                    self.P.op("dve", lambda e: e.reduce_sum(out=ssq[0:64, :], in_=sq[0:64, :].rearrange("p (a b) -> p a b", b=128), axis=AX.X), ["sq"], ["ssq"])
                    self.act(rs8[0:64, :], ssq[0:64, :], AF.Ln, ["ssq", "eps_rms"], ["rs8"], bias=self.eps_rms[0:64, 0:1], scale=1.0 / 128)
                    self.act(rs8[0:64, :], rs8[0:64, :], AF.Exp, ["rs8"], ["rs8"], scale=-0.5)
                    o3 = ost[bs][0:64].rearrange("p n (h v) -> p (n h) v", v=128)
                    self.tt("dve", o3, o3, rs8[0:64, :, None].to_broadcast([64, 8, 128]), ALU.mult, ["rs8"], [("ost", bs)])
                    self.tt("dve", o3, o3, gn[0:64, None, :].to_broadcast([64, 8, 128]), ALU.mult, ["gn"], [("ost", bs)])
                    self.tt("dve", ob16[0:64], ost[bs][0:64], grt[bs][0:64], ALU.mult, [("ost", bs), ("grt", bs)], ["ob16"])
                    TP = bk[7][:, 0:256].bitcast(BF16).rearrange("p (b t) -> p b t", b=4)
                    for n in range(2):
                        for b_ in range(4):
                            self.tr(TP[:, b_, n * 64:(n + 1) * 64], ob16[0:64, n, b_ * 128:(b_ + 1) * 128], self.ident_b[0:64, 0:64], ["ob16", "ident_b"], [("B", 7)])
                    self.cp("act", gst[bs], TP, [], [("gst", bs), ("B", 7)])
                    self.st(SC["glT"][:, t0:t0 + 128].rearrange("(b p) t -> p b t", p=128), gst[bs], [("gst", bs)], ["glT"])

            prep(0, groups[0])
            for gi, g in enumerate(groups):
                if gi + 1 < len(groups):
                    prep(gi + 1, groups[gi + 1])
                chunks(gi, g)
        P.barrier()

    def phase5(self, l):
        A, P, I, SC = self.A, self.P, self.inp, self.scr
        A.reset(self.const_top)
        bk = self.banks
        self.affT = A.alloc([S], F32)
        self.p6_base = A.off
        wg = A.alloc([8, 3072], BF16)
        wup = [A.alloc([4, D], BF16) for _ in range(3)]
        wout = A.alloc([8, D], BF16)
        rw = A.alloc([8, 16], F32)
        bgr = A.alloc([3072], BF16)
        ones1 = A.alloc([128], BF16)
        gam = A.alloc([D], F32)
        bet = A.alloc([D], F32)
        self.ln_tmp = A.alloc([D], F32)
        ht = [A.alloc([D], F32) for _ in range(2)]
        hb = A.alloc([D], BF16)
        hT = A.alloc([8, 128], BF16)
        gates = A.alloc([3072], F32)
        xT = [[A.alloc([4, 128], BF16)] * 2 for _ in range(3)]
        mrg = A.alloc([D], F32)
        tmpm = A.alloc([D], F32)
        mb = A.alloc([D], BF16)
        mT = A.alloc([8, 128], BF16)
        h1b = A.alloc([D], BF16)
        h1a = A.alloc([D], F32)
        h1T = A.alloc([8, 128], F32)
        st6 = A.alloc([2, 6], F32); mv = A.alloc([2], F32); rstd = A.alloc([1], F32); nb = A.alloc([1], F32)
        rmax = A.alloc([1], F32); rsum = A.alloc([1], F32); aff = A.alloc([16], F32)
        self.load_cast(wg, I["w_in"][l][:, C_GATE:C_GATE + 3072].rearrange("(c p) n -> p c n", p=128), 3072, "wg")
        for i, nm in enumerate(("w_up_a", "w_up_b", "w_up_c")):
            self.load_cast(wup[i], I[nm][l].rearrange("(c p) n -> p c n", p=128), D, ("wup", i))
        self.load_cast(wout, I["w_out"][l].rearrange("(c p) n -> p c n", p=128), D, "wout")
        self.ld(rw, I["router_w"][l].rearrange("(c p) e -> p c e", p=128), (), ["rw"])
        for a_ in range(0, 3072, 1536):
            self.P.dma(lambda e, a_=a_: e.dma_start(out=bgr[0:1, a_:a_ + 1536], in_=I["b_gate"][l:l + 1, a_:a_ + 1536]), (), ["bgr"], q="pool")
        self.memset("pool", ones1, 1.0, ["ones1"])
        self.ld(gam, I["ln1_g"][l][None, :].to_broadcast([128, D]), (), ["lnpar"])
        self.ld(bet, I["ln1_b"][l][None, :].to_broadcast([128, D]), (), ["lnpar"])
        srcs = ("pmT", "mlT", "glT")
        pT = bk[0][:, :].bitcast(BF16).rearrange("p (a b) -> p a b", a=8)
        nt = (self.ns_limit or NS) * 4

        def X(t):
            bs = t % 2
            r0 = t * 128
            self.ld(ht[bs], SC["hbuf"][r0:r0 + 128, :], ["hbuf"], [("ht", bs)])
            for i in range(3):
                self.ld(xT[i][bs], SC[srcs[i]][:, r0:r0 + 128].rearrange("(c p) t -> p c t", p=128), [srcs[i]], [("xT", i)])
            self.cp("act", hb, ht[bs], [("ht", bs)], ["hb"])
            for c in range(8):
                self.tr(pT[:, c, :], hb[:, c * 128:(c + 1) * 128], self.ident_b, ["hb", "ident_b"], [("B", 0)])
            self.cp("dve", hT, pT, [], ["hT", ("B", 0)])
            for gI in range(6):
                pb = bk[1 + gI % 2]
                pk = ("B", 1 + gI % 2)
                cs = slice(gI * 512, (gI + 1) * 512)
                for c in range(8):
                    self.mm(pb[:, :], hT[:, c, :], wg[:, c, cs], c == 0, False, ["hT", "wg"], [pk])
                self.mm(pb[:, :], ones1[0:1, :], bgr[0:1, cs], False, True, ["ones1", "bgr"], [pk])
                self.act(gates[:, cs], pb[:, :], AF.Sigmoid, [], [("gates", gI), pk])
            for i in range(3):
                for g2 in range(2):
                    pb = bk[3 + (i * 2 + g2) % 2]
                    pk = ("B", 3 + (i * 2 + g2) % 2)
                    cs = slice(g2 * 512, (g2 + 1) * 512)
                    for c in range(4):
                        self.mm(pb[:, :], xT[i][bs][:, c, :], wup[i][:, c, cs], c == 0, c == 3, [("xT", i), ("wup", i)], [pk])
                    gsl = gates[:, i * D + g2 * 512:i * D + (g2 + 1) * 512]
                    gk_ = ("gates", i * 2 + g2)
                    if i == 0:
                        self.tt("dve", mrg[:, cs], pb[:, :], gsl, ALU.mult, [gk_], [("mrg", g2), pk])
                    else:
                        self.tt("dve", tmpm[:, cs], pb[:, :], gsl, ALU.mult, [gk_], [("tmpm", g2), pk])
                        if i == 1:
                            self.tt("dve", mrg[:, cs], mrg[:, cs], tmpm[:, cs], ALU.add, [("tmpm", g2)], [("mrg", g2)])
                        else:
                            self.tt("dve", mb[:, cs], mrg[:, cs], tmpm[:, cs], ALU.add, [("tmpm", g2), ("mrg", g2)], [("mb", g2)])
            for c in range(8):
                self.tr(pT[:, c, :], mb[:, c * 128:(c + 1) * 128], self.ident_b, [("mb", c // 4), "ident_b"], [("B", 0)])
            self.cp("dve", mT, pT, [], ["mT", ("B", 0)])
            for g2 in range(2):
                pb = bk[5 + g2]
                pk = ("B", 5 + g2)
                cs = slice(g2 * 512, (g2 + 1) * 512)
                for c in range(8):
                    self.mm(pb[:, :], mT[:, c, :], wout[:, c, cs], c == 0, c == 7, ["mT", "wout"], [pk])
                self.stt("dve", ht[bs][:, cs], ht[bs][:, cs], DN_ALPHA, pb[:, :], ALU.mult, ALU.add, [], [("ht", bs), pk])

        def Y(t):
            bs = t % 2
            r0 = t * 128
            h1 = ht[bs]
            self.layernorm_tile(h1, h1, h1b, gam, bet, st6, mv, rstd, nb, "ln", [("ht", bs)], [("ht", bs)], ["h1b"])
            self.st(SC["h1"][r0:r0 + 128, :], h1, [("ht", bs)], ["h1d"])
            self.st(SC["h1b"][r0:r0 + 128, :], h1b, ["h1b"], ["h1bd"])
            self.act(h1a, h1, AF.Identity, [("ht", bs)], ["h1a"], scale=DN_ALPHA)
            self.st(SC["acc"][r0:r0 + 128, :], h1a, ["h1a"], ["acc"])
            pTf = [bk[7][:, :].rearrange("p (a b) -> p a b", a=4), bk[0][:, :].rearrange("p (a b) -> p a b", a=4)]
            for c in range(8):
                self.tr(pTf[c // 4][:, c % 4, :], h1[:, c * 128:(c + 1) * 128], self.ident_f, [("ht", bs), "ident_f"], [("B", 7 if c < 4 else 0)])
            self.cp("dve", h1T[:, 0:4, :], pTf[0], [], [("h1T", 0), ("B", 7)])
            self.cp("dve", h1T[:, 4:8, :], pTf[1], [], [("h1T", 1), ("B", 0)])
            lg = bk[7]
            for c in range(8):
                self.mm(lg[:, 0:16], h1T[:, c, :], rw[:, c, :], c == 0, c == 7, [("h1T", c // 4), "rw"], [("B", 7)])
            self.P.op("dve", lambda e: e.reduce_max(out=rmax, in_=lg[:, 0:16], axis=AX.X), [], ["rmax", ("B", 7)])
            self.ts("dve", rmax, rmax, -1.0, None, ALU.mult, None, ["rmax"], ["rmax"])
            self.act(aff, lg[:, 0:16], AF.Exp, ["rmax"], ["aff", "rsum", ("B", 7)], bias=rmax[:, 0:1], scale=1.0, accum_out=rsum[:, 0:1])
            self.P.op("dve", lambda e: e.reciprocal(out=rsum, in_=rsum), ["rsum"], ["rsum"])
            self.ts("dve", aff, aff, rsum[:, 0:1], None, ALU.mult, None, ["rsum"], ["aff"])
            self.tr(bk[7][0:16, 128:256], aff, self.ident_f, ["aff", "ident_f"], [("B", 7)])
            self.cp("dve", self.affT[0:16, r0:r0 + 128], bk[7][0:16, 128:256], [], ["affT", ("B", 7)])

        X(0)
        for t in range(nt):
            if t + 1 < nt:
                X(t + 1)
            Y(t)
        P.barrier()

    def phase6(self, l):
        A, P = self.A, self.P
        A.reset(self.p6_base)
        bk = self.banks
        CAP = S // 8
        NJ = CAP // 128
        work = A.alloc([S], F32)
        vals = A.alloc([CAP], F32)
        idxu = A.alloc([CAP], U32)
        idxf = A.alloc([CAP], F32)
        af = self.affT[0:16, :]
        cur = af
        curk = "affT"
        for it in range(CAP // 8):
            v8 = vals[0:16, it * 8:(it + 1) * 8]
            self.P.op("dve", lambda e, v8=v8, cur=cur: e.max(out=v8, in_=cur), [curk], ["vals"])
            i8 = idxu[0:16, it * 8:(it + 1) * 8]
            self.P.op("dve", lambda e, v8=v8, cur=cur, i8=i8: e.max_index(out=i8, in_max=v8, in_values=cur), [curk, "vals"], ["idxu"])
            self.P.op("dve", lambda e, v8=v8, cur=cur: e.match_replace(out=work[0:16, :], in_to_replace=v8, in_values=cur, imm_value=-1.0), [curk, "vals"], ["work"])
            cur = work[0:16, :]
            curk = "work"
        self.cp("dve", idxf[0:16, :], idxu[0:16, :], ["idxu"], ["idxf"])
        for j in range(NJ):
            pj = bk[j % 2]
            pk = ("B", j % 2)
            self.tr(pj[:, 0:16], idxf[0:16, j * 128:(j + 1) * 128], self.ident_f[0:16, 0:16], ["idxf", "ident_f"], [pk])
            self.tr(pj[:, 16:32], vals[0:16, j * 128:(j + 1) * 128], self.ident_f[0:16, 0:16], ["vals", "ident_f"], [pk])
            self.cp("dve", self.IDX[:, j, :], pj[:, 0:16], [], ["IDX", pk])
            self.cp("dve", self.GATE[:, j, :], pj[:, 16:32], [], ["GATE", pk])
        P.barrier()

    def phase7(self, l):
        A, P, I, SC = self.A, self.P, self.inp, self.scr
        A.reset(self.const_top)
        bk = self.banks
        CAP = S // 8
        NJ = CAP // 128
        wgh = [A.alloc([8, 1024], BF16) for _ in range(2)]
        wuh = [A.alloc([8, 1024], BF16) for _ in range(2)]
        wdt = A.alloc([16, D], BF16)
        xg = A.alloc([NJ, D], BF16)
        xeT = A.alloc([8, CAP], BF16)
        hidT = A.alloc([16, CAP], BF16)
        sil = [A.alloc([512], F32) for _ in range(2)]
        ysb = [A.alloc([D], F32) for _ in range(2)]
        pT = bk[0][:, :].bitcast(BF16).rearrange("p (a b) -> p a b", a=8)
        nexp = self.ns_limit and min(16, self.ns_limit * 2) or 16
        NH = (CAP + 511) // 512
        HW_ = CAP // NH

        def ld_gu(ex, hf):
            fs = slice(hf * 1024, (hf + 1) * 1024)
            self.load_cast(wgh[hf], I["exp_w_gate"][l, ex][:, fs].rearrange("(c p) n -> p c n", p=128), 1024, ("wg", hf))
            self.load_cast(wuh[hf], I["exp_w_up"][l, ex][:, fs].rearrange("(c p) n -> p c n", p=128), 1024, ("wu", hf))

        def ld_d(ex):
            self.load_cast(wdt, I["exp_w_down"][l, ex].rearrange("(c p) n -> p c n", p=128), D, "wdt")

        def gathers(ex):
            for j in range(NJ):
                self.P.dma(lambda e, j=j, ex=ex: e.indirect_dma_start(
                    out=xg[:, j, :], out_offset=None, in_=SC["h1b"][:, :],
                    in_offset=bass.IndirectOffsetOnAxis(ap=self.IDX[:, j, ex:ex + 1], axis=0)),
                    ["IDX", "h1bd"], [("xg", j)], q="pool")

        ld_gu(0, 0)
        ld_gu(0, 1)
        gathers(0)
        ld_d(0)
        cnt = 0
        for ex in range(nexp):
            for j in range(NJ):
                for c in range(8):
                    self.tr(pT[:, c, :], xg[:, j, c * 128:(c + 1) * 128], self.ident_b, [("xg", j), "ident_b"], [("B", 0)])
                self.cp("dve", xeT[:, :, j * 128:(j + 1) * 128], pT, [], ["xeT", ("B", 0)])
            for fb in range(16):
                fh = fb // 8
                fs = slice((fb % 8) * 128, (fb % 8 + 1) * 128)
                for hf in range(NH):
                    ss_ = slice(hf * HW_, (hf + 1) * HW_)
                    gb, ub = bk[1 + cnt % 2], bk[3 + cnt % 2]
                    gk_, uk_ = ("B", 1 + cnt % 2), ("B", 3 + cnt % 2)
                    sl_ = sil[cnt % 2]
                    sk_ = ("sil", cnt % 2)
                    cnt += 1
                    for c in range(8):
                        self.mm(gb[:, 0:HW_], wgh[fh][:, c, fs], xeT[:, c, ss_], c == 0, c == 7, [("wg", fh), "xeT"], [gk_])
                    for c in range(8):
                        self.mm(ub[:, 0:HW_], wuh[fh][:, c, fs], xeT[:, c, ss_], c == 0, c == 7, [("wu", fh), "xeT"], [uk_])
                    self.act(sl_[:, 0:HW_], gb[:, 0:HW_], AF.Silu, [], [sk_, gk_])
                    self.tt("dve", hidT[:, fb, ss_], ub[:, 0:HW_], sl_[:, 0:HW_], ALU.mult, [sk_], ["hidT", uk_])
                if fb % 8 == 7 and ex + 1 < nexp:
                    ld_gu(ex + 1, fh)
            if ex + 1 < nexp:
                gathers(ex + 1)
            for j in range(NJ):
                yb = ysb[j % 2]
                yk = ("ysb", j % 2)
                for g2 in range(2):
                    pb = bk[5 + g2]
                    pk = ("B", 5 + g2)
                    cs = slice(g2 * 512, (g2 + 1) * 512)
                    for fb in range(16):
                        self.mm(pb[:, :], hidT[:, fb, j * 128:(j + 1) * 128], wdt[:, fb, cs], fb == 0, fb == 15, ["hidT", "wdt"], [pk])
                    self.act(yb[:, cs], pb[:, :], AF.Identity, ["GATE"], [yk, pk], scale=self.GATE[:, j, ex:ex + 1])
                self.P.dma(lambda e, j=j, ex=ex, yb=yb: e.indirect_dma_start(
                    out=SC["acc"][:, :], out_offset=bass.IndirectOffsetOnAxis(ap=self.IDX[:, j, ex:ex + 1], axis=0),
                    in_=yb, in_offset=None, compute_op=ALU.add),
                    ["IDX", yk], ["acc"], q="pool")
            if ex + 1 < nexp:
                ld_d(ex + 1)
        P.barrier()

    def phase8(self, l, final):
        A, P, I, SC = self.A, self.P, self.inp, self.scr
        A.reset(self.const_top)
        gam = A.alloc([D], F32)
        bet = A.alloc([D], F32)
        self.ln_tmp = A.alloc([D], F32)
        at = [A.alloc([4, D], F32) for _ in range(2)]
        st6 = A.alloc([2, 6], F32); mv = A.alloc([2], F32); rstd = A.alloc([1], F32); nb = A.alloc([1], F32)
        self.ld(gam, I["ln2_g"][l][None, :].to_broadcast([128, D]), (), ["lnpar"])
        self.ld(bet, I["ln2_b"][l][None, :].to_broadcast([128, D]), (), ["lnpar"])
        dst = self.out if final else SC["hbuf"]
        for s_ in range(self.ns_limit or NS):
            bs = s_ % 2
            t0 = s_ * 512
            ak = ("at", bs)
            self.ld(at[bs], SC["acc"][t0:t0 + 512, :].rearrange("(j p) d -> p j d", p=128), ["acc"], [ak])
            for j in range(4):
                self.layernorm_tile(at[bs][:, j, :], at[bs][:, j, :], None, gam, bet, st6, mv, rstd, nb, "ln", [ak], [ak], None)
            self.st(dst[t0:t0 + 512, :].rearrange("(j p) d -> p j d", p=128), at[bs], [ak], ["dst"], is_out=final)
        P.barrier()

    def build(self):
        self.declare()
        self.consts()
        self.phase0()
        for l in self.layers:
            self.phase1(l)
            self.phase2(l)
            self.phase3(l)
            self.phase4(l)
            self.phase5(l)
            self.phase6(l)
            self.phase7(l)
            self.phase8(l, final=(l == self.layers[-1]))
        return self.P.finalize()

def _consts():
    half = 16
    freq = (10000.0 ** (-np.arange(half, dtype=np.float32) / half)).astype(np.float32)
    tp = np.arange(64)[:, None]
    ii = np.arange(64)[None, :]
    tri = np.stack([tp <= ii, tp > ii, tp >= ii, tp < ii]).astype(np.float32)
    return {"c_ident": np.eye(128, dtype=np.float32), "c_freq": freq, "c_tri": tri}


_CACHE = {}


def kernel(**inputs):
    if "nc" not in _CACHE:
        k = K()
        _CACHE["nc"] = k.build()
        _CACHE["names"] = list(k.inp.keys())
    nc = _CACHE["nc"]
    cst = _consts()
    maps = []
    for b in range(2):
        m = {}
        for n in _CACHE["names"]:
            if n == "x":
                m[n] = np.ascontiguousarray(inputs["x"][b], dtype=np.float32)
            elif n == "positions":
                m[n] = np.ascontiguousarray(inputs["positions"][b], dtype=np.int32)
            elif n in cst:
                m[n] = cst[n]
            else:
                m[n] = np.ascontiguousarray(inputs[n], dtype=np.float32)
        maps.append(m)
    res = run_bass_kernel_spmd(nc, maps, core_ids=[0, 1])
    return np.stack([np.asarray(res.results[b]["out"], dtype=np.float32) for b in range(2)], axis=0)
```

```python
from contextlib import ExitStack
import math
import numpy as np
import concourse.bass as bass
import concourse.mybir as mybir
from concourse.bass_utils import run_bass_kernel_spmd

F32 = mybir.dt.float32
BF16 = mybir.dt.bfloat16
I32 = mybir.dt.int32
U32 = mybir.dt.uint32
AF = mybir.ActivationFunctionType
ALU = mybir.AluOpType
AX = mybir.AxisListType

COMPUTE = ("pe", "act", "dve", "pool")
DMAQ_K = 8


def dsize(dt):
    return {F32: 4, BF16: 2, I32: 4, U32: 4}[dt]


class Op:
    __slots__ = ("eng", "fn", "deps", "is_dma", "idx", "marked", "done", "q")

    def __init__(self, eng, fn, is_dma=False, q=None):
        self.eng = eng
        self.fn = fn
        self.deps = []
        self.is_dma = is_dma
        self.idx = None
        self.marked = False
        self.done = None
        self.q = q


class Prog:
    def __init__(self):
        self.nc = bass.Bass("TRN2", target_bir_lowering=False)
        self.stack = ExitStack()
        self.ops = {e: [] for e in ("pe", "act", "dve", "pool", "sp")}
        self.reg = {}
        self.dma_count = {"sp": 0, "pool": 0}
        self.dma_ops = {"sp": [], "pool": []}
        self.out_dmas = []
        self.same_engine_sync = True
        self.debug_out = set()

    def sb(self, name, shape, dt):
        return self.stack.enter_context(self.nc.sbuf_tensor(name, list(shape), dt))

    def ps(self, name, shape, dt):
        return self.stack.enter_context(self.nc.psum_tensor(name, list(shape), dt))

    def dram(self, name, shape, dt, kind="Internal"):
        if kind == "Internal" and name in self.debug_out:
            kind = "ExternalOutput"
        return self.nc.dram_tensor(name, list(shape), dt, kind=kind).ap()

    def _deps(self, op, reads, writes):
        for k in reads:
            st = self.reg.get(k)
            if st is None:
                st = self.reg[k] = [None, []]
            if st[0] is not None:
                op.deps.append(st[0])
            st[1].append(op)
        for k in writes:
            st = self.reg.get(k)
            if st is None:
                st = self.reg[k] = [None, []]
            if st[0] is not None:
                op.deps.append(st[0])
            for r in st[1]:
                if r is not op:
                    op.deps.append(r)
            st[0] = op
            st[1] = []

    def op(self, eng, fn, reads=(), writes=()):
        o = Op(eng, fn)
        self._deps(o, reads, writes)
        self.ops[eng].append(o)
        return o

    def dma(self, fn, reads=(), writes=(), q="sp", is_out=False):
        o = Op(q, fn, is_dma=True, q=q)
        o.idx = self.dma_count[q]
        self.dma_count[q] += 1
        self.dma_ops[q].append(o)
        self._deps(o, reads, writes)
        self.ops[q].append(o)
        if is_out:
            self.out_dmas.append(o)
        return o

    def barrier(self):
        lasts = []
        for e in COMPUTE:
            for o in reversed(self.ops[e]):
                if not o.is_dma and o.fn is not None:
                    lasts.append(o)
                    break
        for q in ("sp", "pool"):
            lasts.extend(self.dma_ops[q][-DMAQ_K:])
        for e in ("pe", "act", "dve", "pool", "sp"):
            o = Op(e, None)
            o.deps = list(lasts)
            self.ops[e].append(o)
        self.reg = {}

    def finalize(self):
        nc = self.nc
        sync_same = self.same_engine_sync

        def skip(d, o):
            return (not d.is_dma) and (not o.is_dma) and d.eng == o.eng and o.fn is not None and (d.eng == "pe" or not sync_same)

        for e, lst in self.ops.items():
            for o in lst:
                for d in o.deps:
                    if d.is_dma or skip(d, o):
                        continue
                    d.marked = True
        sems = {}
        for e in COMPUTE:
            sems[e] = self.stack.enter_context(nc.semaphore("s_" + e))
            n = 0
            for o in self.ops[e]:
                if o.is_dma or o.fn is None:
                    continue
                if o.marked:
                    n += 1
                    o.done = (sems[e], n)
        dsems = {}
        for q in ("sp", "pool"):
            if self.dma_count[q] == 0:
                continue
            dsems[q] = [self.stack.enter_context(nc.semaphore("d_%s%d" % (q, i))) for i in range(DMAQ_K)]
            for o in self.dma_ops[q]:
                o.done = (dsems[q][o.idx % DMAQ_K], 16 * (o.idx // DMAQ_K + 1))
        block = self.stack.enter_context(nc.Block())

        def emit(ename, e):
            waited = {}
            for o in self.ops[ename]:
                need = {}
                for d in o.deps:
                    if d.done is None or skip(d, o):
                        continue
                    s, v = d.done
                    if need.get(s.num, (None, 0))[1] < v:
                        need[s.num] = (s, v)
                if o.is_dma and o.idx >= DMAQ_K:
                    s = dsems[o.q][o.idx % DMAQ_K]
                    v = 16 * (o.idx // DMAQ_K)
                    if need.get(s.num, (None, 0))[1] < v:
                        need[s.num] = (s, v)
                for sn, (s, v) in need.items():
                    if waited.get(sn, 0) < v:
                        e.wait_ge(s, v)
                        waited[sn] = v
                if o.fn is None:
                    continue
                ins = o.fn(e)
                if o.is_dma:
                    ins.then_inc(o.done[0], 16)
                elif o.marked:
                    ins.then_inc(o.done[0], 1)
            if ename == "sp":
                for o in self.out_dmas:
                    s, v = o.done
                    if waited.get(s.num, 0) < v:
                        e.wait_ge(s, v)
                        waited[s.num] = v

        @block.tensor
        def _(e):
            emit("pe", e)

        @block.scalar
        def _(e):
            emit("act", e)

        @block.vector
        def _(e):
            emit("dve", e)

        @block.gpsimd
        def _(e):
            emit("pool", e)

        @block.sync
        def _(e):
            emit("sp", e)

        self.stack.close()
        return nc


class Arena:
    def __init__(self, P, kbytes):
        self.words = kbytes * 256
        self.t = P.sb("arena", [128, self.words], F32)
        self.off = 0

    def reset(self, to=0):
        self.off = to

    def alloc(self, free_shape, dt, parts=128, p0=0):
        n = 1
        for s in free_shape:
            n *= s
        nb = n * dsize(dt)
        w = (nb + 31) // 32 * 8
        assert self.off + w <= self.words, ("arena overflow", self.off, w, self.words)
        v = self.t[p0:p0 + parts, self.off:self.off + w]
        self.off += w
        if dt != F32:
            v = v.bitcast(dt)
        v = v[:, 0:n]
        if len(free_shape) == 2:
            v = v.rearrange("p (a b) -> p a b", a=free_shape[0])
        elif len(free_shape) == 3:
            v = v.rearrange("p (a b c) -> p a b c", a=free_shape[0], b=free_shape[1])
        return v


S = 8192
D = 1024
NT = S // 128
NS = S // 512
POOL_WINDOWS = (2, 4, 8, 16)
LN_EPS = 1e-5
RMS_EPS = 1e-6
DN_ALPHA = 4 ** 0.25
QSCALE = 96 ** -0.5
TWO_PI = 2.0 * math.pi
CW1 = 6.28125
CW2 = TWO_PI - CW1

C_UP, C_CQ, C_CKV, C_KR, C_GQ, C_GK, C_GV, C_GR, C_GD, C_GATE = 0, 512, 896, 1152, 1184, 1440, 1696, 2208, 2720, 2752


class K:
    def __init__(self, debug_out=(), layers=(0, 1), phases=None, ns_limit=None, skip=(), no_big=False):
        self.ns_limit = ns_limit
        self.skip = set(skip)
        self.no_big = no_big
        self.P = P = Prog()
        P.debug_out = set(debug_out)
        self.layers = layers
        self.phases = phases
        self.nc = P.nc
        self.A = Arena(P, 204)
        self.banks = [P.ps("bank%d" % i, [128, 512], F32) for i in range(8)]
        self.inp = {}
        self.scr = {}

    def din(self, name, shape, dt=F32):
        self.inp[name] = self.P.dram(name, shape, dt, kind="ExternalInput")
        return self.inp[name]

    def scratch(self, name, shape, dt):
        self.scr[name] = self.P.dram(name, shape, dt)
        return self.scr[name]

    def mm(self, out, lhsT, rhs, start, stop, reads, writes):
        self.P.op("pe", lambda e: e.matmul(out, lhsT=lhsT, rhs=rhs, start=start, stop=stop), reads, writes)

    def tr(self, out, in_, ident, reads, writes):
        self.P.op("pe", lambda e: e.transpose(out=out, in_=in_, identity=ident), reads, writes)

    def act(self, out, in_, func, reads, writes, bias=None, scale=None, accum_out=None):
        kw = {}
        if bias is not None:
            kw["bias"] = bias
        if scale is not None:
            kw["scale"] = scale
        if accum_out is not None:
            kw["accum_out"] = accum_out
        self.P.op("act", lambda e: e.activation(out=out, in_=in_, func=func, **kw), reads, writes)

    def ts(self, eng, out, in0, s1, s2, op0, op1, reads, writes):
        if op1 is None:
            self.P.op(eng, lambda e: e.tensor_scalar(out=out, in0=in0, scalar1=s1, scalar2=None, op0=op0), reads, writes)
        else:
            self.P.op(eng, lambda e: e.tensor_scalar(out=out, in0=in0, scalar1=s1, scalar2=s2, op0=op0, op1=op1), reads, writes)

    def tt(self, eng, out, in0, in1, op, reads, writes):
        self.P.op(eng, lambda e: e.tensor_tensor(out=out, in0=in0, in1=in1, op=op), reads, writes)

    def stt(self, eng, out, in0, scalar, in1, op0, op1, reads, writes):
        self.P.op(eng, lambda e: e.scalar_tensor_tensor(out=out, in0=in0, scalar=scalar, in1=in1, op0=op0, op1=op1), reads, writes)

    def cp(self, eng, out, in_, reads, writes):
        if eng == "act":
            self.P.op("act", lambda e: e.copy(out=out, in_=in_), reads, writes)
        else:
            self.P.op(eng, lambda e: e.tensor_copy(out=out, in_=in_), reads, writes)

    def memset(self, eng, ap, val, writes):
        self.P.op(eng, lambda e: e.memset(ap, val), (), writes)

    def ld(self, out, in_, reads, writes, q="sp", slow=False):
        if slow:
            self.P.dma(lambda e: e.dma_start(out=out, in_=in_, allow_slow_non_contiguous=True), reads, writes, q=q)
        else:
            self.P.dma(lambda e: e.dma_start(out=out, in_=in_), reads, writes, q=q)

    def st(self, out, in_, reads, writes, q="pool", is_out=False):
        self.P.dma(lambda e: e.dma_start(out=out, in_=in_), reads, writes, q=q, is_out=is_out)

    def declare(self):
        din = self.din
        din("x", [S, D])
        din("positions", [S], I32)
        din("ln0_g", [D]); din("ln0_b", [D])
        din("w_in", [2, D, 5824]); din("b_gate", [2, 3072])
        din("pool_w", [2, 4, 128, 128]); din("pool_scale", [2, 512]); din("w_up_a", [2, 512, D])
        din("mla_q_norm", [2, 384]); din("mla_w_uq", [2, 384, 768]); din("mla_kv_norm", [2, 256])
        din("mla_w_ukv", [2, 256, 1024]); din("w_up_b", [2, 512, D])
        din("gla_w_dec", [2, 2, 16, 256]); din("gla_b_dec", [2, 2, 256]); din("gla_norm", [2, 128])
        din("w_up_c", [2, 512, D]); din("w_out", [2, D, D])
        din("ln1_g", [2, D]); din("ln1_b", [2, D]); din("router_w", [2, D, 16])
        if not self.no_big:
            din("exp_w_gate", [2, 16, D, 2048]); din("exp_w_up", [2, 16, D, 2048]); din("exp_w_down", [2, 16, 2048, D])
        din("ln2_g", [2, D]); din("ln2_b", [2, D])
        din("c_ident", [128, 128])
        din("c_freq", [16])
        din("c_tri", [4, 64, 64])
        self.out = self.P.dram("out", [S, D], F32, kind="ExternalOutput")
        sc = self.scratch
        sc("hbuf", [S, D], F32)
        sc("cosF", [32, S], F32); sc("sinF", [32, S], F32)
        sc("uT", [512, S], F32)
        sc("gqT", [256, S], F32); sc("gkT", [256, S], F32); sc("gdT", [32, S], F32)
        sc("gk_tok", [S, 256], F32); sc("gv", [S, 512], BF16); sc("gr", [S, 512], BF16)
        sc("QT", [8, 96, S], BF16); sc("KT", [8, 64, S], BF16); sc("KRT", [32, S], BF16); sc("V", [S, 512], BF16)
        sc("pmT", [512, S], BF16); sc("mlT", [512, S], BF16); sc("glT", [512, S], BF16)
        sc("h1", [S, D], F32); sc("h1b", [S, D], BF16); sc("acc", [S, D], F32)

    def consts(self):
        A = self.A
        self.ident_f = A.alloc([128], F32)
        self.ident_b = A.alloc([128], BF16)
        self.eps_ln = A.alloc([1], F32)
        self.eps_rms = A.alloc([1], F32)
        self.cosT = A.alloc([NT, 16], F32)
        self.sinT = A.alloc([NT, 16], F32)
        self.ld(self.ident_f, self.inp["c_ident"][:, :], (), ["ident_f"])
        self.cp("dve", self.ident_b, self.ident_f, ["ident_f"], ["ident_b"])
        self.memset("pool", self.eps_ln, LN_EPS, ["eps_ln"])
        self.memset("pool", self.eps_rms, RMS_EPS, ["eps_rms"])
        self.IDX = A.alloc([S // 1024, 16], I32)
        self.GATE = A.alloc([S // 1024, 16], F32)
        self.const_top = A.off

    def range_reduce_sin(self, ang, tmp, tmpi, res, shift, k_ang, k_tmp, k_res):
        self.ts("dve", tmp, ang, 1.0 / TWO_PI, None, ALU.mult, None, [k_ang], [k_tmp])
        self.cp("dve", tmpi, tmp, [k_tmp], [k_tmp + "i"])
        self.cp("dve", tmp, tmpi, [k_tmp + "i"], [k_tmp])
        self.stt("dve", res, tmp, -CW1, ang, ALU.mult, ALU.add, [k_tmp, k_ang], [k_res])
        self.stt("dve", res, tmp, -CW2, res, ALU.mult, ALU.add, [k_tmp, k_res], [k_res])
        if shift != 0.0:
            self.ts("dve", res, res, shift, None, ALU.add, None, [k_res], [k_res])
        self.ts("dve", tmp, res, math.pi, None, ALU.is_gt, None, [k_res], [k_tmp])
        self.stt("dve", res, tmp, -TWO_PI, res, ALU.mult, ALU.add, [k_tmp, k_res], [k_res])
        self.ts("dve", res, res, math.pi, -math.pi, ALU.min, ALU.max, [k_res], [k_res])
        self.act(res, res, AF.Sin, [k_res], [k_res])

    def phase0(self):
        A, P = self.A, self.P
        A.reset(self.const_top)
        posi = A.alloc([NT], I32)
        posf = A.alloc([NT], F32)
        freq = A.alloc([16], F32)
        ang = A.alloc([NT, 16], F32)
        tmp = A.alloc([NT, 16], F32)
        tmpi = A.alloc([NT, 16], I32)
        pos = self.inp["positions"]
        pv = pos.rearrange("(j p) -> p j", p=128)
        for a in range(0, NT, 8):
            self.ld(posi[:, a:a + 8], pv[:, a:a + 8], (), ["posi"], slow=True)
        self.ld(freq, self.inp["c_freq"][None, :].to_broadcast([128, 16]), (), ["freq"])
        self.cp("dve", posf, posi, ["posi"], ["posf"])
        for j in range(NT):
            self.ts("dve", ang[:, j, :], freq, posf[:, j:j + 1], None, ALU.mult, None, ["posf", "freq"], ["angT"])
        self.range_reduce_sin(ang, tmp, tmpi, self.sinT, 0.0, "angT", "tmpT", "sinT")
        self.range_reduce_sin(ang, tmp, tmpi, self.cosT, math.pi / 2, "angT", "tmpT", "cosT")
        CH = min(2048, S)
        prow_i = A.alloc([CH], I32)
        prow = A.alloc([CH], F32)
        fcol = A.alloc([1], F32)
        scol_c = A.alloc([1], F32)
        scol_s = A.alloc([1], F32)
        angF = A.alloc([CH], F32)
        tmpF = A.alloc([CH], F32)
        tmpFi = A.alloc([CH], I32)
        resF = A.alloc([CH], F32)
        fr = self.inp["c_freq"]
        self.ld(fcol[64:80, :], fr.rearrange("(a b) -> a b", b=1), (), ["fcol"], slow=True)
        self.ld(fcol[80:96, :], fr.rearrange("(a b) -> a b", b=1), ["fcol"], ["fcol"], slow=True)
        self.memset("pool", scol_c[64:96, :], QSCALE, ["scol_c"])
        self.memset("pool", scol_s[64:96, :], QSCALE, ["scol_s"])
        self.memset("pool", scol_s[64:80, :], -QSCALE, ["scol_s"])
        sl = slice(64, 96)
        for c in range(S // CH):
            self.ld(prow_i[sl, :], pos[None, c * CH:(c + 1) * CH].to_broadcast([32, CH]), (), ["prow_i"])
            self.cp("dve", prow[sl, :], prow_i[sl, :], ["prow_i"], ["prow"])
            self.ts("dve", angF[sl, :], prow[sl, :], fcol[sl, 0:1], None, ALU.mult, None, ["prow", "fcol"], ["angF"])
            for (shift, scol, dst) in ((0.0, scol_s, "sinF"), (math.pi / 2, scol_c, "cosF")):
                self.range_reduce_sin(angF[sl, :], tmpF[sl, :], tmpFi[sl, :], resF[sl, :], shift, "angF", "tmpF", "resF")
                self.ts("dve", resF[sl, :], resF[sl, :], scol[sl, 0:1], None, ALU.mult, None, ["resF", "scol_s", "scol_c"], ["resF"])
                self.st(self.scr[dst][:, c * CH:(c + 1) * CH], resF[sl, :], ["resF"], [dst])
        P.barrier()

    def layernorm_tile(self, x, out_f, out_b, gam, bet, st6, mv, rstd, nb, key, reads, writes_f, writes_b, cast_eng="pool"):
        for c in range(2):
            self.P.op("dve", lambda e, c=c: e.bn_stats(out=st6[:, c, :], in_=x[:, c * 512:(c + 1) * 512]), reads, [(key, "st", c)])
        self.P.op("dve", lambda e: e.bn_aggr(out=mv, in_=st6.rearrange("p a b -> p (a b)")), [(key, "st", 0), (key, "st", 1)], [(key, "mv")])
        self.act(rstd, mv[:, 1:2], AF.Ln, [(key, "mv"), "eps_ln"], [(key, "rstd")], bias=self.eps_ln[:, 0:1], scale=1.0)
        self.act(rstd, rstd, AF.Exp, [(key, "rstd")], [(key, "rstd")], scale=-0.5)
        self.stt("dve", nb, mv[:, 0:1], -1.0, rstd, ALU.mult, ALU.mult, [(key, "mv"), (key, "rstd")], [(key, "nb")])
        tmpk = (key, "xn")
        self.act(self.ln_tmp, x, AF.Identity, list(reads) + [(key, "rstd"), (key, "nb")], [tmpk], bias=nb[:, 0:1], scale=rstd[:, 0:1])
        self.tt("dve", self.ln_tmp, self.ln_tmp, gam, ALU.mult, [tmpk, "lnpar"], [tmpk])
        if out_f is not None:
            self.tt("dve", out_f, self.ln_tmp, bet, ALU.add, [tmpk, "lnpar"], writes_f)
            if out_b is not None:
                self.cp(cast_eng, out_b, out_f, writes_f, writes_b)
        else:
            self.tt("dve", out_b, self.ln_tmp, bet, ALU.add, [tmpk, "lnpar"], writes_b)


    def load_cast(self, dst, src, n_inner, key, reads=()):
        shp = list(dst.shape)
        if len(shp) == 3:
            C = shp[1]
            cstep = max(1, 1024 // shp[0])
            nsp = (n_inner + 2047) // 2048
            step = (n_inner + nsp - 1) // nsp
            for c0 in range(0, C, cstep):
                c1 = min(C, c0 + cstep)
                for a in range(0, n_inner, step):
                    b = min(n_inner, a + step)
                    self.P.dma(lambda e, a=a, b=b, c0=c0, c1=c1: e.dma_start(out=dst[:, c0:c1, a:b], in_=src[:, c0:c1, a:b]), reads, [key], q="pool")
        else:
            self.P.dma(lambda e: e.dma_start(out=dst, in_=src), reads, [key], q="pool")

    def phase1(self, l):
        A, P, I, SC = self.A, self.P, self.inp, self.scr
        A.reset(self.const_top)
        bk = self.banks
        win = A.alloc([8, 2752], BF16)
        wuq = A.alloc([3, 768], BF16)
        wuqr = A.alloc([3, 768], BF16)
        wukv = A.alloc([2, 1024], BF16)
        qng = A.alloc([3], F32)
        kvng = A.alloc([2], F32)
        gam = A.alloc([D], F32)
        bet = A.alloc([D], F32)
        self.ln_tmp = A.alloc([D], F32)
        xt = [A.alloc([4, D], F32) for _ in range(2)]
        hb = A.alloc([4, D], BF16)
        hT = [A.alloc([8, 512], BF16) for _ in range(2)]
        st6 = A.alloc([2, 6], F32); mv = A.alloc([2], F32); rstd = A.alloc([1], F32); nb = A.alloc([1], F32)
        ss = A.alloc([4], F32)
        cqn = A.alloc([384], BF16)
        ckvn = A.alloc([256], BF16)
        cqnT = A.alloc([3, 512], BF16)
        ckvnT = A.alloc([2, 512], BF16)
        junk = A.alloc([384], F32)
        junk2 = A.alloc([256], F32)
        st_u = A.alloc([4, 512], F32)
        st_q = A.alloc([2, 512], F32)
        st_k = A.alloc([2, 512], F32)
        st_d = A.alloc([512], F32)
        st_kt = A.alloc([4, 256], F32)
        st_v = A.alloc([4, 512], BF16)
        st_r = A.alloc([4, 512], BF16)
        st_V = A.alloc([4, 512], BF16)
        st_Q = A.alloc([8, 512], BF16)
        st_K = A.alloc([8, 512], BF16)
        st_kr = A.alloc([512], BF16)
        krt = A.alloc([32], F32)
        krb = A.alloc([128], BF16)
        rt1 = A.alloc([16], F32); rt2 = A.alloc([16], F32)
        cosF = A.alloc([512], F32); sinF = A.alloc([512], F32)
        qt1 = A.alloc([512], F32); qt2 = A.alloc([512], F32)

        self.memset("pool", krb, 0.0, ["krb"])
        w_in = I["w_in"][l]
        self.load_cast(win, w_in[:, 0:2752].rearrange("(c p) n -> p c n", p=128), 2752, "win")
        self.load_cast(wuq, I["mla_w_uq"][l].rearrange("(c p) n -> p c n", p=128), 768, "wuq")
        wq4 = I["mla_w_uq"][l].rearrange("(c p) (h x) -> p c h x", p=128, x=96)
        wuqr4 = wuqr.rearrange("p c (h x) -> p c h x", x=96)
        for c in range(3):
            self.load_cast(wuqr4[:, c, :, 64:80], wq4[:, c, :, 80:96], 16, "wuqr")
            self.load_cast(wuqr4[:, c, :, 80:96], wq4[:, c, :, 64:80], 16, "wuqr")
        self.memset("pool", wuqr4[:, :, :, 0:64], 0.0, ["wuqr"])
        self.load_cast(wukv, I["mla_w_ukv"][l].rearrange("(c p) n -> p c n", p=128), 1024, "wukv")
        self.ld(qng, I["mla_q_norm"][l].rearrange("(c p) -> p c", p=128), (), ["qng"], slow=True)
        self.ld(kvng, I["mla_kv_norm"][l].rearrange("(c p) -> p c", p=128), (), ["kvng"], slow=True)
        if l == 0:
            self.ld(gam, I["ln0_g"][None, :].to_broadcast([128, D]), (), ["lnpar"])
            self.ld(bet, I["ln0_b"][None, :].to_broadcast([128, D]), (), ["lnpar"])
        src = I["x"] if l == 0 else SC["hbuf"]
        pT = bk[0][:, 0:512].bitcast(BF16).rearrange("p (a b) -> p a b", a=8)
        fm_blocks = [("u", g, C_UP + g * 128, 128) for g in range(4)] + [("q", m, C_GQ + m * 128, 128) for m in range(2)] \
            + [("k", m, C_GK + m * 128, 128) for m in range(2)] + [("d", 0, C_GD, 32)]
        nfm = 0
        ntm = 0
        nsup = self.ns_limit or NS

        def head_load(s):
            sl = s % 2
            t0 = s * 512
            self.ld(xt[sl], src[t0:t0 + 512, :].rearrange("(j p) d -> p j d", p=128), (), [("xt", sl)])

        def head_tile(s, j):
            sl = s % 2
            t0 = s * 512
            xk = ("xt", sl)
            hbk = ("hb", j)
            if l == 0:
                self.layernorm_tile(xt[sl][:, j, :], xt[sl][:, j, :], hb[:, j, :], gam, bet, st6, mv, rstd, nb, "ln", [xk], [xk], [hbk], cast_eng="act")
            else:
                self.cp("act", hb[:, j, :], xt[sl][:, j, :], [xk], [hbk])
            for c in range(8):
                self.tr(pT[:, c, :], hb[:, j, c * 128:(c + 1) * 128], self.ident_b, [hbk, "ident_b"], ["pT"])
            self.cp("dve", hT[sl][:, :, j * 128:(j + 1) * 128], pT, [], [("hT", sl), "pT"])
            if l == 0 and j == 3:
                self.st(SC["hbuf"][t0:t0 + 512, :].rearrange("(j p) d -> p j d", p=128), xt[sl], [xk], ["hbuf"])

        head_load(0)
        for j in range(4):
            head_tile(0, j)
        for s in range(nsup):
            sl = s % 2
            t0 = s * 512
            xk = ("xt", sl)
            if s + 1 < nsup:
                head_load(s + 1)
            self.ld(cosF[64:96, :], SC["cosF"][:, t0:t0 + 512], (), ["cosFs"])
            self.ld(sinF[64:96, :], SC["sinF"][:, t0:t0 + 512], (), ["sinFs"])
            hTk = ("hT", sl)
            for (kind, idx, c0, w) in (fm_blocks if 'fm' not in self.skip else []):
                pb = bk[1 + nfm % 2]
                pk = ("fm", nfm % 2)
                nfm += 1
                for c in range(8):
                    self.mm(pb[0:w, :], win[:, c, c0:c0 + w], hT[sl][:, c, :], c == 0, c == 7, ["win", hTk], [pk])
                if kind == "u":
                    self.cp("act", st_u[:, idx, :], pb[:, :], [], ["st_u", pk])
                elif kind == "q":
                    self.act(st_q[:, idx, :], pb[:, :], AF.Identity, [], ["st_q", pk], scale=0.125)
                elif kind == "k":
                    self.cp("dve", st_k[:, idx, :], pb[:, :], [], ["st_k", pk])
                else:
                    self.cp("dve", st_d[0:32, :], pb[0:32, :], [], ["st_d", pk])
            if 'fm' not in self.skip:
                self.st(SC["uT"][:, t0:t0 + 512].rearrange("(g p) t -> p g t", p=128), st_u, ["st_u"], ["uT"])
                self.st(SC["gqT"][:, t0:t0 + 512].rearrange("(g p) t -> p g t", p=128), st_q, ["st_q"], ["gqT"])
                self.st(SC["gkT"][:, t0:t0 + 512].rearrange("(g p) t -> p g t", p=128), st_k, ["st_k"], ["gkT"])
                self.st(SC["gdT"][:, t0:t0 + 512], st_d[0:32, :], ["st_d"], ["gdT"])
            for j in range(4):
                tt = s * 4 + j
                lh = lambda c: hT[sl][:, c, j * 128:(j + 1) * 128]

                def tmb(pb, pk, c0, w):
                    for c in range(8):
                        self.mm(pb[:, 0:w], lh(c), win[:, c, c0:c0 + w], c == 0, c == 7, ["win", hTk], [pk])

                def tm(c0, w):
                    nonlocal ntm
                    pb = bk[3 + ntm % 2]
                    pk = ("tm", ntm % 2)
                    ntm += 1
                    tmb(pb, pk, c0, w)
                    return pb, pk
                pbq, pkq = bk[6], "pqa"
                pbk, pkk_ = bk[7], "pqb"
                tmb(pbq, pkq, C_CQ, 384)
                tmb(pbk, pkk_, C_CKV, 288)
                self.act(junk[:, 0:384], pbq[:, 0:384], AF.Square, [], ["junk", "ss0", pkq], accum_out=ss[:, 0:1])
                self.act(ss[:, 1:2], ss[:, 0:1], AF.Ln, ["ss0", "eps_rms"], ["ss1"], bias=self.eps_rms[:, 0:1], scale=1.0 / 384)
                self.act(ss[:, 1:2], ss[:, 1:2], AF.Exp, ["ss1"], ["ss1"], scale=-0.5)
                self.ts("dve", cqn, pbq[:, 0:384], ss[:, 1:2], None, ALU.mult, None, ["ss1"], ["cqn", pkq])
                self.act(junk2[:, 0:256], pbk[:, 0:256], AF.Square, [], ["junk2", "ss2", pkk_], accum_out=ss[:, 2:3])
                self.act(ss[:, 3:4], ss[:, 2:3], AF.Ln, ["ss2", "eps_rms"], ["ss3"], bias=self.eps_rms[:, 0:1], scale=1.0 / 256)
                self.act(ss[:, 3:4], ss[:, 3:4], AF.Exp, ["ss3"], ["ss3"], scale=-0.5)
                self.ts("dve", ckvn, pbk[:, 0:256], ss[:, 3:4], None, ALU.mult, None, ["ss3"], ["ckvn", pkk_])
                cs = self.cosT[:, tt, :]
                sn = self.sinT[:, tt, :]
                self.cp("dve", krt, pbk[:, 256:288], [], ["krt", pkk_])
                self.tt("dve", rt1, krt[:, 0:16], cs, ALU.mult, ["krt", "cosT"], ["rt1"])
                self.tt("dve", rt2, krt[:, 16:32], sn, ALU.mult, ["krt", "sinT"], ["rt2"])
                self.tt("dve", krb[:, 0:16], rt1, rt2, ALU.subtract, ["rt1", "rt2"], ["krb"])
                self.tt("dve", rt1, krt[:, 0:16], sn, ALU.mult, ["krt", "sinT"], ["rt1"])
                self.tt("dve", rt2, krt[:, 16:32], cs, ALU.mult, ["krt", "cosT"], ["rt2"])
                self.tt("dve", krb[:, 16:32], rt1, rt2, ALU.add, ["rt1", "rt2"], ["krb"])
                pb, pk = tm(C_GK, 256)
                self.cp("act", st_kt[:, j, :], pb[:, 0:256], [], ["st_kt", pk])
                pb, pk = tm(C_GV, 512)
                self.cp("dve", st_v[:, j, :], pb[:, :], [], ["st_v", pk])
                pb, pk = tm(C_GR, 512)
                self.act(st_r[:, j, :], pb[:, :], AF.Silu, [], ["st_r", pk])
                pq = bk[5][:, 0:192].bitcast(BF16).rearrange("p (a b) -> p a b", a=3)
                for c in range(3):
                    self.tr(pq[:, c, :], cqn[:, c * 128:(c + 1) * 128], self.ident_b, ["cqn", "ident_b"], ["b5"])
                self.tt("dve", cqnT[:, :, j * 128:(j + 1) * 128], pq, qng[:, :, None].to_broadcast([128, 3, 128]), ALU.mult, ["qng"], ["cqnT", "b5"])
                pkv = bk[5][:, 256:448].bitcast(BF16).rearrange("p (a b) -> p a b", a=3)
                for c in range(2):
                    self.tr(pkv[:, c, :], ckvn[:, c * 128:(c + 1) * 128], self.ident_b, ["ckvn", "ident_b"], ["b5"])
                self.tr(pkv[:, 2, :], krb[:, :], self.ident_b, ["krb", "ident_b"], ["b5"])
                self.tt("dve", ckvnT[:, :, j * 128:(j + 1) * 128], pkv[:, 0:2, :], kvng[:, :, None].to_broadcast([128, 2, 128]), ALU.mult, ["kvng"], ["ckvnT", "b5"])
                self.cp("act", st_kr[0:32, j * 128:(j + 1) * 128], pkv[0:32, 2, :], [], ["st_kr", "b5"])
                if s + 1 < nsup:
                    head_tile(s + 1, j)
            rows = lambda name: SC[name][t0:t0 + 512, :].rearrange("(j p) d -> p j d", p=128)
            self.st(rows("gk_tok"), st_kt, ["st_kt"], ["gk_tok"])
            self.st(rows("gv"), st_v, ["st_v"], ["gv"])
            self.st(rows("gr"), st_r, ["st_r"], ["gr"])
            self.st(SC["KRT"][:, t0:t0 + 512], st_kr[0:32, :], ["st_kr"], ["KRT"])
            for h in (range(8) if 'mlaup' not in self.skip else []):
                pqa, ka = (bk[6], "pqa") if h % 2 == 0 else (bk[3], ("tm", 0))
                pqb, kb = (bk[7], "pqb") if h % 2 == 0 else (bk[4], ("tm", 1))
                for c in range(3):
                    self.mm(pqa[0:96, :], wuq[:, c, h * 96:(h + 1) * 96], cqnT[:, c, :], c == 0, c == 2, ["wuq", "cqnT"], [ka])
                for c in range(3):
                    self.mm(pqb[0:96, :], wuqr[:, c, h * 96:(h + 1) * 96], cqnT[:, c, :], c == 0, c == 2, ["wuqr", "cqnT"], [kb])
                self.act(st_Q[0:64, h, :], pqa[0:64, :], AF.Identity, [], ["st_Q", ka], scale=QSCALE)
                self.tt("dve", qt1[64:96, :], pqa[64:96, :], cosF[64:96, :], ALU.mult, ["cosFs"], ["qt1", ka])
                self.tt("dve", qt2[64:96, :], pqb[64:96, :], sinF[64:96, :], ALU.mult, ["sinFs"], ["qt2", kb])
                self.tt("pool", st_Q[64:96, h, :], qt1[64:96, :], qt2[64:96, :], ALU.add, ["qt1", "qt2"], ["st_Q"])
                pka = bk[1 + h % 2]
                pkk = ("fm", h % 2)
                for c in range(2):
                    self.mm(pka[0:64, :], wukv[:, c, h * 128:h * 128 + 64], ckvnT[:, c, :], c == 0, c == 1, ["wukv", "ckvnT"], [pkk])
                self.cp("act", st_K[0:64, h, :], pka[0:64, :], [], ["st_K", pkk])
            self.st(SC["QT"][:, :, t0:t0 + 512].rearrange("h r t -> r h t"), st_Q[0:96, :, :], ["st_Q"], ["QT"])
            self.st(SC["KT"][:, :, t0:t0 + 512].rearrange("h r t -> r h t"), st_K[0:64, :, :], ["st_K"], ["KT"])
            wv = wukv.rearrange("p c (h x) -> p c h x", x=128)
            for j in (range(4) if 'vproj' not in self.skip else []):
                pb = bk[3 + ntm % 2]
                pk = ("tm", ntm % 2)
                ntm += 1
                for c in range(2):
                    self.mm(pb[:, :].rearrange("p (h x) -> p h x", x=64), ckvnT[:, c, j * 128:(j + 1) * 128], wv[:, c, :, 64:128], c == 0, c == 1, ["wukv", "ckvnT"], [pk])
                self.cp("dve", st_V[:, j, :], pb[:, :], [], ["st_V", pk])
            self.st(rows("V"), st_V, ["st_V"], ["V"])
        P.barrier()

    def phase2(self, l):
        A, P, SC = self.A, self.P, self.scr
        A.reset(self.const_top)
        bk = self.banks
        KTs = [A.alloc([S], BF16) for _ in range(2)]
        Vh = [A.alloc([NT, 65], BF16) for _ in range(2)]
        Qs = [A.alloc([512], BF16) for _ in range(2)]
        pts = [A.alloc([512], BF16) for _ in range(4)]
        ones = A.alloc([64], F32)
        rcp = A.alloc([512], F32)
        bc = A.alloc([512], F32)
        ost = [A.alloc([512], BF16) for _ in range(2)]
        self.memset("pool", ones, 1.0, ["ones"])
        for i in range(2):
            self.memset("pool", Vh[i][:, :, 64:65], 1.0, [("Vh", i)])
        nheads = self.ns_limit or 8
        nqg = self.ns_limit or NS
        LOOK = 2
        items = [(h, qg) for h in range(nheads) for qg in range(nqg)]
        loaded_heads = set()

        def load_head(h):
            if h in loaded_heads or h >= nheads:
                return
            loaded_heads.add(h)
            hs = h % 2
            self.ld(KTs[hs][0:64, :], SC["KT"][h], ["KT"], [("KTs", hs)])
            self.ld(KTs[hs][64:96, :], SC["KRT"][:, :], ["KRT"], [("KTs", hs)])
            vsrc = SC["V"][:, h * 64:(h + 1) * 64].rearrange("(kt p) v -> p kt v", p=128)
            for k0 in range(0, NT, 8):
                self.ld(Vh[hs][:, k0:k0 + 8, 0:64], vsrc[:, k0:k0 + 8, :], ["V"], [("Vh", hs)])

        def load_q(ii):
            if ii >= len(items):
                return
            h, qg = items[ii]
            qs = ii % 2
            self.ld(Qs[qs][0:96, :], SC["QT"][h, :, qg * 512:qg * 512 + 512], ["QT"], [("Qs", qs)])

        def qk(ii, kt):
            h, qg = items[ii]
            hs, qs = h % 2, ii % 2
            g = (ii * NT + kt) % 4
            self.mm(bk[g][:, :], KTs[hs][0:96, kt * 128:(kt + 1) * 128], Qs[qs][0:96, :], True, True, [("KTs", hs), ("Qs", qs)], [("B", g)])

        def epilogue(ii):
            h, qg = items[ii]
            qs = ii % 2
            q0 = qg * 512
            ob = bk[4 + qs]
            ok = ("B", 4 + qs)
            self.P.op("dve", lambda e, ob=ob: e.reciprocal(out=rcp[64:65, :], in_=ob[64:65, :]), [], ["rcp", ok])
            self.mm(bk[6][0:64, :], ones[64:65, :], rcp[64:65, :], True, True, ["ones", "rcp"], [("B", 6)])
            self.cp("act", bc[0:64, :], bk[6][0:64, :], [], ["bc", ("B", 6)])
            self.tt("dve", ost[qs][0:64, :], ob[0:64, :], bc[0:64, :], ALU.mult, ["bc"], [("ost", qs), ok])
            self.st(SC["mlT"][h * 64:(h + 1) * 64, q0:q0 + 512], ost[qs][0:64, :], [("ost", qs)], ["mlT"])

        load_head(0)
        load_q(0)
        for kt in range(LOOK):
            qk(0, kt)
        for ii, (h, qg) in enumerate(items):
            hs, qs = h % 2, ii % 2
            if qg == 0:
                load_head(h + 1)
            load_q(ii + 1)
            ob = bk[4 + qs]
            ok = ("B", 4 + qs)
            for kt in range(NT):
                nk = kt + LOOK
                if nk < NT:
                    qk(ii, nk)
                elif ii + 1 < len(items):
                    qk(ii + 1, nk - NT)
                g = (ii * NT + kt) % 4
                pt = pts[g]
                pk = ("pt", g)
                self.act(pt, bk[g][:, :], AF.Exp, [], [pk, ("B", g)])
                self.mm(ob[0:65, :], Vh[hs][:, kt, :], pt, kt == 0, kt == NT - 1, [("Vh", hs), pk], [ok])
                if kt == 3 and ii > 0:
                    epilogue(ii - 1)
            if ii == len(items) - 1:
                epilogue(ii)
        P.barrier()

    def phase4(self, l):
        A, P, I, SC = self.A, self.P, self.inp, self.scr
        A.reset(self.const_top)
        bk = self.banks
        wp = A.alloc([4, 128], BF16)
        psc = A.alloc([4], F32)
        inv_first = A.alloc([4, 512], F32)
        inv_last = A.alloc([4, 512], F32)
        U = [A.alloc([528], F32) for _ in range(2)]
        sA = A.alloc([528], F32)
        sB = A.alloc([528], F32)
        pl = A.alloc([512], BF16)
        pst = [A.alloc([512], BF16) for _ in range(2)]
        self.load_cast(wp, I["pool_w"][l].rearrange("g c d -> c g d"), 128, "wp")
        self.ld(psc, I["pool_scale"][l].rearrange("(g p) -> p g", p=128), (), ["psc"], slow=True)
        for g, w in enumerate(POOL_WINDOWS):
            self.memset("pool", inv_first[:, g, :], 1.0 / w, ["inv_first"])
            self.memset("pool", inv_last[:, g, :], 1.0 / w, ["inv_last"])
            for t in range(w // 2):
                self.memset("pool", inv_first[:, g, t:t + 1], 1.0 / (t + w // 2), ["inv_first"])
            for j in range(1, w // 2):
                self.memset("pool", inv_last[:, g, 512 - j:512 - j + 1], 1.0 / (j + w // 2), ["inv_last"])
        nch = self.ns_limit or NS
        it = 0
        for g, w in enumerate(POOL_WINDOWS):
            for ch in range(nch):
                us = it % 2
                it += 1
                t0 = ch * 512
                uk = ("U", us)
                lo = max(t0 - 8, 0)
                hi = min(t0 + 520, S)
                if ch == 0:
                    self.memset("pool", U[us][:, 0:8], 0.0, [uk])
                if ch == NS - 1:
                    self.memset("pool", U[us][:, 520:528], 0.0, [uk])
                self.ld(U[us][:, lo - (t0 - 8):hi - (t0 - 8)], SC["uT"][g * 128:(g + 1) * 128, lo:hi], ["uT"], [uk])
                cur = U[us]
                curk = uk
                k = 1
                dst = [sA, sB]
                di = 0
                while k < w:
                    d_ = dst[di]
                    dk = ("s", di)
                    self.tt("dve", d_[:, k:528], cur[:, k:528], cur[:, 0:528 - k], ALU.add, [curk], [dk])
                    cur, curk = d_, dk
                    di ^= 1
                    k *= 2
                off = 8 + w // 2 - 1
                sw = cur[:, off:off + 512]
                if ch == 0 or ch == NS - 1:
                    inv = inv_first if ch == 0 else inv_last
                    other, okey = (sB, ("s", 1)) if cur is sA else (sA, ("s", 0))
                    self.tt("dve", other[:, 0:512], sw, inv[:, g, :], ALU.mult, [curk, "inv_first", "inv_last"], [okey])
                    self.tt("dve", pl, other[:, 0:512], U[us][:, 8:520], ALU.subtract, [okey, uk], ["pl"])
                else:
                    self.stt("dve", pl, sw, 1.0 / w, U[us][:, 8:520], ALU.mult, ALU.subtract, [curk, uk], ["pl"])
                pb = bk[it % 2]
                pk = ("B", it % 2)
                self.mm(pb[:, :], wp[:, g, :], pl, True, True, ["wp", "pl"], [pk])
                self.ts("dve", pst[us], pb[:, :], psc[:, g:g + 1], None, ALU.mult, None, ["psc"], [("pst", us), pk])
                self.st(SC["pmT"][g * 128:(g + 1) * 128, t0:t0 + 512], pst[us], [("pst", us)], ["pmT"])
        P.barrier()

    def phase3(self, l):
        A, P, I, SC = self.A, self.P, self.inp, self.scr
        A.reset(self.const_top)
        bk = self.banks
        if "of" not in SC:
            self.scratch("of", [S, 512], F32)
        tri = A.alloc([4, 64], F32)
        one_c = A.alloc([1], F32)
        wdec = A.alloc([2, 256], F32)
        gn = A.alloc([128], F32)
        dl = [A.alloc([128], F32) for _ in range(2)]
        gq = [A.alloc([4, 128], F32) for _ in range(2)]
        gk = [A.alloc([4, 128], F32) for _ in range(2)]
        gkt = [A.alloc([2, 256], F32) for _ in range(2)]
        vv = [A.alloc([2, 512], BF16) for _ in range(2)]
        of = [A.alloc([2, 512], F32) for _ in range(2)]
        grt = [A.alloc([2, 512], BF16) for _ in range(2)]
        esb = A.alloc([512], F32)
        gp = A.alloc([2, 256], F32)
        E1s = [A.alloc([4, 2, 64], F32) for _ in range(2)]
        E2 = A.alloc([4, 2, 64], F32)
        E3 = A.alloc([2, 256], F32)
        qins = [A.alloc([4, 128], BF16) for _ in range(2)]
        kins = [A.alloc([4, 128], BF16) for _ in range(2)]
        ksts = [A.alloc([2, 256], BF16) for _ in range(2)]
        ATm = A.alloc([4, 64], BF16)
        St = A.alloc([4, 128], F32)
        Sb = A.alloc([4, 128], BF16)
        ost = [A.alloc([2, 512], F32) for _ in range(2)]
        sq = A.alloc([1024], F32)
        ssq = A.alloc([8], F32)
        rs8 = A.alloc([8], F32)
        ob16 = A.alloc([2, 512], BF16)
        gst = [A.alloc([4, 128], BF16) for _ in range(2)]
        self.ld(tri[0:64, :, :], I["c_tri"].rearrange("k a b -> a k b"), (), ["tri"])
        self.memset("pool", one_c, 1.0, ["one_c"])
        for d_ in range(2):
            self.ld(wdec[0:16, d_, :], I["gla_w_dec"][l, d_], (), ["wdec"])
            self.ld(wdec[16:17, d_, :], I["gla_b_dec"][l, d_:d_ + 1, :], (), ["wdec"])
            self.memset("pool", dl[d_][0:17, :], 1.0, [("dl", d_)])
        self.ld(gn[0:64, :], I["gla_norm"][l][None, :].to_broadcast([64, 128]), (), ["gn"])
        LG, BT, CC, AT, OO = bk[0], bk[1], bk[2], bk[3], bk[4]
        BTv = BT[0:64, :].rearrange("p (h n i) -> p h n i", h=4, n=2)
        NG = S // 128
        ng = self.ns_limit or NG
        for dirn in range(2):
            self.memset("pool", St[0:64], 0.0, ["St"])
            self.memset("pool", Sb[0:64], 0.0, ["Sb"])
            inc_i, rest_i = (0, 1) if dirn == 0 else (2, 3)
            groups = list(range(ng)) if dirn == 0 else list(range(NG - 1, NG - 1 - ng, -1))
            def prep(gi, g):
                bs = gi % 2
                t0 = g * 128
                E1, qin, kin, kst = E1s[bs], qins[bs], kins[bs], ksts[bs]
                kE1, kq, kk, kks = ("E1", bs), ("qin", bs), ("kin", bs), ("kst", bs)
                self.ld(dl[bs][0:16, :], SC["gdT"][dirn * 16:(dirn + 1) * 16, t0:t0 + 128], ["gdT"], [("dl", bs)])
                self.ld(gq[bs][0:64], SC["gqT"][:, t0:t0 + 128].rearrange("(h d) t -> d h t", d=64), ["gqT"], [("gq", bs)])
                self.ld(gk[bs][0:64], SC["gkT"][:, t0:t0 + 128].rearrange("(h d) t -> d h t", d=64), ["gkT"], [("gk", bs)])
                self.ld(gkt[bs][0:64], SC["gk_tok"][t0:t0 + 128, :].rearrange("(n p) c -> p n c", p=64), ["gk_tok"], [("gkt", bs)])
                self.ld(vv[bs][0:64], SC["gv"][t0:t0 + 128, :].rearrange("(n p) c -> p n c", p=64), ["gv"], [("vv", bs)])
                if dirn == 1:
                    self.ld(of[bs][0:64], SC["of"][t0:t0 + 128, :].rearrange("(n p) c -> p n c", p=64), ["of"], [("of", bs)])
                    self.ld(grt[bs][0:64], SC["gr"][t0:t0 + 128, :].rearrange("(n p) c -> p n c", p=64), ["gr"], [("grt", bs)])
                for n in range(2):
                    self.mm(LG[0:64, n * 256:(n + 1) * 256], dl[bs][0:17, n * 64:(n + 1) * 64], wdec[0:17, dirn, :], True, True, [("dl", bs), "wdec"], [("B", 0)])
                self.act(esb[0:64, :], LG[0:64, :], AF.Exp, [], ["esb", ("B", 0)], scale=-1.0)
                self.act(gp[0:64].rearrange("p n c -> p (n c)"), esb[0:64, :], AF.Ln, ["esb", "one_c"], ["gp"], bias=one_c[0:64, 0:1], scale=1.0)
                for h in range(4):
                    for n in range(2):
                        self.mm(BTv[:, h, n, :], gp[0:64, n, h * 64:(h + 1) * 64], tri[0:64, inc_i, :], True, True, ["gp", "tri"], [("B", 1)])
                for n in range(2):
                    self.mm(CC[0:64, n * 256:(n + 1) * 256], tri[0:64, rest_i, :], gp[0:64, n, :], True, True, ["gp", "tri"], [("B", 2)])
                f = lambda t: t[0:64].rearrange("p a b c -> p (a b c)")
                self.act(f(E1), BT[0:64, :], AF.Exp, [], [kE1, ("B", 1)], scale=-1.0 / 16)
                self.act(f(E2), BT[0:64, :], AF.Exp, [], ["E2", ("B", 1)], scale=1.0 / 16)
                self.act(E3[0:64].rearrange("p n c -> p (n c)"), CC[0:64, :], AF.Exp, [], ["E3", ("B", 2)], scale=-1.0 / 16)
                self.tt("dve", qin[0:64], gq[bs][0:64], E1[0:64].rearrange("p h n i -> p h (n i)"), ALU.mult, [("gq", bs), kE1], [kq])
                self.tt("dve", kin[0:64], gk[bs][0:64], E2[0:64].rearrange("p h n i -> p h (n i)"), ALU.mult, [("gk", bs), "E2"], [kk])
                self.tt("dve", kst[0:64], gkt[bs][0:64], E3[0:64], ALU.mult, [("gkt", bs), "E3"], [kks])

            def chunks(gi, g):
                bs = gi % 2
                t0 = g * 128
                E1, qin, kin, kst = E1s[bs], qins[bs], kins[bs], ksts[bs]
                kE1, kq, kk, kks = ("E1", bs), ("qin", bs), ("kin", bs), ("kst", bs)
                order = (0, 1) if dirn == 0 else (1, 0)
                for ci, n in enumerate(order):
                    cs = slice(n * 64, (n + 1) * 64)
                    for h in range(4):
                        self.mm(AT[0:64, h * 64:(h + 1) * 64], kin[0:64, h, cs], qin[0:64, h, cs], True, True, [kk, kq], [("B", 3)])
                    self.tt("dve", ATm[0:64], AT[0:64, 0:256].rearrange("p (h i) -> p h i", h=4), tri[0:64, inc_i:inc_i + 1, :].to_broadcast([64, 4, 64]), ALU.mult, ["tri"], ["ATm", ("B", 3)])
                    for h in range(4):
                        hv = slice(h * 128, (h + 1) * 128)
                        self.mm(OO[0:64, hv], ATm[0:64, h, :], vv[bs][0:64, n, hv], True, False, ["ATm", ("vv", bs)], [("B", 4)])
                        self.mm(OO[0:64, hv], qin[0:64, h, cs], Sb[0:64, h, :], False, True, [kq, "Sb"], [("B", 4)])
                    kvb = bk[5 + (gi * 2 + ci) % 2]
                    kvk = ("B", 5 + (gi * 2 + ci) % 2)
                    for h in range(4):
                        hv = slice(h * 128, (h + 1) * 128)
                        self.mm(kvb[0:64, hv], kst[0:64, n, h * 64:(h + 1) * 64], vv[bs][0:64, n, hv], True, True, [kks, ("vv", bs)], [kvk])
                    dcol = 63 if dirn == 0 else 0
                    for h in range(4):
                        hv = slice(h * 128, (h + 1) * 128)
                        self.stt("dve", St[0:64, h, :], St[0:64, h, :], E1[0:64, h, n, dcol:dcol + 1], kvb[0:64, hv], ALU.mult, ALU.add, [kE1], ["St", kvk])
                    self.cp("act", Sb[0:64].rearrange("p h v -> p (h v)"), St[0:64].rearrange("p h v -> p (h v)"), ["St"], ["Sb"])
                    if dirn == 0:
                        self.cp("act", ost[bs][0:64, n, :], OO[0:64, :], [], [("ost", bs), ("B", 4)])
                    else:
                        self.tt("dve", ost[bs][0:64, n, :], OO[0:64, :], of[bs][0:64, n, :], ALU.add, [("of", bs)], [("ost", bs), ("B", 4)])
                if dirn == 0:
                    self.st(SC["of"][t0:t0 + 128, :].rearrange("(n p) c -> p n c", p=64), ost[bs][0:64], [("ost", bs)], ["of"])
                else:
                    o2 = ost[bs][0:64].rearrange("p n c -> p (n c)")
                    self.tt("dve", sq[0:64, :], o2, o2, ALU.mult, [("ost", bs)], ["sq"])
                    self.P.op("dve", lambda e: e.reduce_sum(out=ssq[0:64, :], in_=sq[0:64, :].rearrange("p (a b) -> p a b", b=128), axis=AX.X), ["sq"], ["ssq"])
                    self.act(rs8[0:64, :], ssq[0:64, :], AF.Ln, ["ssq", "eps_rms"], ["rs8"], bias=self.eps_rms[0:64, 0:1], scale=1.0 / 128)
                    self.act(rs8[0:64, :], rs8[0:64, :], AF.Exp, ["rs8"], ["rs8"], scale=-0.5)
                    o3 = ost[bs][0:64].rearrange("p n (h v) -> p (n h) v", v=128)
                    self.tt("dve", o3, o3, rs8[0:64, :, None].to_broadcast([64, 8, 128]), ALU.mult, ["rs8"], [("ost", bs)])
                    self.tt("dve", o3, o3, gn[0:64, None, :].to_broadcast([64, 8, 128]), ALU.mult, ["gn"], [("ost", bs)])
                    self.tt("dve", ob16[0:64], ost[bs][0:64], grt[bs][0:64], ALU.mult, [("ost", bs), ("grt", bs)], ["ob16"])
                    TP = bk[7][:, 0:256].bitcast(BF16).rearrange("p (b t) -> p b t", b=4)
                    for n in range(2):
                        for b_ in range(4):
                            self.tr(TP[:, b_, n * 64:(n + 1) * 64], ob16[0:64, n, b_ * 128:(b_ + 1) * 128], self.ident_b[0:64, 0:64], ["ob16", "ident_b"], [("B", 7)])
                    self.cp("act", gst[bs], TP, [], [("gst", bs), ("B", 7)])
                    self.st(SC["glT"][:, t0:t0 + 128].rearrange("(b p) t -> p b t", p=128), gst[bs], [("gst", bs)], ["glT"])

            prep(0, groups[0])
            for gi, g in enumerate(groups):
                if gi + 1 < len(groups):
                    prep(gi + 1, groups[gi + 1])
                chunks(gi, g)
        P.barrier()

    def phase5(self, l):
        A, P, I, SC = self.A, self.P, self.inp, self.scr
        A.reset(self.const_top)
        bk = self.banks
        self.affT = A.alloc([S], F32)
        self.p6_base = A.off
        wg = A.alloc([8, 3072], BF16)
        wup = [A.alloc([4, D], BF16) for _ in range(3)]
        wout = A.alloc([8, D], BF16)
        rw = A.alloc([8, 16], F32)
        bgr = A.alloc([3072], BF16)
        ones1 = A.alloc([128], BF16)
        gam = A.alloc([D], F32)
        bet = A.alloc([D], F32)
        self.ln_tmp = A.alloc([D], F32)
        ht = [A.alloc([D], F32) for _ in range(2)]
        hb = A.alloc([D], BF16)
        hT = A.alloc([8, 128], BF16)
        gates = A.alloc([3072], F32)
        xT = [[A.alloc([4, 128], BF16)] * 2 for _ in range(3)]
        mrg = A.alloc([D], F32)
        tmpm = A.alloc([D], F32)
        mb = A.alloc([D], BF16)
        mT = A.alloc([8, 128], BF16)
        h1b = A.alloc([D], BF16)
        h1a = A.alloc([D], F32)
        h1buf = A.alloc([D], F32)
        h1T = A.alloc([8, 128], F32)
        st6 = A.alloc([2, 6], F32); mv = A.alloc([2], F32); rstd = A.alloc([1], F32); nb = A.alloc([1], F32)
        rmax = A.alloc([1], F32); rsum = A.alloc([1], F32); aff = A.alloc([16], F32)
        self.load_cast(wg, I["w_in"][l][:, C_GATE:C_GATE + 3072].rearrange("(c p) n -> p c n", p=128), 3072, "wg")
        for i, nm in enumerate(("w_up_a", "w_up_b", "w_up_c")):
            self.load_cast(wup[i], I[nm][l].rearrange("(c p) n -> p c n", p=128), D, ("wup", i))
        self.load_cast(wout, I["w_out"][l].rearrange("(c p) n -> p c n", p=128), D, "wout")
        self.ld(rw, I["router_w"][l].rearrange("(c p) e -> p c e", p=128), (), ["rw"])
        for a_ in range(0, 3072, 1536):
            self.P.dma(lambda e, a_=a_: e.dma_start(out=bgr[0:1, a_:a_ + 1536], in_=I["b_gate"][l:l + 1, a_:a_ + 1536]), (), ["bgr"], q="pool")
        self.memset("pool", ones1, 1.0, ["ones1"])
        self.ld(gam, I["ln1_g"][l][None, :].to_broadcast([128, D]), (), ["lnpar"])
        self.ld(bet, I["ln1_b"][l][None, :].to_broadcast([128, D]), (), ["lnpar"])
        srcs = ("pmT", "mlT", "glT")
        pT = bk[0][:, :].bitcast(BF16).rearrange("p (a b) -> p a b", a=8)
        nt = (self.ns_limit or NS) * 4

        def X(t):
            bs = t % 2
            r0 = t * 128
            self.ld(ht[bs], SC["hbuf"][r0:r0 + 128, :], ["hbuf"], [("ht", bs)])
            for i in range(3):
                self.ld(xT[i][bs], SC[srcs[i]][:, r0:r0 + 128].rearrange("(c p) t -> p c t", p=128), [srcs[i]], [("xT", i)])
            self.cp("act", hb, ht[bs], [("ht", bs)], ["hb"])
            for c in range(8):
                self.tr(pT[:, c, :], hb[:, c * 128:(c + 1) * 128], self.ident_b, ["hb", "ident_b"], [("B", 0)])
            self.cp("dve", hT, pT, [], ["hT", ("B", 0)])
            for gI in range(6):
                pb = bk[1 + gI % 2]
                pk = ("B", 1 + gI % 2)
                cs = slice(gI * 512, (gI + 1) * 512)
                for c in range(8):
                    self.mm(pb[:, :], hT[:, c, :], wg[:, c, cs], c == 0, False, ["hT", "wg"], [pk])
                self.mm(pb[:, :], ones1[0:1, :], bgr[0:1, cs], False, True, ["ones1", "bgr"], [pk])
                self.act(gates[:, cs], pb[:, :], AF.Sigmoid, [], [("gates", gI), pk])
            for i in range(3):
                for g2 in range(2):
                    pb = bk[3 + (i * 2 + g2) % 2]
                    pk = ("B", 3 + (i * 2 + g2) % 2)
                    cs = slice(g2 * 512, (g2 + 1) * 512)
                    for c in range(4):
                        self.mm(pb[:, :], xT[i][bs][:, c, :], wup[i][:, c, cs], c == 0, c == 3, [("xT", i), ("wup", i)], [pk])
                    gsl = gates[:, i * D + g2 * 512:i * D + (g2 + 1) * 512]
                    gk_ = ("gates", i * 2 + g2)
                    if i == 0:
                        self.tt("dve", mrg[:, cs], pb[:, :], gsl, ALU.mult, [gk_], [("mrg", g2), pk])
                    else:
                        self.tt("dve", tmpm[:, cs], pb[:, :], gsl, ALU.mult, [gk_], [("tmpm", g2), pk])
                        if i == 1:
                            self.tt("pool", mrg[:, cs], mrg[:, cs], tmpm[:, cs], ALU.add, [("tmpm", g2)], [("mrg", g2)])
                        else:
                            self.tt("pool", mb[:, cs], mrg[:, cs], tmpm[:, cs], ALU.add, [("tmpm", g2), ("mrg", g2)], [("mb", g2)])
            for c in range(8):
                self.tr(pT[:, c, :], mb[:, c * 128:(c + 1) * 128], self.ident_b, [("mb", c // 4), "ident_b"], [("B", 0)])
            self.cp("dve", mT, pT, [], ["mT", ("B", 0)])
            for g2 in range(2):
                pb = bk[5 + g2]
                pk = ("B", 5 + g2)
                cs = slice(g2 * 512, (g2 + 1) * 512)
                for c in range(8):
                    self.mm(pb[:, :], mT[:, c, :], wout[:, c, cs], c == 0, c == 7, ["mT", "wout"], [pk])
                self.stt("dve", ht[bs][:, cs], ht[bs][:, cs], DN_ALPHA, pb[:, :], ALU.mult, ALU.add, [], [("ht", bs), pk])

        def Y(t):
            bs = t % 2
            r0 = t * 128
            h1 = h1buf
            self.layernorm_tile(ht[bs], h1, h1b, gam, bet, st6, mv, rstd, nb, "ln", [("ht", bs)], ["h1"], ["h1b"])
            self.st(SC["h1"][r0:r0 + 128, :], h1, ["h1"], ["h1d"])
            self.st(SC["h1b"][r0:r0 + 128, :], h1b, ["h1b"], ["h1bd"])
            self.act(h1a, h1, AF.Identity, ["h1"], ["h1a"], scale=DN_ALPHA)
            self.st(SC["acc"][r0:r0 + 128, :], h1a, ["h1a"], ["acc"])
            pTf = [bk[7][:, :].rearrange("p (a b) -> p a b", a=4), bk[0][:, :].rearrange("p (a b) -> p a b", a=4)]
            for c in range(8):
                self.tr(pTf[c // 4][:, c % 4, :], h1[:, c * 128:(c + 1) * 128], self.ident_f, ["h1", "ident_f"], [("B", 7 if c < 4 else 0)])
            self.cp("dve", h1T[:, 0:4, :], pTf[0], [], [("h1T", 0), ("B", 7)])
            self.cp("dve", h1T[:, 4:8, :], pTf[1], [], [("h1T", 1), ("B", 0)])
            lg = bk[7]
            for c in range(8):
                self.mm(lg[:, 0:16], h1T[:, c, :], rw[:, c, :], c == 0, c == 7, [("h1T", c // 4), "rw"], [("B", 7)])
            self.P.op("dve", lambda e: e.reduce_max(out=rmax, in_=lg[:, 0:16], axis=AX.X), [], ["rmax", ("B", 7)])
            self.ts("dve", rmax, rmax, -1.0, None, ALU.mult, None, ["rmax"], ["rmax"])
            self.act(aff, lg[:, 0:16], AF.Exp, ["rmax"], ["aff", "rsum", ("B", 7)], bias=rmax[:, 0:1], scale=1.0, accum_out=rsum[:, 0:1])
            self.P.op("dve", lambda e: e.reciprocal(out=rsum, in_=rsum), ["rsum"], ["rsum"])
            self.ts("dve", aff, aff, rsum[:, 0:1], None, ALU.mult, None, ["rsum"], ["aff"])
            self.tr(bk[7][0:16, 128:256], aff, self.ident_f, ["aff", "ident_f"], [("B", 7)])
            self.cp("dve", self.affT[0:16, r0:r0 + 128], bk[7][0:16, 128:256], [], ["affT", ("B", 7)])

        X(0)
        for t in range(nt):
            if t + 1 < nt:
                X(t + 1)
            Y(t)
        P.barrier()

    def phase6(self, l):
        A, P = self.A, self.P
        A.reset(self.p6_base)
        bk = self.banks
        CAP = S // 8
        NJ = CAP // 128
        work = A.alloc([S], F32)
        vals = A.alloc([CAP], F32)
        idxu = A.alloc([CAP], U32)
        idxf = A.alloc([CAP], F32)
        af = self.affT[0:16, :]
        cur = af
        curk = "affT"
        for it in range(CAP // 8):
            v8 = vals[0:16, it * 8:(it + 1) * 8]
            self.P.op("dve", lambda e, v8=v8, cur=cur: e.max(out=v8, in_=cur), [curk], ["vals"])
            i8 = idxu[0:16, it * 8:(it + 1) * 8]
            self.P.op("dve", lambda e, v8=v8, cur=cur, i8=i8: e.max_index(out=i8, in_max=v8, in_values=cur), [curk, "vals"], ["idxu"])
            self.P.op("dve", lambda e, v8=v8, cur=cur: e.match_replace(out=work[0:16, :], in_to_replace=v8, in_values=cur, imm_value=-1.0), [curk, "vals"], ["work"])
            cur = work[0:16, :]
            curk = "work"
        self.cp("dve", idxf[0:16, :], idxu[0:16, :], ["idxu"], ["idxf"])
        for j in range(NJ):
            pj = bk[j % 2]
            pk = ("B", j % 2)
            self.tr(pj[:, 0:16], idxf[0:16, j * 128:(j + 1) * 128], self.ident_f[0:16, 0:16], ["idxf", "ident_f"], [pk])
            self.tr(pj[:, 16:32], vals[0:16, j * 128:(j + 1) * 128], self.ident_f[0:16, 0:16], ["vals", "ident_f"], [pk])
            self.cp("dve", self.IDX[:, j, :], pj[:, 0:16], [], ["IDX", pk])
            self.cp("dve", self.GATE[:, j, :], pj[:, 16:32], [], ["GATE", pk])
        P.barrier()

    def phase7(self, l):
        A, P, I, SC = self.A, self.P, self.inp, self.scr
        A.reset(self.const_top)
        bk = self.banks
        CAP = S // 8
        NJ = CAP // 128
        wgh = [A.alloc([8, 1024], BF16) for _ in range(2)]
        wuh = [A.alloc([8, 1024], BF16) for _ in range(2)]
        wdt = A.alloc([16, D], BF16)
        xg = A.alloc([NJ, D], BF16)
        xeT = A.alloc([8, CAP], BF16)
        hidT = A.alloc([16, CAP], BF16)
        sil = [A.alloc([512], F32) for _ in range(2)]
        ysb = [A.alloc([D], F32) for _ in range(2)]
        pT = bk[0][:, :].bitcast(BF16).rearrange("p (a b) -> p a b", a=8)
        nexp = self.ns_limit and min(16, self.ns_limit * 2) or 16
        NH = (CAP + 511) // 512
        HW_ = CAP // NH

        def ld_gu(ex, hf):
            fs = slice(hf * 1024, (hf + 1) * 1024)
            self.load_cast(wgh[hf], I["exp_w_gate"][l, ex][:, fs].rearrange("(c p) n -> p c n", p=128), 1024, ("wg", hf))
            self.load_cast(wuh[hf], I["exp_w_up"][l, ex][:, fs].rearrange("(c p) n -> p c n", p=128), 1024, ("wu", hf))

        def ld_d(ex):
            self.load_cast(wdt, I["exp_w_down"][l, ex].rearrange("(c p) n -> p c n", p=128), D, "wdt")

        def gathers(ex):
            for j in range(NJ):
                self.P.dma(lambda e, j=j, ex=ex: e.indirect_dma_start(
                    out=xg[:, j, :], out_offset=None, in_=SC["h1b"][:, :],
                    in_offset=bass.IndirectOffsetOnAxis(ap=self.IDX[:, j, ex:ex + 1], axis=0)),
                    ["IDX", "h1bd"], [("xg", j)], q="pool")

        ld_gu(0, 0)
        ld_gu(0, 1)
        gathers(0)
        ld_d(0)
        cnt = 0
        for ex in range(nexp):
            for j in range(NJ):
                for c in range(8):
                    self.tr(pT[:, c, :], xg[:, j, c * 128:(c + 1) * 128], self.ident_b, [("xg", j), "ident_b"], [("B", 0)])
                self.cp("dve", xeT[:, :, j * 128:(j + 1) * 128], pT, [], ["xeT", ("B", 0)])
            for fb in range(16):
                fh = fb // 8
                fs = slice((fb % 8) * 128, (fb % 8 + 1) * 128)
                for hf in range(NH):
                    ss_ = slice(hf * HW_, (hf + 1) * HW_)
                    gb, ub = bk[1 + cnt % 2], bk[3 + cnt % 2]
                    gk_, uk_ = ("B", 1 + cnt % 2), ("B", 3 + cnt % 2)
                    sl_ = sil[cnt % 2]
                    sk_ = ("sil", cnt % 2)
                    cnt += 1
                    for c in range(8):
                        self.mm(gb[:, 0:HW_], wgh[fh][:, c, fs], xeT[:, c, ss_], c == 0, c == 7, [("wg", fh), "xeT"], [gk_])
                    for c in range(8):
                        self.mm(ub[:, 0:HW_], wuh[fh][:, c, fs], xeT[:, c, ss_], c == 0, c == 7, [("wu", fh), "xeT"], [uk_])
                    self.act(sl_[:, 0:HW_], gb[:, 0:HW_], AF.Silu, [], [sk_, gk_])
                    self.tt("dve", hidT[:, fb, ss_], ub[:, 0:HW_], sl_[:, 0:HW_], ALU.mult, [sk_], ["hidT", uk_])
                if fb % 8 == 7 and ex + 1 < nexp:
                    ld_gu(ex + 1, fh)
            if ex + 1 < nexp:
                gathers(ex + 1)
            for j in range(NJ):
                yb = ysb[j % 2]
                yk = ("ysb", j % 2)
                for g2 in range(2):
                    pb = bk[5 + g2]
                    pk = ("B", 5 + g2)
                    cs = slice(g2 * 512, (g2 + 1) * 512)
                    for fb in range(16):
                        self.mm(pb[:, :], hidT[:, fb, j * 128:(j + 1) * 128], wdt[:, fb, cs], fb == 0, fb == 15, ["hidT", "wdt"], [pk])
                    self.act(yb[:, cs], pb[:, :], AF.Identity, ["GATE"], [yk, pk], scale=self.GATE[:, j, ex:ex + 1])
                self.P.dma(lambda e, j=j, ex=ex, yb=yb: e.indirect_dma_start(
                    out=SC["acc"][:, :], out_offset=bass.IndirectOffsetOnAxis(ap=self.IDX[:, j, ex:ex + 1], axis=0),
                    in_=yb, in_offset=None, compute_op=ALU.add),
                    ["IDX", yk], ["acc"], q="pool")
            if ex + 1 < nexp:
                ld_d(ex + 1)
        P.barrier()

    def phase8(self, l, final):
        A, P, I, SC = self.A, self.P, self.inp, self.scr
        A.reset(self.const_top)
        gam = A.alloc([D], F32)
        bet = A.alloc([D], F32)
        self.ln_tmp = A.alloc([D], F32)
        at = [A.alloc([4, D], F32) for _ in range(2)]
        st6 = A.alloc([2, 6], F32); mv = A.alloc([2], F32); rstd = A.alloc([1], F32); nb = A.alloc([1], F32)
        self.ld(gam, I["ln2_g"][l][None, :].to_broadcast([128, D]), (), ["lnpar"])
        self.ld(bet, I["ln2_b"][l][None, :].to_broadcast([128, D]), (), ["lnpar"])
        dst = self.out if final else SC["hbuf"]
        for s_ in range(self.ns_limit or NS):
            bs = s_ % 2
            t0 = s_ * 512
            ak = ("at", bs)
            self.ld(at[bs], SC["acc"][t0:t0 + 512, :].rearrange("(j p) d -> p j d", p=128), ["acc"], [ak])
            for j in range(4):
                self.layernorm_tile(at[bs][:, j, :], at[bs][:, j, :], None, gam, bet, st6, mv, rstd, nb, "ln", [ak], [ak], None)
            self.st(dst[t0:t0 + 512, :].rearrange("(j p) d -> p j d", p=128), at[bs], [ak], ["dst"], is_out=final)
        P.barrier()

    def build(self):
        self.declare()
        self.consts()
        self.phase0()
        for l in self.layers:
            self.phase1(l)
            self.phase2(l)
            self.phase3(l)
            self.phase4(l)
            self.phase5(l)
            self.phase6(l)
            self.phase7(l)
            self.phase8(l, final=(l == self.layers[-1]))
        return self.P.finalize()

def _consts():
    half = 16
    freq = (10000.0 ** (-np.arange(half, dtype=np.float32) / half)).astype(np.float32)
    tp = np.arange(64)[:, None]
    ii = np.arange(64)[None, :]
    tri = np.stack([tp <= ii, tp > ii, tp >= ii, tp < ii]).astype(np.float32)
    return {"c_ident": np.eye(128, dtype=np.float32), "c_freq": freq, "c_tri": tri}


_CACHE = {}


def kernel(**inputs):
    if "nc" not in _CACHE:
        k = K()
        _CACHE["nc"] = k.build()
        _CACHE["names"] = list(k.inp.keys())
    nc = _CACHE["nc"]
    cst = _consts()
    maps = []
    for b in range(2):
        m = {}
        for n in _CACHE["names"]:
            if n == "x":
                m[n] = np.ascontiguousarray(inputs["x"][b], dtype=np.float32)
            elif n == "positions":
                m[n] = np.ascontiguousarray(inputs["positions"][b], dtype=np.int32)
            elif n in cst:
                m[n] = cst[n]
            else:
                m[n] = np.ascontiguousarray(inputs[n], dtype=np.float32)
        maps.append(m)
    res = run_bass_kernel_spmd(nc, maps, core_ids=[0, 1])
    return np.stack([np.asarray(res.results[b]["out"], dtype=np.float32) for b in range(2)], axis=0)
```

```python
from contextlib import ExitStack
import math
import numpy as np
import concourse.bass as bass
import concourse.mybir as mybir
from concourse.bass_utils import run_bass_kernel_spmd

F32 = mybir.dt.float32
BF16 = mybir.dt.bfloat16
I32 = mybir.dt.int32
U32 = mybir.dt.uint32
AF = mybir.ActivationFunctionType
ALU = mybir.AluOpType
AX = mybir.AxisListType

COMPUTE = ("pe", "act", "dve", "pool")
DMAQ_K = 8


def dsize(dt):
    return {F32: 4, BF16: 2, I32: 4, U32: 4}[dt]


class Op:
    __slots__ = ("eng", "fn", "deps", "is_dma", "idx", "marked", "done", "q")

    def __init__(self, eng, fn, is_dma=False, q=None):
        self.eng = eng
        self.fn = fn
        self.deps = []
        self.is_dma = is_dma
        self.idx = None
        self.marked = False
        self.done = None
        self.q = q


class Prog:
    def __init__(self):
        self.nc = bass.Bass("TRN2", target_bir_lowering=False)
        self.stack = ExitStack()
        self.ops = {e: [] for e in ("pe", "act", "dve", "pool", "sp")}
        self.reg = {}
        self.dma_count = {"sp": 0, "pool": 0}
        self.dma_ops = {"sp": [], "pool": []}
        self.out_dmas = []
        self.same_engine_sync = True
        self.debug_out = set()

    def sb(self, name, shape, dt):
        return self.stack.enter_context(self.nc.sbuf_tensor(name, list(shape), dt))

    def ps(self, name, shape, dt):
        return self.stack.enter_context(self.nc.psum_tensor(name, list(shape), dt))

    def dram(self, name, shape, dt, kind="Internal"):
        if kind == "Internal" and name in self.debug_out:
            kind = "ExternalOutput"
        return self.nc.dram_tensor(name, list(shape), dt, kind=kind).ap()

    def _deps(self, op, reads, writes):
        for k in reads:
            st = self.reg.get(k)
            if st is None:
                st = self.reg[k] = [None, []]
            if st[0] is not None:
                op.deps.append(st[0])
            st[1].append(op)
        for k in writes:
            st = self.reg.get(k)
            if st is None:
                st = self.reg[k] = [None, []]
            if st[0] is not None:
                op.deps.append(st[0])
            for r in st[1]:
                if r is not op:
                    op.deps.append(r)
            st[0] = op
            st[1] = []

    def op(self, eng, fn, reads=(), writes=()):
        o = Op(eng, fn)
        self._deps(o, reads, writes)
        self.ops[eng].append(o)
        return o

    def dma(self, fn, reads=(), writes=(), q="sp", is_out=False):
        o = Op(q, fn, is_dma=True, q=q)
        o.idx = self.dma_count[q]
        self.dma_count[q] += 1
        self.dma_ops[q].append(o)
        self._deps(o, reads, writes)
        self.ops[q].append(o)
        if is_out:
            self.out_dmas.append(o)
        return o

    def barrier(self):
        lasts = []
        for e in COMPUTE:
            for o in reversed(self.ops[e]):
                if not o.is_dma and o.fn is not None:
                    lasts.append(o)
                    break
        for q in ("sp", "pool"):
            lasts.extend(self.dma_ops[q][-DMAQ_K:])
        for e in ("pe", "act", "dve", "pool", "sp"):
            o = Op(e, None)
            o.deps = list(lasts)
            self.ops[e].append(o)
        self.reg = {}

    def finalize(self):
        nc = self.nc
        sync_same = self.same_engine_sync

        def skip(d, o):
            return (not d.is_dma) and (not o.is_dma) and d.eng == o.eng and o.fn is not None and (d.eng == "pe" or not sync_same)

        for e, lst in self.ops.items():
            for o in lst:
                for d in o.deps:
                    if d.is_dma or skip(d, o):
                        continue
                    d.marked = True
        sems = {}
        for e in COMPUTE:
            sems[e] = self.stack.enter_context(nc.semaphore("s_" + e))
            n = 0
            for o in self.ops[e]:
                if o.is_dma or o.fn is None:
                    continue
                if o.marked:
                    n += 1
                    o.done = (sems[e], n)
        dsems = {}
        for q in ("sp", "pool"):
            if self.dma_count[q] == 0:
                continue
            dsems[q] = [self.stack.enter_context(nc.semaphore("d_%s%d" % (q, i))) for i in range(DMAQ_K)]
            for o in self.dma_ops[q]:
                o.done = (dsems[q][o.idx % DMAQ_K], 16 * (o.idx // DMAQ_K + 1))
        block = self.stack.enter_context(nc.Block())

        def emit(ename, e):
            waited = {}
            for o in self.ops[ename]:
                need = {}
                for d in o.deps:
                    if d.done is None or skip(d, o):
                        continue
                    s, v = d.done
                    if need.get(s.num, (None, 0))[1] < v:
                        need[s.num] = (s, v)
                if o.is_dma and o.idx >= DMAQ_K:
                    s = dsems[o.q][o.idx % DMAQ_K]
                    v = 16 * (o.idx // DMAQ_K)
                    if need.get(s.num, (None, 0))[1] < v:
                        need[s.num] = (s, v)
                for sn, (s, v) in need.items():
                    if waited.get(sn, 0) < v:
                        e.wait_ge(s, v)
                        waited[sn] = v
                if o.fn is None:
                    continue
                ins = o.fn(e)
                if o.is_dma:
                    ins.then_inc(o.done[0], 16)
                elif o.marked:
                    ins.then_inc(o.done[0], 1)
            if ename == "sp":
                for o in self.out_dmas:
                    s, v = o.done
                    if waited.get(s.num, 0) < v:
                        e.wait_ge(s, v)
                        waited[s.num] = v

        @block.tensor
        def _(e):
            emit("pe", e)

        @block.scalar
        def _(e):
            emit("act", e)

        @block.vector
        def _(e):
            emit("dve", e)

        @block.gpsimd
        def _(e):
            emit("pool", e)

        @block.sync
        def _(e):
            emit("sp", e)

        self.stack.close()
        return nc


class Arena:
    def __init__(self, P, kbytes):
        self.words = kbytes * 256
        self.t = P.sb("arena", [128, self.words], F32)
        self.off = 0

    def reset(self, to=0):
        self.off = to

    def alloc(self, free_shape, dt, parts=128, p0=0):
        n = 1
        for s in free_shape:
            n *= s
        nb = n * dsize(dt)
        w = (nb + 31) // 32 * 8
        assert self.off + w <= self.words, ("arena overflow", self.off, w, self.words)
        v = self.t[p0:p0 + parts, self.off:self.off + w]
        self.off += w
        if dt != F32:
            v = v.bitcast(dt)
        v = v[:, 0:n]
        if len(free_shape) == 2:
            v = v.rearrange("p (a b) -> p a b", a=free_shape[0])
        elif len(free_shape) == 3:
            v = v.rearrange("p (a b c) -> p a b c", a=free_shape[0], b=free_shape[1])
        return v


S = 8192
D = 1024
NT = S // 128
NS = S // 512
POOL_WINDOWS = (2, 4, 8, 16)
LN_EPS = 1e-5
RMS_EPS = 1e-6
DN_ALPHA = 4 ** 0.25
QSCALE = 96 ** -0.5
TWO_PI = 2.0 * math.pi
CW1 = 6.28125
CW2 = TWO_PI - CW1

C_UP, C_CQ, C_CKV, C_KR, C_GQ, C_GK, C_GV, C_GR, C_GD, C_GATE = 0, 512, 896, 1152, 1184, 1440, 1696, 2208, 2720, 2752


class K:
    def __init__(self, debug_out=(), layers=(0, 1), phases=None, ns_limit=None, skip=(), no_big=False):
        self.ns_limit = ns_limit
        self.skip = set(skip)
        self.no_big = no_big
        self.P = P = Prog()
        P.debug_out = set(debug_out)
        self.layers = layers
        self.phases = phases
        self.nc = P.nc
        self.A = Arena(P, 204)
        self.banks = [P.ps("bank%d" % i, [128, 512], F32) for i in range(8)]
        self.inp = {}
        self.scr = {}

    def din(self, name, shape, dt=F32):
        self.inp[name] = self.P.dram(name, shape, dt, kind="ExternalInput")
        return self.inp[name]

    def scratch(self, name, shape, dt):
        self.scr[name] = self.P.dram(name, shape, dt)
        return self.scr[name]

    def mm(self, out, lhsT, rhs, start, stop, reads, writes):
        self.P.op("pe", lambda e: e.matmul(out, lhsT=lhsT, rhs=rhs, start=start, stop=stop), reads, writes)

    def tr(self, out, in_, ident, reads, writes):
        self.P.op("pe", lambda e: e.transpose(out=out, in_=in_, identity=ident), reads, writes)

    def act(self, out, in_, func, reads, writes, bias=None, scale=None, accum_out=None):
        kw = {}
        if bias is not None:
            kw["bias"] = bias
        if scale is not None:
            kw["scale"] = scale
        if accum_out is not None:
            kw["accum_out"] = accum_out
        self.P.op("act", lambda e: e.activation(out=out, in_=in_, func=func, **kw), reads, writes)

    def ts(self, eng, out, in0, s1, s2, op0, op1, reads, writes):
        if op1 is None:
            self.P.op(eng, lambda e: e.tensor_scalar(out=out, in0=in0, scalar1=s1, scalar2=None, op0=op0), reads, writes)
        else:
            self.P.op(eng, lambda e: e.tensor_scalar(out=out, in0=in0, scalar1=s1, scalar2=s2, op0=op0, op1=op1), reads, writes)

    def tt(self, eng, out, in0, in1, op, reads, writes):
        self.P.op(eng, lambda e: e.tensor_tensor(out=out, in0=in0, in1=in1, op=op), reads, writes)

    def stt(self, eng, out, in0, scalar, in1, op0, op1, reads, writes):
        self.P.op(eng, lambda e: e.scalar_tensor_tensor(out=out, in0=in0, scalar=scalar, in1=in1, op0=op0, op1=op1), reads, writes)

    def cp(self, eng, out, in_, reads, writes):
        if eng == "act":
            self.P.op("act", lambda e: e.copy(out=out, in_=in_), reads, writes)
        else:
            self.P.op(eng, lambda e: e.tensor_copy(out=out, in_=in_), reads, writes)

    def memset(self, eng, ap, val, writes):
        self.P.op(eng, lambda e: e.memset(ap, val), (), writes)

    def ld(self, out, in_, reads, writes, q="sp", slow=False):
        if slow:
            self.P.dma(lambda e: e.dma_start(out=out, in_=in_, allow_slow_non_contiguous=True), reads, writes, q=q)
        else:
            self.P.dma(lambda e: e.dma_start(out=out, in_=in_), reads, writes, q=q)

    def st(self, out, in_, reads, writes, q="pool", is_out=False):
        self.P.dma(lambda e: e.dma_start(out=out, in_=in_), reads, writes, q=q, is_out=is_out)

    def declare(self):
        din = self.din
        din("x", [S, D])
        din("positions", [S], I32)
        din("ln0_g", [D]); din("ln0_b", [D])
        din("w_in", [2, D, 5824]); din("b_gate", [2, 3072])
        din("pool_w", [2, 4, 128, 128]); din("pool_scale", [2, 512]); din("w_up_a", [2, 512, D])
        din("mla_q_norm", [2, 384]); din("mla_w_uq", [2, 384, 768]); din("mla_kv_norm", [2, 256])
        din("mla_w_ukv", [2, 256, 1024]); din("w_up_b", [2, 512, D])
        din("gla_w_dec", [2, 2, 16, 256]); din("gla_b_dec", [2, 2, 256]); din("gla_norm", [2, 128])
        din("w_up_c", [2, 512, D]); din("w_out", [2, D, D])
        din("ln1_g", [2, D]); din("ln1_b", [2, D]); din("router_w", [2, D, 16])
        if not self.no_big:
            din("exp_w_gate", [2, 16, D, 2048]); din("exp_w_up", [2, 16, D, 2048]); din("exp_w_down", [2, 16, 2048, D])
        din("ln2_g", [2, D]); din("ln2_b", [2, D])
        din("c_ident", [128, 128])
        din("c_freq", [16])
        din("c_tri", [4, 64, 64])
        self.out = self.P.dram("out", [S, D], F32, kind="ExternalOutput")
        sc = self.scratch
        sc("hbuf", [S, D], F32)
        sc("cosF", [32, S], F32); sc("sinF", [32, S], F32)
        sc("uT", [512, S], F32)
        sc("gqT", [256, S], F32); sc("gkT", [256, S], F32); sc("gdT", [32, S], F32)
        sc("gk_tok", [S, 256], F32); sc("gv", [S, 512], BF16); sc("gr", [S, 512], BF16)
        sc("QT", [8, 96, S], BF16); sc("KT", [8, 64, S], BF16); sc("KRT", [32, S], BF16); sc("V", [S, 512], BF16)
        sc("pmT", [512, S], BF16); sc("mlT", [512, S], BF16); sc("glT", [512, S], BF16)
        sc("h1", [S, D], F32); sc("h1b", [S, D], BF16); sc("acc", [S, D], F32)

    def consts(self):
        A = self.A
        self.ident_f = A.alloc([128], F32)
        self.ident_b = A.alloc([128], BF16)
        self.eps_ln = A.alloc([1], F32)
        self.eps_rms = A.alloc([1], F32)
        self.cosT = A.alloc([NT, 16], F32)
        self.sinT = A.alloc([NT, 16], F32)
        self.ld(self.ident_f, self.inp["c_ident"][:, :], (), ["ident_f"])
        self.cp("dve", self.ident_b, self.ident_f, ["ident_f"], ["ident_b"])
        self.memset("pool", self.eps_ln, LN_EPS, ["eps_ln"])
        self.memset("pool", self.eps_rms, RMS_EPS, ["eps_rms"])
        self.IDX = A.alloc([S // 1024, 16], I32)
        self.GATE = A.alloc([S // 1024, 16], F32)
        self.const_top = A.off

    def range_reduce_sin(self, ang, tmp, tmpi, res, shift, k_ang, k_tmp, k_res):
        self.ts("dve", tmp, ang, 1.0 / TWO_PI, None, ALU.mult, None, [k_ang], [k_tmp])
        self.cp("dve", tmpi, tmp, [k_tmp], [k_tmp + "i"])
        self.cp("dve", tmp, tmpi, [k_tmp + "i"], [k_tmp])
        self.stt("dve", res, tmp, -CW1, ang, ALU.mult, ALU.add, [k_tmp, k_ang], [k_res])
        self.stt("dve", res, tmp, -CW2, res, ALU.mult, ALU.add, [k_tmp, k_res], [k_res])
        if shift != 0.0:
            self.ts("dve", res, res, shift, None, ALU.add, None, [k_res], [k_res])
        self.ts("dve", tmp, res, math.pi, None, ALU.is_gt, None, [k_res], [k_tmp])
        self.stt("dve", res, tmp, -TWO_PI, res, ALU.mult, ALU.add, [k_tmp, k_res], [k_res])
        self.ts("dve", res, res, math.pi, -math.pi, ALU.min, ALU.max, [k_res], [k_res])
        self.act(res, res, AF.Sin, [k_res], [k_res])

    def phase0(self):
        A, P = self.A, self.P
        A.reset(self.const_top)
        posi = A.alloc([NT], I32)
        posf = A.alloc([NT], F32)
        freq = A.alloc([16], F32)
        ang = A.alloc([NT, 16], F32)
        tmp = A.alloc([NT, 16], F32)
        tmpi = A.alloc([NT, 16], I32)
        pos = self.inp["positions"]
        pv = pos.rearrange("(j p) -> p j", p=128)
        for a in range(0, NT, 8):
            self.ld(posi[:, a:a + 8], pv[:, a:a + 8], (), ["posi"], slow=True)
        self.ld(freq, self.inp["c_freq"][None, :].to_broadcast([128, 16]), (), ["freq"])
        self.cp("dve", posf, posi, ["posi"], ["posf"])
        for j in range(NT):
            self.ts("dve", ang[:, j, :], freq, posf[:, j:j + 1], None, ALU.mult, None, ["posf", "freq"], ["angT"])
        self.range_reduce_sin(ang, tmp, tmpi, self.sinT, 0.0, "angT", "tmpT", "sinT")
        self.range_reduce_sin(ang, tmp, tmpi, self.cosT, math.pi / 2, "angT", "tmpT", "cosT")
        CH = min(2048, S)
        prow_i = A.alloc([CH], I32)
        prow = A.alloc([CH], F32)
        fcol = A.alloc([1], F32)
        scol_c = A.alloc([1], F32)
        scol_s = A.alloc([1], F32)
        angF = A.alloc([CH], F32)
        tmpF = A.alloc([CH], F32)
        tmpFi = A.alloc([CH], I32)
        resF = A.alloc([CH], F32)
        fr = self.inp["c_freq"]
        self.ld(fcol[64:80, :], fr.rearrange("(a b) -> a b", b=1), (), ["fcol"], slow=True)
        self.ld(fcol[80:96, :], fr.rearrange("(a b) -> a b", b=1), ["fcol"], ["fcol"], slow=True)
        self.memset("pool", scol_c[64:96, :], QSCALE, ["scol_c"])
        self.memset("pool", scol_s[64:96, :], QSCALE, ["scol_s"])
        self.memset("pool", scol_s[64:80, :], -QSCALE, ["scol_s"])
        sl = slice(64, 96)
        for c in range(S // CH):
            self.ld(prow_i[sl, :], pos[None, c * CH:(c + 1) * CH].to_broadcast([32, CH]), (), ["prow_i"])
            self.cp("dve", prow[sl, :], prow_i[sl, :], ["prow_i"], ["prow"])
            self.ts("dve", angF[sl, :], prow[sl, :], fcol[sl, 0:1], None, ALU.mult, None, ["prow", "fcol"], ["angF"])
            for (shift, scol, dst) in ((0.0, scol_s, "sinF"), (math.pi / 2, scol_c, "cosF")):
                self.range_reduce_sin(angF[sl, :], tmpF[sl, :], tmpFi[sl, :], resF[sl, :], shift, "angF", "tmpF", "resF")
                self.ts("dve", resF[sl, :], resF[sl, :], scol[sl, 0:1], None, ALU.mult, None, ["resF", "scol_s", "scol_c"], ["resF"])
                self.st(self.scr[dst][:, c * CH:(c + 1) * CH], resF[sl, :], ["resF"], [dst])
        P.barrier()

    def layernorm_tile(self, x, out_f, out_b, gam, bet, st6, mv, rstd, nb, key, reads, writes_f, writes_b, cast_eng="pool"):
        for c in range(2):
            self.P.op("dve", lambda e, c=c: e.bn_stats(out=st6[:, c, :], in_=x[:, c * 512:(c + 1) * 512]), reads, [(key, "st", c)])
        self.P.op("dve", lambda e: e.bn_aggr(out=mv, in_=st6.rearrange("p a b -> p (a b)")), [(key, "st", 0), (key, "st", 1)], [(key, "mv")])
        self.act(rstd, mv[:, 1:2], AF.Ln, [(key, "mv"), "eps_ln"], [(key, "rstd")], bias=self.eps_ln[:, 0:1], scale=1.0)
        self.act(rstd, rstd, AF.Exp, [(key, "rstd")], [(key, "rstd")], scale=-0.5)
        self.stt("dve", nb, mv[:, 0:1], -1.0, rstd, ALU.mult, ALU.mult, [(key, "mv"), (key, "rstd")], [(key, "nb")])
        tmpk = (key, "xn")
        self.act(self.ln_tmp, x, AF.Identity, list(reads) + [(key, "rstd"), (key, "nb")], [tmpk], bias=nb[:, 0:1], scale=rstd[:, 0:1])
        self.tt("dve", self.ln_tmp, self.ln_tmp, gam, ALU.mult, [tmpk, "lnpar"], [tmpk])
        if out_f is not None:
            self.tt("dve", out_f, self.ln_tmp, bet, ALU.add, [tmpk, "lnpar"], writes_f)
            if out_b is not None:
                self.cp(cast_eng, out_b, out_f, writes_f, writes_b)
        else:
            self.tt("dve", out_b, self.ln_tmp, bet, ALU.add, [tmpk, "lnpar"], writes_b)


    def load_cast(self, dst, src, n_inner, key, reads=()):
        shp = list(dst.shape)
        if len(shp) == 3:
            C = shp[1]
            cstep = max(1, 1024 // shp[0])
            nsp = (n_inner + 2047) // 2048
            step = (n_inner + nsp - 1) // nsp
            for c0 in range(0, C, cstep):
                c1 = min(C, c0 + cstep)
                for a in range(0, n_inner, step):
                    b = min(n_inner, a + step)
                    self.P.dma(lambda e, a=a, b=b, c0=c0, c1=c1: e.dma_start(out=dst[:, c0:c1, a:b], in_=src[:, c0:c1, a:b]), reads, [key], q="pool")
        else:
            self.P.dma(lambda e: e.dma_start(out=dst, in_=src), reads, [key], q="pool")

    def phase1(self, l):
        A, P, I, SC = self.A, self.P, self.inp, self.scr
        A.reset(self.const_top)
        bk = self.banks
        win = A.alloc([8, 2752], BF16)
        wuq = A.alloc([3, 768], BF16)
        wuqr = A.alloc([3, 768], BF16)
        wukv = A.alloc([2, 1024], BF16)
        qng = A.alloc([3], F32)
        kvng = A.alloc([2], F32)
        gam = A.alloc([D], F32)
        bet = A.alloc([D], F32)
        self.ln_tmp = A.alloc([D], F32)
        xt = [A.alloc([4, D], F32) for _ in range(2)]
        hb = A.alloc([4, D], BF16)
        hT = [A.alloc([8, 512], BF16) for _ in range(2)]
        st6 = A.alloc([2, 6], F32); mv = A.alloc([2], F32); rstd = A.alloc([1], F32); nb = A.alloc([1], F32)
        ss = A.alloc([4], F32)
        cqn = A.alloc([384], BF16)
        ckvn = A.alloc([256], BF16)
        cqnT = A.alloc([3, 512], BF16)
        ckvnT = A.alloc([2, 512], BF16)
        junk = A.alloc([384], F32)
        junk2 = A.alloc([256], F32)
        st_u = A.alloc([4, 512], F32)
        st_q = A.alloc([2, 512], F32)
        st_k = A.alloc([2, 512], F32)
        st_d = A.alloc([512], F32)
        st_kt = A.alloc([4, 256], F32)
        st_v = A.alloc([4, 512], BF16)
        st_r = A.alloc([4, 512], BF16)
        st_V = A.alloc([4, 512], BF16)
        st_Q = A.alloc([8, 512], BF16)
        st_K = A.alloc([8, 512], BF16)
        st_kr = A.alloc([512], BF16)
        krt = A.alloc([32], F32)
        krb = A.alloc([128], BF16)
        rt1 = A.alloc([16], F32); rt2 = A.alloc([16], F32)
        cosF = A.alloc([512], F32); sinF = A.alloc([512], F32)
        qt1 = A.alloc([512], F32); qt2 = A.alloc([512], F32)

        self.memset("pool", krb, 0.0, ["krb"])
        w_in = I["w_in"][l]
        self.load_cast(win, w_in[:, 0:2752].rearrange("(c p) n -> p c n", p=128), 2752, "win")
        self.load_cast(wuq, I["mla_w_uq"][l].rearrange("(c p) n -> p c n", p=128), 768, "wuq")
        wq4 = I["mla_w_uq"][l].rearrange("(c p) (h x) -> p c h x", p=128, x=96)
        wuqr4 = wuqr.rearrange("p c (h x) -> p c h x", x=96)
        for c in range(3):
            self.load_cast(wuqr4[:, c, :, 64:80], wq4[:, c, :, 80:96], 16, "wuqr")
            self.load_cast(wuqr4[:, c, :, 80:96], wq4[:, c, :, 64:80], 16, "wuqr")
        self.memset("pool", wuqr4[:, :, :, 0:64], 0.0, ["wuqr"])
        self.load_cast(wukv, I["mla_w_ukv"][l].rearrange("(c p) n -> p c n", p=128), 1024, "wukv")
        self.ld(qng, I["mla_q_norm"][l].rearrange("(c p) -> p c", p=128), (), ["qng"], slow=True)
        self.ld(kvng, I["mla_kv_norm"][l].rearrange("(c p) -> p c", p=128), (), ["kvng"], slow=True)
        if l == 0:
            self.ld(gam, I["ln0_g"][None, :].to_broadcast([128, D]), (), ["lnpar"])
            self.ld(bet, I["ln0_b"][None, :].to_broadcast([128, D]), (), ["lnpar"])
        src = I["x"] if l == 0 else SC["hbuf"]
        pT = bk[0][:, 0:512].bitcast(BF16).rearrange("p (a b) -> p a b", a=8)
        fm_blocks = [("u", g, C_UP + g * 128, 128) for g in range(4)] + [("q", m, C_GQ + m * 128, 128) for m in range(2)] \
            + [("k", m, C_GK + m * 128, 128) for m in range(2)] + [("d", 0, C_GD, 32)]
        nfm = 0
        ntm = 0
        nsup = self.ns_limit or NS

        def head_load(s):
            sl = s % 2
            t0 = s * 512
            self.ld(xt[sl], src[t0:t0 + 512, :].rearrange("(j p) d -> p j d", p=128), (), [("xt", sl)])

        def head_tile(s, j):
            sl = s % 2
            t0 = s * 512
            xk = ("xt", sl)
            hbk = ("hb", j)
            if l == 0:
                self.layernorm_tile(xt[sl][:, j, :], xt[sl][:, j, :], hb[:, j, :], gam, bet, st6, mv, rstd, nb, "ln", [xk], [xk], [hbk], cast_eng="act")
            else:
                self.cp("act", hb[:, j, :], xt[sl][:, j, :], [xk], [hbk])
            for c in range(8):
                self.tr(pT[:, c, :], hb[:, j, c * 128:(c + 1) * 128], self.ident_b, [hbk, "ident_b"], ["pT"])
            self.cp("dve", hT[sl][:, :, j * 128:(j + 1) * 128], pT, [], [("hT", sl), "pT"])
            if l == 0 and j == 3:
                self.st(SC["hbuf"][t0:t0 + 512, :].rearrange("(j p) d -> p j d", p=128), xt[sl], [xk], ["hbuf"])

        head_load(0)
        for j in range(4):
            head_tile(0, j)
        for s in range(nsup):
            sl = s % 2
            t0 = s * 512
            xk = ("xt", sl)
            if s + 1 < nsup:
                head_load(s + 1)
            self.ld(cosF[64:96, :], SC["cosF"][:, t0:t0 + 512], (), ["cosFs"])
            self.ld(sinF[64:96, :], SC["sinF"][:, t0:t0 + 512], (), ["sinFs"])
            hTk = ("hT", sl)
            for (kind, idx, c0, w) in (fm_blocks if 'fm' not in self.skip else []):
                pb = bk[1 + nfm % 2]
                pk = ("fm", nfm % 2)
                nfm += 1
                for c in range(8):
                    self.mm(pb[0:w, :], win[:, c, c0:c0 + w], hT[sl][:, c, :], c == 0, c == 7, ["win", hTk], [pk])
                if kind == "u":
                    self.cp("act", st_u[:, idx, :], pb[:, :], [], ["st_u", pk])
                elif kind == "q":
                    self.act(st_q[:, idx, :], pb[:, :], AF.Identity, [], ["st_q", pk], scale=0.125)
                elif kind == "k":
                    self.cp("dve", st_k[:, idx, :], pb[:, :], [], ["st_k", pk])
                else:
                    self.cp("dve", st_d[0:32, :], pb[0:32, :], [], ["st_d", pk])
            if 'fm' not in self.skip:
                self.st(SC["uT"][:, t0:t0 + 512].rearrange("(g p) t -> p g t", p=128), st_u, ["st_u"], ["uT"])
                self.st(SC["gqT"][:, t0:t0 + 512].rearrange("(g p) t -> p g t", p=128), st_q, ["st_q"], ["gqT"])
                self.st(SC["gkT"][:, t0:t0 + 512].rearrange("(g p) t -> p g t", p=128), st_k, ["st_k"], ["gkT"])
                self.st(SC["gdT"][:, t0:t0 + 512], st_d[0:32, :], ["st_d"], ["gdT"])
            for j in range(4):
                tt = s * 4 + j
                lh = lambda c: hT[sl][:, c, j * 128:(j + 1) * 128]

                def tmb(pb, pk, c0, w):
                    for c in range(8):
                        self.mm(pb[:, 0:w], lh(c), win[:, c, c0:c0 + w], c == 0, c == 7, ["win", hTk], [pk])

                def tm(c0, w):
                    nonlocal ntm
                    pb = bk[3 + ntm % 2]
                    pk = ("tm", ntm % 2)
                    ntm += 1
                    tmb(pb, pk, c0, w)
                    return pb, pk
                pbq, pkq = bk[6], "pqa"
                pbk, pkk_ = bk[7], "pqb"
                tmb(pbq, pkq, C_CQ, 384)
                tmb(pbk, pkk_, C_CKV, 288)
                self.act(junk[:, 0:384], pbq[:, 0:384], AF.Square, [], ["junk", "ss0", pkq], accum_out=ss[:, 0:1])
                self.act(ss[:, 1:2], ss[:, 0:1], AF.Ln, ["ss0", "eps_rms"], ["ss1"], bias=self.eps_rms[:, 0:1], scale=1.0 / 384)
                self.act(ss[:, 1:2], ss[:, 1:2], AF.Exp, ["ss1"], ["ss1"], scale=-0.5)
                self.ts("dve", cqn, pbq[:, 0:384], ss[:, 1:2], None, ALU.mult, None, ["ss1"], ["cqn", pkq])
                self.act(junk2[:, 0:256], pbk[:, 0:256], AF.Square, [], ["junk2", "ss2", pkk_], accum_out=ss[:, 2:3])
                self.act(ss[:, 3:4], ss[:, 2:3], AF.Ln, ["ss2", "eps_rms"], ["ss3"], bias=self.eps_rms[:, 0:1], scale=1.0 / 256)
                self.act(ss[:, 3:4], ss[:, 3:4], AF.Exp, ["ss3"], ["ss3"], scale=-0.5)
                self.ts("dve", ckvn, pbk[:, 0:256], ss[:, 3:4], None, ALU.mult, None, ["ss3"], ["ckvn", pkk_])
                cs = self.cosT[:, tt, :]
                sn = self.sinT[:, tt, :]
                self.cp("dve", krt, pbk[:, 256:288], [], ["krt", pkk_])
                self.tt("dve", rt1, krt[:, 0:16], cs, ALU.mult, ["krt", "cosT"], ["rt1"])
                self.tt("dve", rt2, krt[:, 16:32], sn, ALU.mult, ["krt", "sinT"], ["rt2"])
                self.tt("dve", krb[:, 0:16], rt1, rt2, ALU.subtract, ["rt1", "rt2"], ["krb"])
                self.tt("dve", rt1, krt[:, 0:16], sn, ALU.mult, ["krt", "sinT"], ["rt1"])
                self.tt("dve", rt2, krt[:, 16:32], cs, ALU.mult, ["krt", "cosT"], ["rt2"])
                self.tt("dve", krb[:, 16:32], rt1, rt2, ALU.add, ["rt1", "rt2"], ["krb"])
                pb, pk = tm(C_GK, 256)
                self.cp("act", st_kt[:, j, :], pb[:, 0:256], [], ["st_kt", pk])
                pb, pk = tm(C_GV, 512)
                self.cp("dve", st_v[:, j, :], pb[:, :], [], ["st_v", pk])
                pb, pk = tm(C_GR, 512)
                self.act(st_r[:, j, :], pb[:, :], AF.Silu, [], ["st_r", pk])
                pq = bk[5][:, 0:192].bitcast(BF16).rearrange("p (a b) -> p a b", a=3)
                for c in range(3):
                    self.tr(pq[:, c, :], cqn[:, c * 128:(c + 1) * 128], self.ident_b, ["cqn", "ident_b"], ["b5"])
                self.tt("dve", cqnT[:, :, j * 128:(j + 1) * 128], pq, qng[:, :, None].to_broadcast([128, 3, 128]), ALU.mult, ["qng"], ["cqnT", "b5"])
                pkv = bk[5][:, 256:448].bitcast(BF16).rearrange("p (a b) -> p a b", a=3)
                for c in range(2):
                    self.tr(pkv[:, c, :], ckvn[:, c * 128:(c + 1) * 128], self.ident_b, ["ckvn", "ident_b"], ["b5"])
                self.tr(pkv[:, 2, :], krb[:, :], self.ident_b, ["krb", "ident_b"], ["b5"])
                self.tt("dve", ckvnT[:, :, j * 128:(j + 1) * 128], pkv[:, 0:2, :], kvng[:, :, None].to_broadcast([128, 2, 128]), ALU.mult, ["kvng"], ["ckvnT", "b5"])
                self.cp("act", st_kr[0:32, j * 128:(j + 1) * 128], pkv[0:32, 2, :], [], ["st_kr", "b5"])
                if s + 1 < nsup:
                    head_tile(s + 1, j)
            rows = lambda name: SC[name][t0:t0 + 512, :].rearrange("(j p) d -> p j d", p=128)
            self.st(rows("gk_tok"), st_kt, ["st_kt"], ["gk_tok"])
            self.st(rows("gv"), st_v, ["st_v"], ["gv"])
            self.st(rows("gr"), st_r, ["st_r"], ["gr"])
            self.st(SC["KRT"][:, t0:t0 + 512], st_kr[0:32, :], ["st_kr"], ["KRT"])
            for h in (range(8) if 'mlaup' not in self.skip else []):
                pqa, ka = (bk[6], "pqa") if h % 2 == 0 else (bk[3], ("tm", 0))
                pqb, kb = (bk[7], "pqb") if h % 2 == 0 else (bk[4], ("tm", 1))
                for c in range(3):
                    self.mm(pqa[0:96, :], wuq[:, c, h * 96:(h + 1) * 96], cqnT[:, c, :], c == 0, c == 2, ["wuq", "cqnT"], [ka])
                for c in range(3):
                    self.mm(pqb[0:96, :], wuqr[:, c, h * 96:(h + 1) * 96], cqnT[:, c, :], c == 0, c == 2, ["wuqr", "cqnT"], [kb])
                self.act(st_Q[0:64, h, :], pqa[0:64, :], AF.Identity, [], ["st_Q", ka], scale=QSCALE)
                self.tt("dve", qt1[64:96, :], pqa[64:96, :], cosF[64:96, :], ALU.mult, ["cosFs"], ["qt1", ka])
                self.tt("dve", qt2[64:96, :], pqb[64:96, :], sinF[64:96, :], ALU.mult, ["sinFs"], ["qt2", kb])
                self.tt("pool", st_Q[64:96, h, :], qt1[64:96, :], qt2[64:96, :], ALU.add, ["qt1", "qt2"], ["st_Q"])
                pka = bk[1 + h % 2]
                pkk = ("fm", h % 2)
                for c in range(2):
                    self.mm(pka[0:64, :], wukv[:, c, h * 128:h * 128 + 64], ckvnT[:, c, :], c == 0, c == 1, ["wukv", "ckvnT"], [pkk])
                self.cp("act", st_K[0:64, h, :], pka[0:64, :], [], ["st_K", pkk])
            self.st(SC["QT"][:, :, t0:t0 + 512].rearrange("h r t -> r h t"), st_Q[0:96, :, :], ["st_Q"], ["QT"])
            self.st(SC["KT"][:, :, t0:t0 + 512].rearrange("h r t -> r h t"), st_K[0:64, :, :], ["st_K"], ["KT"])
            wv = wukv.rearrange("p c (h x) -> p c h x", x=128)
            for j in (range(4) if 'vproj' not in self.skip else []):
                pb = bk[3 + ntm % 2]
                pk = ("tm", ntm % 2)
                ntm += 1
                for c in range(2):
                    self.mm(pb[:, :].rearrange("p (h x) -> p h x", x=64), ckvnT[:, c, j * 128:(j + 1) * 128], wv[:, c, :, 64:128], c == 0, c == 1, ["wukv", "ckvnT"], [pk])
                self.cp("dve", st_V[:, j, :], pb[:, :], [], ["st_V", pk])
            self.st(rows("V"), st_V, ["st_V"], ["V"])
        P.barrier()

    def phase2(self, l):
        A, P, SC = self.A, self.P, self.scr
        A.reset(self.const_top)
        bk = self.banks
        KTs = [A.alloc([S], BF16) for _ in range(2)]
        Vh = [A.alloc([NT, 65], BF16) for _ in range(2)]
        Qs = [A.alloc([512], BF16) for _ in range(2)]
        pts = [A.alloc([512], BF16) for _ in range(4)]
        ones = A.alloc([64], F32)
        rcp = A.alloc([512], F32)
        bc = A.alloc([512], F32)
        ost = [A.alloc([512], BF16) for _ in range(2)]
        self.memset("pool", ones, 1.0, ["ones"])
        for i in range(2):
            self.memset("pool", Vh[i][:, :, 64:65], 1.0, [("Vh", i)])
        nheads = self.ns_limit or 8
        nqg = self.ns_limit or NS
        LOOK = 2
        items = [(h, qg) for h in range(nheads) for qg in range(nqg)]
        loaded_heads = set()

        def load_head(h):
            if h in loaded_heads or h >= nheads:
                return
            loaded_heads.add(h)
            hs = h % 2
            self.ld(KTs[hs][0:64, :], SC["KT"][h], ["KT"], [("KTs", hs)])
            self.ld(KTs[hs][64:96, :], SC["KRT"][:, :], ["KRT"], [("KTs", hs)])
            vsrc = SC["V"][:, h * 64:(h + 1) * 64].rearrange("(kt p) v -> p kt v", p=128)
            for k0 in range(0, NT, 8):
                self.ld(Vh[hs][:, k0:k0 + 8, 0:64], vsrc[:, k0:k0 + 8, :], ["V"], [("Vh", hs)])

        def load_q(ii):
            if ii >= len(items):
                return
            h, qg = items[ii]
            qs = ii % 2
            self.ld(Qs[qs][0:96, :], SC["QT"][h, :, qg * 512:qg * 512 + 512], ["QT"], [("Qs", qs)])

        def qk(ii, kt):
            h, qg = items[ii]
            hs, qs = h % 2, ii % 2
            g = (ii * NT + kt) % 4
            self.mm(bk[g][:, :], KTs[hs][0:96, kt * 128:(kt + 1) * 128], Qs[qs][0:96, :], True, True, [("KTs", hs), ("Qs", qs)], [("B", g)])

        def epilogue(ii):
            h, qg = items[ii]
            qs = ii % 2
            q0 = qg * 512
            ob = bk[4 + qs]
            ok = ("B", 4 + qs)
            self.P.op("dve", lambda e, ob=ob: e.reciprocal(out=rcp[64:65, :], in_=ob[64:65, :]), [], ["rcp", ok])
            self.mm(bk[6][0:64, :], ones[64:65, :], rcp[64:65, :], True, True, ["ones", "rcp"], [("B", 6)])
            self.cp("act", bc[0:64, :], bk[6][0:64, :], [], ["bc", ("B", 6)])
            self.tt("dve", ost[qs][0:64, :], ob[0:64, :], bc[0:64, :], ALU.mult, ["bc"], [("ost", qs), ok])
            self.st(SC["mlT"][h * 64:(h + 1) * 64, q0:q0 + 512], ost[qs][0:64, :], [("ost", qs)], ["mlT"])

        load_head(0)
        load_q(0)
        for kt in range(LOOK):
            qk(0, kt)
        for ii, (h, qg) in enumerate(items):
            hs, qs = h % 2, ii % 2
            if qg == 0:
                load_head(h + 1)
            load_q(ii + 1)
            ob = bk[4 + qs]
            ok = ("B", 4 + qs)
            for kt in range(NT):
                nk = kt + LOOK
                if nk < NT:
                    qk(ii, nk)
                elif ii + 1 < len(items):
                    qk(ii + 1, nk - NT)
                g = (ii * NT + kt) % 4
                pt = pts[g]
                pk = ("pt", g)
                self.act(pt, bk[g][:, :], AF.Exp, [], [pk, ("B", g)])
                self.mm(ob[0:65, :], Vh[hs][:, kt, :], pt, kt == 0, kt == NT - 1, [("Vh", hs), pk], [ok])
                if kt == 3 and ii > 0:
                    epilogue(ii - 1)
            if ii == len(items) - 1:
                epilogue(ii)
        P.barrier()

    def phase4(self, l):
        A, P, I, SC = self.A, self.P, self.inp, self.scr
        A.reset(self.const_top)
        bk = self.banks
        wp = A.alloc([4, 128], BF16)
        psc = A.alloc([4], F32)
        inv_first = A.alloc([4, 512], F32)
        inv_last = A.alloc([4, 512], F32)
        U = [A.alloc([528], F32) for _ in range(2)]
        sA = A.alloc([528], F32)
        sB = A.alloc([528], F32)
        pl = A.alloc([512], BF16)
        pst = [A.alloc([512], BF16) for _ in range(2)]
        self.load_cast(wp, I["pool_w"][l].rearrange("g c d -> c g d"), 128, "wp")
        self.ld(psc, I["pool_scale"][l].rearrange("(g p) -> p g", p=128), (), ["psc"], slow=True)
        for g, w in enumerate(POOL_WINDOWS):
            self.memset("pool", inv_first[:, g, :], 1.0 / w, ["inv_first"])
            self.memset("pool", inv_last[:, g, :], 1.0 / w, ["inv_last"])
            for t in range(w // 2):
                self.memset("pool", inv_first[:, g, t:t + 1], 1.0 / (t + w // 2), ["inv_first"])
            for j in range(1, w // 2):
                self.memset("pool", inv_last[:, g, 512 - j:512 - j + 1], 1.0 / (j + w // 2), ["inv_last"])
        nch = self.ns_limit or NS
        it = 0
        for g, w in enumerate(POOL_WINDOWS):
            for ch in range(nch):
                us = it % 2
                it += 1
                t0 = ch * 512
                uk = ("U", us)
                lo = max(t0 - 8, 0)
                hi = min(t0 + 520, S)
                if ch == 0:
                    self.memset("pool", U[us][:, 0:8], 0.0, [uk])
                if ch == NS - 1:
                    self.memset("pool", U[us][:, 520:528], 0.0, [uk])
                self.ld(U[us][:, lo - (t0 - 8):hi - (t0 - 8)], SC["uT"][g * 128:(g + 1) * 128, lo:hi], ["uT"], [uk])
                cur = U[us]
                curk = uk
                k = 1
                dst = [sA, sB]
                di = 0
                while k < w:
                    d_ = dst[di]
                    dk = ("s", di)
                    self.tt("dve", d_[:, k:528], cur[:, k:528], cur[:, 0:528 - k], ALU.add, [curk], [dk])
                    cur, curk = d_, dk
                    di ^= 1
                    k *= 2
                off = 8 + w // 2 - 1
                sw = cur[:, off:off + 512]
                if ch == 0 or ch == NS - 1:
                    inv = inv_first if ch == 0 else inv_last
                    other, okey = (sB, ("s", 1)) if cur is sA else (sA, ("s", 0))
                    self.tt("dve", other[:, 0:512], sw, inv[:, g, :], ALU.mult, [curk, "inv_first", "inv_last"], [okey])
                    self.tt("dve", pl, other[:, 0:512], U[us][:, 8:520], ALU.subtract, [okey, uk], ["pl"])
                else:
                    self.stt("dve", pl, sw, 1.0 / w, U[us][:, 8:520], ALU.mult, ALU.subtract, [curk, uk], ["pl"])
                pb = bk[it % 2]
                pk = ("B", it % 2)
                self.mm(pb[:, :], wp[:, g, :], pl, True, True, ["wp", "pl"], [pk])
                self.ts("dve", pst[us], pb[:, :], psc[:, g:g + 1], None, ALU.mult, None, ["psc"], [("pst", us), pk])
                self.st(SC["pmT"][g * 128:(g + 1) * 128, t0:t0 + 512], pst[us], [("pst", us)], ["pmT"])
        P.barrier()

    def phase3(self, l):
        A, P, I, SC = self.A, self.P, self.inp, self.scr
        A.reset(self.const_top)
        bk = self.banks
        if "of" not in SC:
            self.scratch("of", [S, 512], F32)
        tri = A.alloc([4, 64], F32)
        one_c = A.alloc([1], F32)
        wdec = A.alloc([2, 256], F32)
        gn = A.alloc([128], F32)
        dl = [A.alloc([128], F32) for _ in range(2)]
        gq = [A.alloc([4, 128], F32) for _ in range(2)]
        gk = [A.alloc([4, 128], F32) for _ in range(2)]
        gkt = [A.alloc([2, 256], F32) for _ in range(2)]
        vv = [A.alloc([2, 512], BF16) for _ in range(2)]
        of = [A.alloc([2, 512], F32) for _ in range(2)]
        grt = [A.alloc([2, 512], BF16) for _ in range(2)]
        esb = A.alloc([512], F32)
        gp = A.alloc([2, 256], F32)
        E1s = [A.alloc([4, 2, 64], F32) for _ in range(2)]
        E2 = A.alloc([4, 2, 64], F32)
        E3 = A.alloc([2, 256], F32)
        qins = [A.alloc([4, 128], BF16) for _ in range(2)]
        kins = [A.alloc([4, 128], BF16) for _ in range(2)]
        ksts = [A.alloc([2, 256], BF16) for _ in range(2)]
        ATm = A.alloc([4, 64], BF16)
        St = A.alloc([4, 128], F32)
        Sb = A.alloc([4, 128], BF16)
        ost = [A.alloc([2, 512], F32) for _ in range(2)]
        sq = A.alloc([1024], F32)
        ssq = A.alloc([8], F32)
        rs8 = A.alloc([8], F32)
        ob16 = A.alloc([2, 512], BF16)
        gst = [A.alloc([4, 128], BF16) for _ in range(2)]
        self.ld(tri[0:64, :, :], I["c_tri"].rearrange("k a b -> a k b"), (), ["tri"])
        self.memset("pool", one_c, 1.0, ["one_c"])
        for d_ in range(2):
            self.ld(wdec[0:16, d_, :], I["gla_w_dec"][l, d_], (), ["wdec"])
            self.ld(wdec[16:17, d_, :], I["gla_b_dec"][l, d_:d_ + 1, :], (), ["wdec"])
            self.memset("pool", dl[d_][0:17, :], 1.0, [("dl", d_)])
        self.ld(gn[0:64, :], I["gla_norm"][l][None, :].to_broadcast([64, 128]), (), ["gn"])
        LG, BT, CC, AT, OO = bk[0], bk[1], bk[2], bk[3], bk[4]
        BTv = BT[0:64, :].rearrange("p (h n i) -> p h n i", h=4, n=2)
        NG = S // 128
        ng = self.ns_limit or NG
        for dirn in range(2):
            self.memset("pool", St[0:64], 0.0, ["St"])
            self.memset("pool", Sb[0:64], 0.0, ["Sb"])
            inc_i, rest_i = (0, 1) if dirn == 0 else (2, 3)
            groups = list(range(ng)) if dirn == 0 else list(range(NG - 1, NG - 1 - ng, -1))
            def prep(gi, g, stage):
                bs = gi % 2
                t0 = g * 128
                E1, qin, kin, kst = E1s[bs], qins[bs], kins[bs], ksts[bs]
                kE1, kq, kk, kks = ("E1", bs), ("qin", bs), ("kin", bs), ("kst", bs)
                if stage == 'a':
                    self.ld(dl[bs][0:16, :], SC["gdT"][dirn * 16:(dirn + 1) * 16, t0:t0 + 128], ["gdT"], [("dl", bs)])
                    self.ld(gq[bs][0:64], SC["gqT"][:, t0:t0 + 128].rearrange("(h d) t -> d h t", d=64), ["gqT"], [("gq", bs)])
                    self.ld(gk[bs][0:64], SC["gkT"][:, t0:t0 + 128].rearrange("(h d) t -> d h t", d=64), ["gkT"], [("gk", bs)])
                    self.ld(gkt[bs][0:64], SC["gk_tok"][t0:t0 + 128, :].rearrange("(n p) c -> p n c", p=64), ["gk_tok"], [("gkt", bs)])
                    self.ld(vv[bs][0:64], SC["gv"][t0:t0 + 128, :].rearrange("(n p) c -> p n c", p=64), ["gv"], [("vv", bs)])
                    if dirn == 1:
                        self.ld(of[bs][0:64], SC["of"][t0:t0 + 128, :].rearrange("(n p) c -> p n c", p=64), ["of"], [("of", bs)])
                        self.ld(grt[bs][0:64], SC["gr"][t0:t0 + 128, :].rearrange("(n p) c -> p n c", p=64), ["gr"], [("grt", bs)])
                    for n in range(2):
                        self.mm(LG[0:64, n * 256:(n + 1) * 256], dl[bs][0:17, n * 64:(n + 1) * 64], wdec[0:17, dirn, :], True, True, [("dl", bs), "wdec"], [("B", 0)])
                    self.act(esb[0:64, :], LG[0:64, :], AF.Exp, [], ["esb", ("B", 0)], scale=-1.0)
                    self.act(gp[0:64].rearrange("p n c -> p (n c)"), esb[0:64, :], AF.Ln, ["esb", "one_c"], ["gp"], bias=one_c[0:64, 0:1], scale=1.0)
                if stage == 'b':
                    for h in range(4):
                        for n in range(2):
                            self.mm(BTv[:, h, n, :], gp[0:64, n, h * 64:(h + 1) * 64], tri[0:64, inc_i, :], True, True, ["gp", "tri"], [("B", 1)])
                    for n in range(2):
                        self.mm(CC[0:64, n * 256:(n + 1) * 256], tri[0:64, rest_i, :], gp[0:64, n, :], True, True, ["gp", "tri"], [("B", 2)])
                    f = lambda t: t[0:64].rearrange("p a b c -> p (a b c)")
                    self.act(f(E1), BT[0:64, :], AF.Exp, [], [kE1, ("B", 1)], scale=-1.0 / 16)
                    self.act(f(E2), BT[0:64, :], AF.Exp, [], ["E2", ("B", 1)], scale=1.0 / 16)
                    self.act(E3[0:64].rearrange("p n c -> p (n c)"), CC[0:64, :], AF.Exp, [], ["E3", ("B", 2)], scale=-1.0 / 16)
                if stage == 'c':
                    self.tt("dve", qin[0:64], gq[bs][0:64], E1[0:64].rearrange("p h n i -> p h (n i)"), ALU.mult, [("gq", bs), kE1], [kq])
                    self.tt("dve", kin[0:64], gk[bs][0:64], E2[0:64].rearrange("p h n i -> p h (n i)"), ALU.mult, [("gk", bs), "E2"], [kk])
                    self.tt("dve", kst[0:64], gkt[bs][0:64], E3[0:64], ALU.mult, [("gkt", bs), "E3"], [kks])

            def chunks(gi, g, part):
                bs = gi % 2
                t0 = g * 128
                E1, qin, kin, kst = E1s[bs], qins[bs], kins[bs], ksts[bs]
                kE1, kq, kk, kks = ("E1", bs), ("qin", bs), ("kin", bs), ("kst", bs)
                order = (0, 1) if dirn == 0 else (1, 0)
                for ci, n in (enumerate(order) if part != 'tail' else []):
                    if ci != part:
                        continue
                    cs = slice(n * 64, (n + 1) * 64)
                    for h in range(4):
                        self.mm(AT[0:64, h * 64:(h + 1) * 64], kin[0:64, h, cs], qin[0:64, h, cs], True, True, [kk, kq], [("B", 3)])
                    self.tt("dve", ATm[0:64], AT[0:64, 0:256].rearrange("p (h i) -> p h i", h=4), tri[0:64, inc_i:inc_i + 1, :].to_broadcast([64, 4, 64]), ALU.mult, ["tri"], ["ATm", ("B", 3)])
                    for h in range(4):
                        hv = slice(h * 128, (h + 1) * 128)
                        self.mm(OO[0:64, hv], ATm[0:64, h, :], vv[bs][0:64, n, hv], True, False, ["ATm", ("vv", bs)], [("B", 4)])
                        self.mm(OO[0:64, hv], qin[0:64, h, cs], Sb[0:64, h, :], False, True, [kq, "Sb"], [("B", 4)])
                    kvb = bk[5 + (gi * 2 + ci) % 2]
                    kvk = ("B", 5 + (gi * 2 + ci) % 2)
                    for h in range(4):
                        hv = slice(h * 128, (h + 1) * 128)
                        self.mm(kvb[0:64, hv], kst[0:64, n, h * 64:(h + 1) * 64], vv[bs][0:64, n, hv], True, True, [kks, ("vv", bs)], [kvk])
                    dcol = 63 if dirn == 0 else 0
                    for h in range(4):
                        hv = slice(h * 128, (h + 1) * 128)
                        self.stt("dve", St[0:64, h, :], St[0:64, h, :], E1[0:64, h, n, dcol:dcol + 1], kvb[0:64, hv], ALU.mult, ALU.add, [kE1], ["St", kvk])
                    self.cp("act", Sb[0:64].rearrange("p h v -> p (h v)"), St[0:64].rearrange("p h v -> p (h v)"), ["St"], ["Sb"])
                    if dirn == 0:
                        self.cp("act", ost[bs][0:64, n, :], OO[0:64, :], [], [("ost", bs), ("B", 4)])
                    else:
                        self.tt("dve", ost[bs][0:64, n, :], OO[0:64, :], of[bs][0:64, n, :], ALU.add, [("of", bs)], [("ost", bs), ("B", 4)])
                if part == 'tail':
                    if dirn == 0:
                        self.st(SC["of"][t0:t0 + 128, :].rearrange("(n p) c -> p n c", p=64), ost[bs][0:64], [("ost", bs)], ["of"])
                    else:
                        o2 = ost[bs][0:64].rearrange("p n c -> p (n c)")
                        self.tt("dve", sq[0:64, :], o2, o2, ALU.mult, [("ost", bs)], ["sq"])
                        self.P.op("dve", lambda e: e.reduce_sum(out=ssq[0:64, :], in_=sq[0:64, :].rearrange("p (a b) -> p a b", b=128), axis=AX.X), ["sq"], ["ssq"])
                        self.act(rs8[0:64, :], ssq[0:64, :], AF.Ln, ["ssq", "eps_rms"], ["rs8"], bias=self.eps_rms[0:64, 0:1], scale=1.0 / 128)
                        self.act(rs8[0:64, :], rs8[0:64, :], AF.Exp, ["rs8"], ["rs8"], scale=-0.5)
                        o3 = ost[bs][0:64].rearrange("p n (h v) -> p (n h) v", v=128)
                        self.tt("dve", o3, o3, rs8[0:64, :, None].to_broadcast([64, 8, 128]), ALU.mult, ["rs8"], [("ost", bs)])
                        self.tt("dve", o3, o3, gn[0:64, None, :].to_broadcast([64, 8, 128]), ALU.mult, ["gn"], [("ost", bs)])
                        self.tt("dve", ob16[0:64], ost[bs][0:64], grt[bs][0:64], ALU.mult, [("ost", bs), ("grt", bs)], ["ob16"])
                        TP = bk[7][:, 0:256].bitcast(BF16).rearrange("p (b t) -> p b t", b=4)
                        for n in range(2):
                            for b_ in range(4):
                                self.tr(TP[:, b_, n * 64:(n + 1) * 64], ob16[0:64, n, b_ * 128:(b_ + 1) * 128], self.ident_b[0:64, 0:64], ["ob16", "ident_b"], [("B", 7)])
                        self.cp("act", gst[bs], TP, [], [("gst", bs), ("B", 7)])
                        self.st(SC["glT"][:, t0:t0 + 128].rearrange("(b p) t -> p b t", p=128), gst[bs], [("gst", bs)], ["glT"])

            for st_ in "abc":
                prep(0, groups[0], st_)
            for gi, g in enumerate(groups):
                nxt = gi + 1 < len(groups)
                if nxt:
                    prep(gi + 1, groups[gi + 1], "a")
                chunks(gi, g, 0)
                if nxt:
                    prep(gi + 1, groups[gi + 1], "b")
                chunks(gi, g, 1)
                if nxt:
                    prep(gi + 1, groups[gi + 1], "c")
                chunks(gi, g, "tail")
        P.barrier()

    def phase5(self, l):
        A, P, I, SC = self.A, self.P, self.inp, self.scr
        A.reset(self.const_top)
        bk = self.banks
        self.affT = A.alloc([S], F32)
        self.p6_base = A.off
        wg = A.alloc([8, 3072], BF16)
        wup = [A.alloc([4, D], BF16) for _ in range(3)]
        wout = A.alloc([8, D], BF16)
        rw = A.alloc([8, 16], F32)
        bgr = A.alloc([3072], BF16)
        ones1 = A.alloc([128], BF16)
        gam = A.alloc([D], F32)
        bet = A.alloc([D], F32)
        self.ln_tmp = A.alloc([D], F32)
        ht = [A.alloc([D], F32) for _ in range(2)]
        hb = A.alloc([D], BF16)
        hT = A.alloc([8, 128], BF16)
        gates = A.alloc([3072], F32)
        xT = [[A.alloc([4, 128], BF16)] * 2 for _ in range(3)]
        mrg = A.alloc([D], F32)
        tmpm = A.alloc([D], F32)
        mb = A.alloc([D], BF16)
        mT = A.alloc([8, 128], BF16)
        h1b = A.alloc([D], BF16)
        h1a = A.alloc([D], F32)
        h1buf = A.alloc([D], F32)
        h1T = A.alloc([8, 128], F32)
        st6 = A.alloc([2, 6], F32); mv = A.alloc([2], F32); rstd = A.alloc([1], F32); nb = A.alloc([1], F32)
        rmax = A.alloc([1], F32); rsum = A.alloc([1], F32); aff = A.alloc([16], F32)
        self.load_cast(wg, I["w_in"][l][:, C_GATE:C_GATE + 3072].rearrange("(c p) n -> p c n", p=128), 3072, "wg")
        for i, nm in enumerate(("w_up_a", "w_up_b", "w_up_c")):
            self.load_cast(wup[i], I[nm][l].rearrange("(c p) n -> p c n", p=128), D, ("wup", i))
        self.load_cast(wout, I["w_out"][l].rearrange("(c p) n -> p c n", p=128), D, "wout")
        self.ld(rw, I["router_w"][l].rearrange("(c p) e -> p c e", p=128), (), ["rw"])
        for a_ in range(0, 3072, 1536):
            self.P.dma(lambda e, a_=a_: e.dma_start(out=bgr[0:1, a_:a_ + 1536], in_=I["b_gate"][l:l + 1, a_:a_ + 1536]), (), ["bgr"], q="pool")
        self.memset("pool", ones1, 1.0, ["ones1"])
        self.ld(gam, I["ln1_g"][l][None, :].to_broadcast([128, D]), (), ["lnpar"])
        self.ld(bet, I["ln1_b"][l][None, :].to_broadcast([128, D]), (), ["lnpar"])
        srcs = ("pmT", "mlT", "glT")
        pT = bk[0][:, :].bitcast(BF16).rearrange("p (a b) -> p a b", a=8)
        nt = (self.ns_limit or NS) * 4

        def X(t):
            bs = t % 2
            r0 = t * 128
            self.ld(ht[bs], SC["hbuf"][r0:r0 + 128, :], ["hbuf"], [("ht", bs)])
            for i in range(3):
                self.ld(xT[i][bs], SC[srcs[i]][:, r0:r0 + 128].rearrange("(c p) t -> p c t", p=128), [srcs[i]], [("xT", i)])
            self.cp("act", hb, ht[bs], [("ht", bs)], ["hb"])
            for c in range(8):
                self.tr(pT[:, c, :], hb[:, c * 128:(c + 1) * 128], self.ident_b, ["hb", "ident_b"], [("B", 0)])
            self.cp("dve", hT, pT, [], ["hT", ("B", 0)])
            for gI in range(6):
                pb = bk[1 + gI % 2]
                pk = ("B", 1 + gI % 2)
                cs = slice(gI * 512, (gI + 1) * 512)
                for c in range(8):
                    self.mm(pb[:, :], hT[:, c, :], wg[:, c, cs], c == 0, False, ["hT", "wg"], [pk])
                self.mm(pb[:, :], ones1[0:1, :], bgr[0:1, cs], False, True, ["ones1", "bgr"], [pk])
                self.act(gates[:, cs], pb[:, :], AF.Sigmoid, [], [("gates", gI), pk])
            for i in range(3):
                for g2 in range(2):
                    pb = bk[3 + (i * 2 + g2) % 2]
                    pk = ("B", 3 + (i * 2 + g2) % 2)
                    cs = slice(g2 * 512, (g2 + 1) * 512)
                    for c in range(4):
                        self.mm(pb[:, :], xT[i][bs][:, c, :], wup[i][:, c, cs], c == 0, c == 3, [("xT", i), ("wup", i)], [pk])
                    gsl = gates[:, i * D + g2 * 512:i * D + (g2 + 1) * 512]
                    gk_ = ("gates", i * 2 + g2)
                    if i == 0:
                        self.tt("dve", mrg[:, cs], pb[:, :], gsl, ALU.mult, [gk_], [("mrg", g2), pk])
                    else:
                        self.tt("dve", tmpm[:, cs], pb[:, :], gsl, ALU.mult, [gk_], [("tmpm", g2), pk])
                        if i == 1:
                            self.tt("pool", mrg[:, cs], mrg[:, cs], tmpm[:, cs], ALU.add, [("tmpm", g2)], [("mrg", g2)])
                        else:
                            self.tt("pool", mb[:, cs], mrg[:, cs], tmpm[:, cs], ALU.add, [("tmpm", g2), ("mrg", g2)], [("mb", g2)])
            for c in range(8):
                self.tr(pT[:, c, :], mb[:, c * 128:(c + 1) * 128], self.ident_b, [("mb", c // 4), "ident_b"], [("B", 0)])
            self.cp("dve", mT, pT, [], ["mT", ("B", 0)])
            for g2 in range(2):
                pb = bk[5 + g2]
                pk = ("B", 5 + g2)
                cs = slice(g2 * 512, (g2 + 1) * 512)
                for c in range(8):
                    self.mm(pb[:, :], mT[:, c, :], wout[:, c, cs], c == 0, c == 7, ["mT", "wout"], [pk])
                self.stt("dve", ht[bs][:, cs], ht[bs][:, cs], DN_ALPHA, pb[:, :], ALU.mult, ALU.add, [], [("ht", bs), pk])

        def Y(t):
            bs = t % 2
            r0 = t * 128
            h1 = h1buf
            self.layernorm_tile(ht[bs], h1, h1b, gam, bet, st6, mv, rstd, nb, "ln", [("ht", bs)], ["h1"], ["h1b"])
            self.st(SC["h1"][r0:r0 + 128, :], h1, ["h1"], ["h1d"])
            self.st(SC["h1b"][r0:r0 + 128, :], h1b, ["h1b"], ["h1bd"])
            self.act(h1a, h1, AF.Identity, ["h1"], ["h1a"], scale=DN_ALPHA)
            self.st(SC["acc"][r0:r0 + 128, :], h1a, ["h1a"], ["acc"])
            pTf = [bk[7][:, :].rearrange("p (a b) -> p a b", a=4), bk[0][:, :].rearrange("p (a b) -> p a b", a=4)]
            for c in range(8):
                self.tr(pTf[c // 4][:, c % 4, :], h1[:, c * 128:(c + 1) * 128], self.ident_f, ["h1", "ident_f"], [("B", 7 if c < 4 else 0)])
            self.cp("dve", h1T[:, 0:4, :], pTf[0], [], [("h1T", 0), ("B", 7)])
            self.cp("dve", h1T[:, 4:8, :], pTf[1], [], [("h1T", 1), ("B", 0)])
            lg = bk[7]
            for c in range(8):
                self.mm(lg[:, 0:16], h1T[:, c, :], rw[:, c, :], c == 0, c == 7, [("h1T", c // 4), "rw"], [("B", 7)])
            self.P.op("dve", lambda e: e.reduce_max(out=rmax, in_=lg[:, 0:16], axis=AX.X), [], ["rmax", ("B", 7)])
            self.ts("dve", rmax, rmax, -1.0, None, ALU.mult, None, ["rmax"], ["rmax"])
            self.act(aff, lg[:, 0:16], AF.Exp, ["rmax"], ["aff", "rsum", ("B", 7)], bias=rmax[:, 0:1], scale=1.0, accum_out=rsum[:, 0:1])
            self.P.op("dve", lambda e: e.reciprocal(out=rsum, in_=rsum), ["rsum"], ["rsum"])
            self.ts("dve", aff, aff, rsum[:, 0:1], None, ALU.mult, None, ["rsum"], ["aff"])
            self.tr(bk[7][0:16, 128:256], aff, self.ident_f, ["aff", "ident_f"], [("B", 7)])
            self.cp("dve", self.affT[0:16, r0:r0 + 128], bk[7][0:16, 128:256], [], ["affT", ("B", 7)])

        X(0)
        for t in range(nt):
            if t + 1 < nt:
                X(t + 1)
            Y(t)
        P.barrier()

    def phase6(self, l):
        A, P = self.A, self.P
        A.reset(self.p6_base)
        bk = self.banks
        CAP = S // 8
        NJ = CAP // 128
        work = A.alloc([S], F32)
        vals = A.alloc([CAP], F32)
        idxu = A.alloc([CAP], U32)
        idxf = A.alloc([CAP], F32)
        af = self.affT[0:16, :]
        cur = af
        curk = "affT"
        for it in range(CAP // 8):
            v8 = vals[0:16, it * 8:(it + 1) * 8]
            self.P.op("dve", lambda e, v8=v8, cur=cur: e.max(out=v8, in_=cur), [curk], ["vals"])
            i8 = idxu[0:16, it * 8:(it + 1) * 8]
            self.P.op("dve", lambda e, v8=v8, cur=cur, i8=i8: e.max_index(out=i8, in_max=v8, in_values=cur), [curk, "vals"], ["idxu"])
            self.P.op("dve", lambda e, v8=v8, cur=cur: e.match_replace(out=work[0:16, :], in_to_replace=v8, in_values=cur, imm_value=-1.0), [curk, "vals"], ["work"])
            cur = work[0:16, :]
            curk = "work"
        self.cp("dve", idxf[0:16, :], idxu[0:16, :], ["idxu"], ["idxf"])
        for j in range(NJ):
            pj = bk[j % 2]
            pk = ("B", j % 2)
            self.tr(pj[:, 0:16], idxf[0:16, j * 128:(j + 1) * 128], self.ident_f[0:16, 0:16], ["idxf", "ident_f"], [pk])
            self.tr(pj[:, 16:32], vals[0:16, j * 128:(j + 1) * 128], self.ident_f[0:16, 0:16], ["vals", "ident_f"], [pk])
            self.cp("dve", self.IDX[:, j, :], pj[:, 0:16], [], ["IDX", pk])
            self.cp("dve", self.GATE[:, j, :], pj[:, 16:32], [], ["GATE", pk])
        P.barrier()

    def phase7(self, l):
        A, P, I, SC = self.A, self.P, self.inp, self.scr
        A.reset(self.const_top)
        bk = self.banks
        CAP = S // 8
        NJ = CAP // 128
        wgh = [A.alloc([8, 1024], BF16) for _ in range(2)]
        wuh = [A.alloc([8, 1024], BF16) for _ in range(2)]
        wdt = A.alloc([16, D], BF16)
        xg = A.alloc([NJ, D], BF16)
        xeT = A.alloc([8, CAP], BF16)
        hidT = A.alloc([16, CAP], BF16)
        sil = [A.alloc([512], F32) for _ in range(2)]
        ysb = [A.alloc([D], F32) for _ in range(2)]
        pT = bk[0][:, :].bitcast(BF16).rearrange("p (a b) -> p a b", a=8)
        nexp = self.ns_limit and min(16, self.ns_limit * 2) or 16
        NH = (CAP + 511) // 512
        HW_ = CAP // NH

        def ld_gu(ex, hf):
            fs = slice(hf * 1024, (hf + 1) * 1024)
            self.load_cast(wgh[hf], I["exp_w_gate"][l, ex][:, fs].rearrange("(c p) n -> p c n", p=128), 1024, ("wg", hf))
            self.load_cast(wuh[hf], I["exp_w_up"][l, ex][:, fs].rearrange("(c p) n -> p c n", p=128), 1024, ("wu", hf))

        def ld_d(ex):
            self.load_cast(wdt, I["exp_w_down"][l, ex].rearrange("(c p) n -> p c n", p=128), D, "wdt")

        def gathers(ex):
            for j in range(NJ):
                self.P.dma(lambda e, j=j, ex=ex: e.indirect_dma_start(
                    out=xg[:, j, :], out_offset=None, in_=SC["h1b"][:, :],
                    in_offset=bass.IndirectOffsetOnAxis(ap=self.IDX[:, j, ex:ex + 1], axis=0)),
                    ["IDX", "h1bd"], [("xg", j)], q="pool")

        ld_gu(0, 0)
        ld_gu(0, 1)
        gathers(0)
        ld_d(0)
        cnt = 0
        for ex in range(nexp):
            for j in range(NJ):
                for c in range(8):
                    self.tr(pT[:, c, :], xg[:, j, c * 128:(c + 1) * 128], self.ident_b, [("xg", j), "ident_b"], [("B", 0)])
                self.cp("dve", xeT[:, :, j * 128:(j + 1) * 128], pT, [], ["xeT", ("B", 0)])
            for fb in range(16):
                fh = fb // 8
                fs = slice((fb % 8) * 128, (fb % 8 + 1) * 128)
                for hf in range(NH):
                    ss_ = slice(hf * HW_, (hf + 1) * HW_)
                    gb, ub = bk[1 + cnt % 2], bk[3 + cnt % 2]
                    gk_, uk_ = ("B", 1 + cnt % 2), ("B", 3 + cnt % 2)
                    sl_ = sil[cnt % 2]
                    sk_ = ("sil", cnt % 2)
                    cnt += 1
                    for c in range(8):
                        self.mm(gb[:, 0:HW_], wgh[fh][:, c, fs], xeT[:, c, ss_], c == 0, c == 7, [("wg", fh), "xeT"], [gk_])
                    for c in range(8):
                        self.mm(ub[:, 0:HW_], wuh[fh][:, c, fs], xeT[:, c, ss_], c == 0, c == 7, [("wu", fh), "xeT"], [uk_])
                    self.act(sl_[:, 0:HW_], gb[:, 0:HW_], AF.Silu, [], [sk_, gk_])
                    self.tt("dve", hidT[:, fb, ss_], ub[:, 0:HW_], sl_[:, 0:HW_], ALU.mult, [sk_], ["hidT", uk_])
                if fb % 8 == 7 and ex + 1 < nexp:
                    ld_gu(ex + 1, fh)
            if ex + 1 < nexp:
                gathers(ex + 1)
            for j in range(NJ):
                yb = ysb[j % 2]
                yk = ("ysb", j % 2)
                for g2 in range(2):
                    pb = bk[5 + g2]
                    pk = ("B", 5 + g2)
                    cs = slice(g2 * 512, (g2 + 1) * 512)
                    for fb in range(16):
                        self.mm(pb[:, :], hidT[:, fb, j * 128:(j + 1) * 128], wdt[:, fb, cs], fb == 0, fb == 15, ["hidT", "wdt"], [pk])
                    self.act(yb[:, cs], pb[:, :], AF.Identity, ["GATE"], [yk, pk], scale=self.GATE[:, j, ex:ex + 1])
                self.P.dma(lambda e, j=j, ex=ex, yb=yb: e.indirect_dma_start(
                    out=SC["acc"][:, :], out_offset=bass.IndirectOffsetOnAxis(ap=self.IDX[:, j, ex:ex + 1], axis=0),
                    in_=yb, in_offset=None, compute_op=ALU.add),
                    ["IDX", yk], ["acc"], q="pool")
            if ex + 1 < nexp:
                ld_d(ex + 1)
        P.barrier()

    def phase8(self, l, final):
        A, P, I, SC = self.A, self.P, self.inp, self.scr
        A.reset(self.const_top)
        gam = A.alloc([D], F32)
        bet = A.alloc([D], F32)
        self.ln_tmp = A.alloc([D], F32)
        at = [A.alloc([4, D], F32) for _ in range(2)]
        st6 = A.alloc([2, 6], F32); mv = A.alloc([2], F32); rstd = A.alloc([1], F32); nb = A.alloc([1], F32)
        self.ld(gam, I["ln2_g"][l][None, :].to_broadcast([128, D]), (), ["lnpar"])
        self.ld(bet, I["ln2_b"][l][None, :].to_broadcast([128, D]), (), ["lnpar"])
        dst = self.out if final else SC["hbuf"]
        for s_ in range(self.ns_limit or NS):
            bs = s_ % 2
            t0 = s_ * 512
            ak = ("at", bs)
            self.ld(at[bs], SC["acc"][t0:t0 + 512, :].rearrange("(j p) d -> p j d", p=128), ["acc"], [ak])
            for j in range(4):
                self.layernorm_tile(at[bs][:, j, :], at[bs][:, j, :], None, gam, bet, st6, mv, rstd, nb, "ln", [ak], [ak], None)
            self.st(dst[t0:t0 + 512, :].rearrange("(j p) d -> p j d", p=128), at[bs], [ak], ["dst"], is_out=final)
        P.barrier()

    def build(self):
        self.declare()
        self.consts()
        self.phase0()
        for l in self.layers:
            self.phase1(l)
            self.phase2(l)
            self.phase3(l)
            self.phase4(l)
            self.phase5(l)
            self.phase6(l)
            self.phase7(l)
            self.phase8(l, final=(l == self.layers[-1]))
        return self.P.finalize()

def _consts():
    half = 16
    freq = (10000.0 ** (-np.arange(half, dtype=np.float32) / half)).astype(np.float32)
    tp = np.arange(64)[:, None]
    ii = np.arange(64)[None, :]
    tri = np.stack([tp <= ii, tp > ii, tp >= ii, tp < ii]).astype(np.float32)
    return {"c_ident": np.eye(128, dtype=np.float32), "c_freq": freq, "c_tri": tri}


_CACHE = {}


def kernel(**inputs):
    if "nc" not in _CACHE:
        k = K()
        _CACHE["nc"] = k.build()
        _CACHE["names"] = list(k.inp.keys())
    nc = _CACHE["nc"]
    cst = _consts()
    maps = []
    for b in range(2):
        m = {}
        for n in _CACHE["names"]:
            if n == "x":
                m[n] = np.ascontiguousarray(inputs["x"][b], dtype=np.float32)
            elif n == "positions":
                m[n] = np.ascontiguousarray(inputs["positions"][b], dtype=np.int32)
            elif n in cst:
                m[n] = cst[n]
            else:
                m[n] = np.ascontiguousarray(inputs[n], dtype=np.float32)
        maps.append(m)
    res = run_bass_kernel_spmd(nc, maps, core_ids=[0, 1])
    return np.stack([np.asarray(res.results[b]["out"], dtype=np.float32) for b in range(2)], axis=0)
```

```python
from contextlib import ExitStack
import math
import numpy as np
import concourse.bass as bass
import concourse.mybir as mybir
from concourse.bass_utils import run_bass_kernel_spmd

F32 = mybir.dt.float32
BF16 = mybir.dt.bfloat16
I32 = mybir.dt.int32
U32 = mybir.dt.uint32
AF = mybir.ActivationFunctionType
ALU = mybir.AluOpType
AX = mybir.AxisListType

COMPUTE = ("pe", "act", "dve", "pool")
DMAQ_K = 8


def dsize(dt):
    return {F32: 4, BF16: 2, I32: 4, U32: 4}[dt]


class Op:
    __slots__ = ("eng", "fn", "deps", "is_dma", "idx", "marked", "done", "q")

    def __init__(self, eng, fn, is_dma=False, q=None):
        self.eng = eng
        self.fn = fn
        self.deps = []
        self.is_dma = is_dma
        self.idx = None
        self.marked = False
        self.done = None
        self.q = q


class Prog:
    def __init__(self):
        self.nc = bass.Bass("TRN2", target_bir_lowering=False)
        self.stack = ExitStack()
        self.ops = {e: [] for e in ("pe", "act", "dve", "pool", "sp")}
        self.reg = {}
        self.dma_count = {"sp": 0, "pool": 0}
        self.dma_ops = {"sp": [], "pool": []}
        self.out_dmas = []
        self.same_engine_sync = True
        self.debug_out = set()

    def sb(self, name, shape, dt):
        return self.stack.enter_context(self.nc.sbuf_tensor(name, list(shape), dt))

    def ps(self, name, shape, dt):
        return self.stack.enter_context(self.nc.psum_tensor(name, list(shape), dt))

    def dram(self, name, shape, dt, kind="Internal"):
        if kind == "Internal" and name in self.debug_out:
            kind = "ExternalOutput"
        return self.nc.dram_tensor(name, list(shape), dt, kind=kind).ap()

    def _deps(self, op, reads, writes):
        for k in reads:
            st = self.reg.get(k)
            if st is None:
                st = self.reg[k] = [None, []]
            if st[0] is not None:
                op.deps.append(st[0])
            st[1].append(op)
        for k in writes:
            st = self.reg.get(k)
            if st is None:
                st = self.reg[k] = [None, []]
            if st[0] is not None:
                op.deps.append(st[0])
            for r in st[1]:
                if r is not op:
                    op.deps.append(r)
            st[0] = op
            st[1] = []

    def op(self, eng, fn, reads=(), writes=()):
        o = Op(eng, fn)
        self._deps(o, reads, writes)
        self.ops[eng].append(o)
        return o

    def dma(self, fn, reads=(), writes=(), q="sp", is_out=False):
        o = Op(q, fn, is_dma=True, q=q)
        o.idx = self.dma_count[q]
        self.dma_count[q] += 1
        self.dma_ops[q].append(o)
        self._deps(o, reads, writes)
        self.ops[q].append(o)
        if is_out:
            self.out_dmas.append(o)
        return o

    def barrier(self):
        lasts = []
        for e in COMPUTE:
            for o in reversed(self.ops[e]):
                if not o.is_dma and o.fn is not None:
                    lasts.append(o)
                    break
        for q in ("sp", "pool"):
            lasts.extend(self.dma_ops[q][-DMAQ_K:])
        for e in ("pe", "act", "dve", "pool", "sp"):
            o = Op(e, None)
            o.deps = list(lasts)
            self.ops[e].append(o)
        self.reg = {}

    def finalize(self):
        nc = self.nc
        sync_same = self.same_engine_sync

        def skip(d, o):
            return (not d.is_dma) and (not o.is_dma) and d.eng == o.eng and o.fn is not None and (d.eng == "pe" or not sync_same)

        for e, lst in self.ops.items():
            for o in lst:
                for d in o.deps:
                    if d.is_dma or skip(d, o):
                        continue
                    d.marked = True
        sems = {}
        for e in COMPUTE:
            sems[e] = self.stack.enter_context(nc.semaphore("s_" + e))
            n = 0
            for o in self.ops[e]:
                if o.is_dma or o.fn is None:
                    continue
                if o.marked:
                    n += 1
                    o.done = (sems[e], n)
        dsems = {}
        for q in ("sp", "pool"):
            if self.dma_count[q] == 0:
                continue
            dsems[q] = [self.stack.enter_context(nc.semaphore("d_%s%d" % (q, i))) for i in range(DMAQ_K)]
            for o in self.dma_ops[q]:
                o.done = (dsems[q][o.idx % DMAQ_K], 16 * (o.idx // DMAQ_K + 1))
        block = self.stack.enter_context(nc.Block())

        def emit(ename, e):
            waited = {}
            for o in self.ops[ename]:
                need = {}
                for d in o.deps:
                    if d.done is None or skip(d, o):
                        continue
                    s, v = d.done
                    if need.get(s.num, (None, 0))[1] < v:
                        need[s.num] = (s, v)
                if o.is_dma and o.idx >= DMAQ_K:
                    s = dsems[o.q][o.idx % DMAQ_K]
                    v = 16 * (o.idx // DMAQ_K)
                    if need.get(s.num, (None, 0))[1] < v:
                        need[s.num] = (s, v)
                for sn, (s, v) in need.items():
                    if waited.get(sn, 0) < v:
                        e.wait_ge(s, v)
                        waited[sn] = v
                if o.fn is None:
                    continue
                ins = o.fn(e)
                if o.is_dma:
                    ins.then_inc(o.done[0], 16)
                elif o.marked:
                    ins.then_inc(o.done[0], 1)
            if ename == "sp":
                for o in self.out_dmas:
                    s, v = o.done
                    if waited.get(s.num, 0) < v:
                        e.wait_ge(s, v)
                        waited[s.num] = v

        @block.tensor
        def _(e):
            emit("pe", e)

        @block.scalar
        def _(e):
            emit("act", e)

        @block.vector
        def _(e):
            emit("dve", e)

        @block.gpsimd
        def _(e):
            emit("pool", e)

        @block.sync
        def _(e):
            emit("sp", e)

        self.stack.close()
        return nc


class Arena:
    def __init__(self, P, kbytes):
        self.words = kbytes * 256
        self.t = P.sb("arena", [128, self.words], F32)
        self.off = 0

    def reset(self, to=0):
        self.off = to

    def alloc(self, free_shape, dt, parts=128, p0=0):
        n = 1
        for s in free_shape:
            n *= s
        nb = n * dsize(dt)
        w = (nb + 31) // 32 * 8
        assert self.off + w <= self.words, ("arena overflow", self.off, w, self.words)
        v = self.t[p0:p0 + parts, self.off:self.off + w]
        self.off += w
        if dt != F32:
            v = v.bitcast(dt)
        v = v[:, 0:n]
        if len(free_shape) == 2:
            v = v.rearrange("p (a b) -> p a b", a=free_shape[0])
        elif len(free_shape) == 3:
            v = v.rearrange("p (a b c) -> p a b c", a=free_shape[0], b=free_shape[1])
        return v


S = 8192
D = 1024
NT = S // 128
NS = S // 512
POOL_WINDOWS = (2, 4, 8, 16)
LN_EPS = 1e-5
RMS_EPS = 1e-6
DN_ALPHA = 4 ** 0.25
QSCALE = 96 ** -0.5
TWO_PI = 2.0 * math.pi
CW1 = 6.28125
CW2 = TWO_PI - CW1

C_UP, C_CQ, C_CKV, C_KR, C_GQ, C_GK, C_GV, C_GR, C_GD, C_GATE = 0, 512, 896, 1152, 1184, 1440, 1696, 2208, 2720, 2752


class K:
    def __init__(self, debug_out=(), layers=(0, 1), phases=None, ns_limit=None, skip=(), no_big=False):
        self.ns_limit = ns_limit
        self.skip = set(skip)
        self.no_big = no_big
        self.P = P = Prog()
        P.debug_out = set(debug_out)
        self.layers = layers
        self.phases = phases
        self.nc = P.nc
        self.A = Arena(P, 204)
        self.banks = [P.ps("bank%d" % i, [128, 512], F32) for i in range(8)]
        self.inp = {}
        self.scr = {}

    def din(self, name, shape, dt=F32):
        self.inp[name] = self.P.dram(name, shape, dt, kind="ExternalInput")
        return self.inp[name]

    def scratch(self, name, shape, dt):
        self.scr[name] = self.P.dram(name, shape, dt)
        return self.scr[name]

    def mm(self, out, lhsT, rhs, start, stop, reads, writes):
        self.P.op("pe", lambda e: e.matmul(out, lhsT=lhsT, rhs=rhs, start=start, stop=stop), reads, writes)

    def tr(self, out, in_, ident, reads, writes):
        self.P.op("pe", lambda e: e.transpose(out=out, in_=in_, identity=ident), reads, writes)

    def act(self, out, in_, func, reads, writes, bias=None, scale=None, accum_out=None):
        kw = {}
        if bias is not None:
            kw["bias"] = bias
        if scale is not None:
            kw["scale"] = scale
        if accum_out is not None:
            kw["accum_out"] = accum_out
        self.P.op("act", lambda e: e.activation(out=out, in_=in_, func=func, **kw), reads, writes)

    def ts(self, eng, out, in0, s1, s2, op0, op1, reads, writes):
        if op1 is None:
            self.P.op(eng, lambda e: e.tensor_scalar(out=out, in0=in0, scalar1=s1, scalar2=None, op0=op0), reads, writes)
        else:
            self.P.op(eng, lambda e: e.tensor_scalar(out=out, in0=in0, scalar1=s1, scalar2=s2, op0=op0, op1=op1), reads, writes)

    def tt(self, eng, out, in0, in1, op, reads, writes):
        self.P.op(eng, lambda e: e.tensor_tensor(out=out, in0=in0, in1=in1, op=op), reads, writes)

    def stt(self, eng, out, in0, scalar, in1, op0, op1, reads, writes):
        self.P.op(eng, lambda e: e.scalar_tensor_tensor(out=out, in0=in0, scalar=scalar, in1=in1, op0=op0, op1=op1), reads, writes)

    def cp(self, eng, out, in_, reads, writes):
        if eng == "act":
            self.P.op("act", lambda e: e.copy(out=out, in_=in_), reads, writes)
        else:
            self.P.op(eng, lambda e: e.tensor_copy(out=out, in_=in_), reads, writes)

    def memset(self, eng, ap, val, writes):
        self.P.op(eng, lambda e: e.memset(ap, val), (), writes)

    def ld(self, out, in_, reads, writes, q="sp", slow=False):
        if slow:
            self.P.dma(lambda e: e.dma_start(out=out, in_=in_, allow_slow_non_contiguous=True), reads, writes, q=q)
        else:
            self.P.dma(lambda e: e.dma_start(out=out, in_=in_), reads, writes, q=q)

    def st(self, out, in_, reads, writes, q="pool", is_out=False):
        self.P.dma(lambda e: e.dma_start(out=out, in_=in_), reads, writes, q=q, is_out=is_out)

    def declare(self):
        din = self.din
        din("x", [S, D])
        din("positions", [S], I32)
        din("ln0_g", [D]); din("ln0_b", [D])
        din("w_in", [2, D, 5824]); din("b_gate", [2, 3072])
        din("pool_w", [2, 4, 128, 128]); din("pool_scale", [2, 512]); din("w_up_a", [2, 512, D])
        din("mla_q_norm", [2, 384]); din("mla_w_uq", [2, 384, 768]); din("mla_kv_norm", [2, 256])
        din("mla_w_ukv", [2, 256, 1024]); din("w_up_b", [2, 512, D])
        din("gla_w_dec", [2, 2, 16, 256]); din("gla_b_dec", [2, 2, 256]); din("gla_norm", [2, 128])
        din("w_up_c", [2, 512, D]); din("w_out", [2, D, D])
        din("ln1_g", [2, D]); din("ln1_b", [2, D]); din("router_w", [2, D, 16])
        if not self.no_big:
            din("exp_w_gate", [2, 16, D, 2048]); din("exp_w_up", [2, 16, D, 2048]); din("exp_w_down", [2, 16, 2048, D])
        din("ln2_g", [2, D]); din("ln2_b", [2, D])
        din("c_ident", [128, 128])
        din("c_freq", [16])
        din("c_tri", [4, 64, 64])
        self.out = self.P.dram("out", [S, D], F32, kind="ExternalOutput")
        sc = self.scratch
        sc("hbuf", [S, D], F32)
        sc("cosF", [32, S], F32); sc("sinF", [32, S], F32)
        sc("uT", [512, S], F32)
        sc("gqT", [256, S], F32); sc("gkT", [256, S], F32); sc("gdT", [32, S], F32)
        sc("gk_tok", [S, 256], F32); sc("gv", [S, 512], BF16); sc("gr", [S, 512], BF16)
        sc("QT", [8, 96, S], BF16); sc("KT", [8, 64, S], BF16); sc("KRT", [32, S], BF16); sc("V", [S, 512], BF16)
        sc("pmT", [512, S], BF16); sc("mlT", [512, S], BF16); sc("glT", [512, S], BF16)
        sc("h1", [S, D], F32); sc("h1b", [S, D], BF16); sc("acc", [S, D], F32)

    def consts(self):
        A = self.A
        self.ident_f = A.alloc([128], F32)
        self.ident_b = A.alloc([128], BF16)
        self.eps_ln = A.alloc([1], F32)
        self.eps_rms = A.alloc([1], F32)
        self.cosT = A.alloc([NT, 16], F32)
        self.sinT = A.alloc([NT, 16], F32)
        self.ld(self.ident_f, self.inp["c_ident"][:, :], (), ["ident_f"])
        self.cp("dve", self.ident_b, self.ident_f, ["ident_f"], ["ident_b"])
        self.memset("pool", self.eps_ln, LN_EPS, ["eps_ln"])
        self.memset("pool", self.eps_rms, RMS_EPS, ["eps_rms"])
        self.IDX = A.alloc([S // 1024, 16], I32)
        self.GATE = A.alloc([S // 1024, 16], F32)
        self.const_top = A.off

    def range_reduce_sin(self, ang, tmp, tmpi, res, shift, k_ang, k_tmp, k_res):
        self.ts("dve", tmp, ang, 1.0 / TWO_PI, None, ALU.mult, None, [k_ang], [k_tmp])
        self.cp("dve", tmpi, tmp, [k_tmp], [k_tmp + "i"])
        self.cp("dve", tmp, tmpi, [k_tmp + "i"], [k_tmp])
        self.stt("dve", res, tmp, -CW1, ang, ALU.mult, ALU.add, [k_tmp, k_ang], [k_res])
        self.stt("dve", res, tmp, -CW2, res, ALU.mult, ALU.add, [k_tmp, k_res], [k_res])
        if shift != 0.0:
            self.ts("dve", res, res, shift, None, ALU.add, None, [k_res], [k_res])
        self.ts("dve", tmp, res, math.pi, None, ALU.is_gt, None, [k_res], [k_tmp])
        self.stt("dve", res, tmp, -TWO_PI, res, ALU.mult, ALU.add, [k_tmp, k_res], [k_res])
        self.ts("dve", res, res, math.pi, -math.pi, ALU.min, ALU.max, [k_res], [k_res])
        self.act(res, res, AF.Sin, [k_res], [k_res])

    def phase0(self):
        A, P = self.A, self.P
        A.reset(self.const_top)
        posi = A.alloc([NT], I32)
        posf = A.alloc([NT], F32)
        freq = A.alloc([16], F32)
        ang = A.alloc([NT, 16], F32)
        tmp = A.alloc([NT, 16], F32)
        tmpi = A.alloc([NT, 16], I32)
        pos = self.inp["positions"]
        pv = pos.rearrange("(j p) -> p j", p=128)
        for a in range(0, NT, 8):
            self.ld(posi[:, a:a + 8], pv[:, a:a + 8], (), ["posi"], slow=True)
        self.ld(freq, self.inp["c_freq"][None, :].to_broadcast([128, 16]), (), ["freq"])
        self.cp("dve", posf, posi, ["posi"], ["posf"])
        for j in range(NT):
            self.ts("dve", ang[:, j, :], freq, posf[:, j:j + 1], None, ALU.mult, None, ["posf", "freq"], ["angT"])
        self.range_reduce_sin(ang, tmp, tmpi, self.sinT, 0.0, "angT", "tmpT", "sinT")
        self.range_reduce_sin(ang, tmp, tmpi, self.cosT, math.pi / 2, "angT", "tmpT", "cosT")
        CH = min(2048, S)
        prow_i = A.alloc([CH], I32)
        prow = A.alloc([CH], F32)
        fcol = A.alloc([1], F32)
        scol_c = A.alloc([1], F32)
        scol_s = A.alloc([1], F32)
        angF = A.alloc([CH], F32)
        tmpF = A.alloc([CH], F32)
        tmpFi = A.alloc([CH], I32)
        resF = A.alloc([CH], F32)
        fr = self.inp["c_freq"]
        self.ld(fcol[64:80, :], fr.rearrange("(a b) -> a b", b=1), (), ["fcol"], slow=True)
        self.ld(fcol[80:96, :], fr.rearrange("(a b) -> a b", b=1), ["fcol"], ["fcol"], slow=True)
        self.memset("pool", scol_c[64:96, :], QSCALE, ["scol_c"])
        self.memset("pool", scol_s[64:96, :], QSCALE, ["scol_s"])
        self.memset("pool", scol_s[64:80, :], -QSCALE, ["scol_s"])
        sl = slice(64, 96)
        for c in range(S // CH):
            self.ld(prow_i[sl, :], pos[None, c * CH:(c + 1) * CH].to_broadcast([32, CH]), (), ["prow_i"])
            self.cp("dve", prow[sl, :], prow_i[sl, :], ["prow_i"], ["prow"])
            self.ts("dve", angF[sl, :], prow[sl, :], fcol[sl, 0:1], None, ALU.mult, None, ["prow", "fcol"], ["angF"])
            for (shift, scol, dst) in ((0.0, scol_s, "sinF"), (math.pi / 2, scol_c, "cosF")):
                self.range_reduce_sin(angF[sl, :], tmpF[sl, :], tmpFi[sl, :], resF[sl, :], shift, "angF", "tmpF", "resF")
                self.ts("dve", resF[sl, :], resF[sl, :], scol[sl, 0:1], None, ALU.mult, None, ["resF", "scol_s", "scol_c"], ["resF"])
                self.st(self.scr[dst][:, c * CH:(c + 1) * CH], resF[sl, :], ["resF"], [dst])
        P.barrier()

    def layernorm_tile(self, x, out_f, out_b, gam, bet, st6, mv, rstd, nb, key, reads, writes_f, writes_b, cast_eng="pool"):
        for c in range(2):
            self.P.op("dve", lambda e, c=c: e.bn_stats(out=st6[:, c, :], in_=x[:, c * 512:(c + 1) * 512]), reads, [(key, "st", c)])
        self.P.op("dve", lambda e: e.bn_aggr(out=mv, in_=st6.rearrange("p a b -> p (a b)")), [(key, "st", 0), (key, "st", 1)], [(key, "mv")])
        self.act(rstd, mv[:, 1:2], AF.Ln, [(key, "mv"), "eps_ln"], [(key, "rstd")], bias=self.eps_ln[:, 0:1], scale=1.0)
        self.act(rstd, rstd, AF.Exp, [(key, "rstd")], [(key, "rstd")], scale=-0.5)
        self.stt("dve", nb, mv[:, 0:1], -1.0, rstd, ALU.mult, ALU.mult, [(key, "mv"), (key, "rstd")], [(key, "nb")])
        tmpk = (key, "xn")
        self.act(self.ln_tmp, x, AF.Identity, list(reads) + [(key, "rstd"), (key, "nb")], [tmpk], bias=nb[:, 0:1], scale=rstd[:, 0:1])
        self.tt("dve", self.ln_tmp, self.ln_tmp, gam, ALU.mult, [tmpk, "lnpar"], [tmpk])
        if out_f is not None:
            self.tt("dve", out_f, self.ln_tmp, bet, ALU.add, [tmpk, "lnpar"], writes_f)
            if out_b is not None:
                self.cp(cast_eng, out_b, out_f, writes_f, writes_b)
        else:
            self.tt("dve", out_b, self.ln_tmp, bet, ALU.add, [tmpk, "lnpar"], writes_b)


    def load_cast(self, dst, src, n_inner, key, reads=()):
        shp = list(dst.shape)
        if len(shp) == 3:
            C = shp[1]
            cstep = max(1, 1024 // shp[0])
            nsp = (n_inner + 2047) // 2048
            step = (n_inner + nsp - 1) // nsp
            for c0 in range(0, C, cstep):
                c1 = min(C, c0 + cstep)
                for a in range(0, n_inner, step):
                    b = min(n_inner, a + step)
                    self.P.dma(lambda e, a=a, b=b, c0=c0, c1=c1: e.dma_start(out=dst[:, c0:c1, a:b], in_=src[:, c0:c1, a:b]), reads, [key], q="pool")
        else:
            self.P.dma(lambda e: e.dma_start(out=dst, in_=src), reads, [key], q="pool")

    def phase1(self, l):
        A, P, I, SC = self.A, self.P, self.inp, self.scr
        A.reset(self.const_top)
        bk = self.banks
        win = A.alloc([8, 2752], BF16)
        wuq = A.alloc([3, 768], BF16)
        wuqr = A.alloc([3, 768], BF16)
        wukv = A.alloc([2, 1024], BF16)
        qng = A.alloc([3], F32)
        kvng = A.alloc([2], F32)
        gam = A.alloc([D], F32)
        bet = A.alloc([D], F32)
        self.ln_tmp = A.alloc([D], F32)
        xt = [A.alloc([4, D], F32) for _ in range(2)]
        hb = A.alloc([4, D], BF16)
        hT = [A.alloc([8, 512], BF16) for _ in range(2)]
        st6 = A.alloc([2, 6], F32); mv = A.alloc([2], F32); rstd = A.alloc([1], F32); nb = A.alloc([1], F32)
        ss = A.alloc([4], F32)
        cqn = A.alloc([384], BF16)
        ckvn = A.alloc([256], BF16)
        cqnT = A.alloc([3, 512], BF16)
        ckvnT = A.alloc([2, 512], BF16)
        junk = A.alloc([384], F32)
        junk2 = A.alloc([256], F32)
        st_u = A.alloc([4, 512], F32)
        st_q = A.alloc([2, 512], F32)
        st_k = A.alloc([2, 512], F32)
        st_d = A.alloc([512], F32)
        st_kt = A.alloc([4, 256], F32)
        st_v = A.alloc([4, 512], BF16)
        st_r = A.alloc([4, 512], BF16)
        st_V = A.alloc([4, 512], BF16)
        st_Q = A.alloc([8, 512], BF16)
        st_K = A.alloc([8, 512], BF16)
        st_kr = A.alloc([512], BF16)
        krt = A.alloc([32], F32)
        krb = A.alloc([128], BF16)
        rt1 = A.alloc([16], F32); rt2 = A.alloc([16], F32)
        cosF = A.alloc([512], F32); sinF = A.alloc([512], F32)
        qt1 = A.alloc([512], F32); qt2 = A.alloc([512], F32)

        self.memset("pool", krb, 0.0, ["krb"])
        w_in = I["w_in"][l]
        self.load_cast(win, w_in[:, 0:2752].rearrange("(c p) n -> p c n", p=128), 2752, "win")
        self.load_cast(wuq, I["mla_w_uq"][l].rearrange("(c p) n -> p c n", p=128), 768, "wuq")
        wq4 = I["mla_w_uq"][l].rearrange("(c p) (h x) -> p c h x", p=128, x=96)
        wuqr4 = wuqr.rearrange("p c (h x) -> p c h x", x=96)
        for c in range(3):
            self.load_cast(wuqr4[:, c, :, 64:80], wq4[:, c, :, 80:96], 16, "wuqr")
            self.load_cast(wuqr4[:, c, :, 80:96], wq4[:, c, :, 64:80], 16, "wuqr")
        self.memset("pool", wuqr4[:, :, :, 0:64], 0.0, ["wuqr"])
        self.load_cast(wukv, I["mla_w_ukv"][l].rearrange("(c p) n -> p c n", p=128), 1024, "wukv")
        self.ld(qng, I["mla_q_norm"][l].rearrange("(c p) -> p c", p=128), (), ["qng"], slow=True)
        self.ld(kvng, I["mla_kv_norm"][l].rearrange("(c p) -> p c", p=128), (), ["kvng"], slow=True)
        if l == 0:
            self.ld(gam, I["ln0_g"][None, :].to_broadcast([128, D]), (), ["lnpar"])
            self.ld(bet, I["ln0_b"][None, :].to_broadcast([128, D]), (), ["lnpar"])
        src = I["x"] if l == 0 else SC["hbuf"]
        pT = bk[0][:, 0:512].bitcast(BF16).rearrange("p (a b) -> p a b", a=8)
        fm_blocks = [("u", g, C_UP + g * 128, 128) for g in range(4)] + [("q", m, C_GQ + m * 128, 128) for m in range(2)] \
            + [("k", m, C_GK + m * 128, 128) for m in range(2)] + [("d", 0, C_GD, 32)]
        nfm = 0
        ntm = 0
        nsup = self.ns_limit or NS

        def head_load(s):
            sl = s % 2
            t0 = s * 512
            self.ld(xt[sl], src[t0:t0 + 512, :].rearrange("(j p) d -> p j d", p=128), (), [("xt", sl)])

        def head_ln(s, j):
            sl = s % 2
            xk = ("xt", sl)
            hbk = ("hb", j)
            if l == 0:
                self.layernorm_tile(xt[sl][:, j, :], xt[sl][:, j, :], hb[:, j, :], gam, bet, st6, mv, rstd, nb, "ln", [xk], [xk], [hbk], cast_eng="act")
            else:
                self.cp("act", hb[:, j, :], xt[sl][:, j, :], [xk], [hbk])

        def head_tr(s, j):
            sl = s % 2
            t0 = s * 512
            xk = ("xt", sl)
            hbk = ("hb", j)
            for c in range(8):
                self.tr(pT[:, c, :], hb[:, j, c * 128:(c + 1) * 128], self.ident_b, [hbk, "ident_b"], ["pT"])
            self.cp("dve", hT[sl][:, :, j * 128:(j + 1) * 128], pT, [], [("hT", sl), "pT"])
            if l == 0 and j == 3:
                self.st(SC["hbuf"][t0:t0 + 512, :].rearrange("(j p) d -> p j d", p=128), xt[sl], [xk], ["hbuf"])

        def head_tile(s, j):
            head_ln(s, j)
            head_tr(s, j)

        head_load(0)
        for j in range(4):
            head_tile(0, j)
        for s in range(nsup):
            sl = s % 2
            t0 = s * 512
            xk = ("xt", sl)
            if s + 1 < nsup:
                head_load(s + 1)
            self.ld(cosF[64:96, :], SC["cosF"][:, t0:t0 + 512], (), ["cosFs"])
            self.ld(sinF[64:96, :], SC["sinF"][:, t0:t0 + 512], (), ["sinFs"])
            hTk = ("hT", sl)
            for (kind, idx, c0, w) in (fm_blocks if 'fm' not in self.skip else []):
                pb = bk[1 + nfm % 2]
                pk = ("fm", nfm % 2)
                nfm += 1
                for c in range(8):
                    self.mm(pb[0:w, :], win[:, c, c0:c0 + w], hT[sl][:, c, :], c == 0, c == 7, ["win", hTk], [pk])
                if kind == "u":
                    self.cp("act", st_u[:, idx, :], pb[:, :], [], ["st_u", pk])
                elif kind == "q":
                    self.act(st_q[:, idx, :], pb[:, :], AF.Identity, [], ["st_q", pk], scale=0.125)
                elif kind == "k":
                    self.cp("dve", st_k[:, idx, :], pb[:, :], [], ["st_k", pk])
                else:
                    self.cp("dve", st_d[0:32, :], pb[0:32, :], [], ["st_d", pk])
            if 'fm' not in self.skip:
                self.st(SC["uT"][:, t0:t0 + 512].rearrange("(g p) t -> p g t", p=128), st_u, ["st_u"], ["uT"])
                self.st(SC["gqT"][:, t0:t0 + 512].rearrange("(g p) t -> p g t", p=128), st_q, ["st_q"], ["gqT"])
                self.st(SC["gkT"][:, t0:t0 + 512].rearrange("(g p) t -> p g t", p=128), st_k, ["st_k"], ["gkT"])
                self.st(SC["gdT"][:, t0:t0 + 512], st_d[0:32, :], ["st_d"], ["gdT"])
            for j in range(4):
                tt = s * 4 + j
                lh = lambda c: hT[sl][:, c, j * 128:(j + 1) * 128]

                def tmb(pb, pk, c0, w):
                    for c in range(8):
                        self.mm(pb[:, 0:w], lh(c), win[:, c, c0:c0 + w], c == 0, c == 7, ["win", hTk], [pk])

                def tm(c0, w):
                    nonlocal ntm
                    pb = bk[3 + ntm % 2]
                    pk = ("tm", ntm % 2)
                    ntm += 1
                    tmb(pb, pk, c0, w)
                    return pb, pk
                pbq, pkq = bk[6], "pqa"
                pbk, pkk_ = bk[7], "pqb"
                if s + 1 < nsup:
                    head_ln(s + 1, j)
                tmb(pbq, pkq, C_CQ, 384)
                tmb(pbk, pkk_, C_CKV, 288)
                self.act(junk[:, 0:384], pbq[:, 0:384], AF.Square, [], ["junk", "ss0", pkq], accum_out=ss[:, 0:1])
                self.act(ss[:, 1:2], ss[:, 0:1], AF.Ln, ["ss0", "eps_rms"], ["ss1"], bias=self.eps_rms[:, 0:1], scale=1.0 / 384)
                self.act(ss[:, 1:2], ss[:, 1:2], AF.Exp, ["ss1"], ["ss1"], scale=-0.5)
                self.ts("dve", cqn, pbq[:, 0:384], ss[:, 1:2], None, ALU.mult, None, ["ss1"], ["cqn", pkq])
                self.act(junk2[:, 0:256], pbk[:, 0:256], AF.Square, [], ["junk2", "ss2", pkk_], accum_out=ss[:, 2:3])
                self.act(ss[:, 3:4], ss[:, 2:3], AF.Ln, ["ss2", "eps_rms"], ["ss3"], bias=self.eps_rms[:, 0:1], scale=1.0 / 256)
                self.act(ss[:, 3:4], ss[:, 3:4], AF.Exp, ["ss3"], ["ss3"], scale=-0.5)
                self.ts("dve", ckvn, pbk[:, 0:256], ss[:, 3:4], None, ALU.mult, None, ["ss3"], ["ckvn", pkk_])
                cs = self.cosT[:, tt, :]
                sn = self.sinT[:, tt, :]
                self.cp("dve", krt, pbk[:, 256:288], [], ["krt", pkk_])
                self.tt("dve", rt1, krt[:, 0:16], cs, ALU.mult, ["krt", "cosT"], ["rt1"])
                self.tt("dve", rt2, krt[:, 16:32], sn, ALU.mult, ["krt", "sinT"], ["rt2"])
                self.tt("dve", krb[:, 0:16], rt1, rt2, ALU.subtract, ["rt1", "rt2"], ["krb"])
                self.tt("dve", rt1, krt[:, 0:16], sn, ALU.mult, ["krt", "sinT"], ["rt1"])
                self.tt("dve", rt2, krt[:, 16:32], cs, ALU.mult, ["krt", "cosT"], ["rt2"])
                self.tt("dve", krb[:, 16:32], rt1, rt2, ALU.add, ["rt1", "rt2"], ["krb"])
                pb, pk = tm(C_GK, 256)
                self.cp("act", st_kt[:, j, :], pb[:, 0:256], [], ["st_kt", pk])
                pb, pk = tm(C_GV, 512)
                self.cp("dve", st_v[:, j, :], pb[:, :], [], ["st_v", pk])
                pb, pk = tm(C_GR, 512)
                self.act(st_r[:, j, :], pb[:, :], AF.Silu, [], ["st_r", pk])
                pq = bk[5][:, 0:192].bitcast(BF16).rearrange("p (a b) -> p a b", a=3)
                for c in range(3):
                    self.tr(pq[:, c, :], cqn[:, c * 128:(c + 1) * 128], self.ident_b, ["cqn", "ident_b"], ["b5"])
                self.tt("dve", cqnT[:, :, j * 128:(j + 1) * 128], pq, qng[:, :, None].to_broadcast([128, 3, 128]), ALU.mult, ["qng"], ["cqnT", "b5"])
                pkv = bk[5][:, 256:448].bitcast(BF16).rearrange("p (a b) -> p a b", a=3)
                for c in range(2):
                    self.tr(pkv[:, c, :], ckvn[:, c * 128:(c + 1) * 128], self.ident_b, ["ckvn", "ident_b"], ["b5"])
                self.tr(pkv[:, 2, :], krb[:, :], self.ident_b, ["krb", "ident_b"], ["b5"])
                self.tt("dve", ckvnT[:, :, j * 128:(j + 1) * 128], pkv[:, 0:2, :], kvng[:, :, None].to_broadcast([128, 2, 128]), ALU.mult, ["kvng"], ["ckvnT", "b5"])
                self.cp("act", st_kr[0:32, j * 128:(j + 1) * 128], pkv[0:32, 2, :], [], ["st_kr", "b5"])
                if s + 1 < nsup:
                    head_tr(s + 1, j)
            rows = lambda name: SC[name][t0:t0 + 512, :].rearrange("(j p) d -> p j d", p=128)
            self.st(rows("gk_tok"), st_kt, ["st_kt"], ["gk_tok"])
            self.st(rows("gv"), st_v, ["st_v"], ["gv"])
            self.st(rows("gr"), st_r, ["st_r"], ["gr"])
            self.st(SC["KRT"][:, t0:t0 + 512], st_kr[0:32, :], ["st_kr"], ["KRT"])
            for h in (range(8) if 'mlaup' not in self.skip else []):
                pqa, ka = (bk[6], "pqa") if h % 2 == 0 else (bk[3], ("tm", 0))
                pqb, kb = (bk[7], "pqb") if h % 2 == 0 else (bk[4], ("tm", 1))
                for c in range(3):
                    self.mm(pqa[0:96, :], wuq[:, c, h * 96:(h + 1) * 96], cqnT[:, c, :], c == 0, c == 2, ["wuq", "cqnT"], [ka])
                for c in range(3):
                    self.mm(pqb[0:96, :], wuqr[:, c, h * 96:(h + 1) * 96], cqnT[:, c, :], c == 0, c == 2, ["wuqr", "cqnT"], [kb])
                self.act(st_Q[0:64, h, :], pqa[0:64, :], AF.Identity, [], ["st_Q", ka], scale=QSCALE)
                self.tt("dve", qt1[64:96, :], pqa[64:96, :], cosF[64:96, :], ALU.mult, ["cosFs"], ["qt1", ka])
                self.tt("dve", qt2[64:96, :], pqb[64:96, :], sinF[64:96, :], ALU.mult, ["sinFs"], ["qt2", kb])
                self.tt("pool", st_Q[64:96, h, :], qt1[64:96, :], qt2[64:96, :], ALU.add, ["qt1", "qt2"], ["st_Q"])
                pka = bk[1 + h % 2]
                pkk = ("fm", h % 2)
                for c in range(2):
                    self.mm(pka[0:64, :], wukv[:, c, h * 128:h * 128 + 64], ckvnT[:, c, :], c == 0, c == 1, ["wukv", "ckvnT"], [pkk])
                self.cp("act", st_K[0:64, h, :], pka[0:64, :], [], ["st_K", pkk])
            self.st(SC["QT"][:, :, t0:t0 + 512].rearrange("h r t -> r h t"), st_Q[0:96, :, :], ["st_Q"], ["QT"])
            self.st(SC["KT"][:, :, t0:t0 + 512].rearrange("h r t -> r h t"), st_K[0:64, :, :], ["st_K"], ["KT"])
            wv = wukv.rearrange("p c (h x) -> p c h x", x=128)
            for j in (range(4) if 'vproj' not in self.skip else []):
                pb = bk[3 + ntm % 2]
                pk = ("tm", ntm % 2)
                ntm += 1
                for c in range(2):
                    self.mm(pb[:, :].rearrange("p (h x) -> p h x", x=64), ckvnT[:, c, j * 128:(j + 1) * 128], wv[:, c, :, 64:128], c == 0, c == 1, ["wukv", "ckvnT"], [pk])
                self.cp("dve", st_V[:, j, :], pb[:, :], [], ["st_V", pk])
            self.st(rows("V"), st_V, ["st_V"], ["V"])
        P.barrier()

    def phase2(self, l):
        A, P, SC = self.A, self.P, self.scr
        A.reset(self.const_top)
        bk = self.banks
        KTs = [A.alloc([S], BF16) for _ in range(2)]
        Vh = [A.alloc([NT, 65], BF16) for _ in range(2)]
        Qs = [A.alloc([512], BF16) for _ in range(2)]
        pts = [A.alloc([512], BF16) for _ in range(4)]
        ones = A.alloc([64], F32)
        rcp = A.alloc([512], F32)
        bc = A.alloc([512], F32)
        ost = [A.alloc([512], BF16) for _ in range(2)]
        self.memset("pool", ones, 1.0, ["ones"])
        for i in range(2):
            self.memset("pool", Vh[i][:, :, 64:65], 1.0, [("Vh", i)])
        nheads = self.ns_limit or 8
        nqg = self.ns_limit or NS
        LOOK = 2
        items = [(h, qg) for h in range(nheads) for qg in range(nqg)]
        loaded_heads = set()

        def load_head(h):
            if h in loaded_heads or h >= nheads:
                return
            loaded_heads.add(h)
            hs = h % 2
            self.ld(KTs[hs][0:64, :], SC["KT"][h], ["KT"], [("KTs", hs)])
            self.ld(KTs[hs][64:96, :], SC["KRT"][:, :], ["KRT"], [("KTs", hs)])
            vsrc = SC["V"][:, h * 64:(h + 1) * 64].rearrange("(kt p) v -> p kt v", p=128)
            for k0 in range(0, NT, 8):
                self.ld(Vh[hs][:, k0:k0 + 8, 0:64], vsrc[:, k0:k0 + 8, :], ["V"], [("Vh", hs)])

        def load_q(ii):
            if ii >= len(items):
                return
            h, qg = items[ii]
            qs = ii % 2
            self.ld(Qs[qs][0:96, :], SC["QT"][h, :, qg * 512:qg * 512 + 512], ["QT"], [("Qs", qs)])

        def qk(ii, kt):
            h, qg = items[ii]
            hs, qs = h % 2, ii % 2
            g = (ii * NT + kt) % 4
            self.mm(bk[g][:, :], KTs[hs][0:96, kt * 128:(kt + 1) * 128], Qs[qs][0:96, :], True, True, [("KTs", hs), ("Qs", qs)], [("B", g)])

        def epilogue(ii):
            h, qg = items[ii]
            qs = ii % 2
            q0 = qg * 512
            ob = bk[4 + qs]
            ok = ("B", 4 + qs)
            self.P.op("dve", lambda e, ob=ob: e.reciprocal(out=rcp[64:65, :], in_=ob[64:65, :]), [], ["rcp", ok])
            self.mm(bk[6][0:64, :], ones[64:65, :], rcp[64:65, :], True, True, ["ones", "rcp"], [("B", 6)])
            self.cp("act", bc[0:64, :], bk[6][0:64, :], [], ["bc", ("B", 6)])
            self.tt("dve", ost[qs][0:64, :], ob[0:64, :], bc[0:64, :], ALU.mult, ["bc"], [("ost", qs), ok])
            self.st(SC["mlT"][h * 64:(h + 1) * 64, q0:q0 + 512], ost[qs][0:64, :], [("ost", qs)], ["mlT"])

        load_head(0)
        load_q(0)
        for kt in range(LOOK):
            qk(0, kt)
        for ii, (h, qg) in enumerate(items):
            hs, qs = h % 2, ii % 2
            if qg == 0:
                load_head(h + 1)
            load_q(ii + 1)
            ob = bk[4 + qs]
            ok = ("B", 4 + qs)
            for kt in range(NT):
                nk = kt + LOOK
                if nk < NT:
                    qk(ii, nk)
                elif ii + 1 < len(items):
                    qk(ii + 1, nk - NT)
                g = (ii * NT + kt) % 4
                pt = pts[g]
                pk = ("pt", g)
                self.act(pt, bk[g][:, :], AF.Exp, [], [pk, ("B", g)])
                self.mm(ob[0:65, :], Vh[hs][:, kt, :], pt, kt == 0, kt == NT - 1, [("Vh", hs), pk], [ok])
                if kt == 3 and ii > 0:
                    epilogue(ii - 1)
            if ii == len(items) - 1:
                epilogue(ii)
        P.barrier()

    def phase4(self, l):
        A, P, I, SC = self.A, self.P, self.inp, self.scr
        A.reset(self.const_top)
        bk = self.banks
        wp = A.alloc([4, 128], BF16)
        psc = A.alloc([4], F32)
        inv_first = A.alloc([4, 512], F32)
        inv_last = A.alloc([4, 512], F32)
        U = [A.alloc([528], F32) for _ in range(2)]
        sA = A.alloc([528], F32)
        sB = A.alloc([528], F32)
        pl = A.alloc([512], BF16)
        pst = [A.alloc([512], BF16) for _ in range(2)]
        self.load_cast(wp, I["pool_w"][l].rearrange("g c d -> c g d"), 128, "wp")
        self.ld(psc, I["pool_scale"][l].rearrange("(g p) -> p g", p=128), (), ["psc"], slow=True)
        for g, w in enumerate(POOL_WINDOWS):
            self.memset("pool", inv_first[:, g, :], 1.0 / w, ["inv_first"])
            self.memset("pool", inv_last[:, g, :], 1.0 / w, ["inv_last"])
            for t in range(w // 2):
                self.memset("pool", inv_first[:, g, t:t + 1], 1.0 / (t + w // 2), ["inv_first"])
            for j in range(1, w // 2):
                self.memset("pool", inv_last[:, g, 512 - j:512 - j + 1], 1.0 / (j + w // 2), ["inv_last"])
        nch = self.ns_limit or NS
        it = 0
        for g, w in enumerate(POOL_WINDOWS):
            for ch in range(nch):
                us = it % 2
                it += 1
                t0 = ch * 512
                uk = ("U", us)
                lo = max(t0 - 8, 0)
                hi = min(t0 + 520, S)
                if ch == 0:
                    self.memset("pool", U[us][:, 0:8], 0.0, [uk])
                if ch == NS - 1:
                    self.memset("pool", U[us][:, 520:528], 0.0, [uk])
                self.ld(U[us][:, lo - (t0 - 8):hi - (t0 - 8)], SC["uT"][g * 128:(g + 1) * 128, lo:hi], ["uT"], [uk])
                cur = U[us]
                curk = uk
                k = 1
                dst = [sA, sB]
                di = 0
                while k < w:
                    d_ = dst[di]
                    dk = ("s", di)
                    self.tt("dve", d_[:, k:528], cur[:, k:528], cur[:, 0:528 - k], ALU.add, [curk], [dk])
                    cur, curk = d_, dk
                    di ^= 1
                    k *= 2
                off = 8 + w // 2 - 1
                sw = cur[:, off:off + 512]
                if ch == 0 or ch == NS - 1:
                    inv = inv_first if ch == 0 else inv_last
                    other, okey = (sB, ("s", 1)) if cur is sA else (sA, ("s", 0))
                    self.tt("dve", other[:, 0:512], sw, inv[:, g, :], ALU.mult, [curk, "inv_first", "inv_last"], [okey])
                    self.tt("dve", pl, other[:, 0:512], U[us][:, 8:520], ALU.subtract, [okey, uk], ["pl"])
                else:
                    self.stt("dve", pl, sw, 1.0 / w, U[us][:, 8:520], ALU.mult, ALU.subtract, [curk, uk], ["pl"])
                pb = bk[it % 2]
                pk = ("B", it % 2)
                self.mm(pb[:, :], wp[:, g, :], pl, True, True, ["wp", "pl"], [pk])
                self.ts("dve", pst[us], pb[:, :], psc[:, g:g + 1], None, ALU.mult, None, ["psc"], [("pst", us), pk])
                self.st(SC["pmT"][g * 128:(g + 1) * 128, t0:t0 + 512], pst[us], [("pst", us)], ["pmT"])
        P.barrier()

    def phase3(self, l):
        A, P, I, SC = self.A, self.P, self.inp, self.scr
        A.reset(self.const_top)
        bk = self.banks
        if "of" not in SC:
            self.scratch("of", [S, 512], F32)
        tri = A.alloc([4, 64], F32)
        one_c = A.alloc([1], F32)
        wdec = A.alloc([2, 256], F32)
        gn = A.alloc([128], F32)
        dl = [A.alloc([128], F32) for _ in range(2)]
        gq = [A.alloc([4, 128], F32) for _ in range(2)]
        gk = [A.alloc([4, 128], F32) for _ in range(2)]
        gkt = [A.alloc([2, 256], F32) for _ in range(2)]
        vv = [A.alloc([2, 512], BF16) for _ in range(2)]
        of = [A.alloc([2, 512], F32) for _ in range(2)]
        grt = [A.alloc([2, 512], BF16) for _ in range(2)]
        esb = A.alloc([512], F32)
        gp = A.alloc([2, 256], F32)
        E1s = [A.alloc([4, 2, 64], F32) for _ in range(2)]
        E2 = A.alloc([4, 2, 64], F32)
        E3 = A.alloc([2, 256], F32)
        qins = [A.alloc([4, 128], BF16) for _ in range(2)]
        kins = [A.alloc([4, 128], BF16) for _ in range(2)]
        ksts = [A.alloc([2, 256], BF16) for _ in range(2)]
        ATm = A.alloc([4, 64], BF16)
        St = A.alloc([4, 128], F32)
        Sb = A.alloc([4, 128], BF16)
        ost = [A.alloc([2, 512], F32) for _ in range(2)]
        sq = A.alloc([1024], F32)
        ssq = A.alloc([8], F32)
        rs8 = A.alloc([8], F32)
        ob16 = A.alloc([2, 512], BF16)
        gst = [A.alloc([4, 128], BF16) for _ in range(2)]
        self.ld(tri[0:64, :, :], I["c_tri"].rearrange("k a b -> a k b"), (), ["tri"])
        self.memset("pool", one_c, 1.0, ["one_c"])
        for d_ in range(2):
            self.ld(wdec[0:16, d_, :], I["gla_w_dec"][l, d_], (), ["wdec"])
            self.ld(wdec[16:17, d_, :], I["gla_b_dec"][l, d_:d_ + 1, :], (), ["wdec"])
            self.memset("pool", dl[d_][0:17, :], 1.0, [("dl", d_)])
        self.ld(gn[0:64, :], I["gla_norm"][l][None, :].to_broadcast([64, 128]), (), ["gn"])
        LG, BT, CC, AT, OO = bk[0], bk[1], bk[2], bk[3], bk[4]
        BTv = BT[0:64, :].rearrange("p (h n i) -> p h n i", h=4, n=2)
        NG = S // 128
        ng = self.ns_limit or NG
        for dirn in range(2):
            self.memset("pool", St[0:64], 0.0, ["St"])
            self.memset("pool", Sb[0:64], 0.0, ["Sb"])
            inc_i, rest_i = (0, 1) if dirn == 0 else (2, 3)
            groups = list(range(ng)) if dirn == 0 else list(range(NG - 1, NG - 1 - ng, -1))
            def prep(gi, g, stage):
                bs = gi % 2
                t0 = g * 128
                E1, qin, kin, kst = E1s[bs], qins[bs], kins[bs], ksts[bs]
                kE1, kq, kk, kks = ("E1", bs), ("qin", bs), ("kin", bs), ("kst", bs)
                if stage == 'a':
                    self.ld(dl[bs][0:16, :], SC["gdT"][dirn * 16:(dirn + 1) * 16, t0:t0 + 128], ["gdT"], [("dl", bs)])
                    self.ld(gq[bs][0:64], SC["gqT"][:, t0:t0 + 128].rearrange("(h d) t -> d h t", d=64), ["gqT"], [("gq", bs)])
                    self.ld(gk[bs][0:64], SC["gkT"][:, t0:t0 + 128].rearrange("(h d) t -> d h t", d=64), ["gkT"], [("gk", bs)])
                    self.ld(gkt[bs][0:64], SC["gk_tok"][t0:t0 + 128, :].rearrange("(n p) c -> p n c", p=64), ["gk_tok"], [("gkt", bs)])
                    self.ld(vv[bs][0:64], SC["gv"][t0:t0 + 128, :].rearrange("(n p) c -> p n c", p=64), ["gv"], [("vv", bs)])
                    if dirn == 1:
                        self.ld(of[bs][0:64], SC["of"][t0:t0 + 128, :].rearrange("(n p) c -> p n c", p=64), ["of"], [("of", bs)])
                        self.ld(grt[bs][0:64], SC["gr"][t0:t0 + 128, :].rearrange("(n p) c -> p n c", p=64), ["gr"], [("grt", bs)])
                    for n in range(2):
                        self.mm(LG[0:64, n * 256:(n + 1) * 256], dl[bs][0:17, n * 64:(n + 1) * 64], wdec[0:17, dirn, :], True, True, [("dl", bs), "wdec"], [("B", 0)])
                    self.act(esb[0:64, :], LG[0:64, :], AF.Exp, [], ["esb", ("B", 0)], scale=-1.0)
                    self.act(gp[0:64].rearrange("p n c -> p (n c)"), esb[0:64, :], AF.Ln, ["esb", "one_c"], ["gp"], bias=one_c[0:64, 0:1], scale=1.0)
                if stage == 'b':
                    for h in range(4):
                        for n in range(2):
                            self.mm(BTv[:, h, n, :], gp[0:64, n, h * 64:(h + 1) * 64], tri[0:64, inc_i, :], True, True, ["gp", "tri"], [("B", 1)])
                    for n in range(2):
                        self.mm(CC[0:64, n * 256:(n + 1) * 256], tri[0:64, rest_i, :], gp[0:64, n, :], True, True, ["gp", "tri"], [("B", 2)])
                    f = lambda t: t[0:64].rearrange("p a b c -> p (a b c)")
                    self.act(f(E1), BT[0:64, :], AF.Exp, [], [kE1, ("B", 1)], scale=-1.0 / 16)
                    self.act(f(E2), BT[0:64, :], AF.Exp, [], ["E2", ("B", 1)], scale=1.0 / 16)
                    self.act(E3[0:64].rearrange("p n c -> p (n c)"), CC[0:64, :], AF.Exp, [], ["E3", ("B", 2)], scale=-1.0 / 16)
                if stage == 'c':
                    self.tt("dve", qin[0:64], gq[bs][0:64], E1[0:64].rearrange("p h n i -> p h (n i)"), ALU.mult, [("gq", bs), kE1], [kq])
                    self.tt("dve", kin[0:64], gk[bs][0:64], E2[0:64].rearrange("p h n i -> p h (n i)"), ALU.mult, [("gk", bs), "E2"], [kk])
                    self.tt("dve", kst[0:64], gkt[bs][0:64], E3[0:64], ALU.mult, [("gkt", bs), "E3"], [kks])

            def chunks(gi, g, part):
                bs = gi % 2
                t0 = g * 128
                E1, qin, kin, kst = E1s[bs], qins[bs], kins[bs], ksts[bs]
                kE1, kq, kk, kks = ("E1", bs), ("qin", bs), ("kin", bs), ("kst", bs)
                order = (0, 1) if dirn == 0 else (1, 0)
                for ci, n in (enumerate(order) if part != 'tail' else []):
                    if ci != part:
                        continue
                    cs = slice(n * 64, (n + 1) * 64)
                    for h in range(4):
                        self.mm(AT[0:64, h * 64:(h + 1) * 64], kin[0:64, h, cs], qin[0:64, h, cs], True, True, [kk, kq], [("B", 3)])
                    self.tt("dve", ATm[0:64], AT[0:64, 0:256].rearrange("p (h i) -> p h i", h=4), tri[0:64, inc_i:inc_i + 1, :].to_broadcast([64, 4, 64]), ALU.mult, ["tri"], ["ATm", ("B", 3)])
                    for h in range(4):
                        hv = slice(h * 128, (h + 1) * 128)
                        self.mm(OO[0:64, hv], ATm[0:64, h, :], vv[bs][0:64, n, hv], True, False, ["ATm", ("vv", bs)], [("B", 4)])
                        self.mm(OO[0:64, hv], qin[0:64, h, cs], Sb[0:64, h, :], False, True, [kq, "Sb"], [("B", 4)])
                    kvb = bk[5 + (gi * 2 + ci) % 2]
                    kvk = ("B", 5 + (gi * 2 + ci) % 2)
                    for h in range(4):
                        hv = slice(h * 128, (h + 1) * 128)
                        self.mm(kvb[0:64, hv], kst[0:64, n, h * 64:(h + 1) * 64], vv[bs][0:64, n, hv], True, True, [kks, ("vv", bs)], [kvk])
                    dcol = 63 if dirn == 0 else 0
                    for h in range(4):
                        hv = slice(h * 128, (h + 1) * 128)
                        self.stt("dve", St[0:64, h, :], St[0:64, h, :], E1[0:64, h, n, dcol:dcol + 1], kvb[0:64, hv], ALU.mult, ALU.add, [kE1], ["St", kvk])
                    self.cp("act", Sb[0:64].rearrange("p h v -> p (h v)"), St[0:64].rearrange("p h v -> p (h v)"), ["St"], ["Sb"])
                    if dirn == 0:
                        self.cp("act", ost[bs][0:64, n, :], OO[0:64, :], [], [("ost", bs), ("B", 4)])
                    else:
                        self.tt("dve", ost[bs][0:64, n, :], OO[0:64, :], of[bs][0:64, n, :], ALU.add, [("of", bs)], [("ost", bs), ("B", 4)])
                if part == 'tail':
                    if dirn == 0:
                        self.st(SC["of"][t0:t0 + 128, :].rearrange("(n p) c -> p n c", p=64), ost[bs][0:64], [("ost", bs)], ["of"])
                    else:
                        o2 = ost[bs][0:64].rearrange("p n c -> p (n c)")
                        self.tt("dve", sq[0:64, :], o2, o2, ALU.mult, [("ost", bs)], ["sq"])
                        self.P.op("dve", lambda e: e.reduce_sum(out=ssq[0:64, :], in_=sq[0:64, :].rearrange("p (a b) -> p a b", b=128), axis=AX.X), ["sq"], ["ssq"])
                        self.act(rs8[0:64, :], ssq[0:64, :], AF.Ln, ["ssq", "eps_rms"], ["rs8"], bias=self.eps_rms[0:64, 0:1], scale=1.0 / 128)
                        self.act(rs8[0:64, :], rs8[0:64, :], AF.Exp, ["rs8"], ["rs8"], scale=-0.5)
                        o3 = ost[bs][0:64].rearrange("p n (h v) -> p (n h) v", v=128)
                        self.tt("dve", o3, o3, rs8[0:64, :, None].to_broadcast([64, 8, 128]), ALU.mult, ["rs8"], [("ost", bs)])
                        self.tt("dve", o3, o3, gn[0:64, None, :].to_broadcast([64, 8, 128]), ALU.mult, ["gn"], [("ost", bs)])
                        self.tt("dve", ob16[0:64], ost[bs][0:64], grt[bs][0:64], ALU.mult, [("ost", bs), ("grt", bs)], ["ob16"])
                        TP = bk[7][:, 0:256].bitcast(BF16).rearrange("p (b t) -> p b t", b=4)
                        for n in range(2):
                            for b_ in range(4):
                                self.tr(TP[:, b_, n * 64:(n + 1) * 64], ob16[0:64, n, b_ * 128:(b_ + 1) * 128], self.ident_b[0:64, 0:64], ["ob16", "ident_b"], [("B", 7)])
                        self.cp("act", gst[bs], TP, [], [("gst", bs), ("B", 7)])
                        self.st(SC["glT"][:, t0:t0 + 128].rearrange("(b p) t -> p b t", p=128), gst[bs], [("gst", bs)], ["glT"])

            for st_ in "abc":
                prep(0, groups[0], st_)
            for gi, g in enumerate(groups):
                nxt = gi + 1 < len(groups)
                if nxt:
                    prep(gi + 1, groups[gi + 1], "a")
                chunks(gi, g, 0)
                if nxt:
                    prep(gi + 1, groups[gi + 1], "b")
                chunks(gi, g, 1)
                if nxt:
                    prep(gi + 1, groups[gi + 1], "c")
                chunks(gi, g, "tail")
        P.barrier()

    def phase5(self, l):
        A, P, I, SC = self.A, self.P, self.inp, self.scr
        A.reset(self.const_top)
        bk = self.banks
        self.affT = A.alloc([S], F32)
        self.p6_base = A.off
        wg = A.alloc([8, 3072], BF16)
        wup = [A.alloc([4, D], BF16) for _ in range(3)]
        wout = A.alloc([8, D], BF16)
        rw = A.alloc([8, 16], F32)
        bgr = A.alloc([3072], BF16)
        ones1 = A.alloc([128], BF16)
        gam = A.alloc([D], F32)
        bet = A.alloc([D], F32)
        self.ln_tmp = A.alloc([D], F32)
        ht = [A.alloc([D], F32) for _ in range(2)]
        hb = A.alloc([D], BF16)
        hT = A.alloc([8, 128], BF16)
        gates = A.alloc([3072], F32)
        xT = [[A.alloc([4, 128], BF16)] * 2 for _ in range(3)]
        mrg = A.alloc([D], F32)
        tmpm = A.alloc([D], F32)
        mb = A.alloc([D], BF16)
        mT = A.alloc([8, 128], BF16)
        h1b = A.alloc([D], BF16)
        h1a = A.alloc([D], F32)
        h1buf = A.alloc([D], F32)
        h1T = A.alloc([8, 128], F32)
        st6 = A.alloc([2, 6], F32); mv = A.alloc([2], F32); rstd = A.alloc([1], F32); nb = A.alloc([1], F32)
        rmax = A.alloc([1], F32); rsum = A.alloc([1], F32); aff = A.alloc([16], F32)
        self.load_cast(wg, I["w_in"][l][:, C_GATE:C_GATE + 3072].rearrange("(c p) n -> p c n", p=128), 3072, "wg")
        for i, nm in enumerate(("w_up_a", "w_up_b", "w_up_c")):
            self.load_cast(wup[i], I[nm][l].rearrange("(c p) n -> p c n", p=128), D, ("wup", i))
        self.load_cast(wout, I["w_out"][l].rearrange("(c p) n -> p c n", p=128), D, "wout")
        self.ld(rw, I["router_w"][l].rearrange("(c p) e -> p c e", p=128), (), ["rw"])
        for a_ in range(0, 3072, 1536):
            self.P.dma(lambda e, a_=a_: e.dma_start(out=bgr[0:1, a_:a_ + 1536], in_=I["b_gate"][l:l + 1, a_:a_ + 1536]), (), ["bgr"], q="pool")
        self.memset("pool", ones1, 1.0, ["ones1"])
        self.ld(gam, I["ln1_g"][l][None, :].to_broadcast([128, D]), (), ["lnpar"])
        self.ld(bet, I["ln1_b"][l][None, :].to_broadcast([128, D]), (), ["lnpar"])
        srcs = ("pmT", "mlT", "glT")
        pT = bk[0][:, :].bitcast(BF16).rearrange("p (a b) -> p a b", a=8)
        nt = (self.ns_limit or NS) * 4

        def X(t, stage):
            bs = t % 2
            r0 = t * 128
            if stage == 1:
                self.ld(ht[bs], SC["hbuf"][r0:r0 + 128, :], ["hbuf"], [("ht", bs)])
                for i in range(3):
                    self.ld(xT[i][bs], SC[srcs[i]][:, r0:r0 + 128].rearrange("(c p) t -> p c t", p=128), [srcs[i]], [("xT", i)])
                self.cp("act", hb, ht[bs], [("ht", bs)], ["hb"])
                for c in range(8):
                    self.tr(pT[:, c, :], hb[:, c * 128:(c + 1) * 128], self.ident_b, ["hb", "ident_b"], [("B", 0)])
                self.cp("dve", hT, pT, [], ["hT", ("B", 0)])
            if stage == 2:
                for gI in range(6):
                    pb = bk[1 + gI % 2]
                    pk = ("B", 1 + gI % 2)
                    cs = slice(gI * 512, (gI + 1) * 512)
                    for c in range(8):
                        self.mm(pb[:, :], hT[:, c, :], wg[:, c, cs], c == 0, False, ["hT", "wg"], [pk])
                    self.mm(pb[:, :], ones1[0:1, :], bgr[0:1, cs], False, True, ["ones1", "bgr"], [pk])
                    self.act(gates[:, cs], pb[:, :], AF.Sigmoid, [], [("gates", gI), pk])
            if stage == 3:
                for i in range(3):
                    for g2 in range(2):
                        pb = bk[3 + (i * 2 + g2) % 2]
                        pk = ("B", 3 + (i * 2 + g2) % 2)
                        cs = slice(g2 * 512, (g2 + 1) * 512)
                        for c in range(4):
                            self.mm(pb[:, :], xT[i][bs][:, c, :], wup[i][:, c, cs], c == 0, c == 3, [("xT", i), ("wup", i)], [pk])
                        gsl = gates[:, i * D + g2 * 512:i * D + (g2 + 1) * 512]
                        gk_ = ("gates", i * 2 + g2)
                        if i == 0:
                            self.tt("dve", mrg[:, cs], pb[:, :], gsl, ALU.mult, [gk_], [("mrg", g2), pk])
                        else:
                            self.tt("dve", tmpm[:, cs], pb[:, :], gsl, ALU.mult, [gk_], [("tmpm", g2), pk])
                            if i == 1:
                                self.tt("pool", mrg[:, cs], mrg[:, cs], tmpm[:, cs], ALU.add, [("tmpm", g2)], [("mrg", g2)])
                            else:
                                self.tt("pool", mb[:, cs], mrg[:, cs], tmpm[:, cs], ALU.add, [("tmpm", g2), ("mrg", g2)], [("mb", g2)])
            if stage == 4:
                for c in range(8):
                    self.tr(pT[:, c, :], mb[:, c * 128:(c + 1) * 128], self.ident_b, [("mb", c // 4), "ident_b"], [("B", 0)])
                self.cp("dve", mT, pT, [], ["mT", ("B", 0)])
                for g2 in range(2):
                    pb = bk[5 + g2]
                    pk = ("B", 5 + g2)
                    cs = slice(g2 * 512, (g2 + 1) * 512)
                    for c in range(8):
                        self.mm(pb[:, :], mT[:, c, :], wout[:, c, cs], c == 0, c == 7, ["mT", "wout"], [pk])
                    self.stt("dve", ht[bs][:, cs], ht[bs][:, cs], DN_ALPHA, pb[:, :], ALU.mult, ALU.add, [], [("ht", bs), pk])

        def Y(t, stage):
            bs = t % 2
            r0 = t * 128
            h1 = h1buf
            lg = bk[7]
            if stage == 1:
                self.layernorm_tile(ht[bs], h1, h1b, gam, bet, st6, mv, rstd, nb, "ln", [("ht", bs)], ["h1"], ["h1b"])
                self.st(SC["h1"][r0:r0 + 128, :], h1, ["h1"], ["h1d"])
                self.st(SC["h1b"][r0:r0 + 128, :], h1b, ["h1b"], ["h1bd"])
                self.act(h1a, h1, AF.Identity, ["h1"], ["h1a"], scale=DN_ALPHA)
                self.st(SC["acc"][r0:r0 + 128, :], h1a, ["h1a"], ["acc"])
            if stage == 2:
                pTf = [bk[7][:, :].rearrange("p (a b) -> p a b", a=4), bk[0][:, :].rearrange("p (a b) -> p a b", a=4)]
                for c in range(8):
                    self.tr(pTf[c // 4][:, c % 4, :], h1[:, c * 128:(c + 1) * 128], self.ident_f, ["h1", "ident_f"], [("B", 7 if c < 4 else 0)])
                self.cp("dve", h1T[:, 0:4, :], pTf[0], [], [("h1T", 0), ("B", 7)])
                self.cp("dve", h1T[:, 4:8, :], pTf[1], [], [("h1T", 1), ("B", 0)])
                lg = bk[7]
                for c in range(8):
                    self.mm(lg[:, 0:16], h1T[:, c, :], rw[:, c, :], c == 0, c == 7, [("h1T", c // 4), "rw"], [("B", 7)])
            if stage == 3:
                self.P.op("dve", lambda e: e.reduce_max(out=rmax, in_=lg[:, 0:16], axis=AX.X), [], ["rmax", ("B", 7)])
                self.ts("dve", rmax, rmax, -1.0, None, ALU.mult, None, ["rmax"], ["rmax"])
                self.act(aff, lg[:, 0:16], AF.Exp, ["rmax"], ["aff", "rsum", ("B", 7)], bias=rmax[:, 0:1], scale=1.0, accum_out=rsum[:, 0:1])
                self.P.op("dve", lambda e: e.reciprocal(out=rsum, in_=rsum), ["rsum"], ["rsum"])
                self.ts("dve", aff, aff, rsum[:, 0:1], None, ALU.mult, None, ["rsum"], ["aff"])
                self.tr(bk[7][0:16, 128:256], aff, self.ident_f, ["aff", "ident_f"], [("B", 7)])
                self.cp("dve", self.affT[0:16, r0:r0 + 128], bk[7][0:16, 128:256], [], ["affT", ("B", 7)])

        for st_ in (1, 2, 3, 4):
            X(0, st_)
        for t in range(nt):
            nxt = t + 1 < nt
            if nxt:
                X(t + 1, 1)
            Y(t, 1)
            if nxt:
                X(t + 1, 2)
            Y(t, 2)
            if nxt:
                X(t + 1, 3)
            Y(t, 3)
            if nxt:
                X(t + 1, 4)
        P.barrier()

    def phase6(self, l):
        A, P = self.A, self.P
        A.reset(self.p6_base)
        bk = self.banks
        CAP = S // 8
        NJ = CAP // 128
        work = A.alloc([S], F32)
        vals = A.alloc([CAP], F32)
        idxu = A.alloc([CAP], U32)
        idxf = A.alloc([CAP], F32)
        af = self.affT[0:16, :]
        cur = af
        curk = "affT"
        for it in range(CAP // 8):
            v8 = vals[0:16, it * 8:(it + 1) * 8]
            self.P.op("dve", lambda e, v8=v8, cur=cur: e.max(out=v8, in_=cur), [curk], ["vals"])
            i8 = idxu[0:16, it * 8:(it + 1) * 8]
            self.P.op("dve", lambda e, v8=v8, cur=cur, i8=i8: e.max_index(out=i8, in_max=v8, in_values=cur), [curk, "vals"], ["idxu"])
            self.P.op("dve", lambda e, v8=v8, cur=cur: e.match_replace(out=work[0:16, :], in_to_replace=v8, in_values=cur, imm_value=-1.0), [curk, "vals"], ["work"])
            cur = work[0:16, :]
            curk = "work"
        self.cp("dve", idxf[0:16, :], idxu[0:16, :], ["idxu"], ["idxf"])
        for j in range(NJ):
            pj = bk[j % 2]
            pk = ("B", j % 2)
            self.tr(pj[:, 0:16], idxf[0:16, j * 128:(j + 1) * 128], self.ident_f[0:16, 0:16], ["idxf", "ident_f"], [pk])
            self.tr(pj[:, 16:32], vals[0:16, j * 128:(j + 1) * 128], self.ident_f[0:16, 0:16], ["vals", "ident_f"], [pk])
            self.cp("dve", self.IDX[:, j, :], pj[:, 0:16], [], ["IDX", pk])
            self.cp("dve", self.GATE[:, j, :], pj[:, 16:32], [], ["GATE", pk])
        P.barrier()

    def phase7(self, l):
        A, P, I, SC = self.A, self.P, self.inp, self.scr
        A.reset(self.const_top)
        bk = self.banks
        CAP = S // 8
        NJ = CAP // 128
        wgh = [A.alloc([8, 1024], BF16) for _ in range(2)]
        wuh = [A.alloc([8, 1024], BF16) for _ in range(2)]
        wdt = A.alloc([16, D], BF16)
        xg = A.alloc([NJ, D], BF16)
        xeT = A.alloc([8, CAP], BF16)
        hidT = A.alloc([16, CAP], BF16)
        sil = [A.alloc([512], F32) for _ in range(2)]
        ysb = [A.alloc([D], F32) for _ in range(2)]
        pT = bk[0][:, :].bitcast(BF16).rearrange("p (a b) -> p a b", a=8)
        nexp = self.ns_limit and min(16, self.ns_limit * 2) or 16
        NH = (CAP + 511) // 512
        HW_ = CAP // NH

        def ld_gu(ex, hf):
            fs = slice(hf * 1024, (hf + 1) * 1024)
            self.load_cast(wgh[hf], I["exp_w_gate"][l, ex][:, fs].rearrange("(c p) n -> p c n", p=128), 1024, ("wg", hf))
            self.load_cast(wuh[hf], I["exp_w_up"][l, ex][:, fs].rearrange("(c p) n -> p c n", p=128), 1024, ("wu", hf))

        def ld_d(ex):
            self.load_cast(wdt, I["exp_w_down"][l, ex].rearrange("(c p) n -> p c n", p=128), D, "wdt")

        def gathers(ex):
            for j in range(NJ):
                self.P.dma(lambda e, j=j, ex=ex: e.indirect_dma_start(
                    out=xg[:, j, :], out_offset=None, in_=SC["h1b"][:, :],
                    in_offset=bass.IndirectOffsetOnAxis(ap=self.IDX[:, j, ex:ex + 1], axis=0)),
                    ["IDX", "h1bd"], [("xg", j)], q="pool")

        ld_gu(0, 0)
        ld_gu(0, 1)
        gathers(0)
        ld_d(0)
        cnt = 0
        for ex in range(nexp):
            for j in range(NJ):
                for c in range(8):
                    self.tr(pT[:, c, :], xg[:, j, c * 128:(c + 1) * 128], self.ident_b, [("xg", j), "ident_b"], [("B", 0)])
                self.cp("dve", xeT[:, :, j * 128:(j + 1) * 128], pT, [], ["xeT", ("B", 0)])
            for fb in range(16):
                fh = fb // 8
                fs = slice((fb % 8) * 128, (fb % 8 + 1) * 128)
                for hf in range(NH):
                    ss_ = slice(hf * HW_, (hf + 1) * HW_)
                    gb, ub = bk[1 + cnt % 2], bk[3 + cnt % 2]
                    gk_, uk_ = ("B", 1 + cnt % 2), ("B", 3 + cnt % 2)
                    sl_ = sil[cnt % 2]
                    sk_ = ("sil", cnt % 2)
                    cnt += 1
                    for c in range(8):
                        self.mm(gb[:, 0:HW_], wgh[fh][:, c, fs], xeT[:, c, ss_], c == 0, c == 7, [("wg", fh), "xeT"], [gk_])
                    for c in range(8):
                        self.mm(ub[:, 0:HW_], wuh[fh][:, c, fs], xeT[:, c, ss_], c == 0, c == 7, [("wu", fh), "xeT"], [uk_])
                    self.act(sl_[:, 0:HW_], gb[:, 0:HW_], AF.Silu, [], [sk_, gk_])
                    self.tt("dve", hidT[:, fb, ss_], ub[:, 0:HW_], sl_[:, 0:HW_], ALU.mult, [sk_], ["hidT", uk_])
                if fb % 8 == 7 and ex + 1 < nexp:
                    ld_gu(ex + 1, fh)
            if ex + 1 < nexp:
                gathers(ex + 1)
            for j in range(NJ):
                yb = ysb[j % 2]
                yk = ("ysb", j % 2)
                for g2 in range(2):
                    pb = bk[5 + g2]
                    pk = ("B", 5 + g2)
                    cs = slice(g2 * 512, (g2 + 1) * 512)
                    for fb in range(16):
                        self.mm(pb[:, :], hidT[:, fb, j * 128:(j + 1) * 128], wdt[:, fb, cs], fb == 0, fb == 15, ["hidT", "wdt"], [pk])
                    self.act(yb[:, cs], pb[:, :], AF.Identity, ["GATE"], [yk, pk], scale=self.GATE[:, j, ex:ex + 1])
                self.P.dma(lambda e, j=j, ex=ex, yb=yb: e.indirect_dma_start(
                    out=SC["acc"][:, :], out_offset=bass.IndirectOffsetOnAxis(ap=self.IDX[:, j, ex:ex + 1], axis=0),
                    in_=yb, in_offset=None, compute_op=ALU.add),
                    ["IDX", yk], ["acc"], q="pool")
            if ex + 1 < nexp:
                ld_d(ex + 1)
        P.barrier()

    def phase8(self, l, final):
        A, P, I, SC = self.A, self.P, self.inp, self.scr
        A.reset(self.const_top)
        gam = A.alloc([D], F32)
        bet = A.alloc([D], F32)
        self.ln_tmp = A.alloc([D], F32)
        at = [A.alloc([4, D], F32) for _ in range(2)]
        st6 = A.alloc([2, 6], F32); mv = A.alloc([2], F32); rstd = A.alloc([1], F32); nb = A.alloc([1], F32)
        self.ld(gam, I["ln2_g"][l][None, :].to_broadcast([128, D]), (), ["lnpar"])
        self.ld(bet, I["ln2_b"][l][None, :].to_broadcast([128, D]), (), ["lnpar"])
        dst = self.out if final else SC["hbuf"]
        for s_ in range(self.ns_limit or NS):
            bs = s_ % 2
            t0 = s_ * 512
            ak = ("at", bs)
            self.ld(at[bs], SC["acc"][t0:t0 + 512, :].rearrange("(j p) d -> p j d", p=128), ["acc"], [ak])
            for j in range(4):
                self.layernorm_tile(at[bs][:, j, :], at[bs][:, j, :], None, gam, bet, st6, mv, rstd, nb, "ln", [ak], [ak], None)
            self.st(dst[t0:t0 + 512, :].rearrange("(j p) d -> p j d", p=128), at[bs], [ak], ["dst"], is_out=final)
        P.barrier()

    def build(self):
        self.declare()
        self.consts()
        self.phase0()
        for l in self.layers:
            self.phase1(l)
            self.phase2(l)
            self.phase3(l)
            self.phase4(l)
            self.phase5(l)
            self.phase6(l)
            self.phase7(l)
            self.phase8(l, final=(l == self.layers[-1]))
        return self.P.finalize()

def _consts():
    half = 16
    freq = (10000.0 ** (-np.arange(half, dtype=np.float32) / half)).astype(np.float32)
    tp = np.arange(64)[:, None]
    ii = np.arange(64)[None, :]
    tri = np.stack([tp <= ii, tp > ii, tp >= ii, tp < ii]).astype(np.float32)
    return {"c_ident": np.eye(128, dtype=np.float32), "c_freq": freq, "c_tri": tri}


_CACHE = {}


def kernel(**inputs):
    if "nc" not in _CACHE:
        k = K()
        _CACHE["nc"] = k.build()
        _CACHE["names"] = list(k.inp.keys())
    nc = _CACHE["nc"]
    cst = _consts()
    maps = []
    for b in range(2):
        m = {}
        for n in _CACHE["names"]:
            if n == "x":
                m[n] = np.ascontiguousarray(inputs["x"][b], dtype=np.float32)
            elif n == "positions":
                m[n] = np.ascontiguousarray(inputs["positions"][b], dtype=np.int32)
            elif n in cst:
                m[n] = cst[n]
            else:
                m[n] = np.ascontiguousarray(inputs[n], dtype=np.float32)
        maps.append(m)
    res = run_bass_kernel_spmd(nc, maps, core_ids=[0, 1])
    return np.stack([np.asarray(res.results[b]["out"], dtype=np.float32) for b in range(2)], axis=0)
```

```python
from contextlib import ExitStack
import math
import numpy as np
import concourse.bass as bass
import concourse.mybir as mybir
from concourse.bass_utils import run_bass_kernel_spmd

F32 = mybir.dt.float32
BF16 = mybir.dt.bfloat16
I32 = mybir.dt.int32
U32 = mybir.dt.uint32
AF = mybir.ActivationFunctionType
ALU = mybir.AluOpType
AX = mybir.AxisListType

COMPUTE = ("pe", "act", "dve", "pool")
DMAQ_K = 8


def dsize(dt):
    return {F32: 4, BF16: 2, I32: 4, U32: 4}[dt]


class Op:
    __slots__ = ("eng", "fn", "deps", "is_dma", "idx", "marked", "done", "q")

    def __init__(self, eng, fn, is_dma=False, q=None):
        self.eng = eng
        self.fn = fn
        self.deps = []
        self.is_dma = is_dma
        self.idx = None
        self.marked = False
        self.done = None
        self.q = q


class Prog:
    def __init__(self):
        self.nc = bass.Bass("TRN2", target_bir_lowering=False)
        self.stack = ExitStack()
        self.ops = {e: [] for e in ("pe", "act", "dve", "pool", "sp")}
        self.reg = {}
        self.dma_count = {"sp": 0, "pool": 0}
        self.dma_ops = {"sp": [], "pool": []}
        self.out_dmas = []
        self.same_engine_sync = True
        self.debug_out = set()

    def sb(self, name, shape, dt):
        return self.stack.enter_context(self.nc.sbuf_tensor(name, list(shape), dt))

    def ps(self, name, shape, dt):
        return self.stack.enter_context(self.nc.psum_tensor(name, list(shape), dt))

    def dram(self, name, shape, dt, kind="Internal"):
        if kind == "Internal" and name in self.debug_out:
            kind = "ExternalOutput"
        return self.nc.dram_tensor(name, list(shape), dt, kind=kind).ap()

    def _deps(self, op, reads, writes):
        for k in reads:
            st = self.reg.get(k)
            if st is None:
                st = self.reg[k] = [None, []]
            if st[0] is not None:
                op.deps.append(st[0])
            st[1].append(op)
        for k in writes:
            st = self.reg.get(k)
            if st is None:
                st = self.reg[k] = [None, []]
            if st[0] is not None:
                op.deps.append(st[0])
            for r in st[1]:
                if r is not op:
                    op.deps.append(r)
            st[0] = op
            st[1] = []

    def op(self, eng, fn, reads=(), writes=()):
        o = Op(eng, fn)
        self._deps(o, reads, writes)
        self.ops[eng].append(o)
        return o

    def dma(self, fn, reads=(), writes=(), q="sp", is_out=False):
        o = Op(q, fn, is_dma=True, q=q)
        o.idx = self.dma_count[q]
        self.dma_count[q] += 1
        self.dma_ops[q].append(o)
        self._deps(o, reads, writes)
        self.ops[q].append(o)
        if is_out:
            self.out_dmas.append(o)
        return o

    def barrier(self):
        lasts = []
        for e in COMPUTE:
            for o in reversed(self.ops[e]):
                if not o.is_dma and o.fn is not None:
                    lasts.append(o)
                    break
        for q in ("sp", "pool"):
            lasts.extend(self.dma_ops[q][-DMAQ_K:])
        for e in ("pe", "act", "dve", "pool", "sp"):
            o = Op(e, None)
            o.deps = list(lasts)
            self.ops[e].append(o)
        self.reg = {}

    def finalize(self):
        nc = self.nc
        sync_same = self.same_engine_sync

        def skip(d, o):
            return (not d.is_dma) and (not o.is_dma) and d.eng == o.eng and o.fn is not None and (d.eng == "pe" or not sync_same)

        for e, lst in self.ops.items():
            for o in lst:
                for d in o.deps:
                    if d.is_dma or skip(d, o):
                        continue
                    d.marked = True
        sems = {}
        for e in COMPUTE:
            sems[e] = self.stack.enter_context(nc.semaphore("s_" + e))
            n = 0
            for o in self.ops[e]:
                if o.is_dma or o.fn is None:
                    continue
                if o.marked:
                    n += 1
                    o.done = (sems[e], n)
        dsems = {}
        for q in ("sp", "pool"):
            if self.dma_count[q] == 0:
                continue
            dsems[q] = [self.stack.enter_context(nc.semaphore("d_%s%d" % (q, i))) for i in range(DMAQ_K)]
            for o in self.dma_ops[q]:
                o.done = (dsems[q][o.idx % DMAQ_K], 16 * (o.idx // DMAQ_K + 1))
        block = self.stack.enter_context(nc.Block())

        def emit(ename, e):
            waited = {}
            for o in self.ops[ename]:
                need = {}
                for d in o.deps:
                    if d.done is None or skip(d, o):
                        continue
                    s, v = d.done
                    if need.get(s.num, (None, 0))[1] < v:
                        need[s.num] = (s, v)
                if o.is_dma and o.idx >= DMAQ_K:
                    s = dsems[o.q][o.idx % DMAQ_K]
                    v = 16 * (o.idx // DMAQ_K)
                    if need.get(s.num, (None, 0))[1] < v:
                        need[s.num] = (s, v)
                for sn, (s, v) in need.items():
                    if waited.get(sn, 0) < v:
                        e.wait_ge(s, v)
                        waited[sn] = v
                if o.fn is None:
                    continue
                ins = o.fn(e)
                if o.is_dma:
                    ins.then_inc(o.done[0], 16)
                elif o.marked:
                    ins.then_inc(o.done[0], 1)
            if ename == "sp":
                for o in self.out_dmas:
                    s, v = o.done
                    if waited.get(s.num, 0) < v:
                        e.wait_ge(s, v)
                        waited[s.num] = v

        @block.tensor
        def _(e):
            emit("pe", e)

        @block.scalar
        def _(e):
            emit("act", e)

        @block.vector
        def _(e):
            emit("dve", e)

        @block.gpsimd
        def _(e):
            emit("pool", e)

        @block.sync
        def _(e):
            emit("sp", e)

        self.stack.close()
        return nc


class Arena:
    def __init__(self, P, kbytes):
        self.words = kbytes * 256
        self.t = P.sb("arena", [128, self.words], F32)
        self.off = 0

    def reset(self, to=0):
        self.off = to

    def alloc(self, free_shape, dt, parts=128, p0=0):
        n = 1
        for s in free_shape:
            n *= s
        nb = n * dsize(dt)
        w = (nb + 31) // 32 * 8
        assert self.off + w <= self.words, ("arena overflow", self.off, w, self.words)
        v = self.t[p0:p0 + parts, self.off:self.off + w]
        self.off += w
        if dt != F32:
            v = v.bitcast(dt)
        v = v[:, 0:n]
        if len(free_shape) == 2:
            v = v.rearrange("p (a b) -> p a b", a=free_shape[0])
        elif len(free_shape) == 3:
            v = v.rearrange("p (a b c) -> p a b c", a=free_shape[0], b=free_shape[1])
        return v


S = 8192
D = 1024
NT = S // 128
NS = S // 512
POOL_WINDOWS = (2, 4, 8, 16)
LN_EPS = 1e-5
RMS_EPS = 1e-6
DN_ALPHA = 4 ** 0.25
QSCALE = 96 ** -0.5
TWO_PI = 2.0 * math.pi
CW1 = 6.28125
CW2 = TWO_PI - CW1

C_UP, C_CQ, C_CKV, C_KR, C_GQ, C_GK, C_GV, C_GR, C_GD, C_GATE = 0, 512, 896, 1152, 1184, 1440, 1696, 2208, 2720, 2752


class K:
    def __init__(self, debug_out=(), layers=(0, 1), phases=None, ns_limit=None, skip=(), no_big=False):
        self.ns_limit = ns_limit
        self.skip = set(skip)
        self.no_big = no_big
        self.P = P = Prog()
        P.debug_out = set(debug_out)
        self.layers = layers
        self.phases = phases
        self.nc = P.nc
        self.A = Arena(P, 204)
        self.banks = [P.ps("bank%d" % i, [128, 512], F32) for i in range(8)]
        self.inp = {}
        self.scr = {}

    def din(self, name, shape, dt=F32):
        self.inp[name] = self.P.dram(name, shape, dt, kind="ExternalInput")
        return self.inp[name]

    def scratch(self, name, shape, dt):
        self.scr[name] = self.P.dram(name, shape, dt)
        return self.scr[name]

    def mm(self, out, lhsT, rhs, start, stop, reads, writes):
        self.P.op("pe", lambda e: e.matmul(out, lhsT=lhsT, rhs=rhs, start=start, stop=stop), reads, writes)

    def tr(self, out, in_, ident, reads, writes):
        self.P.op("pe", lambda e: e.transpose(out=out, in_=in_, identity=ident), reads, writes)

    def act(self, out, in_, func, reads, writes, bias=None, scale=None, accum_out=None):
        kw = {}
        if bias is not None:
            kw["bias"] = bias
        if scale is not None:
            kw["scale"] = scale
        if accum_out is not None:
            kw["accum_out"] = accum_out
        self.P.op("act", lambda e: e.activation(out=out, in_=in_, func=func, **kw), reads, writes)

    def ts(self, eng, out, in0, s1, s2, op0, op1, reads, writes):
        if op1 is None:
            self.P.op(eng, lambda e: e.tensor_scalar(out=out, in0=in0, scalar1=s1, scalar2=None, op0=op0), reads, writes)
        else:
            self.P.op(eng, lambda e: e.tensor_scalar(out=out, in0=in0, scalar1=s1, scalar2=s2, op0=op0, op1=op1), reads, writes)

    def tt(self, eng, out, in0, in1, op, reads, writes):
        self.P.op(eng, lambda e: e.tensor_tensor(out=out, in0=in0, in1=in1, op=op), reads, writes)

    def stt(self, eng, out, in0, scalar, in1, op0, op1, reads, writes):
        self.P.op(eng, lambda e: e.scalar_tensor_tensor(out=out, in0=in0, scalar=scalar, in1=in1, op0=op0, op1=op1), reads, writes)

    def cp(self, eng, out, in_, reads, writes):
        if eng == "act":
            self.P.op("act", lambda e: e.copy(out=out, in_=in_), reads, writes)
        else:
            self.P.op(eng, lambda e: e.tensor_copy(out=out, in_=in_), reads, writes)

    def memset(self, eng, ap, val, writes):
        self.P.op(eng, lambda e: e.memset(ap, val), (), writes)

    def ld(self, out, in_, reads, writes, q="sp", slow=False):
        if slow:
            self.P.dma(lambda e: e.dma_start(out=out, in_=in_, allow_slow_non_contiguous=True), reads, writes, q=q)
        else:
            self.P.dma(lambda e: e.dma_start(out=out, in_=in_), reads, writes, q=q)

    def st(self, out, in_, reads, writes, q="pool", is_out=False):
        self.P.dma(lambda e: e.dma_start(out=out, in_=in_), reads, writes, q=q, is_out=is_out)

    def declare(self):
        din = self.din
        din("x", [S, D])
        din("positions", [S], I32)
        din("ln0_g", [D]); din("ln0_b", [D])
        din("w_in", [2, D, 5824]); din("b_gate", [2, 3072])
        din("pool_w", [2, 4, 128, 128]); din("pool_scale", [2, 512]); din("w_up_a", [2, 512, D])
        din("mla_q_norm", [2, 384]); din("mla_w_uq", [2, 384, 768]); din("mla_kv_norm", [2, 256])
        din("mla_w_ukv", [2, 256, 1024]); din("w_up_b", [2, 512, D])
        din("gla_w_dec", [2, 2, 16, 256]); din("gla_b_dec", [2, 2, 256]); din("gla_norm", [2, 128])
        din("w_up_c", [2, 512, D]); din("w_out", [2, D, D])
        din("ln1_g", [2, D]); din("ln1_b", [2, D]); din("router_w", [2, D, 16])
        if not self.no_big:
            din("exp_w_gate", [2, 16, D, 2048]); din("exp_w_up", [2, 16, D, 2048]); din("exp_w_down", [2, 16, 2048, D])
        din("ln2_g", [2, D]); din("ln2_b", [2, D])
        din("c_ident", [128, 128])
        din("c_freq", [16])
        din("c_tri", [4, 64, 64])
        self.out = self.P.dram("out", [S, D], F32, kind="ExternalOutput")
        sc = self.scratch
        sc("hbuf", [S, D], F32)
        sc("cosF", [32, S], F32); sc("sinF", [32, S], F32)
        sc("uT", [512, S], F32)
        sc("gqT", [256, S], F32); sc("gkT", [256, S], F32); sc("gdT", [32, S], F32)
        sc("gk_tok", [S, 256], F32); sc("gv", [S, 512], BF16); sc("gr", [S, 512], BF16)
        sc("QT", [8, 96, S], BF16); sc("KT", [8, 64, S], BF16); sc("KRT", [32, S], BF16); sc("V", [S, 512], BF16)
        sc("pmT", [512, S], BF16); sc("mlT", [512, S], BF16); sc("glT", [512, S], BF16)
        sc("h1", [S, D], F32); sc("h1b", [S, D], BF16); sc("acc", [S, D], F32)

    def consts(self):
        A = self.A
        self.ident_f = A.alloc([128], F32)
        self.ident_b = A.alloc([128], BF16)
        self.eps_ln = A.alloc([1], F32)
        self.eps_rms = A.alloc([1], F32)
        self.cosT = A.alloc([NT, 16], F32)
        self.sinT = A.alloc([NT, 16], F32)
        self.ld(self.ident_f, self.inp["c_ident"][:, :], (), ["ident_f"])
        self.cp("dve", self.ident_b, self.ident_f, ["ident_f"], ["ident_b"])
        self.memset("pool", self.eps_ln, LN_EPS, ["eps_ln"])
        self.memset("pool", self.eps_rms, RMS_EPS, ["eps_rms"])
        self.IDX = A.alloc([S // 1024, 16], I32)
        self.GATE = A.alloc([S // 1024, 16], F32)
        self.const_top = A.off

    def range_reduce_sin(self, ang, tmp, tmpi, res, shift, k_ang, k_tmp, k_res):
        self.ts("dve", tmp, ang, 1.0 / TWO_PI, None, ALU.mult, None, [k_ang], [k_tmp])
        self.cp("dve", tmpi, tmp, [k_tmp], [k_tmp + "i"])
        self.cp("dve", tmp, tmpi, [k_tmp + "i"], [k_tmp])
        self.stt("dve", res, tmp, -CW1, ang, ALU.mult, ALU.add, [k_tmp, k_ang], [k_res])
        self.stt("dve", res, tmp, -CW2, res, ALU.mult, ALU.add, [k_tmp, k_res], [k_res])
        if shift != 0.0:
            self.ts("dve", res, res, shift, None, ALU.add, None, [k_res], [k_res])
        self.ts("dve", tmp, res, math.pi, None, ALU.is_gt, None, [k_res], [k_tmp])
        self.stt("dve", res, tmp, -TWO_PI, res, ALU.mult, ALU.add, [k_tmp, k_res], [k_res])
        self.ts("dve", res, res, math.pi, -math.pi, ALU.min, ALU.max, [k_res], [k_res])
        self.act(res, res, AF.Sin, [k_res], [k_res])

    def phase0(self):
        A, P = self.A, self.P
        A.reset(self.const_top)
        posi = A.alloc([NT], I32)
        posf = A.alloc([NT], F32)
        freq = A.alloc([16], F32)
        ang = A.alloc([NT, 16], F32)
        tmp = A.alloc([NT, 16], F32)
        tmpi = A.alloc([NT, 16], I32)
        pos = self.inp["positions"]
        pv = pos.rearrange("(j p) -> p j", p=128)
        for a in range(0, NT, 8):
            self.ld(posi[:, a:a + 8], pv[:, a:a + 8], (), ["posi"], slow=True)
        self.ld(freq, self.inp["c_freq"][None, :].to_broadcast([128, 16]), (), ["freq"])
        self.cp("dve", posf, posi, ["posi"], ["posf"])
        for j in range(NT):
            self.ts("dve", ang[:, j, :], freq, posf[:, j:j + 1], None, ALU.mult, None, ["posf", "freq"], ["angT"])
        self.range_reduce_sin(ang, tmp, tmpi, self.sinT, 0.0, "angT", "tmpT", "sinT")
        self.range_reduce_sin(ang, tmp, tmpi, self.cosT, math.pi / 2, "angT", "tmpT", "cosT")
        CH = min(2048, S)
        prow_i = A.alloc([CH], I32)
        prow = A.alloc([CH], F32)
        fcol = A.alloc([1], F32)
        scol_c = A.alloc([1], F32)
        scol_s = A.alloc([1], F32)
        angF = A.alloc([CH], F32)
        tmpF = A.alloc([CH], F32)
        tmpFi = A.alloc([CH], I32)
        resF = A.alloc([CH], F32)
        fr = self.inp["c_freq"]
        self.ld(fcol[64:80, :], fr.rearrange("(a b) -> a b", b=1), (), ["fcol"], slow=True)
        self.ld(fcol[80:96, :], fr.rearrange("(a b) -> a b", b=1), ["fcol"], ["fcol"], slow=True)
        self.memset("pool", scol_c[64:96, :], QSCALE, ["scol_c"])
        self.memset("pool", scol_s[64:96, :], QSCALE, ["scol_s"])
        self.memset("pool", scol_s[64:80, :], -QSCALE, ["scol_s"])
        sl = slice(64, 96)
        for c in range(S // CH):
            self.ld(prow_i[sl, :], pos[None, c * CH:(c + 1) * CH].to_broadcast([32, CH]), (), ["prow_i"])
            self.cp("dve", prow[sl, :], prow_i[sl, :], ["prow_i"], ["prow"])
            self.ts("dve", angF[sl, :], prow[sl, :], fcol[sl, 0:1], None, ALU.mult, None, ["prow", "fcol"], ["angF"])
            for (shift, scol, dst) in ((0.0, scol_s, "sinF"), (math.pi / 2, scol_c, "cosF")):
                self.range_reduce_sin(angF[sl, :], tmpF[sl, :], tmpFi[sl, :], resF[sl, :], shift, "angF", "tmpF", "resF")
                self.ts("dve", resF[sl, :], resF[sl, :], scol[sl, 0:1], None, ALU.mult, None, ["resF", "scol_s", "scol_c"], ["resF"])
                self.st(self.scr[dst][:, c * CH:(c + 1) * CH], resF[sl, :], ["resF"], [dst])
        P.barrier()

    def layernorm_tile(self, x, out_f, out_b, gam, bet, st6, mv, rstd, nb, key, reads, writes_f, writes_b, cast_eng="pool", norm_eng="act"):
        for c in range(2):
            self.P.op("dve", lambda e, c=c: e.bn_stats(out=st6[:, c, :], in_=x[:, c * 512:(c + 1) * 512]), reads, [(key, "st", c)])
        self.P.op("dve", lambda e: e.bn_aggr(out=mv, in_=st6.rearrange("p a b -> p (a b)")), [(key, "st", 0), (key, "st", 1)], [(key, "mv")])
        self.act(rstd, mv[:, 1:2], AF.Ln, [(key, "mv"), "eps_ln"], [(key, "rstd")], bias=self.eps_ln[:, 0:1], scale=1.0)
        self.act(rstd, rstd, AF.Exp, [(key, "rstd")], [(key, "rstd")], scale=-0.5)
        self.stt("dve", nb, mv[:, 0:1], -1.0, rstd, ALU.mult, ALU.mult, [(key, "mv"), (key, "rstd")], [(key, "nb")])
        tmpk = (key, "xn")
        if norm_eng == "act":
            self.act(self.ln_tmp, x, AF.Identity, list(reads) + [(key, "rstd"), (key, "nb")], [tmpk], bias=nb[:, 0:1], scale=rstd[:, 0:1])
        else:
            self.ts("dve", self.ln_tmp, x, mv[:, 0:1], rstd[:, 0:1], ALU.subtract, ALU.mult, list(reads) + [(key, "rstd"), (key, "mv")], [tmpk])
        self.tt("dve", self.ln_tmp, self.ln_tmp, gam, ALU.mult, [tmpk, "lnpar"], [tmpk])
        if out_f is not None:
            self.tt("dve", out_f, self.ln_tmp, bet, ALU.add, [tmpk, "lnpar"], writes_f)
            if out_b is not None:
                self.cp(cast_eng, out_b, out_f, writes_f, writes_b)
        else:
            self.tt("dve", out_b, self.ln_tmp, bet, ALU.add, [tmpk, "lnpar"], writes_b)


    def load_cast(self, dst, src, n_inner, key, reads=()):
        shp = list(dst.shape)
        if len(shp) == 3:
            C = shp[1]
            cstep = max(1, 1024 // shp[0])
            nsp = (n_inner + 2047) // 2048
            step = (n_inner + nsp - 1) // nsp
            for c0 in range(0, C, cstep):
                c1 = min(C, c0 + cstep)
                for a in range(0, n_inner, step):
                    b = min(n_inner, a + step)
                    self.P.dma(lambda e, a=a, b=b, c0=c0, c1=c1: e.dma_start(out=dst[:, c0:c1, a:b], in_=src[:, c0:c1, a:b]), reads, [key], q="pool")
        else:
            self.P.dma(lambda e: e.dma_start(out=dst, in_=src), reads, [key], q="pool")

    def phase1(self, l):
        A, P, I, SC = self.A, self.P, self.inp, self.scr
        A.reset(self.const_top)
        bk = self.banks
        win = A.alloc([8, 2752], BF16)
        wuq = A.alloc([3, 768], BF16)
        wuqr = A.alloc([3, 768], BF16)
        wukv = A.alloc([2, 1024], BF16)
        qng = A.alloc([3], F32)
        kvng = A.alloc([2], F32)
        gam = A.alloc([D], F32)
        bet = A.alloc([D], F32)
        self.ln_tmp = A.alloc([D], F32)
        xt = [A.alloc([4, D], F32) for _ in range(2)]
        hb = A.alloc([4, D], BF16)
        hT = [A.alloc([8, 512], BF16) for _ in range(2)]
        st6 = A.alloc([2, 6], F32); mv = A.alloc([2], F32); rstd = A.alloc([1], F32); nb = A.alloc([1], F32)
        ss = A.alloc([4], F32)
        cqn = A.alloc([384], BF16)
        ckvn = A.alloc([256], BF16)
        cqnT = A.alloc([3, 512], BF16)
        ckvnT = A.alloc([2, 512], BF16)
        junk = A.alloc([384], F32)
        junk2 = A.alloc([256], F32)
        st_u = A.alloc([4, 512], F32)
        st_q = A.alloc([2, 512], F32)
        st_k = A.alloc([2, 512], F32)
        st_d = A.alloc([512], F32)
        st_kt = A.alloc([4, 256], F32)
        st_v = A.alloc([4, 512], BF16)
        st_r = A.alloc([4, 512], BF16)
        st_V = A.alloc([4, 512], BF16)
        st_Q = A.alloc([8, 512], BF16)
        st_K = A.alloc([8, 512], BF16)
        st_kr = A.alloc([512], BF16)
        krt = A.alloc([32], F32)
        krb = A.alloc([128], BF16)
        rt1 = A.alloc([16], F32); rt2 = A.alloc([16], F32)
        cosF = A.alloc([512], F32); sinF = A.alloc([512], F32)
        qt1 = A.alloc([512], F32); qt2 = A.alloc([512], F32)

        self.memset("pool", krb, 0.0, ["krb"])
        w_in = I["w_in"][l]
        self.load_cast(win, w_in[:, 0:2752].rearrange("(c p) n -> p c n", p=128), 2752, "win")
        self.load_cast(wuq, I["mla_w_uq"][l].rearrange("(c p) n -> p c n", p=128), 768, "wuq")
        wq4 = I["mla_w_uq"][l].rearrange("(c p) (h x) -> p c h x", p=128, x=96)
        wuqr4 = wuqr.rearrange("p c (h x) -> p c h x", x=96)
        for c in range(3):
            self.load_cast(wuqr4[:, c, :, 64:80], wq4[:, c, :, 80:96], 16, "wuqr")
            self.load_cast(wuqr4[:, c, :, 80:96], wq4[:, c, :, 64:80], 16, "wuqr")
        self.memset("pool", wuqr4[:, :, :, 0:64], 0.0, ["wuqr"])
        self.load_cast(wukv, I["mla_w_ukv"][l].rearrange("(c p) n -> p c n", p=128), 1024, "wukv")
        self.ld(qng, I["mla_q_norm"][l].rearrange("(c p) -> p c", p=128), (), ["qng"], slow=True)
        self.ld(kvng, I["mla_kv_norm"][l].rearrange("(c p) -> p c", p=128), (), ["kvng"], slow=True)
        if l == 0:
            self.ld(gam, I["ln0_g"][None, :].to_broadcast([128, D]), (), ["lnpar"])
            self.ld(bet, I["ln0_b"][None, :].to_broadcast([128, D]), (), ["lnpar"])
        src = I["x"] if l == 0 else SC["hbuf"]
        pT = bk[0][:, 0:512].bitcast(BF16).rearrange("p (a b) -> p a b", a=8)
        fm_blocks = [("u", g, C_UP + g * 128, 128) for g in range(4)] + [("q", m, C_GQ + m * 128, 128) for m in range(2)] \
            + [("k", m, C_GK + m * 128, 128) for m in range(2)] + [("d", 0, C_GD, 32)]
        nfm = 0
        ntm = 0
        nsup = self.ns_limit or NS

        def head_load(s):
            sl = s % 2
            t0 = s * 512
            self.ld(xt[sl], src[t0:t0 + 512, :].rearrange("(j p) d -> p j d", p=128), (), [("xt", sl)])

        def head_ln(s, j):
            sl = s % 2
            xk = ("xt", sl)
            hbk = ("hb", j)
            if l == 0:
                self.layernorm_tile(xt[sl][:, j, :], xt[sl][:, j, :], hb[:, j, :], gam, bet, st6, mv, rstd, nb, "ln", [xk], [xk], [hbk], cast_eng="act")
            else:
                self.cp("act", hb[:, j, :], xt[sl][:, j, :], [xk], [hbk])

        def head_tr(s, j):
            sl = s % 2
            t0 = s * 512
            xk = ("xt", sl)
            hbk = ("hb", j)
            for c in range(8):
                self.tr(pT[:, c, :], hb[:, j, c * 128:(c + 1) * 128], self.ident_b, [hbk, "ident_b"], ["pT"])
            self.cp("dve", hT[sl][:, :, j * 128:(j + 1) * 128], pT, [], [("hT", sl), "pT"])
            if l == 0 and j == 3:
                self.st(SC["hbuf"][t0:t0 + 512, :].rearrange("(j p) d -> p j d", p=128), xt[sl], [xk], ["hbuf"])

        def head_tile(s, j):
            head_ln(s, j)
            head_tr(s, j)

        head_load(0)
        for j in range(4):
            head_tile(0, j)
        for s in range(nsup):
            sl = s % 2
            t0 = s * 512
            xk = ("xt", sl)
            if s + 1 < nsup:
                head_load(s + 1)
            self.ld(cosF[64:96, :], SC["cosF"][:, t0:t0 + 512], (), ["cosFs"])
            self.ld(sinF[64:96, :], SC["sinF"][:, t0:t0 + 512], (), ["sinFs"])
            hTk = ("hT", sl)
            for (kind, idx, c0, w) in (fm_blocks if 'fm' not in self.skip else []):
                pb = bk[1 + nfm % 2]
                pk = ("fm", nfm % 2)
                nfm += 1
                for c in range(8):
                    self.mm(pb[0:w, :], win[:, c, c0:c0 + w], hT[sl][:, c, :], c == 0, c == 7, ["win", hTk], [pk])
                if kind == "u":
                    self.cp("act", st_u[:, idx, :], pb[:, :], [], ["st_u", pk])
                elif kind == "q":
                    self.act(st_q[:, idx, :], pb[:, :], AF.Identity, [], ["st_q", pk], scale=0.125)
                elif kind == "k":
                    self.cp("dve", st_k[:, idx, :], pb[:, :], [], ["st_k", pk])
                else:
                    self.cp("dve", st_d[0:32, :], pb[0:32, :], [], ["st_d", pk])
            if 'fm' not in self.skip:
                self.st(SC["uT"][:, t0:t0 + 512].rearrange("(g p) t -> p g t", p=128), st_u, ["st_u"], ["uT"])
                self.st(SC["gqT"][:, t0:t0 + 512].rearrange("(g p) t -> p g t", p=128), st_q, ["st_q"], ["gqT"])
                self.st(SC["gkT"][:, t0:t0 + 512].rearrange("(g p) t -> p g t", p=128), st_k, ["st_k"], ["gkT"])
                self.st(SC["gdT"][:, t0:t0 + 512], st_d[0:32, :], ["st_d"], ["gdT"])
            for j in range(4):
                tt = s * 4 + j
                lh = lambda c: hT[sl][:, c, j * 128:(j + 1) * 128]

                def tmb(pb, pk, c0, w):
                    for c in range(8):
                        self.mm(pb[:, 0:w], lh(c), win[:, c, c0:c0 + w], c == 0, c == 7, ["win", hTk], [pk])

                def tm(c0, w):
                    nonlocal ntm
                    pb = bk[3 + ntm % 2]
                    pk = ("tm", ntm % 2)
                    ntm += 1
                    tmb(pb, pk, c0, w)
                    return pb, pk
                pbq, pkq = bk[6], "pqa"
                pbk, pkk_ = bk[7], "pqb"
                if s + 1 < nsup:
                    head_ln(s + 1, j)
                tmb(pbq, pkq, C_CQ, 384)
                tmb(pbk, pkk_, C_CKV, 288)
                self.act(junk[:, 0:384], pbq[:, 0:384], AF.Square, [], ["junk", "ss0", pkq], accum_out=ss[:, 0:1])
                self.act(ss[:, 1:2], ss[:, 0:1], AF.Ln, ["ss0", "eps_rms"], ["ss1"], bias=self.eps_rms[:, 0:1], scale=1.0 / 384)
                self.act(ss[:, 1:2], ss[:, 1:2], AF.Exp, ["ss1"], ["ss1"], scale=-0.5)
                self.ts("dve", cqn, pbq[:, 0:384], ss[:, 1:2], None, ALU.mult, None, ["ss1"], ["cqn", pkq])
                self.act(junk2[:, 0:256], pbk[:, 0:256], AF.Square, [], ["junk2", "ss2", pkk_], accum_out=ss[:, 2:3])
                self.act(ss[:, 3:4], ss[:, 2:3], AF.Ln, ["ss2", "eps_rms"], ["ss3"], bias=self.eps_rms[:, 0:1], scale=1.0 / 256)
                self.act(ss[:, 3:4], ss[:, 3:4], AF.Exp, ["ss3"], ["ss3"], scale=-0.5)
                self.ts("dve", ckvn, pbk[:, 0:256], ss[:, 3:4], None, ALU.mult, None, ["ss3"], ["ckvn", pkk_])
                cs = self.cosT[:, tt, :]
                sn = self.sinT[:, tt, :]
                self.cp("dve", krt, pbk[:, 256:288], [], ["krt", pkk_])
                self.tt("dve", rt1, krt[:, 0:16], cs, ALU.mult, ["krt", "cosT"], ["rt1"])
                self.tt("dve", rt2, krt[:, 16:32], sn, ALU.mult, ["krt", "sinT"], ["rt2"])
                self.tt("dve", krb[:, 0:16], rt1, rt2, ALU.subtract, ["rt1", "rt2"], ["krb"])
                self.tt("dve", rt1, krt[:, 0:16], sn, ALU.mult, ["krt", "sinT"], ["rt1"])
                self.tt("dve", rt2, krt[:, 16:32], cs, ALU.mult, ["krt", "cosT"], ["rt2"])
                self.tt("dve", krb[:, 16:32], rt1, rt2, ALU.add, ["rt1", "rt2"], ["krb"])
                pb, pk = tm(C_GK, 256)
                self.cp("act", st_kt[:, j, :], pb[:, 0:256], [], ["st_kt", pk])
                pb, pk = tm(C_GV, 512)
                self.cp("dve", st_v[:, j, :], pb[:, :], [], ["st_v", pk])
                pb, pk = tm(C_GR, 512)
                self.act(st_r[:, j, :], pb[:, :], AF.Silu, [], ["st_r", pk])
                pq = bk[5][:, 0:192].bitcast(BF16).rearrange("p (a b) -> p a b", a=3)
                for c in range(3):
                    self.tr(pq[:, c, :], cqn[:, c * 128:(c + 1) * 128], self.ident_b, ["cqn", "ident_b"], ["b5"])
                self.tt("dve", cqnT[:, :, j * 128:(j + 1) * 128], pq, qng[:, :, None].to_broadcast([128, 3, 128]), ALU.mult, ["qng"], ["cqnT", "b5"])
                pkv = bk[5][:, 256:448].bitcast(BF16).rearrange("p (a b) -> p a b", a=3)
                for c in range(2):
                    self.tr(pkv[:, c, :], ckvn[:, c * 128:(c + 1) * 128], self.ident_b, ["ckvn", "ident_b"], ["b5"])
                self.tr(pkv[:, 2, :], krb[:, :], self.ident_b, ["krb", "ident_b"], ["b5"])
                self.tt("dve", ckvnT[:, :, j * 128:(j + 1) * 128], pkv[:, 0:2, :], kvng[:, :, None].to_broadcast([128, 2, 128]), ALU.mult, ["kvng"], ["ckvnT", "b5"])
                self.cp("act", st_kr[0:32, j * 128:(j + 1) * 128], pkv[0:32, 2, :], [], ["st_kr", "b5"])
                if s + 1 < nsup:
                    head_tr(s + 1, j)
            rows = lambda name: SC[name][t0:t0 + 512, :].rearrange("(j p) d -> p j d", p=128)
            self.st(rows("gk_tok"), st_kt, ["st_kt"], ["gk_tok"])
            self.st(rows("gv"), st_v, ["st_v"], ["gv"])
            self.st(rows("gr"), st_r, ["st_r"], ["gr"])
            self.st(SC["KRT"][:, t0:t0 + 512], st_kr[0:32, :], ["st_kr"], ["KRT"])
            for h in (range(8) if 'mlaup' not in self.skip else []):
                pqa, ka = (bk[6], "pqa") if h % 2 == 0 else (bk[3], ("tm", 0))
                pqb, kb = (bk[7], "pqb") if h % 2 == 0 else (bk[4], ("tm", 1))
                for c in range(3):
                    self.mm(pqa[0:96, :], wuq[:, c, h * 96:(h + 1) * 96], cqnT[:, c, :], c == 0, c == 2, ["wuq", "cqnT"], [ka])
                for c in range(3):
                    self.mm(pqb[0:96, :], wuqr[:, c, h * 96:(h + 1) * 96], cqnT[:, c, :], c == 0, c == 2, ["wuqr", "cqnT"], [kb])
                self.act(st_Q[0:64, h, :], pqa[0:64, :], AF.Identity, [], ["st_Q", ka], scale=QSCALE)
                self.tt("dve", qt1[64:96, :], pqa[64:96, :], cosF[64:96, :], ALU.mult, ["cosFs"], ["qt1", ka])
                self.tt("dve", qt2[64:96, :], pqb[64:96, :], sinF[64:96, :], ALU.mult, ["sinFs"], ["qt2", kb])
                self.tt("pool", st_Q[64:96, h, :], qt1[64:96, :], qt2[64:96, :], ALU.add, ["qt1", "qt2"], ["st_Q"])
                pka = bk[1 + h % 2]
                pkk = ("fm", h % 2)
                for c in range(2):
                    self.mm(pka[0:64, :], wukv[:, c, h * 128:h * 128 + 64], ckvnT[:, c, :], c == 0, c == 1, ["wukv", "ckvnT"], [pkk])
                self.cp("act", st_K[0:64, h, :], pka[0:64, :], [], ["st_K", pkk])
            self.st(SC["QT"][:, :, t0:t0 + 512].rearrange("h r t -> r h t"), st_Q[0:96, :, :], ["st_Q"], ["QT"])
            self.st(SC["KT"][:, :, t0:t0 + 512].rearrange("h r t -> r h t"), st_K[0:64, :, :], ["st_K"], ["KT"])
            wv = wukv.rearrange("p c (h x) -> p c h x", x=128)
            for j in (range(4) if 'vproj' not in self.skip else []):
                pb = bk[3 + ntm % 2]
                pk = ("tm", ntm % 2)
                ntm += 1
                for c in range(2):
                    self.mm(pb[:, :].rearrange("p (h x) -> p h x", x=64), ckvnT[:, c, j * 128:(j + 1) * 128], wv[:, c, :, 64:128], c == 0, c == 1, ["wukv", "ckvnT"], [pk])
                self.cp("dve", st_V[:, j, :], pb[:, :], [], ["st_V", pk])
            self.st(rows("V"), st_V, ["st_V"], ["V"])
        P.barrier()

    def phase2(self, l):
        A, P, SC = self.A, self.P, self.scr
        A.reset(self.const_top)
        bk = self.banks
        KTs = [A.alloc([S], BF16) for _ in range(2)]
        Vh = [A.alloc([NT, 65], BF16) for _ in range(2)]
        Qs = [A.alloc([512], BF16) for _ in range(2)]
        pts = [A.alloc([512], BF16) for _ in range(4)]
        ones = A.alloc([64], F32)
        rcp = A.alloc([512], F32)
        bc = A.alloc([512], F32)
        ost = [A.alloc([512], BF16) for _ in range(2)]
        self.memset("pool", ones, 1.0, ["ones"])
        for i in range(2):
            self.memset("pool", Vh[i][:, :, 64:65], 1.0, [("Vh", i)])
        nheads = self.ns_limit or 8
        nqg = self.ns_limit or NS
        LOOK = 2
        items = [(h, qg) for h in range(nheads) for qg in range(nqg)]
        loaded_heads = set()

        def load_head(h):
            if h in loaded_heads or h >= nheads:
                return
            loaded_heads.add(h)
            hs = h % 2
            self.ld(KTs[hs][0:64, :], SC["KT"][h], ["KT"], [("KTs", hs)])
            self.ld(KTs[hs][64:96, :], SC["KRT"][:, :], ["KRT"], [("KTs", hs)])
            vsrc = SC["V"][:, h * 64:(h + 1) * 64].rearrange("(kt p) v -> p kt v", p=128)
            for k0 in range(0, NT, 8):
                self.ld(Vh[hs][:, k0:k0 + 8, 0:64], vsrc[:, k0:k0 + 8, :], ["V"], [("Vh", hs)])

        def load_q(ii):
            if ii >= len(items):
                return
            h, qg = items[ii]
            qs = ii % 2
            self.ld(Qs[qs][0:96, :], SC["QT"][h, :, qg * 512:qg * 512 + 512], ["QT"], [("Qs", qs)])

        def qk(ii, kt):
            h, qg = items[ii]
            hs, qs = h % 2, ii % 2
            g = (ii * NT + kt) % 4
            self.mm(bk[g][:, :], KTs[hs][0:96, kt * 128:(kt + 1) * 128], Qs[qs][0:96, :], True, True, [("KTs", hs), ("Qs", qs)], [("B", g)])

        def epilogue(ii):
            h, qg = items[ii]
            qs = ii % 2
            q0 = qg * 512
            ob = bk[4 + qs]
            ok = ("B", 4 + qs)
            self.P.op("dve", lambda e, ob=ob: e.reciprocal(out=rcp[64:65, :], in_=ob[64:65, :]), [], ["rcp", ok])
            self.mm(bk[6][0:64, :], ones[64:65, :], rcp[64:65, :], True, True, ["ones", "rcp"], [("B", 6)])
            self.cp("act", bc[0:64, :], bk[6][0:64, :], [], ["bc", ("B", 6)])
            self.tt("dve", ost[qs][0:64, :], ob[0:64, :], bc[0:64, :], ALU.mult, ["bc"], [("ost", qs), ok])
            self.st(SC["mlT"][h * 64:(h + 1) * 64, q0:q0 + 512], ost[qs][0:64, :], [("ost", qs)], ["mlT"])

        n4, p4_dve, p4_mm = self.phase4_parts(l)
        p4_state = {"d": 0, "m": 0}

        def p4_step(ii):
            if ii % 2 == 0 and p4_state["d"] < n4:
                p4_dve(p4_state["d"])
                p4_state["d"] += 1
            elif ii % 2 == 1 and p4_state["m"] < p4_state["d"]:
                p4_mm(p4_state["m"])
                p4_state["m"] += 1

        load_head(0)
        load_q(0)
        for kt in range(LOOK):
            qk(0, kt)
        for ii, (h, qg) in enumerate(items):
            hs, qs = h % 2, ii % 2
            if qg == 0:
                load_head(h + 1)
            load_q(ii + 1)
            ob = bk[4 + qs]
            ok = ("B", 4 + qs)
            for kt in range(NT):
                nk = kt + LOOK
                if nk < NT:
                    qk(ii, nk)
                elif ii + 1 < len(items):
                    qk(ii + 1, nk - NT)
                g = (ii * NT + kt) % 4
                pt = pts[g]
                pk = ("pt", g)
                self.act(pt, bk[g][:, :], AF.Exp, [], [pk, ("B", g)])
                self.mm(ob[0:65, :], Vh[hs][:, kt, :], pt, kt == 0, kt == NT - 1, [("Vh", hs), pk], [ok])
                if kt == 3 and ii > 0:
                    epilogue(ii - 1)
                if kt == 12:
                    p4_step(ii)
            if ii == len(items) - 1:
                epilogue(ii)
        while p4_state["m"] < n4:
            if p4_state["d"] <= p4_state["m"]:
                p4_dve(p4_state["d"])
                p4_state["d"] += 1
            p4_mm(p4_state["m"])
            p4_state["m"] += 1
        P.barrier()

    def phase4_parts(self, l):
        A, P, I, SC = self.A, self.P, self.inp, self.scr
        bk = self.banks
        wp = A.alloc([4, 128], BF16)
        psc = A.alloc([4], F32)
        inv_first = A.alloc([4, 512], F32)
        inv_last = A.alloc([4, 512], F32)
        U = [A.alloc([528], F32) for _ in range(2)]
        sA = A.alloc([528], F32)
        sB = A.alloc([528], F32)
        pl = A.alloc([512], BF16)
        pst = [A.alloc([512], BF16) for _ in range(2)]
        self.load_cast(wp, I["pool_w"][l].rearrange("g c d -> c g d"), 128, "wp")
        self.ld(psc, I["pool_scale"][l].rearrange("(g p) -> p g", p=128), (), ["psc"], slow=True)
        for g, w in enumerate(POOL_WINDOWS):
            self.memset("pool", inv_first[:, g, :], 1.0 / w, ["inv_first"])
            self.memset("pool", inv_last[:, g, :], 1.0 / w, ["inv_last"])
            for t in range(w // 2):
                self.memset("pool", inv_first[:, g, t:t + 1], 1.0 / (t + w // 2), ["inv_first"])
            for j in range(1, w // 2):
                self.memset("pool", inv_last[:, g, 512 - j:512 - j + 1], 1.0 / (j + w // 2), ["inv_last"])
        nch = self.ns_limit or NS
        its = [(g, w, ch) for g, w in enumerate(POOL_WINDOWS) for ch in range(nch)]

        def dve_part(i):
            g, w, ch = its[i]
            us = i % 2
            t0 = ch * 512
            uk = ("U", us)
            lo = max(t0 - 8, 0)
            hi = min(t0 + 520, S)
            if ch == 0:
                self.memset("pool", U[us][:, 0:8], 0.0, [uk])
            if ch == NS - 1:
                self.memset("pool", U[us][:, 520:528], 0.0, [uk])
            self.ld(U[us][:, lo - (t0 - 8):hi - (t0 - 8)], SC["uT"][g * 128:(g + 1) * 128, lo:hi], ["uT"], [uk])
            cur = U[us]
            curk = uk
            k = 1
            dst = [sA, sB]
            di = 0
            while k < w:
                d_ = dst[di]
                dk = ("s", di)
                self.tt("dve", d_[:, k:528], cur[:, k:528], cur[:, 0:528 - k], ALU.add, [curk], [dk])
                cur, curk = d_, dk
                di ^= 1
                k *= 2
            off = 8 + w // 2 - 1
            sw = cur[:, off:off + 512]
            if ch == 0 or ch == NS - 1:
                inv = inv_first if ch == 0 else inv_last
                other, okey = (sB, ("s", 1)) if cur is sA else (sA, ("s", 0))
                self.tt("dve", other[:, 0:512], sw, inv[:, g, :], ALU.mult, [curk, "inv_first", "inv_last"], [okey])
                self.tt("dve", pl, other[:, 0:512], U[us][:, 8:520], ALU.subtract, [okey, uk], ["pl"])
            else:
                self.stt("dve", pl, sw, 1.0 / w, U[us][:, 8:520], ALU.mult, ALU.subtract, [curk, uk], ["pl"])

        def mm_part(i):
            g, w, ch = its[i]
            us = i % 2
            t0 = ch * 512
            pb = bk[7]
            pk = ("B", 7)
            self.mm(pb[:, :], wp[:, g, :], pl, True, True, ["wp", "pl"], [pk])
            self.ts("dve", pst[us], pb[:, :], psc[:, g:g + 1], None, ALU.mult, None, ["psc"], [("pst", us), pk])
            self.st(SC["pmT"][g * 128:(g + 1) * 128, t0:t0 + 512], pst[us], [("pst", us)], ["pmT"])

        return len(its), dve_part, mm_part

    def phase3(self, l):
        A, P, I, SC = self.A, self.P, self.inp, self.scr
        A.reset(self.const_top)
        bk = self.banks
        if "of" not in SC:
            self.scratch("of", [S, 512], F32)
        tri = A.alloc([4, 64], F32)
        one_c = A.alloc([1], F32)
        wdec = A.alloc([2, 256], F32)
        gn = A.alloc([128], F32)
        dl = [A.alloc([128], F32) for _ in range(2)]
        gq = [A.alloc([4, 128], F32) for _ in range(2)]
        gk = [A.alloc([4, 128], F32) for _ in range(2)]
        gkt = [A.alloc([2, 256], F32) for _ in range(2)]
        vv = [A.alloc([2, 512], BF16) for _ in range(2)]
        of = [A.alloc([2, 512], F32) for _ in range(2)]
        grt = [A.alloc([2, 512], BF16) for _ in range(2)]
        esb = A.alloc([512], F32)
        gp = A.alloc([2, 256], F32)
        E1s = [A.alloc([4, 2, 64], F32) for _ in range(2)]
        E2 = A.alloc([4, 2, 64], F32)
        E3 = A.alloc([2, 256], F32)
        qins = [A.alloc([4, 128], BF16) for _ in range(2)]
        kins = [A.alloc([4, 128], BF16) for _ in range(2)]
        ksts = [A.alloc([2, 256], BF16) for _ in range(2)]
        ATm = A.alloc([4, 64], BF16)
        St = A.alloc([4, 128], F32)
        Sb = A.alloc([4, 128], BF16)
        ost = [A.alloc([2, 512], F32) for _ in range(2)]
        sq = A.alloc([1024], F32)
        ssq = A.alloc([8], F32)
        rs8 = A.alloc([8], F32)
        ob16 = A.alloc([2, 512], BF16)
        gst = [A.alloc([4, 128], BF16) for _ in range(2)]
        self.ld(tri[0:64, :, :], I["c_tri"].rearrange("k a b -> a k b"), (), ["tri"])
        self.memset("pool", one_c, 1.0, ["one_c"])
        for d_ in range(2):
            self.ld(wdec[0:16, d_, :], I["gla_w_dec"][l, d_], (), ["wdec"])
            self.ld(wdec[16:17, d_, :], I["gla_b_dec"][l, d_:d_ + 1, :], (), ["wdec"])
            self.memset("pool", dl[d_][0:17, :], 1.0, [("dl", d_)])
        self.ld(gn[0:64, :], I["gla_norm"][l][None, :].to_broadcast([64, 128]), (), ["gn"])
        LG, BT, CC, AT, OO = bk[0], bk[1], bk[2], bk[3], bk[4]
        BTv = BT[0:64, :].rearrange("p (h n i) -> p h n i", h=4, n=2)
        NG = S // 128
        ng = self.ns_limit or NG
        for dirn in range(2):
            self.memset("pool", St[0:64], 0.0, ["St"])
            self.memset("pool", Sb[0:64], 0.0, ["Sb"])
            inc_i, rest_i = (0, 1) if dirn == 0 else (2, 3)
            groups = list(range(ng)) if dirn == 0 else list(range(NG - 1, NG - 1 - ng, -1))
            def prep(gi, g, stage):
                bs = gi % 2
                t0 = g * 128
                E1, qin, kin, kst = E1s[bs], qins[bs], kins[bs], ksts[bs]
                kE1, kq, kk, kks = ("E1", bs), ("qin", bs), ("kin", bs), ("kst", bs)
                if stage == 'a':
                    self.ld(dl[bs][0:16, :], SC["gdT"][dirn * 16:(dirn + 1) * 16, t0:t0 + 128], ["gdT"], [("dl", bs)])
                    self.ld(gq[bs][0:64], SC["gqT"][:, t0:t0 + 128].rearrange("(h d) t -> d h t", d=64), ["gqT"], [("gq", bs)])
                    self.ld(gk[bs][0:64], SC["gkT"][:, t0:t0 + 128].rearrange("(h d) t -> d h t", d=64), ["gkT"], [("gk", bs)])
                    self.ld(gkt[bs][0:64], SC["gk_tok"][t0:t0 + 128, :].rearrange("(n p) c -> p n c", p=64), ["gk_tok"], [("gkt", bs)])
                    self.ld(vv[bs][0:64], SC["gv"][t0:t0 + 128, :].rearrange("(n p) c -> p n c", p=64), ["gv"], [("vv", bs)])
                    if dirn == 1:
                        self.ld(of[bs][0:64], SC["of"][t0:t0 + 128, :].rearrange("(n p) c -> p n c", p=64), ["of"], [("of", bs)])
                        self.ld(grt[bs][0:64], SC["gr"][t0:t0 + 128, :].rearrange("(n p) c -> p n c", p=64), ["gr"], [("grt", bs)])
                    for n in range(2):
                        self.mm(LG[0:64, n * 256:(n + 1) * 256], dl[bs][0:17, n * 64:(n + 1) * 64], wdec[0:17, dirn, :], True, True, [("dl", bs), "wdec"], [("B", 0)])
                    self.act(esb[0:64, :], LG[0:64, :], AF.Exp, [], ["esb", ("B", 0)], scale=-1.0)
                    self.act(gp[0:64].rearrange("p n c -> p (n c)"), esb[0:64, :], AF.Ln, ["esb", "one_c"], ["gp"], bias=one_c[0:64, 0:1], scale=1.0)
                if stage == 'b':
                    for h in range(4):
                        for n in range(2):
                            self.mm(BTv[:, h, n, :], gp[0:64, n, h * 64:(h + 1) * 64], tri[0:64, inc_i, :], True, True, ["gp", "tri"], [("B", 1)])
                    for n in range(2):
                        self.mm(CC[0:64, n * 256:(n + 1) * 256], tri[0:64, rest_i, :], gp[0:64, n, :], True, True, ["gp", "tri"], [("B", 2)])
                    f = lambda t: t[0:64].rearrange("p a b c -> p (a b c)")
                    self.act(f(E1), BT[0:64, :], AF.Exp, [], [kE1, ("B", 1)], scale=-1.0 / 16)
                    self.act(f(E2), BT[0:64, :], AF.Exp, [], ["E2", ("B", 1)], scale=1.0 / 16)
                    self.act(E3[0:64].rearrange("p n c -> p (n c)"), CC[0:64, :], AF.Exp, [], ["E3", ("B", 2)], scale=-1.0 / 16)
                if stage == 'c':
                    self.tt("dve", qin[0:64], gq[bs][0:64], E1[0:64].rearrange("p h n i -> p h (n i)"), ALU.mult, [("gq", bs), kE1], [kq])
                    self.tt("dve", kin[0:64], gk[bs][0:64], E2[0:64].rearrange("p h n i -> p h (n i)"), ALU.mult, [("gk", bs), "E2"], [kk])
                    self.tt("dve", kst[0:64], gkt[bs][0:64], E3[0:64], ALU.mult, [("gkt", bs), "E3"], [kks])

            def chunks(gi, g, part):
                bs = gi % 2
                t0 = g * 128
                E1, qin, kin, kst = E1s[bs], qins[bs], kins[bs], ksts[bs]
                kE1, kq, kk, kks = ("E1", bs), ("qin", bs), ("kin", bs), ("kst", bs)
                order = (0, 1) if dirn == 0 else (1, 0)
                for ci, n in (enumerate(order) if part != 'tail' else []):
                    if ci != part:
                        continue
                    cs = slice(n * 64, (n + 1) * 64)
                    for h in range(4):
                        self.mm(AT[0:64, h * 64:(h + 1) * 64], kin[0:64, h, cs], qin[0:64, h, cs], True, True, [kk, kq], [("B", 3)])
                    self.tt("dve", ATm[0:64], AT[0:64, 0:256].rearrange("p (h i) -> p h i", h=4), tri[0:64, inc_i:inc_i + 1, :].to_broadcast([64, 4, 64]), ALU.mult, ["tri"], ["ATm", ("B", 3)])
                    for h in range(4):
                        hv = slice(h * 128, (h + 1) * 128)
                        self.mm(OO[0:64, hv], ATm[0:64, h, :], vv[bs][0:64, n, hv], True, False, ["ATm", ("vv", bs)], [("B", 4)])
                        self.mm(OO[0:64, hv], qin[0:64, h, cs], Sb[0:64, h, :], False, True, [kq, "Sb"], [("B", 4)])
                    kvb = bk[5 + (gi * 2 + ci) % 2]
                    kvk = ("B", 5 + (gi * 2 + ci) % 2)
                    for h in range(4):
                        hv = slice(h * 128, (h + 1) * 128)
                        self.mm(kvb[0:64, hv], kst[0:64, n, h * 64:(h + 1) * 64], vv[bs][0:64, n, hv], True, True, [kks, ("vv", bs)], [kvk])
                    dcol = 63 if dirn == 0 else 0
                    for h in range(4):
                        hv = slice(h * 128, (h + 1) * 128)
                        self.stt("dve", St[0:64, h, :], St[0:64, h, :], E1[0:64, h, n, dcol:dcol + 1], kvb[0:64, hv], ALU.mult, ALU.add, [kE1], ["St", kvk])
                    self.cp("act", Sb[0:64].rearrange("p h v -> p (h v)"), St[0:64].rearrange("p h v -> p (h v)"), ["St"], ["Sb"])
                    if dirn == 0:
                        self.cp("act", ost[bs][0:64, n, :], OO[0:64, :], [], [("ost", bs), ("B", 4)])
                    else:
                        self.tt("dve", ost[bs][0:64, n, :], OO[0:64, :], of[bs][0:64, n, :], ALU.add, [("of", bs)], [("ost", bs), ("B", 4)])
                if part == 'tail':
                    if dirn == 0:
                        self.st(SC["of"][t0:t0 + 128, :].rearrange("(n p) c -> p n c", p=64), ost[bs][0:64], [("ost", bs)], ["of"])
                    else:
                        o2 = ost[bs][0:64].rearrange("p n c -> p (n c)")
                        self.tt("dve", sq[0:64, :], o2, o2, ALU.mult, [("ost", bs)], ["sq"])
                        self.P.op("dve", lambda e: e.reduce_sum(out=ssq[0:64, :], in_=sq[0:64, :].rearrange("p (a b) -> p a b", b=128), axis=AX.X), ["sq"], ["ssq"])
                        self.act(rs8[0:64, :], ssq[0:64, :], AF.Ln, ["ssq", "eps_rms"], ["rs8"], bias=self.eps_rms[0:64, 0:1], scale=1.0 / 128)
                        self.act(rs8[0:64, :], rs8[0:64, :], AF.Exp, ["rs8"], ["rs8"], scale=-0.5)
                        o3 = ost[bs][0:64].rearrange("p n (h v) -> p (n h) v", v=128)
                        self.tt("dve", o3, o3, rs8[0:64, :, None].to_broadcast([64, 8, 128]), ALU.mult, ["rs8"], [("ost", bs)])
                        self.tt("dve", o3, o3, gn[0:64, None, :].to_broadcast([64, 8, 128]), ALU.mult, ["gn"], [("ost", bs)])
                        self.tt("dve", ob16[0:64], ost[bs][0:64], grt[bs][0:64], ALU.mult, [("ost", bs), ("grt", bs)], ["ob16"])
                        TP = bk[7][:, 0:256].bitcast(BF16).rearrange("p (b t) -> p b t", b=4)
                        for n in range(2):
                            for b_ in range(4):
                                self.tr(TP[:, b_, n * 64:(n + 1) * 64], ob16[0:64, n, b_ * 128:(b_ + 1) * 128], self.ident_b[0:64, 0:64], ["ob16", "ident_b"], [("B", 7)])
                        self.cp("act", gst[bs], TP, [], [("gst", bs), ("B", 7)])
                        self.st(SC["glT"][:, t0:t0 + 128].rearrange("(b p) t -> p b t", p=128), gst[bs], [("gst", bs)], ["glT"])

            for st_ in "abc":
                prep(0, groups[0], st_)
            for gi, g in enumerate(groups):
                nxt = gi + 1 < len(groups)
                if nxt:
                    prep(gi + 1, groups[gi + 1], "a")
                chunks(gi, g, 0)
                if nxt:
                    prep(gi + 1, groups[gi + 1], "b")
                chunks(gi, g, 1)
                if nxt:
                    prep(gi + 1, groups[gi + 1], "c")
                chunks(gi, g, "tail")
        P.barrier()

    def phase5(self, l):
        A, P, I, SC = self.A, self.P, self.inp, self.scr
        A.reset(self.const_top)
        bk = self.banks
        self.affT = A.alloc([S], F32)
        self.p6_base = A.off
        wg = A.alloc([8, 3072], BF16)
        wup = [A.alloc([4, D], BF16) for _ in range(3)]
        wout = A.alloc([8, D], BF16)
        rw = A.alloc([8, 16], F32)
        bgr = A.alloc([3072], BF16)
        ones1 = A.alloc([128], BF16)
        gam = A.alloc([D], F32)
        bet = A.alloc([D], F32)
        self.ln_tmp = A.alloc([D], F32)
        ht = [A.alloc([D], F32) for _ in range(2)]
        hb = A.alloc([D], BF16)
        hT = A.alloc([8, 128], BF16)
        gates = A.alloc([3072], F32)
        xT = [[A.alloc([4, 128], BF16)] * 2 for _ in range(3)]
        mrg = A.alloc([D], F32)
        tmpm = A.alloc([D], F32)
        mb = A.alloc([D], BF16)
        mT = A.alloc([8, 128], BF16)
        h1b = A.alloc([D], BF16)
        h1a = A.alloc([D], F32)
        h1buf = A.alloc([D], F32)
        h1T = A.alloc([8, 128], F32)
        st6 = A.alloc([2, 6], F32); mv = A.alloc([2], F32); rstd = A.alloc([1], F32); nb = A.alloc([1], F32)
        rmax = A.alloc([1], F32); rsum = A.alloc([1], F32); aff = A.alloc([16], F32)
        self.load_cast(wg, I["w_in"][l][:, C_GATE:C_GATE + 3072].rearrange("(c p) n -> p c n", p=128), 3072, "wg")
        for i, nm in enumerate(("w_up_a", "w_up_b", "w_up_c")):
            self.load_cast(wup[i], I[nm][l].rearrange("(c p) n -> p c n", p=128), D, ("wup", i))
        self.load_cast(wout, I["w_out"][l].rearrange("(c p) n -> p c n", p=128), D, "wout")
        self.ld(rw, I["router_w"][l].rearrange("(c p) e -> p c e", p=128), (), ["rw"])
        for a_ in range(0, 3072, 1536):
            self.P.dma(lambda e, a_=a_: e.dma_start(out=bgr[0:1, a_:a_ + 1536], in_=I["b_gate"][l:l + 1, a_:a_ + 1536]), (), ["bgr"], q="pool")
        self.memset("pool", ones1, 1.0, ["ones1"])
        self.ld(gam, I["ln1_g"][l][None, :].to_broadcast([128, D]), (), ["lnpar"])
        self.ld(bet, I["ln1_b"][l][None, :].to_broadcast([128, D]), (), ["lnpar"])
        srcs = ("pmT", "mlT", "glT")
        pT = bk[0][:, :].bitcast(BF16).rearrange("p (a b) -> p a b", a=8)
        nt = (self.ns_limit or NS) * 4

        def X(t, stage):
            bs = t % 2
            r0 = t * 128
            if stage == 1:
                self.ld(ht[bs], SC["hbuf"][r0:r0 + 128, :], ["hbuf"], [("ht", bs)])
                for i in range(3):
                    self.ld(xT[i][bs], SC[srcs[i]][:, r0:r0 + 128].rearrange("(c p) t -> p c t", p=128), [srcs[i]], [("xT", i)])
                self.cp("act", hb, ht[bs], [("ht", bs)], ["hb"])
                for c in range(8):
                    self.tr(pT[:, c, :], hb[:, c * 128:(c + 1) * 128], self.ident_b, ["hb", "ident_b"], [("B", 0)])
                self.cp("dve", hT, pT, [], ["hT", ("B", 0)])
            if stage == 2:
                for gI in range(6):
                    pb = bk[1 + gI % 2]
                    pk = ("B", 1 + gI % 2)
                    cs = slice(gI * 512, (gI + 1) * 512)
                    for c in range(8):
                        self.mm(pb[:, :], hT[:, c, :], wg[:, c, cs], c == 0, False, ["hT", "wg"], [pk])
                    self.mm(pb[:, :], ones1[0:1, :], bgr[0:1, cs], False, True, ["ones1", "bgr"], [pk])
                    self.act(gates[:, cs], pb[:, :], AF.Sigmoid, [], [("gates", gI), pk])
            if stage == 3:
                for i in range(3):
                    for g2 in range(2):
                        pb = bk[3 + (i * 2 + g2) % 2]
                        pk = ("B", 3 + (i * 2 + g2) % 2)
                        cs = slice(g2 * 512, (g2 + 1) * 512)
                        for c in range(4):
                            self.mm(pb[:, :], xT[i][bs][:, c, :], wup[i][:, c, cs], c == 0, c == 3, [("xT", i), ("wup", i)], [pk])
                        gsl = gates[:, i * D + g2 * 512:i * D + (g2 + 1) * 512]
                        gk_ = ("gates", i * 2 + g2)
                        if i == 0:
                            self.tt("dve", mrg[:, cs], pb[:, :], gsl, ALU.mult, [gk_], [("mrg", g2), pk])
                        else:
                            self.tt("dve", tmpm[:, cs], pb[:, :], gsl, ALU.mult, [gk_], [("tmpm", g2), pk])
                            if i == 1:
                                self.tt("dve", mrg[:, cs], mrg[:, cs], tmpm[:, cs], ALU.add, [("tmpm", g2)], [("mrg", g2)])
                            else:
                                self.tt("dve", mb[:, cs], mrg[:, cs], tmpm[:, cs], ALU.add, [("tmpm", g2), ("mrg", g2)], [("mb", g2)])
            if stage == 4:
                for c in range(8):
                    self.tr(pT[:, c, :], mb[:, c * 128:(c + 1) * 128], self.ident_b, [("mb", c // 4), "ident_b"], [("B", 0)])
                self.cp("dve", mT, pT, [], ["mT", ("B", 0)])
                for g2 in range(2):
                    pb = bk[5 + g2]
                    pk = ("B", 5 + g2)
                    cs = slice(g2 * 512, (g2 + 1) * 512)
                    for c in range(8):
                        self.mm(pb[:, :], mT[:, c, :], wout[:, c, cs], c == 0, c == 7, ["mT", "wout"], [pk])
                    self.stt("dve", ht[bs][:, cs], ht[bs][:, cs], DN_ALPHA, pb[:, :], ALU.mult, ALU.add, [], [("ht", bs), pk])

        def Y(t, stage):
            bs = t % 2
            r0 = t * 128
            h1 = h1buf
            lg = bk[7]
            if stage == 1:
                self.layernorm_tile(ht[bs], h1, h1b, gam, bet, st6, mv, rstd, nb, "ln", [("ht", bs)], ["h1"], ["h1b"], norm_eng="dve")
                self.st(SC["h1"][r0:r0 + 128, :], h1, ["h1"], ["h1d"])
                self.st(SC["h1b"][r0:r0 + 128, :], h1b, ["h1b"], ["h1bd"])
            if stage == 2:
                pTf = [bk[7][:, :].rearrange("p (a b) -> p a b", a=4), bk[0][:, :].rearrange("p (a b) -> p a b", a=4)]
                for c in range(8):
                    self.tr(pTf[c // 4][:, c % 4, :], h1[:, c * 128:(c + 1) * 128], self.ident_f, ["h1", "ident_f"], [("B", 7 if c < 4 else 0)])
                self.cp("dve", h1T[:, 0:4, :], pTf[0], [], [("h1T", 0), ("B", 7)])
                self.cp("dve", h1T[:, 4:8, :], pTf[1], [], [("h1T", 1), ("B", 0)])
                lg = bk[7]
                for c in range(8):
                    self.mm(lg[:, 0:16], h1T[:, c, :], rw[:, c, :], c == 0, c == 7, [("h1T", c // 4), "rw"], [("B", 7)])
            if stage == 3:
                self.act(h1a, h1, AF.Identity, ["h1"], ["h1a"], scale=DN_ALPHA)
                self.st(SC["acc"][r0:r0 + 128, :], h1a, ["h1a"], ["acc"])
                self.P.op("dve", lambda e: e.reduce_max(out=rmax, in_=lg[:, 0:16], axis=AX.X), [], ["rmax", ("B", 7)])
                self.ts("dve", rmax, rmax, -1.0, None, ALU.mult, None, ["rmax"], ["rmax"])
                self.act(aff, lg[:, 0:16], AF.Exp, ["rmax"], ["aff", "rsum", ("B", 7)], bias=rmax[:, 0:1], scale=1.0, accum_out=rsum[:, 0:1])
                self.P.op("dve", lambda e: e.reciprocal(out=rsum, in_=rsum), ["rsum"], ["rsum"])
                self.ts("dve", aff, aff, rsum[:, 0:1], None, ALU.mult, None, ["rsum"], ["aff"])
                self.tr(bk[7][0:16, 128:256], aff, self.ident_f, ["aff", "ident_f"], [("B", 7)])
                self.cp("dve", self.affT[0:16, r0:r0 + 128], bk[7][0:16, 128:256], [], ["affT", ("B", 7)])

        for st_ in (1, 2, 3, 4):
            X(0, st_)
        for t in range(nt):
            nxt = t + 1 < nt
            if nxt:
                X(t + 1, 1)
            Y(t, 1)
            if nxt:
                X(t + 1, 2)
            Y(t, 2)
            if nxt:
                X(t + 1, 3)
            Y(t, 3)
            if nxt:
                X(t + 1, 4)
        P.barrier()

    def phase6(self, l):
        A, P = self.A, self.P
        A.reset(self.p6_base)
        bk = self.banks
        CAP = S // 8
        NJ = CAP // 128
        work = A.alloc([S], F32)
        vals = A.alloc([CAP], F32)
        idxu = A.alloc([CAP], U32)
        idxf = A.alloc([CAP], F32)
        af = self.affT[0:16, :]
        cur = af
        curk = "affT"
        for it in range(CAP // 8):
            v8 = vals[0:16, it * 8:(it + 1) * 8]
            self.P.op("dve", lambda e, v8=v8, cur=cur: e.max(out=v8, in_=cur), [curk], ["vals"])
            i8 = idxu[0:16, it * 8:(it + 1) * 8]
            self.P.op("dve", lambda e, v8=v8, cur=cur, i8=i8: e.max_index(out=i8, in_max=v8, in_values=cur), [curk, "vals"], ["idxu"])
            self.P.op("dve", lambda e, v8=v8, cur=cur: e.match_replace(out=work[0:16, :], in_to_replace=v8, in_values=cur, imm_value=-1.0), [curk, "vals"], ["work"])
            cur = work[0:16, :]
            curk = "work"
        self.cp("dve", idxf[0:16, :], idxu[0:16, :], ["idxu"], ["idxf"])
        for j in range(NJ):
            pj = bk[j % 2]
            pk = ("B", j % 2)
            self.tr(pj[:, 0:16], idxf[0:16, j * 128:(j + 1) * 128], self.ident_f[0:16, 0:16], ["idxf", "ident_f"], [pk])
            self.tr(pj[:, 16:32], vals[0:16, j * 128:(j + 1) * 128], self.ident_f[0:16, 0:16], ["vals", "ident_f"], [pk])
            self.cp("dve", self.IDX[:, j, :], pj[:, 0:16], [], ["IDX", pk])
            self.cp("dve", self.GATE[:, j, :], pj[:, 16:32], [], ["GATE", pk])
        P.barrier()

    def phase7(self, l):
        A, P, I, SC = self.A, self.P, self.inp, self.scr
        A.reset(self.const_top)
        bk = self.banks
        CAP = S // 8
        NJ = CAP // 128
        wgh = [A.alloc([8, 1024], BF16) for _ in range(2)]
        wuh = [A.alloc([8, 1024], BF16) for _ in range(2)]
        wdt = A.alloc([16, D], BF16)
        xg = A.alloc([NJ, D], BF16)
        xeT = A.alloc([8, CAP], BF16)
        hidT = A.alloc([16, CAP], BF16)
        sil = [A.alloc([512], F32) for _ in range(2)]
        ysb = [A.alloc([D], F32) for _ in range(2)]
        pT = bk[0][:, :].bitcast(BF16).rearrange("p (a b) -> p a b", a=8)
        nexp = self.ns_limit and min(16, self.ns_limit * 2) or 16
        NH = (CAP + 511) // 512
        HW_ = CAP // NH

        def ld_gu(ex, hf):
            fs = slice(hf * 1024, (hf + 1) * 1024)
            self.load_cast(wgh[hf], I["exp_w_gate"][l, ex][:, fs].rearrange("(c p) n -> p c n", p=128), 1024, ("wg", hf))
            self.load_cast(wuh[hf], I["exp_w_up"][l, ex][:, fs].rearrange("(c p) n -> p c n", p=128), 1024, ("wu", hf))

        def ld_d(ex):
            self.load_cast(wdt, I["exp_w_down"][l, ex].rearrange("(c p) n -> p c n", p=128), D, "wdt")

        def gathers(ex):
            for j in range(NJ):
                self.P.dma(lambda e, j=j, ex=ex: e.indirect_dma_start(
                    out=xg[:, j, :], out_offset=None, in_=SC["h1b"][:, :],
                    in_offset=bass.IndirectOffsetOnAxis(ap=self.IDX[:, j, ex:ex + 1], axis=0)),
                    ["IDX", "h1bd"], [("xg", j)], q="pool")

        ld_gu(0, 0)
        ld_gu(0, 1)
        gathers(0)
        ld_d(0)
        cnt = 0
        for ex in range(nexp):
            for j in range(NJ):
                for c in range(8):
                    self.tr(pT[:, c, :], xg[:, j, c * 128:(c + 1) * 128], self.ident_b, [("xg", j), "ident_b"], [("B", 0)])
                self.cp("dve", xeT[:, :, j * 128:(j + 1) * 128], pT, [], ["xeT", ("B", 0)])
            for fb in range(16):
                fh = fb // 8
                fs = slice((fb % 8) * 128, (fb % 8 + 1) * 128)
                for hf in range(NH):
                    ss_ = slice(hf * HW_, (hf + 1) * HW_)
                    gb, ub = bk[1 + cnt % 2], bk[3 + cnt % 2]
                    gk_, uk_ = ("B", 1 + cnt % 2), ("B", 3 + cnt % 2)
                    sl_ = sil[cnt % 2]
                    sk_ = ("sil", cnt % 2)
                    cnt += 1
                    for c in range(8):
                        self.mm(gb[:, 0:HW_], wgh[fh][:, c, fs], xeT[:, c, ss_], c == 0, c == 7, [("wg", fh), "xeT"], [gk_])
                    for c in range(8):
                        self.mm(ub[:, 0:HW_], wuh[fh][:, c, fs], xeT[:, c, ss_], c == 0, c == 7, [("wu", fh), "xeT"], [uk_])
                    self.act(sl_[:, 0:HW_], gb[:, 0:HW_], AF.Silu, [], [sk_, gk_])
                    self.tt("dve", hidT[:, fb, ss_], ub[:, 0:HW_], sl_[:, 0:HW_], ALU.mult, [sk_], ["hidT", uk_])
                if fb % 8 == 7 and ex + 1 < nexp:
                    ld_gu(ex + 1, fh)
            if ex + 1 < nexp:
                gathers(ex + 1)
            for j in range(NJ):
                yb = ysb[j % 2]
                yk = ("ysb", j % 2)
                for g2 in range(2):
                    pb = bk[5 + g2]
                    pk = ("B", 5 + g2)
                    cs = slice(g2 * 512, (g2 + 1) * 512)
                    for fb in range(16):
                        self.mm(pb[:, :], hidT[:, fb, j * 128:(j + 1) * 128], wdt[:, fb, cs], fb == 0, fb == 15, ["hidT", "wdt"], [pk])
                    self.act(yb[:, cs], pb[:, :], AF.Identity, ["GATE"], [yk, pk], scale=self.GATE[:, j, ex:ex + 1])
                self.P.dma(lambda e, j=j, ex=ex, yb=yb: e.indirect_dma_start(
                    out=SC["acc"][:, :], out_offset=bass.IndirectOffsetOnAxis(ap=self.IDX[:, j, ex:ex + 1], axis=0),
                    in_=yb, in_offset=None, compute_op=ALU.add),
                    ["IDX", yk], ["acc"], q="pool")
            if ex + 1 < nexp:
                ld_d(ex + 1)
        P.barrier()

    def phase8(self, l, final):
        A, P, I, SC = self.A, self.P, self.inp, self.scr
        A.reset(self.const_top)
        gam = A.alloc([D], F32)
        bet = A.alloc([D], F32)
        self.ln_tmp = A.alloc([D], F32)
        at = [A.alloc([4, D], F32) for _ in range(2)]
        st6 = A.alloc([2, 6], F32); mv = A.alloc([2], F32); rstd = A.alloc([1], F32); nb = A.alloc([1], F32)
        self.ld(gam, I["ln2_g"][l][None, :].to_broadcast([128, D]), (), ["lnpar"])
        self.ld(bet, I["ln2_b"][l][None, :].to_broadcast([128, D]), (), ["lnpar"])
        dst = self.out if final else SC["hbuf"]
        for s_ in range(self.ns_limit or NS):
            bs = s_ % 2
            t0 = s_ * 512
            ak = ("at", bs)
            self.ld(at[bs], SC["acc"][t0:t0 + 512, :].rearrange("(j p) d -> p j d", p=128), ["acc"], [ak])
            for j in range(4):
                self.layernorm_tile(at[bs][:, j, :], at[bs][:, j, :], None, gam, bet, st6, mv, rstd, nb, "ln", [ak], [ak], None)
            self.st(dst[t0:t0 + 512, :].rearrange("(j p) d -> p j d", p=128), at[bs], [ak], ["dst"], is_out=final)
        P.barrier()

    def build(self):
        self.declare()
        self.consts()
        self.phase0()
        for l in self.layers:
            self.phase1(l)
            self.phase2(l)
            self.phase3(l)
            self.phase5(l)
            self.phase6(l)
            self.phase7(l)
            self.phase8(l, final=(l == self.layers[-1]))
        return self.P.finalize()

def _consts():
    half = 16
    freq = (10000.0 ** (-np.arange(half, dtype=np.float32) / half)).astype(np.float32)
    tp = np.arange(64)[:, None]
    ii = np.arange(64)[None, :]
    tri = np.stack([tp <= ii, tp > ii, tp >= ii, tp < ii]).astype(np.float32)
    return {"c_ident": np.eye(128, dtype=np.float32), "c_freq": freq, "c_tri": tri}


_CACHE = {}


def kernel(**inputs):
    if "nc" not in _CACHE:
        k = K()
        _CACHE["nc"] = k.build()
        _CACHE["names"] = list(k.inp.keys())
    nc = _CACHE["nc"]
    cst = _consts()
    maps = []
    for b in range(2):
        m = {}
        for n in _CACHE["names"]:
            if n == "x":
                m[n] = np.ascontiguousarray(inputs["x"][b], dtype=np.float32)
            elif n == "positions":
                m[n] = np.ascontiguousarray(inputs["positions"][b], dtype=np.int32)
            elif n in cst:
                m[n] = cst[n]
            else:
                m[n] = np.ascontiguousarray(inputs[n], dtype=np.float32)
        maps.append(m)
    res = run_bass_kernel_spmd(nc, maps, core_ids=[0, 1])
    return np.stack([np.asarray(res.results[b]["out"], dtype=np.float32) for b in range(2)], axis=0)
```
